# Optimizing a Trainium2 kernel written in Bass

```python
import math
import jax, jax.numpy as jnp
from jax import lax
import numpy as np

D_MODEL = 1024
BATCH = 4
SEQ = 8192
DEPTH = 4

CHUNK = 64
Q_BLOCK = 128
EPS = 1e-6

FOX_HEADS = 4
FOX_DIM = 128
GDN_HEADS = 4
GDN_DK = 128
GDN_DV = 128
GDN_CONV = 4
SB_HEADS = 4
SB_DIM = 128
MEM_TOKENS = 256
MEM_HEADS = 4
MEM_DIM = 128
D_FF = 2816
FFN_CONV = 3
N_BRANCH = 3

FOX_W = FOX_HEADS * FOX_DIM
GDN_KW = GDN_HEADS * GDN_DK
GDN_VW = GDN_HEADS * GDN_DV
SB_W = SB_HEADS * SB_DIM
MEM_W = MEM_HEADS * MEM_DIM

IN_SPLITS = (
    FOX_W, FOX_W, FOX_W, FOX_HEADS,
    GDN_KW, GDN_KW, GDN_VW, GDN_HEADS, GDN_HEADS, GDN_VW,
    SB_W, SB_W, SB_W,
    N_BRANCH * D_MODEL,
)
N_IN = sum(IN_SPLITS)

kernel_name = "hybrid_fox_gdn_stickbreak_encoder"


def rmsnorm(x, g):
    xf = x.astype(jnp.float32)
    y = xf * lax.rsqrt(jnp.mean(xf * xf, axis=-1, keepdims=True) + EPS)
    return (y * g.astype(jnp.float32)).astype(x.dtype)


def l2norm(x):
    xf = x.astype(jnp.float32)
    return xf * lax.rsqrt(jnp.sum(xf * xf, axis=-1, keepdims=True) + EPS)


def heads(x, n):
    return x.reshape(x.shape[:-1] + (n, -1))


def causal_dwconv(x, w):
    width, ch = w.shape
    return lax.conv_general_dilated(
        x, w[:, None, :].astype(x.dtype), window_strides=(1,), padding=[(width - 1, 0)],
        dimension_numbers=("NWC", "WIO", "NWC"), feature_group_count=ch)


def split_projection(p):
    points = [int(s) for s in np.cumsum(IN_SPLITS)[:-1]]
    return jnp.split(p, points, axis=-1)


def fox_attention(q, k, v, logf):
    B, S, H, dh = q.shape
    c = jnp.cumsum(logf, axis=1).transpose(0, 2, 1)
    qh = q.transpose(0, 2, 1, 3)
    kh = k.transpose(0, 2, 1, 3)
    vh = v.transpose(0, 2, 1, 3)
    scale = dh ** -0.5
    outs = []
    for i in range(S // Q_BLOCK):
        lo, hi = i * Q_BLOCK, (i + 1) * Q_BLOCK
        s = (jnp.einsum("bhqd,bhkd->bhqk", qh[:, :, lo:hi], kh[:, :, :hi]).astype(jnp.float32) * scale
             + (c[:, :, lo:hi, None] - c[:, :, None, :hi]))
        mask = jnp.arange(hi)[None, :] <= jnp.arange(lo, hi)[:, None]
        p = jax.nn.softmax(jnp.where(mask, s, -jnp.inf), axis=-1).astype(vh.dtype)
        outs.append(jnp.einsum("bhqk,bhkd->bhqd", p, vh[:, :, :hi]))
    o = jnp.concatenate(outs, axis=2)
    return o.transpose(0, 2, 1, 3).reshape(B, S, H * dh)


def stick_breaking_attention(q, k, v):
    B, S, H, dh = q.shape
    qh = q.transpose(0, 2, 1, 3)
    kh = k.transpose(0, 2, 1, 3)
    vh = v.transpose(0, 2, 1, 3)
    scale = dh ** -0.5
    idx = jnp.arange(Q_BLOCK)
    tri_in = (idx[:, None] >= idx[None, :]).astype(jnp.float32)
    outs = []
    for i in range(S // Q_BLOCK):
        lo, hi = i * Q_BLOCK, (i + 1) * Q_BLOCK
        nk = i + 1
        z = jnp.einsum("bhqd,bhkd->bhqk", qh[:, :, lo:hi], kh[:, :, :hi]).astype(jnp.float32) * scale
        mask = jnp.arange(hi)[None, :] < jnp.arange(lo, hi)[:, None]
        log_keep = jnp.where(mask, -jax.nn.softplus(z), 0.0)
        lk = log_keep.reshape(B, H, Q_BLOCK, nk, Q_BLOCK)
        within = jnp.einsum("bhqnj,jk->bhqnk", lk, tri_in)
        tot = jnp.sum(lk, axis=-1)
        blk = jnp.arange(nk)
        tri_blk = (blk[:, None] > blk[None, :]).astype(jnp.float32)
        after = jnp.einsum("bhqm,mn->bhqn", tot, tri_blk)
        rc = (within + after[..., None]).reshape(B, H, Q_BLOCK, hi)
        a = jnp.exp(jnp.where(mask, z + rc, -jnp.inf)).astype(vh.dtype)
        outs.append(jnp.einsum("bhqk,bhkd->bhqd", a, vh[:, :, :hi]))
    o = jnp.concatenate(outs, axis=2)
    return o.transpose(0, 2, 1, 3).reshape(B, S, H * dh)


def gated_delta_rule(q, k, v, g, beta):
    B, T, H, dk = q.shape
    dv = v.shape[-1]
    N = T // CHUNK
    f32 = jnp.float32

    def chunks(a):
        a = a.astype(f32).reshape((B, N, CHUNK, H) + a.shape[3:])
        return jnp.moveaxis(a, (1, 3), (0, 2))

    qc = chunks(q) * dk ** -0.5
    kc = chunks(k)
    vc = chunks(v)
    bc = chunks(beta)
    gc = jnp.cumsum(chunks(g), axis=-1)
    idx = jnp.arange(CHUNK)
    causal = idx[:, None] >= idx[None, :]
    strict = idx[:, None] > idx[None, :]
    decay = jnp.exp(jnp.where(causal, gc[..., :, None] - gc[..., None, :], -jnp.inf))
    kk = jnp.einsum("nbhcd,nbhed->nbhce", kc, kc)
    a_mat = jnp.where(strict, bc[..., :, None] * kk * decay, 0.0) + jnp.eye(CHUNK, dtype=f32)
    rhs = jnp.concatenate([vc * bc[..., None], kc * (bc * jnp.exp(gc))[..., None]], axis=-1)
    sol = lax.linalg.triangular_solve(a_mat, rhs, left_side=True, lower=True)
    u, w = sol[..., :dv], sol[..., dv:]
    attn = jnp.where(causal, jnp.einsum("nbhcd,nbhed->nbhce", qc, kc) * decay, 0.0)
    g_last = gc[..., -1]
    k_tail = kc * jnp.exp(g_last[..., None] - gc)[..., None]

    def step(state, xs):
        q_n, u_n, w_n, g_n, attn_n, kt_n, gl_n = xs
        v_new = u_n - jnp.einsum("bhck,bhkv->bhcv", w_n, state)
        o = (jnp.einsum("bhck,bhkv->bhcv", q_n * jnp.exp(g_n)[..., None], state)
             + jnp.einsum("bhce,bhev->bhcv", attn_n, v_new))
        state = state * jnp.exp(gl_n)[..., None, None] + jnp.einsum("bhck,bhcv->bhkv", kt_n, v_new)
        return state, o

    s0 = jnp.zeros((B, H, dk, dv), f32)
    _, o = lax.scan(step, s0, (qc, u, w, gc, attn, k_tail, g_last))
    return jnp.moveaxis(o, (0, 2), (1, 3)).reshape(B, T, H, dv)


def memory_cross_attention(h, m, w_q, w_kv, q_g, k_g, w_o):
    B, S, _ = h.shape
    q = rmsnorm(heads(h @ w_q, MEM_HEADS), q_g)
    k, v = jnp.split(m @ w_kv, 2, axis=-1)
    k = rmsnorm(heads(k, MEM_HEADS), k_g)
    v = heads(v, MEM_HEADS)
    s = jnp.einsum("bqhd,bkhd->bhqk", q, k).astype(jnp.float32) * MEM_DIM ** -0.5
    p = jax.nn.softmax(s, axis=-1).astype(v.dtype)
    o = jnp.einsum("bhqk,bkhd->bqhd", p, v).reshape(B, S, MEM_W)
    return o @ w_o


def conv_ffn(h, w_up, conv_w, conv_b, w_down):
    u = causal_dwconv(h @ w_up, conv_w) + conv_b
    a, b = jnp.split(u, 2, axis=-1)
    return (jax.nn.silu(a) * b) @ w_down


def setup_inputs(seed: int = 0) -> dict:
    key = jax.random.key(seed)
    ks = iter(jax.random.split(key, 40))
    L, D = DEPTH, D_MODEL
    f32 = jnp.float32

    def nrm(shape, scale):
        return jax.random.normal(next(ks), shape, f32) * scale

    def gain(shape):
        return 1.0 + 0.02 * jax.random.normal(next(ks), shape, f32)

    x = nrm((BATCH, SEQ, D), 1.0)
    mem = nrm((BATCH, MEM_TOKENS, D), 1.0)
    norm_mix = gain((L, D))
    w_in = nrm((L, D, N_IN), D ** -0.5)
    fox_fbias = jax.random.uniform(next(ks), (L, FOX_HEADS), f32, minval=1.0, maxval=5.0)
    fox_qnorm = gain((L, FOX_DIM))
    fox_knorm = gain((L, FOX_DIM))
    gdn_conv = nrm((L, GDN_CONV, 2 * GDN_KW + GDN_VW), GDN_CONV ** -0.5)
    gdn_a_log = jnp.log(jax.random.uniform(next(ks), (L, GDN_HEADS), f32, minval=1.0, maxval=16.0))
    dt = jnp.exp(jax.random.uniform(next(ks), (L, GDN_HEADS), f32,
                                    minval=math.log(1e-3), maxval=math.log(1e-1)))
    gdn_dt_bias = dt + jnp.log(-jnp.expm1(-dt))
    gdn_onorm = gain((L, GDN_DV))
    gate_bias = nrm((L, N_BRANCH * D), 0.01)
    w_oa = nrm((L, FOX_W, D), FOX_W ** -0.5)
    w_ob = nrm((L, GDN_VW, D), GDN_VW ** -0.5)
    w_oc = nrm((L, SB_W, D), SB_W ** -0.5)
    w_out = nrm((L, D, D), D ** -0.5)
    norm_xq = gain((L, D))
    norm_mem = gain((L, D))
    w_mq = nrm((L, D, MEM_W), D ** -0.5)
    w_mkv = nrm((L, D, 2 * MEM_W), D ** -0.5)
    mq_norm = gain((L, MEM_DIM))
    mk_norm = gain((L, MEM_DIM))
    w_mo = nrm((L, MEM_W, D), MEM_W ** -0.5)
    norm_ffn = gain((L, D))
    w_up = nrm((L, D, 2 * D_FF), D ** -0.5)
    ffn_conv = nrm((L, FFN_CONV, 2 * D_FF), FFN_CONV ** -0.5)
    ffn_conv_b = nrm((L, 2 * D_FF), 0.01)
    w_down = nrm((L, D_FF, D), D_FF ** -0.5)
    return {
        "x": x, "mem": mem, "norm_mix": norm_mix, "w_in": w_in,
        "fox_fbias": fox_fbias, "fox_qnorm": fox_qnorm, "fox_knorm": fox_knorm,
        "gdn_conv": gdn_conv, "gdn_a_log": gdn_a_log, "gdn_dt_bias": gdn_dt_bias,
        "gdn_onorm": gdn_onorm, "gate_bias": gate_bias,
        "w_oa": w_oa, "w_ob": w_ob, "w_oc": w_oc, "w_out": w_out,
        "norm_xq": norm_xq, "norm_mem": norm_mem, "w_mq": w_mq, "w_mkv": w_mkv,
        "mq_norm": mq_norm, "mk_norm": mk_norm, "w_mo": w_mo,
        "norm_ffn": norm_ffn, "w_up": w_up, "ffn_conv": ffn_conv,
        "ffn_conv_b": ffn_conv_b, "w_down": w_down,
    }


def reference(x, mem, norm_mix, w_in, fox_fbias, fox_qnorm, fox_knorm, gdn_conv, gdn_a_log,
              gdn_dt_bias, gdn_onorm, gate_bias, w_oa, w_ob, w_oc, w_out, norm_xq, norm_mem,
              w_mq, w_mkv, mq_norm, mk_norm, w_mo, norm_ffn, w_up, ffn_conv, ffn_conv_b, w_down):
    B, S, D = x.shape
    f32 = jnp.float32
    for l in range(DEPTH):
        h = rmsnorm(x, norm_mix[l])
        (fq, fk, fv, ff, gq, gk, gv, gb, ga, gz, sq, sk, sv, gates) = split_projection(h @ w_in[l])

        fq = rmsnorm(heads(fq, FOX_HEADS), fox_qnorm[l])
        fk = rmsnorm(heads(fk, FOX_HEADS), fox_knorm[l])
        logf = jax.nn.log_sigmoid((ff + fox_fbias[l]).astype(f32))
        ya = fox_attention(fq, fk, heads(fv, FOX_HEADS), logf)

        qkv = jax.nn.silu(causal_dwconv(jnp.concatenate([gq, gk, gv], axis=-1), gdn_conv[l]))
        cq, ck, cv = jnp.split(qkv, [GDN_KW, 2 * GDN_KW], axis=-1)
        beta = jax.nn.sigmoid(gb.astype(f32))
        g_log = -jnp.exp(gdn_a_log[l].astype(f32)) * jax.nn.softplus((ga + gdn_dt_bias[l]).astype(f32))
        o = gated_delta_rule(l2norm(heads(cq, GDN_HEADS)), l2norm(heads(ck, GDN_HEADS)),
                             heads(cv, GDN_HEADS), g_log, beta)
        yb = (rmsnorm(o, gdn_onorm[l]) * jax.nn.silu(heads(gz, GDN_HEADS).astype(f32)))
        yb = yb.astype(x.dtype).reshape(B, S, GDN_VW)

        yc = stick_breaking_attention(heads(sq, SB_HEADS), heads(sk, SB_HEADS), heads(sv, SB_HEADS))

        g = jax.nn.sigmoid((gates + gate_bias[l]).astype(f32)).astype(x.dtype).reshape(B, S, N_BRANCH, D)
        mixed = (g[..., 0, :] * (ya @ w_oa[l]) + g[..., 1, :] * (yb @ w_ob[l])
                 + g[..., 2, :] * (yc @ w_oc[l]))
        x = x + mixed @ w_out[l]

        x = x + memory_cross_attention(rmsnorm(x, norm_xq[l]), rmsnorm(mem, norm_mem[l]),
                                       w_mq[l], w_mkv[l], mq_norm[l], mk_norm[l], w_mo[l])

        x = x + conv_ffn(rmsnorm(x, norm_ffn[l]), w_up[l], ffn_conv[l], ffn_conv_b[l], w_down[l])
    return x
```

```python
import numpy as np
from contextlib import ExitStack
import concourse.bass as bass
import concourse.mybir as mybir
from concourse.bass_utils import run_bass_kernel_spmd

F32 = mybir.dt.float32
BF16 = mybir.dt.bfloat16
ALU = mybir.AluOpType
AF = mybir.ActivationFunctionType

D = 1024
NH = 4
DH = 128
EPS = 1e-6
MEMT = 256
DFF = 2816
OFF = dict(FQ=0, FK=512, FV=1024, FF=1536, GQ=1540, GK=2052, GV=2564, GB=3076, GA=3080,
           GZ=3084, SQ=3596, SK=4108, SV=4620, GT=5132, END=8204)
SCALE = DH ** -0.5
NEG = -30000.0

CV = {}
_o = 0
for _n, _w in [("norm_mix", 8), ("gate_bias", 24), ("fox_qnorm", 1), ("fox_knorm", 1),
               ("gdn_conv", 48), ("gdn_onorm", 1), ("norm_xq", 8), ("norm_mem", 8),
               ("mq_norm", 1), ("mk_norm", 1), ("norm_ffn", 8), ("ffn_conv", 132),
               ("ffn_conv_b", 44), ("small", 3)]:
    CV[_n] = _o
    _o += _w
NCV = _o


class Tile:
    def __init__(self, t, name, psum=False):
        self.t = t
        self.name = name
        self.psum = psum
        self.w = {}
        self.rs = {}

    def __getitem__(self, idx):
        return self.t[idx]


class Trk:
    def __init__(self):
        self.w = {}
        self.rs = {}


class Eng:
    def __init__(self, name, eng, sem):
        self.name = name
        self.eng = eng
        self.sem = sem
        self.n = 0
        self.seen = {}


class Sched:
    def __init__(self, nc, es, ndsem=12):
        self.nc = nc
        self.es = es
        self.sems = {}
        self.E = {}
        for name, eng in [("pe", nc.tensor), ("act", nc.scalar), ("dve", nc.vector),
                          ("pool", nc.gpsimd), ("sp", nc.sync)]:
            sem = es.enter_context(nc.semaphore("sem_" + name))
            self.E[name] = Eng(name, eng, sem)
            self.sems[name] = sem
        self.dpool = {}
        self.dnext = {}
        for q in ("sp", "pool", "act"):
            lst = []
            for i in range(ndsem):
                key = "d_%s_%d" % (q, i)
                sem = es.enter_context(nc.semaphore(key))
                self.sems[key] = sem
                lst.append([sem, 0, key])
            self.dpool[q] = lst
            self.dnext[q] = 0
        self.ninst = 0

    def _deps(self, reads, writes):
        toks = []
        for r in reads:
            for k, v in r.w.items():
                toks.append((k, v, True))
            if getattr(r, "psum", False):
                for k, v in r.rs.items():
                    toks.append((k, v, False))
        for w in writes:
            for k, v in w.w.items():
                toks.append((k, v, False))
            for k, v in w.rs.items():
                toks.append((k, v, False))
        return toks

    def _wait(self, E, toks):
        for key, val, raw in toks:
            if key == E.name:
                if not raw or E.name == "pe":
                    continue
            if E.seen.get(key, 0) >= val:
                continue
            E.eng.wait_ge(self.sems[key], val)
            E.seen[key] = val
            self.ninst += 1
            E.nw = getattr(E, "nw", 0) + 1

    def _mark(self, tok, reads, writes):
        k, v = tok
        for r in reads:
            if r.rs.get(k, 0) < v:
                r.rs[k] = v
        for w in writes:
            w.w[k] = v
            w.rs = {}

    def op(self, en, fn, reads, writes):
        E = self.E[en]
        self._wait(E, self._deps(reads, writes))
        ins = fn(E.eng)
        E.n += 1
        ins.then_inc(E.sem, 1)
        self.ninst += 1
        self._mark((E.name, E.n), reads, writes)

    def dma(self, q, out, in_, reads, writes, **kw):
        Q = self.E[q]
        toks = self._deps(reads, writes)
        pool = self.dpool[q]
        i = self.dnext[q]
        self.dnext[q] = (i + 1) % len(pool)
        sem, cnt, key = pool[i]
        if cnt > 0:
            toks.append((key, cnt, False))
        self._wait(Q, toks)
        Q.eng.dma_start(out=out, in_=in_, **kw).then_inc(sem, 16)
        pool[i][1] = cnt + 16
        self.ninst += 1
        self._mark((key, cnt + 16), reads, writes)

    def barrier(self):
        toks = [(n, e.n, True) for n, e in self.E.items() if e.n > 0]
        for q, pool in self.dpool.items():
            for sem, cnt, key in pool:
                if cnt > 0:
                    toks.append((key, cnt, False))
        for E in self.E.values():
            self._wait(E, toks)

    def mm(self, out, lhsT, rhs, reads, writes, start=True, stop=True):
        self.op("pe", lambda e: e.matmul(out, lhsT, rhs, start=start, stop=stop), reads, writes)

    def tr(self, out, in_, ident, reads, writes):
        self.op("pe", lambda e: e.transpose(out, in_, ident), reads, writes)

    def act(self, out, in_, func, reads, writes, bias=None, scale=None, en="act"):
        kw = {}
        if bias is not None:
            kw["bias"] = bias
        if scale is not None:
            kw["scale"] = scale
        self.op(en, lambda e: e.activation(out, in_, func, **kw), reads, writes)

    def tt(self, en, out, in0, in1, op, reads, writes):
        self.op(en, lambda e: e.tensor_tensor(out, in0, in1, op), reads, writes)

    def ts(self, en, out, in0, s1, s2, op0, op1, reads, writes):
        if op1 is None:
            self.op(en, lambda e: e.tensor_scalar(out, in0, s1, None, op0), reads, writes)
        else:
            self.op(en, lambda e: e.tensor_scalar(out, in0, s1, s2, op0, op1), reads, writes)

    def stt(self, out, in0, scalar, in1, op0, op1, reads, writes):
        self.op("dve", lambda e: e.scalar_tensor_tensor(out, in0, scalar, in1, op0, op1), reads, writes)

    def copy(self, en, out, in_, reads, writes):
        if en == "act":
            self.op(en, lambda e: e.activation(out, in_, AF.Copy), reads, writes)
        else:
            self.op(en, lambda e: e.tensor_copy(out, in_), reads, writes)

    def memset(self, en, ap, val, writes):
        self.op(en, lambda e: e.memset(ap, val), [], writes)


class _SqAlias:
    def __init__(self, base):
        self.base = base

    @property
    def w(self):
        return self.base.w

    @property
    def rs(self):
        return self.base.rs

    @rs.setter
    def rs(self, v):
        self.base.rs = v

    def __getitem__(self, idx):
        if idx == slice(None):
            return self.base.t[:, 0:8, :]
        return self.base.t[idx]


class Builder:
    def __init__(self, T, L, dbg=()):
        self.T = T
        self.L = L
        self.NG = T // 512
        self.dbg = set(dbg)
        self.nc = bass.Bass("TRN2", target_bir_lowering=False)
        self.outs = []

    def dram_in(self, name, shape, dt=F32):
        return Tile(self.nc.dram_tensor(name, list(shape), dt, kind="ExternalInput"), name)

    def dram(self, name, shape, dt):
        kind = "Internal"
        if name in self.dbg or name == "yT":
            kind = "ExternalOutput"
            self.outs.append(name)
        return Tile(self.nc.dram_tensor(name, list(shape), dt, kind=kind), name)

    def sb(self, es, name, shape, dt):
        self.uid = getattr(self, "uid", 0) + 1
        name = "%s_%d" % (name, self.uid)
        return Tile(es.enter_context(self.nc.sbuf_tensor(name, list(shape), dt)), name)

    def build(self):
        nc, T, L = self.nc, self.T, self.L
        self.inp = {}
        I = self.inp
        I["xT"] = self.dram_in("xT", [D, T])
        I["memT"] = self.dram_in("memT", [D, MEMT])
        I["w_in"] = self.dram_in("w_in", [L, D, OFF["END"]])
        I["cv"] = self.dram_in("cv", [L, 128, NCV])
        I["w_oa"] = self.dram_in("w_oa", [L, 512, D])
        I["w_ob"] = self.dram_in("w_ob", [L, 512, D])
        I["w_oc"] = self.dram_in("w_oc", [L, 512, D])
        I["w_out"] = self.dram_in("w_out", [L, D, D])
        I["w_mq"] = self.dram_in("w_mq", [L, D, 512])
        I["w_mkv"] = self.dram_in("w_mkv", [L, D, 1024])
        I["w_mo"] = self.dram_in("w_mo", [L, 512, D])
        I["w_up"] = self.dram_in("w_up", [L, D, 2 * DFF])
        I["w_down"] = self.dram_in("w_down", [L, DFF, D])
        I["cmask"] = self.dram_in("cmask", [128, 9, 128])
        self.X = [self.dram("xres0", [D, T], F32), self.dram("xres1", [D, T], F32)]
        self.yT = self.dram("yT", [D, T], F32)
        self.HT = self.dram("HT", [D, T], BF16)
        self.QK = {n: self.dram(n, [512, T], BF16) for n in
                   ("FQ", "FK", "SQ", "SK", "GQ", "GK", "GV", "GZ")}
        self.VT = {n: self.dram(n, [T, 512], BF16) for n in ("FV", "SV")}
        self.ROW = {n: self.dram(n, [4, T], F32) for n in ("CROW", "BETA", "GLOG", "EGC")}
        self.GATES = self.dram("GATES", [3072, T], BF16)
        self.Y = {n: self.dram(n, [512, T], BF16) for n in ("YA", "YB", "YC")}
        with ExitStack() as es:
            self.es = es
            self.S = Sched(nc, es)
            self.consts(es)
            for l in range(L):
                self.layer(l)
            self.finish()
        return nc

    def consts(self, es):
        S = self.S
        self.PS = [Tile(es.enter_context(self.nc.psum_tensor("ps%d" % i, [128, 512], F32)), "ps%d" % i, psum=True)
                   for i in range(8)]
        self.ones_bf = self.sb(es, "ones_bf", [128, 128], BF16)
        S.memset("dve", self.ones_bf[:], 1.0, [self.ones_bf])
        self.cm32 = self.sb(es, "cm32", [128, 9, 128], F32)
        S.dma("sp", self.cm32[:], self.inp["cmask"][:, :, :], [self.inp["cmask"]], [self.cm32])
        self.cmbf = self.sb(es, "cmbf", [128, 9, 128], BF16)
        S.copy("dve", self.cmbf[:], self.cm32[:], [self.cm32], [self.cmbf])
        self.cvt = self.sb(es, "cvt", [128, NCV], F32)
        self.dvt = self.sb(es, "dvt", [128, 4], F32)

    def finish(self):
        S = self.S
        for n in self.dbg:
            if n.startswith("nops"):
                for i in range(int(n[4:])):
                    S.dma("sp", self.cvt[:], self.inp["cv"][0, :, :], [self.inp["cv"]], [self.cvt])
        S.barrier()

    def layer(self, l):
        S = self.S
        S.barrier()
        S.dma("sp", self.cvt[:], self.inp["cv"][l, :, :], [self.inp["cv"]], [self.cvt])
        cs = CV["small"]
        S.ts("dve", self.dvt[:, 0:1], self.cvt[:, cs:cs + 1], -1.0, None, ALU.mult, None, [self.cvt], [self.dvt])
        S.act(self.dvt[:, 1:2], self.cvt[:, cs + 2:cs + 3], AF.Exp, [self.cvt], [self.dvt])
        S.ts("dve", self.dvt[:, 1:2], self.dvt[:, 1:2], -1.0, None, ALU.mult, None, [self.dvt], [self.dvt])
        xin = self.inp["xT"] if l == 0 else self.X[1]
        self.pass_a(l, xin)
        nh = 2 if "nh2" in self.dbg else (1 if "nh1" in self.dbg else NH)
        if "nofox" not in self.dbg:
            for h in range(nh):
                self.fox_head(h)
        if "nosb" not in self.dbg:
            for h in range(nh):
                self.sb_head(h)
        self.pass_b(l)
        if "nogdn" not in self.dbg:
            for h in range(nh):
                self.gdn_head(h)
        self.pass_c(l, xin, self.X[0])
        self.pass_d(l, self.X[0], self.yT if l == self.L - 1 else self.X[1])

    def load_w(self, stg, W, wcol, src, row0, c0, ncols, gain_col=None, kchunks=8, flip=[0]):
        S = self.S
        for k in range(kchunks):
            done = 0
            while done < ncols:
                n = min(1024, ncols - done)
                st = stg[flip[0] % len(stg)]
                flip[0] += 1
                S.dma("sp", st[:, 0:n], src[row0 + k * 128: row0 + (k + 1) * 128, c0 + done: c0 + done + n],
                      [], [st])
                dst = W[:, k, wcol + done: wcol + done + n]
                if gain_col is None:
                    en = "dve" if flip[0] % 2 else "pool"
                    S.copy(en, dst, st[:, 0:n], [st], [W])
                else:
                    g = self.cvt[:, gain_col + k: gain_col + k + 1]
                    if flip[0] % 2:
                        S.ts("dve", dst, st[:, 0:n], g, None, ALU.mult, None, [st, self.cvt], [W])
                    else:
                        S.act(dst, st[:, 0:n], AF.Copy, [st, self.cvt], [W], scale=g)
                done += n

    def norm_group(self, xt, hT, sq, lnv, rstd, ps):
        S = self.S
        S.act(sq[:], xt[:], AF.Square, [xt], [sq])
        for k in range(8):
            S.mm(ps[:], self.ones_bf[:], sq[:, k, :], [self.ones_bf, sq], [ps], start=(k == 0), stop=(k == 7))
        S.act(lnv[:], ps[:], AF.Ln, [ps], [lnv], bias=EPS, scale=1.0 / D)
        S.act(rstd[:], lnv[:], AF.Exp, [lnv], [rstd], scale=-0.5)
        for k in range(8):
            S.tt("dve" if k % 2 == 0 else "pool", hT[:, k, :], xt[:, k, :], rstd[:], ALU.mult, [xt, rstd], [hT])

    def pass_a(self, l, xin):
        S, T, NG = self.S, self.T, self.NG
        w_in = self.inp["w_in"]
        with ExitStack() as es:
            NWA = 2048 + 1024
            W = self.sb(es, "WA", [128, 8, NWA], BF16)
            Wsm = self.sb(es, "WAsm", [128, 8, 96], BF16)
            stg = [self.sb(es, "stgA%d" % i, [128, 1024], F32) for i in range(2)]
            S.memset("pool", Wsm[:], 0.0, [Wsm])
            gm = CV["norm_mix"]
            wl = w_in.t[l]
            for (name, wc) in (("FQ", 0), ("FK", 512), ("SQ", 1024), ("SK", 1536), ("FV", 2048), ("SV", 2560)):
                self.load_w(stg, W, wc, wl, 0, OFF[name], 512, gain_col=gm)
            for (name, wc) in (("FF", 0), ("GB", 32), ("GA", 64)):
                self.load_w(stg, Wsm, wc, wl, 0, OFF[name], 4, gain_col=gm)
            xt = self.sb(es, "xtA", [128, 8, 512], F32)
            hT = self.sb(es, "hTA", [128, 8, 512], BF16)
            sq = self.sb(es, "sqA", [128, 8, 512], BF16)
            lnv = self.sb(es, "lnvA", [128, 512], F32)
            rstd = self.sb(es, "rstdA", [128, 512], F32)
            sq2 = [self.sb(es, "sq2A%d" % i, [128, 512], BF16) for i in range(2)]
            ln2 = [self.sb(es, "ln2A%d" % i, [128, 512], F32) for i in range(2)]
            ob = [self.sb(es, "obA%d" % i, [128, 512], BF16) for i in range(4)]
            sm = {n: self.sb(es, "smA_" + n, [128, 512], F32) for n in ("e1", "l1", "c", "beta", "glog", "gc", "egc")}
            ones4 = self.sb(es, "ones4", [128, 512], F32)
            rmask = self.sb(es, "rmask", [128, 512], F32)
            S.memset("pool", ones4[:], 1.0, [ones4])
            S.memset("pool", rmask[:], 1.0, [rmask])
            for c in range(8):
                S.memset("pool", rmask[:, c * 64: c * 64 + 1], 0.0, [rmask])
            ccarry = self.sb(es, "ccarry", [128, 1], F32)
            S.memset("pool", ccarry[:], 0.0, [ccarry])
            xv = xin.t.rearrange("(k p) t -> p k t", p=128)
            hv = self.HT.t.rearrange("(k p) t -> p k t", p=128)
            cvs = CV["small"]
            nob = 0
            pi = 0
            for g in range(NG):
                ts_ = slice(g * 512, (g + 1) * 512)
                S.dma("sp", xt[:], xv[:, :, ts_], [xin], [xt])
                self.norm_group(xt, hT, sq, lnv, rstd, self.PS[0])
                S.dma("pool", hv[:, :, ts_], hT[:], [hT], [self.HT])
                for fam, wc, kind in (("FQ", 0, "nq"), ("FK", 512, "nk"), ("SQ", 1024, "s"), ("SK", 1536, "c")):
                    for h in range(4):
                        ps = self.PS[1 + pi % 3]
                        pi += 1
                        for k in range(8):
                            S.mm(ps[:], W[:, k, wc + h * 128: wc + (h + 1) * 128], hT[:, k, :], [W, hT], [ps],
                                 start=(k == 0), stop=(k == 7))
                        o = ob[nob % 4]
                        nob += 1
                        if kind in ("nq", "nk"):
                            s2 = sq2[nob % 2]
                            l2 = ln2[nob % 2]
                            ps2 = self.PS[4 + nob % 2]
                            S.act(s2[:], ps[:], AF.Square, [ps], [s2])
                            S.mm(ps2[:], self.ones_bf[:], s2[:], [self.ones_bf, s2], [ps2])
                            S.act(l2[:], ps2[:], AF.Ln, [ps2], [l2], bias=EPS, scale=1.0 / DH)
                            S.act(l2[:], l2[:], AF.Exp, [l2], [l2], scale=-0.5)
                            gcol = CV["fox_qnorm"] if kind == "nq" else CV["fox_knorm"]
                            S.stt(o[:], ps[:], self.cvt[:, gcol:gcol + 1], l2[:], ALU.mult, ALU.mult,
                                  [ps, self.cvt, l2], [o])
                            if kind == "nq":
                                S.ts("pool", o[:], o[:], SCALE, 1.0, ALU.mult, ALU.mult, [o], [o])
                        elif kind == "s":
                            S.act(o[:], ps[:], AF.Copy, [ps], [o], scale=SCALE)
                        else:
                            S.copy("dve", o[:], ps[:], [ps], [o])
                        dst = self.QK[fam]
                        S.dma("pool", dst.t[h * 128:(h + 1) * 128, ts_], o[:], [o], [dst])
                for fam, wc in (("FV", 2048), ("SV", 2560)):
                    for sub in range(4):
                        ps = self.PS[1 + pi % 3]
                        pi += 1
                        for k in range(8):
                            S.mm(ps[:], hT[:, k, sub * 128:(sub + 1) * 128], W[:, k, wc: wc + 512], [W, hT], [ps],
                                 start=(k == 0), stop=(k == 7))
                        o = ob[nob % 4]
                        nob += 1
                        S.copy("dve" if sub % 2 else "act", o[:], ps[:], [ps], [o])
                        dst = self.VT[fam]
                        r0 = g * 512 + sub * 128
                        S.dma("pool", dst.t[r0:r0 + 128, :], o[:], [o], [dst])
                ps = self.PS[6]
                for k in range(8):
                    S.mm(ps[0:96, :], Wsm[:, k, :], hT[:, k, :], [Wsm, hT], [ps], start=(k == 0), stop=(k == 7))
                cv = self.cvt
                S.act(sm["e1"][0:4, :], ps[0:4, :], AF.Exp, [ps, self.dvt], [sm["e1"]], bias=self.dvt[0:4, 0:1], scale=-1.0)
                S.act(sm["l1"][0:4, :], sm["e1"][0:4, :], AF.Ln, [sm["e1"]], [sm["l1"]], bias=1.0)
                S.op("dve", lambda e: e.tensor_tensor_scan(sm["c"][0:4, :], ones4[0:4, :], sm["l1"][0:4, :],
                                                           ccarry[0:4, 0:1], ALU.mult, ALU.subtract),
                     [ones4, sm["l1"], ccarry], [sm["c"]])
                S.copy("dve", ccarry[0:4, :], sm["c"][0:4, 511:512], [sm["c"]], [ccarry])
                S.dma("pool", self.ROW["CROW"].t[:, ts_], sm["c"][0:4, :], [sm["c"]], [self.ROW["CROW"]])
                S.act(sm["beta"][32:36, :], ps[32:36, :], AF.Sigmoid, [ps], [sm["beta"]])
                S.dma("pool", self.ROW["BETA"].t[:, ts_], sm["beta"][32:36, :], [sm["beta"]], [self.ROW["BETA"]])
                S.act(sm["e1"][64:68, :], ps[64:68, :], AF.Exp, [ps, cv], [sm["e1"]], bias=cv[64:68, cvs + 1:cvs + 2])
                S.act(sm["l1"][64:68, :], sm["e1"][64:68, :], AF.Ln, [sm["e1"]], [sm["l1"]], bias=1.0)
                S.ts("dve", sm["glog"][64:68, :], sm["l1"][64:68, :], self.dvt[64:68, 1:2], None, ALU.mult, None,
                     [sm["l1"], self.dvt], [sm["glog"]])
                S.op("dve", lambda e: e.tensor_tensor_scan(sm["gc"][64:68, :], rmask[64:68, :], sm["glog"][64:68, :],
                                                           0.0, ALU.mult, ALU.add),
                     [rmask, sm["glog"]], [sm["gc"]])
                S.act(sm["egc"][64:68, :], sm["gc"][64:68, :], AF.Exp, [sm["gc"]], [sm["egc"]])
                S.dma("pool", self.ROW["GLOG"].t[:, ts_], sm["glog"][64:68, :], [sm["glog"]], [self.ROW["GLOG"]])
                S.dma("pool", self.ROW["EGC"].t[:, ts_], sm["egc"][64:68, :], [sm["egc"]], [self.ROW["EGC"]])
            S.barrier()


    def pass_b(self, l):
        S, T, NG = self.S, self.T, self.NG
        w_in = self.inp["w_in"]
        PS = self.PS
        cv = self.cvt
        with ExitStack() as es:
            W = self.sb(es, "WB", [128, 8, 5120], BF16)
            stg = [self.sb(es, "stgB%d" % i, [128, 1024], F32) for i in range(2)]
            gm = CV["norm_mix"]
            wl = w_in.t[l]
            self.load_w(stg, W, 0, wl, 0, OFF["GQ"], 1536, gain_col=gm)
            self.load_w(stg, W, 1536, wl, 0, OFF["GZ"], 512, gain_col=gm)
            self.load_w(stg, W, 2048, wl, 0, OFF["GT"], 3072, gain_col=gm)
            hT = [self.sb(es, "hTB%d" % i, [128, 8, 512], BF16) for i in range(2)]
            halo = self.sb(es, "haloB", [128, 12, 4], F32)
            buf = [self.sb(es, "bufB%d" % i, [128, 516], F32) for i in range(2)]
            acc = [self.sb(es, "accB%d" % i, [128, 512], F32) for i in range(2)]
            sil = [self.sb(es, "silB%d" % i, [128, 512], F32) for i in range(2)]
            s2 = [self.sb(es, "sq2B%d" % i, [128, 512], BF16) for i in range(2)]
            l2 = [self.sb(es, "ln2B%d" % i, [128, 512], F32) for i in range(2)]
            ob = [self.sb(es, "obB%d" % i, [128, 512], BF16) for i in range(4)]
            S.memset("pool", halo[:], 0.0, [halo])
            hv = self.HT.t.rearrange("(k p) t -> p k t", p=128)
            pi = 0
            nob = 0
            cw = CV["gdn_conv"]
            for g in range(NG):
                ts_ = slice(g * 512, (g + 1) * 512)
                h_ = hT[g % 2]
                S.dma("sp", h_[:], hv[:, :, ts_], [self.HT], [h_])
                for c in range(12):
                    ps = PS[pi % 3]
                    pi += 1
                    for k in range(8):
                        S.mm(ps[:], W[:, k, c * 128:(c + 1) * 128], h_[:, k, :], [W, h_], [ps], start=(k == 0), stop=(k == 7))
                    b = buf[c % 2]
                    a = acc[c % 2]
                    sl = sil[c % 2]
                    S.copy("pool", b[:, 0:3], halo[:, c, 0:3], [halo], [b])
                    S.copy("act", b[:, 3:515], ps[:], [ps], [b])
                    S.copy("pool", halo[:, c, 0:3], b[:, 512:515], [b], [halo])
                    S.ts("dve", a[:], b[:, 0:512], cv[:, cw + c:cw + c + 1], None, ALU.mult, None, [b, cv], [a])
                    for tap in (1, 2, 3):
                        S.stt(a[:], b[:, tap:tap + 512], cv[:, cw + tap * 12 + c: cw + tap * 12 + c + 1], a[:],
                              ALU.mult, ALU.add, [b, cv, a], [a])
                    o = ob[nob % 4]
                    nob += 1
                    fam = ("GQ", "GK", "GV")[c // 4]
                    hh = c % 4
                    if fam == "GV":
                        S.act(o[:], a[:], AF.Silu, [a], [o])
                    else:
                        S.act(sl[:], a[:], AF.Silu, [a], [sl])
                        q2 = s2[c % 2]
                        ll = l2[c % 2]
                        ps2 = PS[3 + c % 2]
                        S.act(q2[:], sl[:], AF.Square, [sl], [q2])
                        S.mm(ps2[:], self.ones_bf[:], q2[:], [self.ones_bf, q2], [ps2])
                        S.act(ll[:], ps2[:], AF.Ln, [ps2], [ll], bias=EPS)
                        S.act(ll[:], ll[:], AF.Exp, [ll], [ll], scale=-0.5)
                        if fam == "GQ":
                            S.stt(o[:], sl[:], SCALE, ll[:], ALU.mult, ALU.mult, [sl, ll], [o])
                        else:
                            S.tt("dve", o[:], sl[:], ll[:], ALU.mult, [sl, ll], [o])
                    S.dma("pool", self.QK[fam].t[hh * 128:(hh + 1) * 128, ts_], o[:], [o], [self.QK[fam]])
                for c in range(4):
                    ps = PS[pi % 3]
                    pi += 1
                    for k in range(8):
                        S.mm(ps[:], W[:, k, 1536 + c * 128:1536 + (c + 1) * 128], h_[:, k, :], [W, h_], [ps],
                             start=(k == 0), stop=(k == 7))
                    o = ob[nob % 4]
                    nob += 1
                    S.act(o[:], ps[:], AF.Silu, [ps], [o])
                    S.dma("pool", self.QK["GZ"].t[c * 128:(c + 1) * 128, ts_], o[:], [o], [self.QK["GZ"]])
                gb = CV["gate_bias"]
                for c in range(24):
                    ps = PS[pi % 3]
                    pi += 1
                    for k in range(8):
                        S.mm(ps[:], W[:, k, 2048 + c * 128:2048 + (c + 1) * 128], h_[:, k, :], [W, h_], [ps],
                             start=(k == 0), stop=(k == 7))
                    o = ob[nob % 4]
                    nob += 1
                    S.act(o[:], ps[:], AF.Sigmoid, [ps, cv], [o], bias=cv[:, gb + c:gb + c + 1])
                    S.dma("pool", self.GATES.t[c * 128:(c + 1) * 128, ts_], o[:], [o], [self.GATES])
            S.barrier()

    def head_norm(self, ps, ncols, gcol, out, sq, ll, ps2, post_scale=None):
        S = self.S
        S.act(sq[:, 0:ncols], ps[:, 0:ncols], AF.Square, [ps], [sq])
        S.mm(ps2[:, 0:ncols], self.ones_bf[:], sq[:, 0:ncols], [self.ones_bf, sq], [ps2])
        S.act(ll[:, 0:ncols], ps2[:, 0:ncols], AF.Ln, [ps2], [ll], bias=EPS, scale=1.0 / DH)
        S.act(ll[:, 0:ncols], ll[:, 0:ncols], AF.Exp, [ll], [ll], scale=-0.5)
        S.stt(out, ps[:, 0:ncols], self.cvt[:, gcol:gcol + 1], ll[:, 0:ncols], ALU.mult, ALU.mult,
              [ps, self.cvt, ll], [out.tile] if hasattr(out, "tile") else [])

    def pass_c(self, l, xin, xout):
        S, T, NG = self.S, self.T, self.NG
        I = self.inp
        PS = self.PS
        cv = self.cvt
        with ExitStack() as es:
            Wo = [self.sb(es, "WCo%d" % i, [128, 4, D], BF16) for i in range(3)]
            Wout = self.sb(es, "WCout", [128, 8, D], BF16)
            Wmq = self.sb(es, "WCmq", [128, 8, 512], BF16)
            Wmo = self.sb(es, "WCmo", [128, 4, D], BF16)
            Wkv = self.sb(es, "WCkv", [128, 8, D], BF16)
            stg = [self.sb(es, "stgC%d" % i, [128, 1024], F32) for i in range(2)]
            for i, n in enumerate(("w_oa", "w_ob", "w_oc")):
                self.load_w(stg, Wo[i], 0, I[n].t[l], 0, 0, D, kchunks=4)
            self.load_w(stg, Wout, 0, I["w_out"].t[l], 0, 0, D)
            self.load_w(stg, Wmq, 0, I["w_mq"].t[l], 0, 0, 512, gain_col=CV["norm_xq"])
            self.load_w(stg, Wmo, 0, I["w_mo"].t[l], 0, 0, D, kchunks=4)
            self.load_w(stg, Wkv, 0, I["w_mkv"].t[l], 0, 0, D, gain_col=CV["norm_mem"])
            xt = self.sb(es, "xtC", [128, 8, 512], F32)
            hT = self.sb(es, "hTC", [128, 8, 512], BF16)
            sq = self.sb(es, "sqC", [128, 8, 512], BF16)
            lnv = self.sb(es, "lnvC", [128, 512], F32)
            rstd = self.sb(es, "rstdC", [128, 512], F32)
            yb_ = [self.sb(es, "yC%d" % i, [128, 4, 512], BF16) for i in range(3)]
            gt = self.sb(es, "gtC", [128, 24, 512], BF16)
            tA = self.sb(es, "tAC", [128, 512], F32)
            tB = self.sb(es, "tBC", [128, 512], F32)
            mix = self.sb(es, "mixC", [128, 8, 512], BF16)
            om = self.sb(es, "omC", [128, 4, 512], BF16)
            sq2 = self.sb(es, "sq2C", [128, 512], BF16)
            ll = self.sb(es, "llC", [128, 512], F32)
            qn = self.sb(es, "qnC", [128, 512], BF16)
            pt = [self.sb(es, "ptC%d" % i, [128, 512], BF16) for i in range(2)]
            Km = self.sb(es, "KmC", [128, 4, MEMT], BF16)
            Vm = self.sb(es, "VmC", [128, 2, 512], BF16)
            mv = I["memT"].t.rearrange("(k p) t -> p k t", p=128)
            S.dma("sp", xt[:, :, 0:MEMT], mv, [I["memT"]], [xt])
            S.act(sq[:, :, 0:MEMT], xt[:, :, 0:MEMT], AF.Square, [xt], [sq])
            for k in range(8):
                S.mm(PS[0][:, 0:MEMT], self.ones_bf[:], sq[:, k, 0:MEMT], [self.ones_bf, sq], [PS[0]], start=(k == 0), stop=(k == 7))
            S.act(lnv[:, 0:MEMT], PS[0][:, 0:MEMT], AF.Ln, [PS[0]], [lnv], bias=EPS, scale=1.0 / D)
            S.act(rstd[:, 0:MEMT], lnv[:, 0:MEMT], AF.Exp, [lnv], [rstd], scale=-0.5)
            for k in range(8):
                S.tt("dve", hT[:, k, 0:MEMT], xt[:, k, 0:MEMT], rstd[:, 0:MEMT], ALU.mult, [xt, rstd], [hT])
            for h in range(4):
                ps = PS[1 + h % 2]
                for k in range(8):
                    S.mm(ps[:, 0:MEMT], Wkv[:, k, h * 128:(h + 1) * 128], hT[:, k, 0:MEMT], [Wkv, hT], [ps], start=(k == 0), stop=(k == 7))
                S.act(sq2[:, 0:MEMT], ps[:, 0:MEMT], AF.Square, [ps], [sq2])
                S.mm(PS[3][:, 0:MEMT], self.ones_bf[:], sq2[:, 0:MEMT], [self.ones_bf, sq2], [PS[3]])
                S.act(ll[:, 0:MEMT], PS[3][:, 0:MEMT], AF.Ln, [PS[3]], [ll], bias=EPS, scale=1.0 / DH)
                S.act(ll[:, 0:MEMT], ll[:, 0:MEMT], AF.Exp, [ll], [ll], scale=-0.5)
                gk = CV["mk_norm"]
                S.stt(Km[:, h, :], ps[:, 0:MEMT], cv[:, gk:gk + 1], ll[:, 0:MEMT], ALU.mult, ALU.mult, [ps, cv, ll], [Km])
            for blk in range(2):
                ps = PS[1 + blk % 2]
                for k in range(8):
                    S.mm(ps[:], hT[:, k, blk * 128:(blk + 1) * 128], Wkv[:, k, 512:1024], [Wkv, hT], [ps], start=(k == 0), stop=(k == 7))
                S.copy("act", Vm[:, blk, :], ps[:], [ps], [Vm])
            xv = xin.t.rearrange("(k p) t -> p k t", p=128)
            xo = xout.t.rearrange("(k p) t -> p k t", p=128)
            yv = [self.Y[n].t.rearrange("(h p) t -> p h t", p=128) for n in ("YA", "YB", "YC")]
            gv = self.GATES.t.rearrange("(c p) t -> p c t", p=128)
            pi = 0
            for g in range(NG):
                ts_ = slice(g * 512, (g + 1) * 512)
                S.dma("sp", xt[:], xv[:, :, ts_], [xin], [xt])
                for i, n in enumerate(("YA", "YB", "YC")):
                    S.dma("sp", yb_[i][:], yv[i][:, :, ts_], [self.Y[n]], [yb_[i]])
                S.dma("sp", gt[:], gv[:, :, ts_], [self.GATES], [gt])
                for oc in range(8):
                    ocs = slice(oc * 128, (oc + 1) * 128)
                    for br in range(3):
                        ps = PS[pi % 3]
                        pi += 1
                        for hh in range(4):
                            S.mm(ps[:], Wo[br][:, hh, ocs], yb_[br][:, hh, :], [Wo[br], yb_[br]], [ps], start=(hh == 0), stop=(hh == 3))
                        if br == 0:
                            S.tt("dve", tA[:], ps[:], gt[:, oc, :], ALU.mult, [ps, gt], [tA])
                        else:
                            S.tt("dve", tB[:], ps[:], gt[:, br * 8 + oc, :], ALU.mult, [ps, gt], [tB])
                            if br == 1:
                                S.tt("pool", tA[:], tA[:], tB[:], ALU.add, [tA, tB], [tA])
                            else:
                                S.tt("pool", mix[:, oc, :], tA[:], tB[:], ALU.add, [tA, tB], [mix])
                for oc in range(8):
                    ocs = slice(oc * 128, (oc + 1) * 128)
                    ps = PS[pi % 3]
                    pi += 1
                    for k in range(8):
                        S.mm(ps[:], Wout[:, k, ocs], mix[:, k, :], [Wout, mix], [ps], start=(k == 0), stop=(k == 7))
                    S.tt("dve", xt[:, oc, :], xt[:, oc, :], ps[:], ALU.add, [xt, ps], [xt])
                self.norm_group(xt, hT, sq, lnv, rstd, PS[3])
                for h in range(4):
                    ps = PS[pi % 3]
                    pi += 1
                    for k in range(8):
                        S.mm(ps[:], Wmq[:, k, h * 128:(h + 1) * 128], hT[:, k, :], [Wmq, hT], [ps], start=(k == 0), stop=(k == 7))
                    S.act(sq2[:], ps[:], AF.Square, [ps], [sq2])
                    S.mm(PS[3][:], self.ones_bf[:], sq2[:], [self.ones_bf, sq2], [PS[3]])
                    S.act(ll[:], PS[3][:], AF.Ln, [PS[3]], [ll], bias=EPS, scale=1.0 / DH)
                    S.act(ll[:], ll[:], AF.Exp, [ll], [ll], scale=-0.5)
                    gq = CV["mq_norm"]
                    S.stt(tA[:], ps[:], cv[:, gq:gq + 1], ll[:], ALU.mult, ALU.mult, [ps, cv, ll], [tA])
                    S.ts("pool", qn[:], tA[:], SCALE, 1.0, ALU.mult, ALU.mult, [tA], [qn])
                    O = PS[4 + h % 2]
                    DN = PS[6 + h % 2]
                    for kb in range(2):
                        ps = PS[pi % 3]
                        pi += 1
                        p_ = pt[kb]
                        S.mm(ps[:], Km[:, h, kb * 128:(kb + 1) * 128], qn[:], [Km, qn], [ps])
                        S.act(p_[:], ps[:], AF.Exp, [ps], [p_])
                        S.mm(O[:], Vm[:, kb, h * 128:(h + 1) * 128], p_[:], [Vm, p_], [O], start=(kb == 0), stop=(kb == 1))
                        S.mm(DN[:], self.ones_bf[:], p_[:], [self.ones_bf, p_], [DN], start=(kb == 0), stop=(kb == 1))
                    S.act(ll[:], DN[:], AF.Ln, [DN], [ll])
                    S.act(ll[:], ll[:], AF.Exp, [ll], [ll], scale=-1.0)
                    S.tt("dve", om[:, h, :], O[:], ll[:], ALU.mult, [O, ll], [om])
                for oc in range(8):
                    ocs = slice(oc * 128, (oc + 1) * 128)
                    ps = PS[pi % 3]
                    pi += 1
                    for hh in range(4):
                        S.mm(ps[:], Wmo[:, hh, ocs], om[:, hh, :], [Wmo, om], [ps], start=(hh == 0), stop=(hh == 3))
                    S.tt("dve", xt[:, oc, :], xt[:, oc, :], ps[:], ALU.add, [xt, ps], [xt])
                S.dma("pool", xo[:, :, ts_], xt[:], [xt], [xout])
            S.barrier()

    def pass_d(self, l, xin, xout):
        S, T, NG = self.S, self.T, self.NG
        I = self.inp
        PS = self.PS
        cv = self.cvt
        NJ = DFF // 128
        with ExitStack() as es:
            Wup = self.sb(es, "WDup", [128, 8, 2 * DFF], BF16)
            Wdn = self.sb(es, "WDdn", [128, NJ, D], BF16)
            with ExitStack() as es2:
                stg = [self.sb(es2, "stgD%d" % i, [128, 1024], F32) for i in range(2)]
                self.load_w(stg, Wup, 0, I["w_up"].t[l], 0, 0, 2 * DFF, gain_col=CV["norm_ffn"])
                self.load_w(stg, Wdn, 0, I["w_down"].t[l], 0, 0, D, kchunks=NJ)
                S.barrier()
            xt = self.sb(es, "xtD", [128, 8, 512], F32)
            hT = self.sb(es, "hTD", [128, 8, 512], BF16)
            gT = self.sb(es, "gTD", [128, NJ, 512], BF16)
            lnv = self.sb(es, "lnvD", [128, 512], F32)
            rstd = self.sb(es, "rstdD", [128, 512], F32)
            halo = self.sb(es, "haloD", [128, 2 * NJ, 2], F32)
            buf = [self.sb(es, "bufD%d" % i, [128, 516], F32) for i in range(2)]
            acc = [self.sb(es, "accD%d" % i, [128, 512], F32) for i in range(2)]
            sa = self.sb(es, "saD", [128, 512], F32)
            S.memset("pool", halo[:], 0.0, [halo])
            xv = xin.t.rearrange("(k p) t -> p k t", p=128)
            xo = xout.t.rearrange("(k p) t -> p k t", p=128)
            cw = CV["ffn_conv"]
            cb = CV["ffn_conv_b"]
            pi = 0
            for g in range(NG):
                ts_ = slice(g * 512, (g + 1) * 512)
                S.dma("sp", xt[:], xv[:, :, ts_], [xin], [xt])
                self.norm_group(xt, hT, _SqAlias(gT), lnv, rstd, PS[3])
                for j in range(NJ):
                    for ab in range(2):
                        c = ab * NJ + j
                        ps = PS[pi % 3]
                        pi += 1
                        for k in range(8):
                            S.mm(ps[:], Wup[:, k, c * 128:(c + 1) * 128], hT[:, k, :], [Wup, hT], [ps], start=(k == 0), stop=(k == 7))
                        b = buf[ab]
                        a = acc[ab]
                        S.copy("pool", b[:, 0:2], halo[:, c, 0:2], [halo], [b])
                        S.copy("act", b[:, 2:514], ps[:], [ps], [b])
                        S.copy("pool", halo[:, c, 0:2], b[:, 512:514], [b], [halo])
                        S.ts("dve", a[:], b[:, 0:512], cv[:, cw + c:cw + c + 1], cv[:, cb + c:cb + c + 1], ALU.mult, ALU.add,
                             [b, cv], [a])
                        for tap in (1, 2):
                            S.stt(a[:], b[:, tap:tap + 512], cv[:, cw + tap * 2 * NJ + c: cw + tap * 2 * NJ + c + 1], a[:],
                                  ALU.mult, ALU.add, [b, cv, a], [a])
                    S.act(sa[:], acc[0][:], AF.Silu, [acc[0]], [sa])
                    S.tt("pool", gT[:, j, :], sa[:], acc[1][:], ALU.mult, [sa, acc[1]], [gT])
                for oc in range(8):
                    ocs = slice(oc * 128, (oc + 1) * 128)
                    ps = PS[4 + oc % 2]
                    for j in range(NJ):
                        S.mm(ps[:], Wdn[:, j, ocs], gT[:, j, :], [Wdn, gT], [ps], start=(j == 0), stop=(j == NJ - 1))
                    S.tt("dve", xt[:, oc, :], xt[:, oc, :], ps[:], ALU.add, [xt, ps], [xt])
                S.dma("pool", xo[:, :, ts_], xt[:], [xt], [xout])
            S.barrier()

    def gdn_head(self, h):
        S, T = self.S, self.T
        NP = T // 128
        PS = self.PS
        cm = self.cm32
        ident = cm[:, 0, :]
        with ExitStack() as es:
            Qf = self.sb(es, "gQ", [128, T], BF16)
            Kf = self.sb(es, "gK", [128, T], BF16)
            Vf = self.sb(es, "gV", [128, T], BF16)
            Kb = self.sb(es, "gKb", [128, T], BF16)
            bc = self.sb(es, "gbc", [128, T], F32)
            G2 = self.sb(es, "gG2", [128, 128], F32)
            B2 = self.sb(es, "gB2", [128, 128], F32)
            gc2 = self.sb(es, "ggc2", [128, 128], F32)
            gl2 = self.sb(es, "ggl2", [128, 128], F32)
            rm2 = self.sb(es, "grm2", [128, 128], F32)
            gcol = self.sb(es, "ggcol", [128, NP], F32)
            bcol = self.sb(es, "gbcol", [128, NP], F32)
            eglc = self.sb(es, "geglc", [128, NP], F32)
            Sst = self.sb(es, "gS", [128, 128], F32)
            hs = slice(h * 128, (h + 1) * 128)
            S.dma("sp", Qf[:], self.QK["GQ"].t[hs, :], [self.QK["GQ"]], [Qf])
            S.dma("sp", Kf[:], self.QK["GK"].t[hs, :], [self.QK["GK"]], [Kf])
            S.dma("sp", Vf[:], self.QK["GV"].t[hs, :], [self.QK["GV"]], [Vf])
            R = self.ROW
            S.dma("sp", bc[:], R["BETA"].t[h:h + 1, :].partition_broadcast(128), [R["BETA"]], [bc])
            S.dma("sp", G2[0:NP, :], R["GLOG"].t[h:h + 1, :].rearrange("o (n p) -> (o n) p", p=128), [R["GLOG"]], [G2])
            S.dma("sp", B2[0:NP, :], R["BETA"].t[h:h + 1, :].rearrange("o (n p) -> (o n) p", p=128), [R["BETA"]], [B2])
            for c0 in range(0, T, 2048):
                c1 = min(T, c0 + 2048)
                S.tt("dve", Kb[:, c0:c1], Kf[:, c0:c1], bc[:, c0:c1], ALU.mult, [Kf, bc], [Kb])
            S.dma("sp", bc[:], R["EGC"].t[h:h + 1, :].partition_broadcast(128), [R["EGC"], Kb], [bc])
            S.memset("pool", rm2[:], 1.0, [rm2])
            S.memset("pool", rm2[:, 0:1], 0.0, [rm2])
            S.memset("pool", rm2[:, 64:65], 0.0, [rm2])
            S.op("dve", lambda e: e.tensor_tensor_scan(gc2[0:NP, :], rm2[0:NP, :], G2[0:NP, :], 0.0, ALU.mult, ALU.add),
                 [rm2, G2], [gc2])
            for half in range(2):
                cs = slice(64 * half, 64 * half + 64)
                tot = gc2[0:NP, 64 * half + 63: 64 * half + 64]
                S.ts("dve", gl2[0:NP, cs], gc2[0:NP, cs], tot, -1.0, ALU.subtract, ALU.mult, [gc2], [gl2])
            S.act(gl2[0:NP, :], gl2[0:NP, :], AF.Exp, [gl2], [gl2])
            for src, dst in ((G2, gcol), (B2, bcol), (gl2, eglc)):
                S.tr(PS[7][:, 0:NP], src[0:NP, :], cm[0:NP, 0, 0:NP], [src, cm], [PS[7]])
                S.copy("dve", dst[:], PS[7][:, 0:NP], [PS[7]], [dst])
            S.memset("pool", Sst[:], 0.0, [Sst])

            def f32t(n, k=2):
                return [self.sb(es, n + str(i), [128, 128], F32) for i in range(k)]
            lg = f32t("g_lg"); Dl = f32t("g_Dl"); DTs = f32t("g_DTs"); DTi = f32t("g_DTi")
            Lb = [f32t("g_L%d_" % j) for j in range(2)]
            Ub = [f32t("g_U%d_" % j) for j in range(2)]
            X = f32t("g_X"); TT = f32t("g_TT"); At = f32t("g_At")
            K32 = f32t("g_K32"); V32 = f32t("g_V32"); Kg = f32t("g_Kg"); Qg = f32t("g_Qg")
            Kt = f32t("g_Kt"); Vt = f32t("g_Vt")
            Y = f32t("g_Y"); vn = f32t("g_vn")
            OT = [self.sb(es, "g_OT%d" % i, [128, 512], F32) for i in range(2)]
            gz = [self.sb(es, "g_gz%d" % i, [128, 512], BF16) for i in range(2)]
            sq = self.sb(es, "g_sq", [128, 512], BF16)
            ll = self.sb(es, "g_ll", [128, 512], F32)
            t32 = self.sb(es, "g_t32", [128, 512], F32)
            ob = [self.sb(es, "g_ob%d" % i, [128, 512], BF16) for i in range(2)]

            def pre(r):
                i = r % 2
                cs = slice(r * 128, (r + 1) * 128)
                S.ts("pool", lg[i][:], cm[:, 4, :], gcol[:, r:r + 1], 1.0, ALU.mult, ALU.mult, [cm, gcol], [lg[i]])
                P0 = PS[0]
                S.mm(P0[:, 0:128], lg[i][:], cm[:, 5, :], [lg[i], cm], [P0], start=True, stop=False)
                S.mm(P0[:, 0:128], ident, cm[:, 6, :], [cm], [P0], start=False, stop=True)
                S.mm(P0[:, 128:256], cm[:, 5, :], lg[i][:], [lg[i], cm], [P0], start=True, stop=False)
                S.mm(P0[:, 128:256], ident, cm[:, 7, :], [cm], [P0], start=False, stop=True)
                S.mm(P0[:, 256:384], cm[:, 5, :], lg[i][:], [lg[i], cm], [P0], start=True, stop=False)
                S.mm(P0[:, 256:384], ident, cm[:, 8, :], [cm], [P0], start=False, stop=True)
                S.act(Dl[i][:], P0[:, 0:128], AF.Exp, [P0], [Dl[i]])
                S.act(DTs[i][:], P0[:, 128:256], AF.Exp, [P0], [DTs[i]])
                S.act(DTi[i][:], P0[:, 256:384], AF.Exp, [P0], [DTi[i]])
                P1 = PS[1]
                S.mm(P1[:, 0:128], Kb[:, cs], Kf[:, cs], [Kb, Kf], [P1])
                S.mm(P1[:, 128:256], Kf[:, cs], Kb[:, cs], [Kb, Kf], [P1])
                S.mm(P1[:, 256:384], Kf[:, cs], Qf[:, cs], [Qf, Kf], [P1])
                L = Lb[0][i]
                U = Ub[0][i]
                S.tt("dve", L[:], P1[:, 0:128], Dl[i][:], ALU.mult, [P1, Dl[i]], [L])
                S.tt("dve", U[:], P1[:, 128:256], DTs[i][:], ALU.mult, [P1, DTs[i]], [U])
                S.tt("dve", At[i][:], P1[:, 256:384], DTi[i][:], ALU.mult, [P1, DTi[i]], [At[i]])
                S.tt("pool", X[i][:], ident, U[:], ALU.subtract, [cm, U], [X[i]])
                for k in range(1, 6):
                    P2 = PS[2 + k % 2]
                    Ln_ = Lb[k % 2][i]
                    Un_ = Ub[k % 2][i]
                    S.mm(P2[:, 0:128], U[:], L[:], [U, L], [P2])
                    S.copy("act", Ln_[:], P2[:, 0:128], [P2], [Ln_])
                    if k < 5:
                        S.mm(P2[:, 128:256], L[:], U[:], [U, L], [P2])
                        S.copy("dve", Un_[:], P2[:, 128:256], [P2], [Un_])
                    S.mm(P2[:, 256:384], Ln_[:], X[i][:], [Ln_, X[i]], [P2])
                    S.tt("dve", X[i][:], X[i][:], P2[:, 256:384], ALU.add, [X[i], P2], [X[i]])
                    L, U = Ln_, Un_
                S.ts("dve", TT[i][:], X[i][:], bcol[:, r:r + 1], None, ALU.mult, None, [X[i], bcol], [TT[i]])
                S.copy("pool", K32[i][:], Kf[:, cs], [Kf], [K32[i]])
                S.copy("pool", V32[i][:], Vf[:, cs], [Vf], [V32[i]])
                P4 = PS[4]
                S.tr(P4[:, 0:128], K32[i][:], ident, [K32[i], cm], [P4])
                S.tr(P4[:, 128:256], V32[i][:], ident, [V32[i], cm], [P4])
                S.ts("dve", Kt[i][:], P4[:, 0:128], eglc[:, r:r + 1], None, ALU.mult, None, [P4, eglc], [Kt[i]])
                S.copy("act", Vt[i][:], P4[:, 128:256], [P4], [Vt[i]])
                S.tt("pool", Kg[i][:], K32[i][:], bc[:, cs], ALU.mult, [K32[i], bc], [Kg[i]])
                S.tt("pool", Qg[i][:], Qf[:, cs], bc[:, cs], ALU.mult, [Qf, bc], [Qg[i]])

            def scan(r):
                i = r % 2
                ot = OT[(r // 4) % 2]
                for par in range(2):
                    rows = slice(64 * par, 64 * par + 64)
                    col = r * 128 + 64 * par + 63
                    P5, P6, P7 = PS[5], PS[6], PS[7]
                    S.mm(P5[:, 0:128], Kg[i][:], Sst[:], [Kg[i], Sst], [P5])
                    S.tt("dve", Y[par][rows, :], Vt[i][rows, :], P5[rows, 0:128], ALU.subtract, [Vt[i], P5], [Y[par]])
                    S.mm(P6[:, 0:128], TT[i][rows, :], Y[par][rows, :], [TT[i], Y[par]], [P6])
                    S.copy("act", vn[par][rows, :], P6[rows, 0:128], [P6], [vn[par]])
                    S.mm(P7[:, 0:64], Sst[:], Qg[i][:, rows], [Sst, Qg[i]], [P7], start=True, stop=False)
                    S.mm(P7[:, 0:64], vn[par][rows, :], At[i][rows, rows], [vn[par], At[i]], [P7], start=False, stop=True)
                    oc = (r % 4) * 128 + 64 * par
                    S.copy("act", ot[:, oc:oc + 64], P7[:, 0:64], [P7], [ot])
                    S.mm(P5[:, 128:256], Kt[i][rows, :], vn[par][rows, :], [Kt[i], vn[par]], [P5])
                    S.stt(Sst[:], Sst[:], bc[:, col:col + 1], P5[:, 128:256], ALU.mult, ALU.add, [Sst, bc, P5], [Sst])

            def post(blk):
                ot = OT[blk % 2]
                ts_ = slice(blk * 512, (blk + 1) * 512)
                z = gz[blk % 2]
                S.dma("sp", z[:], self.QK["GZ"].t[hs, ts_], [self.QK["GZ"]], [z])
                S.act(sq[:], ot[:], AF.Square, [ot], [sq])
                S.mm(PS[4][:], self.ones_bf[:], sq[:], [self.ones_bf, sq], [PS[4]])
                S.act(ll[:], PS[4][:], AF.Ln, [PS[4]], [ll], bias=EPS, scale=1.0 / DH)
                S.act(ll[:], ll[:], AF.Exp, [ll], [ll], scale=-0.5)
                go = CV["gdn_onorm"]
                S.stt(t32[:], ot[:], self.cvt[:, go:go + 1], ll[:], ALU.mult, ALU.mult, [ot, self.cvt, ll], [t32])
                o = ob[blk % 2]
                S.tt("pool", o[:], t32[:], z[:], ALU.mult, [t32, z], [o])
                S.dma("pool", self.Y["YB"].t[hs, ts_], o[:], [o], [self.Y["YB"]])

            pre(0)
            for r in range(NP):
                if r + 1 < NP:
                    pre(r + 1)
                scan(r)
                if r % 4 == 3:
                    post(r // 4)
            S.barrier()

    def fox_head(self, h):
        S, T, NG = self.S, self.T, self.NG
        NB = T // 128
        PS = self.PS
        with ExitStack() as es:
            Qf = self.sb(es, "fxQ", [128, T], BF16)
            Kf = self.sb(es, "fxK", [128, T], BF16)
            Vt = self.sb(es, "fxV", [128, NB, 128], BF16)
            cbc = self.sb(es, "fxcbc", [128, T], F32)
            c2 = self.sb(es, "fxc2", [128, 128], F32)
            ncc = self.sb(es, "fxncc", [128, NB], F32)
            tmp = [self.sb(es, "fxtmp%d" % i, [128, 512], F32) for i in range(2)]
            PT = [self.sb(es, "fxPT%d" % i, [128, 512], BF16) for i in range(3)]
            lnd = self.sb(es, "fxlnd", [128, 512], F32)
            ob = [self.sb(es, "fxob%d" % i, [128, 512], BF16) for i in range(2)]
            hs = slice(h * 128, (h + 1) * 128)
            S.dma("sp", Qf[:], self.QK["FQ"].t[hs, :], [self.QK["FQ"]], [Qf])
            S.dma("sp", Kf[:], self.QK["FK"].t[hs, :], [self.QK["FK"]], [Kf])
            S.dma("sp", Vt[:], self.VT["FV"].t.rearrange("(n p) c -> p n c", p=128)[:, :, hs], [self.VT["FV"]], [Vt])
            crow = self.ROW["CROW"]
            S.dma("sp", cbc[:], crow.t[h:h + 1, :].partition_broadcast(128), [crow], [cbc])
            S.dma("sp", c2[0:NB, :], crow.t[h:h + 1, :].rearrange("o (n p) -> (o n) p", p=128), [crow], [c2])
            S.tr(PS[6][:, 0:NB], c2[0:NB, :], self.cm32[0:NB, 0, 0:NB], [c2, self.cm32], [PS[6]])
            S.ts("dve", ncc[:], PS[6][:, 0:NB], -1.0, None, ALU.mult, None, [PS[6]], [ncc])
            it = 0
            for g in range(NG):
                O = PS[2 + g % 2]
                DN = PS[4 + g % 2]
                tiles = [(kb, 0) for kb in range(4 * g)] + [(4 * g + j, 128 * j) for j in range(4)]
                for ti, (kb, c0) in enumerate(tiles):
                    ps = PS[it % 2]
                    tm = tmp[it % 2]
                    pt = PT[it % 3]
                    it += 1
                    q0 = g * 512 + c0
                    q1 = (g + 1) * 512
                    S.mm(ps[:, c0:512], Kf[:, kb * 128:(kb + 1) * 128], Qf[:, q0:q1], [Kf, Qf], [ps])
                    S.tt("dve", tm[:, c0:512], ps[:, c0:512], cbc[:, q0:q1], ALU.add, [ps, cbc], [tm])
                    S.act(pt[:, c0:512], tm[:, c0:512], AF.Exp, [tm, ncc], [pt], bias=ncc[:, kb:kb + 1])
                    if kb >= 4 * g:
                        S.tt("pool", pt[:, c0:c0 + 128], pt[:, c0:c0 + 128], self.cmbf[:, 1, :], ALU.mult,
                             [pt, self.cmbf], [pt])
                    first = ti == 0
                    last = ti == len(tiles) - 1
                    S.mm(O[:, c0:512], Vt[:, kb, :], pt[:, c0:512], [Vt, pt], [O], start=first, stop=last)
                    S.mm(DN[:, c0:512], self.ones_bf[:], pt[:, c0:512], [self.ones_bf, pt], [DN], start=first, stop=last)
                S.act(lnd[:], DN[:], AF.Ln, [DN], [lnd])
                S.act(lnd[:], lnd[:], AF.Exp, [lnd], [lnd], scale=-1.0)
                o = ob[g % 2]
                S.tt("dve", o[:], O[:], lnd[:], ALU.mult, [O, lnd], [o])
                S.dma("pool", self.Y["YA"].t[hs, g * 512:(g + 1) * 512], o[:], [o], [self.Y["YA"]])
            S.barrier()

    def sb_head(self, h):
        S, T, NG = self.S, self.T, self.NG
        NB = T // 128
        PS = self.PS
        with ExitStack() as es:
            Qf = self.sb(es, "sbQ", [128, T], BF16)
            Kf = self.sb(es, "sbK", [128, T], BF16)
            Vt = self.sb(es, "sbV", [128, NB, 128], BF16)
            ntri = self.sb(es, "sbntri", [128, 128], BF16)
            nones = self.sb(es, "sbnones", [128, 128], BF16)
            zeros = self.sb(es, "sbzeros", [128, 128], BF16)
            E = [self.sb(es, "sbE%d" % i, [128, 512], F32) for i in range(2)]
            SP = [self.sb(es, "sbSP%d" % i, [128, 512], BF16) for i in range(2)]
            AT = [self.sb(es, "sbAT%d" % i, [128, 512], BF16) for i in range(2)]
            cum = [self.sb(es, "sbcum%d" % i, [128, 512], BF16) for i in range(2)]
            ob = [self.sb(es, "sbob%d" % i, [128, 512], BF16) for i in range(2)]
            hs = slice(h * 128, (h + 1) * 128)
            S.dma("sp", Qf[:], self.QK["SQ"].t[hs, :], [self.QK["SQ"]], [Qf])
            S.dma("sp", Kf[:], self.QK["SK"].t[hs, :], [self.QK["SK"]], [Kf])
            S.dma("sp", Vt[:], self.VT["SV"].t.rearrange("(n p) c -> p n c", p=128)[:, :, hs], [self.VT["SV"]], [Vt])
            S.ts("dve", ntri[:], self.cm32[:, 3, :], -1.0, None, ALU.mult, None, [self.cm32], [ntri])
            S.memset("dve", nones[:], -1.0, [nones])
            S.memset("dve", zeros[:], 0.0, [zeros])
            it = 0
            for g in range(NG):
                O = PS[4 + g % 2]
                cm = cum[g % 2]
                S.memset("pool", cm[:], 0.0, [cm])
                S.mm(O[:], zeros[:], Qf[:, g * 512:(g + 1) * 512], [zeros, Qf], [O], start=True, stop=False)
                tiles = [(4 * g + j, 128 * j) for j in (3, 2, 1, 0)] + [(kb, 0) for kb in range(4 * g - 1, -1, -1)]
                for ti, (kb, c0) in enumerate(tiles):
                    A = PS[it % 2]
                    B = PS[2 + it % 2]
                    e = E[it % 2]
                    sp = SP[it % 2]
                    at = AT[it % 2]
                    it += 1
                    q0 = g * 512 + c0
                    q1 = (g + 1) * 512
                    kT = Kf[:, kb * 128:(kb + 1) * 128]
                    S.mm(A[:, c0:512], kT, Qf[:, q0:q1], [Kf, Qf], [A])
                    S.act(e[:, c0:512], A[:, c0:512], AF.Exp, [A], [e])
                    S.act(sp[:, c0:512], e[:, c0:512], AF.Ln, [e], [sp], bias=1.0)
                    if kb >= 4 * g:
                        S.tt("pool", sp[:, c0:c0 + 128], sp[:, c0:c0 + 128], self.cmbf[:, 2, :], ALU.mult,
                             [sp, self.cmbf], [sp])
                    S.mm(B[:, c0:512], kT, Qf[:, q0:q1], [Kf, Qf], [B], start=True, stop=False)
                    S.mm(B[:, c0:512], ntri[:], sp[:, c0:512], [ntri, sp], [B], start=False, stop=(ti == 0))
                    if ti > 0:
                        S.mm(B[:, c0:512], nones[:], cm[:, c0:512], [nones, cm], [B], start=False, stop=True)
                    S.act(at[:, c0:512], B[:, c0:512], AF.Exp, [B], [at])
                    if kb >= 4 * g:
                        S.tt("pool", at[:, c0:c0 + 128], at[:, c0:c0 + 128], self.cmbf[:, 2, :], ALU.mult,
                             [at, self.cmbf], [at])
                    S.mm(O[:, c0:512], Vt[:, kb, :], at[:, c0:512], [Vt, at], [O], start=False,
                         stop=(ti == len(tiles) - 1))
                    if ti < len(tiles) - 1:
                        S.tt("dve", cm[:, c0:512], cm[:, c0:512], sp[:, c0:512], ALU.add, [cm, sp], [cm])
                o = ob[g % 2]
                S.copy("dve", o[:], O[:], [O], [o])
                S.dma("pool", self.Y["YC"].t[hs, g * 512:(g + 1) * 512], o[:], [o], [self.Y["YC"]])
            S.barrier()


def host_consts():
    p = np.arange(128)[:, None]
    f = np.arange(128)[None, :]
    cm = np.zeros((128, 9, 128), np.float32)
    cm[:, 0, :] = (p == f)
    cm[:, 1, :] = (f >= p)
    cm[:, 2, :] = (f > p)
    cm[:, 3, :] = (p >= f)
    same = (p // 64) == (f // 64)
    cm[:, 4, :] = (p <= f) & same
    cm[:, 5, :] = (p > f) & same
    cm[:, 6, :] = np.where((p > f) & same, 0.0, NEG)
    cm[:, 7, :] = np.where((f > p) & same, 0.0, NEG)
    cm[:, 8, :] = np.where((f >= p) & same, 0.0, NEG)
    return cm


def pack_cv(inp, L):
    cv = np.zeros((L, 128, NCV), np.float32)

    def chunks(v):
        return np.ascontiguousarray(v.reshape(-1, 128).T)
    for l in range(L):
        c = cv[l]
        c[:, CV["norm_mix"]:CV["norm_mix"] + 8] = chunks(inp["norm_mix"][l])
        c[:, CV["gate_bias"]:CV["gate_bias"] + 24] = chunks(inp["gate_bias"][l])
        c[:, CV["fox_qnorm"]] = inp["fox_qnorm"][l]
        c[:, CV["fox_knorm"]] = inp["fox_knorm"][l]
        gc = inp["gdn_conv"][l]
        for tap in range(4):
            c[:, CV["gdn_conv"] + tap * 12: CV["gdn_conv"] + (tap + 1) * 12] = chunks(gc[tap])
        c[:, CV["gdn_onorm"]] = inp["gdn_onorm"][l]
        c[:, CV["norm_xq"]:CV["norm_xq"] + 8] = chunks(inp["norm_xq"][l])
        c[:, CV["norm_mem"]:CV["norm_mem"] + 8] = chunks(inp["norm_mem"][l])
        c[:, CV["mq_norm"]] = inp["mq_norm"][l]
        c[:, CV["mk_norm"]] = inp["mk_norm"][l]
        c[:, CV["norm_ffn"]:CV["norm_ffn"] + 8] = chunks(inp["norm_ffn"][l])
        fc = inp["ffn_conv"][l]
        for tap in range(3):
            c[:, CV["ffn_conv"] + tap * 44: CV["ffn_conv"] + (tap + 1) * 44] = chunks(fc[tap])
        c[:, CV["ffn_conv_b"]:CV["ffn_conv_b"] + 44] = chunks(inp["ffn_conv_b"][l])
        s = CV["small"]
        c[0:4, s] = inp["fox_fbias"][l]
        c[64:68, s + 1] = inp["gdn_dt_bias"][l]
        c[64:68, s + 2] = inp["gdn_a_log"][l]
    return cv


def prep_inputs(inp, b, T, L):
    m = {}
    m["xT"] = np.ascontiguousarray(inp["x"][b, :T].T)
    m["memT"] = np.ascontiguousarray(inp["mem"][b].T)
    for n in ("w_in", "w_oa", "w_ob", "w_oc", "w_out", "w_mq", "w_mkv", "w_mo", "w_up", "w_down"):
        m[n] = np.ascontiguousarray(inp[n][:L])
    m["cv"] = pack_cv(inp, L)
    m["cmask"] = host_consts()
    return m


_CACHE = {}


def kernel(**inputs):
    inp = {k: np.asarray(v) for k, v in inputs.items()}
    B, T, _ = inp["x"].shape
    L = inp["w_in"].shape[0]
    key = (T, L)
    nc = Builder(T, L).build()
    ncores = 8
    in_maps = [prep_inputs(inp, c % B, T, L) for c in range(B)]
    in_maps = [in_maps[c % B] for c in range(ncores)]
    res = run_bass_kernel_spmd(nc, in_maps, core_ids=list(range(ncores)))
    out = np.empty((B, T, D), np.float32)
    for b in range(B):
        out[b] = np.asarray(res.results[b]["yT"]).T
    return out
```

```python
import numpy as np
from contextlib import ExitStack
import concourse.bass as bass
import concourse.mybir as mybir
from concourse.bass_utils import run_bass_kernel_spmd

F32 = mybir.dt.float32
BF16 = mybir.dt.bfloat16
ALU = mybir.AluOpType
AF = mybir.ActivationFunctionType

D = 1024
NH = 4
DH = 128
EPS = 1e-6
MEMT = 256
DFF = 2816
OFF = dict(FQ=0, FK=512, FV=1024, FF=1536, GQ=1540, GK=2052, GV=2564, GB=3076, GA=3080,
           GZ=3084, SQ=3596, SK=4108, SV=4620, GT=5132, END=8204)
SCALE = DH ** -0.5
NEG = -30000.0

CV = {}
_o = 0
for _n, _w in [("norm_mix", 8), ("gate_bias", 24), ("fox_qnorm", 1), ("fox_knorm", 1),
               ("gdn_conv", 48), ("gdn_onorm", 1), ("norm_xq", 8), ("norm_mem", 8),
               ("mq_norm", 1), ("mk_norm", 1), ("norm_ffn", 8), ("ffn_conv", 132),
               ("ffn_conv_b", 44), ("small", 3)]:
    CV[_n] = _o
    _o += _w
NCV = _o


class Tile:
    def __init__(self, t, name, psum=False):
        self.t = t
        self.name = name
        self.psum = psum
        self.w = {}
        self.rs = {}

    def __getitem__(self, idx):
        return self.t[idx]


class Trk:
    def __init__(self):
        self.w = {}
        self.rs = {}


class Eng:
    def __init__(self, name, eng, sem):
        self.name = name
        self.eng = eng
        self.sem = sem
        self.n = 0
        self.seen = {}


class Sched:
    def __init__(self, nc, es, ndsem=12):
        self.nc = nc
        self.es = es
        self.sems = {}
        self.E = {}
        for name, eng in [("pe", nc.tensor), ("act", nc.scalar), ("dve", nc.vector),
                          ("pool", nc.gpsimd), ("sp", nc.sync)]:
            sem = es.enter_context(nc.semaphore("sem_" + name))
            self.E[name] = Eng(name, eng, sem)
            self.sems[name] = sem
        self.dpool = {}
        self.dnext = {}
        for q in ("sp", "pool", "act"):
            lst = []
            for i in range(ndsem):
                key = "d_%s_%d" % (q, i)
                sem = es.enter_context(nc.semaphore(key))
                self.sems[key] = sem
                lst.append([sem, 0, key])
            self.dpool[q] = lst
            self.dnext[q] = 0
        self.ninst = 0

    def _deps(self, reads, writes):
        toks = []
        for r in reads:
            for k, v in r.w.items():
                toks.append((k, v, True))
            if getattr(r, "psum", False):
                for k, v in r.rs.items():
                    toks.append((k, v, False))
        for w in writes:
            for k, v in w.w.items():
                toks.append((k, v, False))
            for k, v in w.rs.items():
                toks.append((k, v, False))
        return toks

    def _wait(self, E, toks):
        for key, val, raw in toks:
            if key == E.name:
                if not raw or E.name == "pe":
                    continue
            if E.seen.get(key, 0) >= val:
                continue
            E.eng.wait_ge(self.sems[key], val)
            E.seen[key] = val
            self.ninst += 1
            E.nw = getattr(E, "nw", 0) + 1

    def _mark(self, tok, reads, writes):
        k, v = tok
        for r in reads:
            if r.rs.get(k, 0) < v:
                r.rs[k] = v
        for w in writes:
            w.w[k] = v
            w.rs = {}

    def op(self, en, fn, reads, writes):
        E = self.E[en]
        self._wait(E, self._deps(reads, writes))
        ins = fn(E.eng)
        E.n += 1
        ins.then_inc(E.sem, 1)
        self.ninst += 1
        self._mark((E.name, E.n), reads, writes)

    def dma(self, q, out, in_, reads, writes, **kw):
        Q = self.E[q]
        toks = self._deps(reads, writes)
        pool = self.dpool[q]
        i = self.dnext[q]
        self.dnext[q] = (i + 1) % len(pool)
        sem, cnt, key = pool[i]
        if cnt > 0:
            toks.append((key, cnt, False))
        self._wait(Q, toks)
        Q.eng.dma_start(out=out, in_=in_, **kw).then_inc(sem, 16)
        pool[i][1] = cnt + 16
        self.ninst += 1
        self._mark((key, cnt + 16), reads, writes)

    def barrier(self):
        toks = [(n, e.n, True) for n, e in self.E.items() if e.n > 0]
        for q, pool in self.dpool.items():
            for sem, cnt, key in pool:
                if cnt > 0:
                    toks.append((key, cnt, False))
        for E in self.E.values():
            self._wait(E, toks)

    def mm(self, out, lhsT, rhs, reads, writes, start=True, stop=True):
        self.op("pe", lambda e: e.matmul(out, lhsT, rhs, start=start, stop=stop), reads, writes)

    def tr(self, out, in_, ident, reads, writes):
        self.op("pe", lambda e: e.transpose(out, in_, ident), reads, writes)

    def act(self, out, in_, func, reads, writes, bias=None, scale=None, en="act"):
        kw = {}
        if bias is not None:
            kw["bias"] = bias
        if scale is not None:
            kw["scale"] = scale
        self.op(en, lambda e: e.activation(out, in_, func, **kw), reads, writes)

    def tt(self, en, out, in0, in1, op, reads, writes):
        self.op(en, lambda e: e.tensor_tensor(out, in0, in1, op), reads, writes)

    def ts(self, en, out, in0, s1, s2, op0, op1, reads, writes):
        if op1 is None:
            self.op(en, lambda e: e.tensor_scalar(out, in0, s1, None, op0), reads, writes)
        else:
            self.op(en, lambda e: e.tensor_scalar(out, in0, s1, s2, op0, op1), reads, writes)

    def stt(self, out, in0, scalar, in1, op0, op1, reads, writes):
        self.op("dve", lambda e: e.scalar_tensor_tensor(out, in0, scalar, in1, op0, op1), reads, writes)

    def copy(self, en, out, in_, reads, writes):
        if en == "act":
            self.op(en, lambda e: e.activation(out, in_, AF.Copy), reads, writes)
        else:
            self.op(en, lambda e: e.tensor_copy(out, in_), reads, writes)

    def memset(self, en, ap, val, writes):
        self.op(en, lambda e: e.memset(ap, val), [], writes)


class _SqAlias:
    def __init__(self, base):
        self.base = base

    @property
    def w(self):
        return self.base.w

    @property
    def rs(self):
        return self.base.rs

    @rs.setter
    def rs(self, v):
        self.base.rs = v

    def __getitem__(self, idx):
        if idx == slice(None):
            return self.base.t[:, 0:8, :]
        return self.base.t[idx]


class Builder:
    def __init__(self, T, L, dbg=()):
        self.T = T
        self.L = L
        self.NG = T // 512
        self.dbg = set(dbg)
        self.nc = bass.Bass("TRN2", target_bir_lowering=False)
        self.outs = []

    def dram_in(self, name, shape, dt=F32):
        return Tile(self.nc.dram_tensor(name, list(shape), dt, kind="ExternalInput"), name)

    def dram(self, name, shape, dt):
        kind = "Internal"
        if name in self.dbg or name == "yT":
            kind = "ExternalOutput"
            self.outs.append(name)
        return Tile(self.nc.dram_tensor(name, list(shape), dt, kind=kind), name)

    def sb(self, es, name, shape, dt):
        self.uid = getattr(self, "uid", 0) + 1
        name = "%s_%d" % (name, self.uid)
        return Tile(es.enter_context(self.nc.sbuf_tensor(name, list(shape), dt)), name)

    def build(self):
        nc, T, L = self.nc, self.T, self.L
        self.inp = {}
        I = self.inp
        I["xT"] = self.dram_in("xT", [D, T])
        I["memT"] = self.dram_in("memT", [D, MEMT])
        I["w_in"] = self.dram_in("w_in", [L, D, OFF["END"]])
        I["cv"] = self.dram_in("cv", [L, 128, NCV])
        I["w_oa"] = self.dram_in("w_oa", [L, 512, D])
        I["w_ob"] = self.dram_in("w_ob", [L, 512, D])
        I["w_oc"] = self.dram_in("w_oc", [L, 512, D])
        I["w_out"] = self.dram_in("w_out", [L, D, D])
        I["w_mq"] = self.dram_in("w_mq", [L, D, 512])
        I["w_mkv"] = self.dram_in("w_mkv", [L, D, 1024])
        I["w_mo"] = self.dram_in("w_mo", [L, 512, D])
        I["w_up"] = self.dram_in("w_up", [L, D, 2 * DFF])
        I["w_down"] = self.dram_in("w_down", [L, DFF, D])
        I["cmask"] = self.dram_in("cmask", [128, 9, 128])
        self.X = [self.dram("xres0", [D, T], F32), self.dram("xres1", [D, T], F32)]
        self.yT = self.dram("yT", [D, T], F32)
        self.HT = self.dram("HT", [D, T], BF16)
        self.QK = {n: self.dram(n, [512, T], BF16) for n in
                   ("FQ", "FK", "SQ", "SK", "GQ", "GK", "GV", "GZ")}
        self.VT = {n: self.dram(n, [T, 512], BF16) for n in ("FV", "SV")}
        self.ROW = {n: self.dram(n, [4, T], F32) for n in ("CROW", "BETA", "GLOG", "EGC")}
        self.GATES = self.dram("GATES", [3072, T], BF16)
        self.Y = {n: self.dram(n, [512, T], BF16) for n in ("YA", "YB", "YC")}
        with ExitStack() as es:
            self.es = es
            self.S = Sched(nc, es)
            self.consts(es)
            for l in range(L):
                self.layer(l)
            self.finish()
        return nc

    def consts(self, es):
        S = self.S
        self.PS = [Tile(es.enter_context(self.nc.psum_tensor("ps%d" % i, [128, 512], F32)), "ps%d" % i, psum=True)
                   for i in range(8)]
        self.ones_bf = self.sb(es, "ones_bf", [128, 128], BF16)
        S.memset("dve", self.ones_bf[:], 1.0, [self.ones_bf])
        self.cm32 = self.sb(es, "cm32", [128, 9, 128], F32)
        S.dma("sp", self.cm32[:], self.inp["cmask"][:, :, :], [self.inp["cmask"]], [self.cm32])
        self.cmbf = self.sb(es, "cmbf", [128, 9, 128], BF16)
        S.copy("dve", self.cmbf[:], self.cm32[:], [self.cm32], [self.cmbf])
        self.cvt = self.sb(es, "cvt", [128, NCV], F32)
        self.dvt = self.sb(es, "dvt", [128, 4], F32)

    def finish(self):
        S = self.S
        for n in self.dbg:
            if n.startswith("nops"):
                for i in range(int(n[4:])):
                    S.dma("sp", self.cvt[:], self.inp["cv"][0, :, :], [self.inp["cv"]], [self.cvt])
        S.barrier()

    def layer(self, l):
        S = self.S
        S.barrier()
        S.dma("sp", self.cvt[:], self.inp["cv"][l, :, :], [self.inp["cv"]], [self.cvt])
        cs = CV["small"]
        S.ts("dve", self.dvt[:, 0:1], self.cvt[:, cs:cs + 1], -1.0, None, ALU.mult, None, [self.cvt], [self.dvt])
        S.act(self.dvt[:, 1:2], self.cvt[:, cs + 2:cs + 3], AF.Exp, [self.cvt], [self.dvt])
        S.ts("dve", self.dvt[:, 1:2], self.dvt[:, 1:2], -1.0, None, ALU.mult, None, [self.dvt], [self.dvt])
        xin = self.inp["xT"] if l == 0 else self.X[1]
        self.pass_a(l, xin)
        nh = 2 if "nh2" in self.dbg else (1 if "nh1" in self.dbg else NH)
        if "nofox" not in self.dbg:
            for h in range(nh):
                self.fox_head(h)
        if "nosb" not in self.dbg:
            for h in range(nh):
                self.sb_head(h)
        self.pass_b(l)
        if "nogdn" not in self.dbg:
            for h in range(nh):
                self.gdn_head(h)
        self.pass_c(l, xin, self.X[0])
        self.pass_d(l, self.X[0], self.yT if l == self.L - 1 else self.X[1])

    def load_w(self, stg, W, wcol, src, row0, c0, ncols, gain_col=None, kchunks=8, flip=[0]):
        S = self.S
        for k in range(kchunks):
            done = 0
            while done < ncols:
                n = min(1024, ncols - done)
                st = stg[flip[0] % len(stg)]
                flip[0] += 1
                S.dma("sp", st[:, 0:n], src[row0 + k * 128: row0 + (k + 1) * 128, c0 + done: c0 + done + n],
                      [], [st])
                dst = W[:, k, wcol + done: wcol + done + n]
                if gain_col is None:
                    en = "dve" if flip[0] % 2 else "pool"
                    S.copy(en, dst, st[:, 0:n], [st], [W])
                else:
                    g = self.cvt[:, gain_col + k: gain_col + k + 1]
                    if flip[0] % 2:
                        S.ts("dve", dst, st[:, 0:n], g, None, ALU.mult, None, [st, self.cvt], [W])
                    else:
                        S.act(dst, st[:, 0:n], AF.Copy, [st, self.cvt], [W], scale=g)
                done += n

    def norm_group(self, xt, hT, sq, lnv, rstd, ps):
        S = self.S
        S.act(sq[:], xt[:], AF.Square, [xt], [sq])
        for k in range(8):
            S.mm(ps[:], self.ones_bf[:], sq[:, k, :], [self.ones_bf, sq], [ps], start=(k == 0), stop=(k == 7))
        S.act(lnv[:], ps[:], AF.Ln, [ps], [lnv], bias=EPS, scale=1.0 / D)
        S.act(rstd[:], lnv[:], AF.Exp, [lnv], [rstd], scale=-0.5)
        for k in range(8):
            S.tt("dve" if k % 2 == 0 else "pool", hT[:, k, :], xt[:, k, :], rstd[:], ALU.mult, [xt, rstd], [hT])

    def pass_a(self, l, xin):
        S, T, NG = self.S, self.T, self.NG
        w_in = self.inp["w_in"]
        with ExitStack() as es:
            NWA = 2048 + 1024
            W = self.sb(es, "WA", [128, 8, NWA], BF16)
            Wsm = self.sb(es, "WAsm", [128, 8, 96], BF16)
            stg = [self.sb(es, "stgA%d" % i, [128, 1024], F32) for i in range(2)]
            S.memset("pool", Wsm[:], 0.0, [Wsm])
            gm = CV["norm_mix"]
            wl = w_in.t[l]
            for (name, wc) in (("FQ", 0), ("FK", 512), ("SQ", 1024), ("SK", 1536), ("FV", 2048), ("SV", 2560)):
                self.load_w(stg, W, wc, wl, 0, OFF[name], 512, gain_col=gm)
            for (name, wc) in (("FF", 0), ("GB", 32), ("GA", 64)):
                self.load_w(stg, Wsm, wc, wl, 0, OFF[name], 4, gain_col=gm)
            xt = self.sb(es, "xtA", [128, 8, 512], F32)
            hT = self.sb(es, "hTA", [128, 8, 512], BF16)
            sq = self.sb(es, "sqA", [128, 8, 512], BF16)
            lnv = self.sb(es, "lnvA", [128, 512], F32)
            rstd = self.sb(es, "rstdA", [128, 512], F32)
            sq2 = [self.sb(es, "sq2A%d" % i, [128, 512], BF16) for i in range(2)]
            ln2 = [self.sb(es, "ln2A%d" % i, [128, 512], F32) for i in range(2)]
            ob = [self.sb(es, "obA%d" % i, [128, 512], BF16) for i in range(4)]
            sm = {n: self.sb(es, "smA_" + n, [128, 512], F32) for n in ("e1", "l1", "c", "beta", "glog", "gc", "egc")}
            ones4 = self.sb(es, "ones4", [128, 512], F32)
            rmask = self.sb(es, "rmask", [128, 512], F32)
            S.memset("pool", ones4[:], 1.0, [ones4])
            S.memset("pool", rmask[:], 1.0, [rmask])
            for c in range(8):
                S.memset("pool", rmask[:, c * 64: c * 64 + 1], 0.0, [rmask])
            ccarry = self.sb(es, "ccarry", [128, 1], F32)
            S.memset("pool", ccarry[:], 0.0, [ccarry])
            xv = xin.t.rearrange("(k p) t -> p k t", p=128)
            hv = self.HT.t.rearrange("(k p) t -> p k t", p=128)
            cvs = CV["small"]
            nob = 0
            pi = 0
            for g in range(NG):
                ts_ = slice(g * 512, (g + 1) * 512)
                S.dma("sp", xt[:], xv[:, :, ts_], [xin], [xt])
                self.norm_group(xt, hT, sq, lnv, rstd, self.PS[0])
                S.dma("pool", hv[:, :, ts_], hT[:], [hT], [self.HT])
                for fam, wc, kind in (("FQ", 0, "nq"), ("FK", 512, "nk"), ("SQ", 1024, "s"), ("SK", 1536, "c")):
                    for h in range(4):
                        ps = self.PS[1 + pi % 3]
                        pi += 1
                        for k in range(8):
                            S.mm(ps[:], W[:, k, wc + h * 128: wc + (h + 1) * 128], hT[:, k, :], [W, hT], [ps],
                                 start=(k == 0), stop=(k == 7))
                        o = ob[nob % 4]
                        nob += 1
                        if kind in ("nq", "nk"):
                            s2 = sq2[nob % 2]
                            l2 = ln2[nob % 2]
                            ps2 = self.PS[4 + nob % 2]
                            S.act(s2[:], ps[:], AF.Square, [ps], [s2])
                            S.mm(ps2[:], self.ones_bf[:], s2[:], [self.ones_bf, s2], [ps2])
                            S.act(l2[:], ps2[:], AF.Ln, [ps2], [l2], bias=EPS, scale=1.0 / DH)
                            S.act(l2[:], l2[:], AF.Exp, [l2], [l2], scale=-0.5)
                            gcol = CV["fox_qnorm"] if kind == "nq" else CV["fox_knorm"]
                            S.stt(o[:], ps[:], self.cvt[:, gcol:gcol + 1], l2[:], ALU.mult, ALU.mult,
                                  [ps, self.cvt, l2], [o])
                            if kind == "nq":
                                S.ts("pool", o[:], o[:], SCALE, 1.0, ALU.mult, ALU.mult, [o], [o])
                        elif kind == "s":
                            S.act(o[:], ps[:], AF.Copy, [ps], [o], scale=SCALE)
                        else:
                            S.copy("dve", o[:], ps[:], [ps], [o])
                        dst = self.QK[fam]
                        S.dma("pool", dst.t[h * 128:(h + 1) * 128, ts_], o[:], [o], [dst])
                for fam, wc in (("FV", 2048), ("SV", 2560)):
                    for sub in range(4):
                        ps = self.PS[1 + pi % 3]
                        pi += 1
                        for k in range(8):
                            S.mm(ps[:], hT[:, k, sub * 128:(sub + 1) * 128], W[:, k, wc: wc + 512], [W, hT], [ps],
                                 start=(k == 0), stop=(k == 7))
                        o = ob[nob % 4]
                        nob += 1
                        S.copy("dve" if sub % 2 else "act", o[:], ps[:], [ps], [o])
                        dst = self.VT[fam]
                        r0 = g * 512 + sub * 128
                        S.dma("pool", dst.t[r0:r0 + 128, :], o[:], [o], [dst])
                ps = self.PS[6]
                for k in range(8):
                    S.mm(ps[0:96, :], Wsm[:, k, :], hT[:, k, :], [Wsm, hT], [ps], start=(k == 0), stop=(k == 7))
                cv = self.cvt
                S.act(sm["e1"][0:4, :], ps[0:4, :], AF.Exp, [ps, self.dvt], [sm["e1"]], bias=self.dvt[0:4, 0:1], scale=-1.0)
                S.act(sm["l1"][0:4, :], sm["e1"][0:4, :], AF.Ln, [sm["e1"]], [sm["l1"]], bias=1.0)
                S.op("dve", lambda e: e.tensor_tensor_scan(sm["c"][0:4, :], ones4[0:4, :], sm["l1"][0:4, :],
                                                           ccarry[0:4, 0:1], ALU.mult, ALU.subtract),
                     [ones4, sm["l1"], ccarry], [sm["c"]])
                S.copy("dve", ccarry[0:4, :], sm["c"][0:4, 511:512], [sm["c"]], [ccarry])
                S.dma("pool", self.ROW["CROW"].t[:, ts_], sm["c"][0:4, :], [sm["c"]], [self.ROW["CROW"]])
                S.act(sm["beta"][32:36, :], ps[32:36, :], AF.Sigmoid, [ps], [sm["beta"]])
                S.dma("pool", self.ROW["BETA"].t[:, ts_], sm["beta"][32:36, :], [sm["beta"]], [self.ROW["BETA"]])
                S.act(sm["e1"][64:68, :], ps[64:68, :], AF.Exp, [ps, cv], [sm["e1"]], bias=cv[64:68, cvs + 1:cvs + 2])
                S.act(sm["l1"][64:68, :], sm["e1"][64:68, :], AF.Ln, [sm["e1"]], [sm["l1"]], bias=1.0)
                S.ts("dve", sm["glog"][64:68, :], sm["l1"][64:68, :], self.dvt[64:68, 1:2], None, ALU.mult, None,
                     [sm["l1"], self.dvt], [sm["glog"]])
                S.op("dve", lambda e: e.tensor_tensor_scan(sm["gc"][64:68, :], rmask[64:68, :], sm["glog"][64:68, :],
                                                           0.0, ALU.mult, ALU.add),
                     [rmask, sm["glog"]], [sm["gc"]])
                S.act(sm["egc"][64:68, :], sm["gc"][64:68, :], AF.Exp, [sm["gc"]], [sm["egc"]])
                S.dma("pool", self.ROW["GLOG"].t[:, ts_], sm["glog"][64:68, :], [sm["glog"]], [self.ROW["GLOG"]])
                S.dma("pool", self.ROW["EGC"].t[:, ts_], sm["egc"][64:68, :], [sm["egc"]], [self.ROW["EGC"]])
            S.barrier()


    def pass_b(self, l):
        S, T, NG = self.S, self.T, self.NG
        w_in = self.inp["w_in"]
        PS = self.PS
        cv = self.cvt
        with ExitStack() as es:
            W = self.sb(es, "WB", [128, 8, 5120], BF16)
            stg = [self.sb(es, "stgB%d" % i, [128, 1024], F32) for i in range(2)]
            gm = CV["norm_mix"]
            wl = w_in.t[l]
            self.load_w(stg, W, 0, wl, 0, OFF["GQ"], 1536, gain_col=gm)
            self.load_w(stg, W, 1536, wl, 0, OFF["GZ"], 512, gain_col=gm)
            self.load_w(stg, W, 2048, wl, 0, OFF["GT"], 3072, gain_col=gm)
            hT = [self.sb(es, "hTB%d" % i, [128, 8, 512], BF16) for i in range(2)]
            halo = self.sb(es, "haloB", [128, 12, 4], F32)
            buf = [self.sb(es, "bufB%d" % i, [128, 516], F32) for i in range(2)]
            acc = [self.sb(es, "accB%d" % i, [128, 512], F32) for i in range(2)]
            sil = [self.sb(es, "silB%d" % i, [128, 512], F32) for i in range(2)]
            s2 = [self.sb(es, "sq2B%d" % i, [128, 512], BF16) for i in range(2)]
            l2 = [self.sb(es, "ln2B%d" % i, [128, 512], F32) for i in range(2)]
            ob = [self.sb(es, "obB%d" % i, [128, 512], BF16) for i in range(4)]
            S.memset("pool", halo[:], 0.0, [halo])
            hv = self.HT.t.rearrange("(k p) t -> p k t", p=128)
            pi = 0
            nob = 0
            cw = CV["gdn_conv"]
            for g in range(NG):
                ts_ = slice(g * 512, (g + 1) * 512)
                h_ = hT[g % 2]
                S.dma("sp", h_[:], hv[:, :, ts_], [self.HT], [h_])
                for c in range(12):
                    ps = PS[pi % 3]
                    pi += 1
                    for k in range(8):
                        S.mm(ps[:], W[:, k, c * 128:(c + 1) * 128], h_[:, k, :], [W, h_], [ps], start=(k == 0), stop=(k == 7))
                    b = buf[c % 2]
                    a = acc[c % 2]
                    sl = sil[c % 2]
                    S.copy("pool", b[:, 0:3], halo[:, c, 0:3], [halo], [b])
                    S.copy("act", b[:, 3:515], ps[:], [ps], [b])
                    S.copy("pool", halo[:, c, 0:3], b[:, 512:515], [b], [halo])
                    S.ts("dve", a[:], b[:, 0:512], cv[:, cw + c:cw + c + 1], None, ALU.mult, None, [b, cv], [a])
                    for tap in (1, 2, 3):
                        S.stt(a[:], b[:, tap:tap + 512], cv[:, cw + tap * 12 + c: cw + tap * 12 + c + 1], a[:],
                              ALU.mult, ALU.add, [b, cv, a], [a])
                    o = ob[nob % 4]
                    nob += 1
                    fam = ("GQ", "GK", "GV")[c // 4]
                    hh = c % 4
                    if fam == "GV":
                        S.act(o[:], a[:], AF.Silu, [a], [o])
                    else:
                        S.act(sl[:], a[:], AF.Silu, [a], [sl])
                        q2 = s2[c % 2]
                        ll = l2[c % 2]
                        ps2 = PS[3 + c % 2]
                        S.act(q2[:], sl[:], AF.Square, [sl], [q2])
                        S.mm(ps2[:], self.ones_bf[:], q2[:], [self.ones_bf, q2], [ps2])
                        S.act(ll[:], ps2[:], AF.Ln, [ps2], [ll], bias=EPS)
                        S.act(ll[:], ll[:], AF.Exp, [ll], [ll], scale=-0.5)
                        if fam == "GQ":
                            S.stt(o[:], sl[:], SCALE, ll[:], ALU.mult, ALU.mult, [sl, ll], [o])
                        else:
                            S.tt("dve", o[:], sl[:], ll[:], ALU.mult, [sl, ll], [o])
                    S.dma("pool", self.QK[fam].t[hh * 128:(hh + 1) * 128, ts_], o[:], [o], [self.QK[fam]])
                for c in range(4):
                    ps = PS[pi % 3]
                    pi += 1
                    for k in range(8):
                        S.mm(ps[:], W[:, k, 1536 + c * 128:1536 + (c + 1) * 128], h_[:, k, :], [W, h_], [ps],
                             start=(k == 0), stop=(k == 7))
                    o = ob[nob % 4]
                    nob += 1
                    S.act(o[:], ps[:], AF.Silu, [ps], [o])
                    S.dma("pool", self.QK["GZ"].t[c * 128:(c + 1) * 128, ts_], o[:], [o], [self.QK["GZ"]])
                gb = CV["gate_bias"]
                for c in range(24):
                    ps = PS[pi % 3]
                    pi += 1
                    for k in range(8):
                        S.mm(ps[:], W[:, k, 2048 + c * 128:2048 + (c + 1) * 128], h_[:, k, :], [W, h_], [ps],
                             start=(k == 0), stop=(k == 7))
                    o = ob[nob % 4]
                    nob += 1
                    S.act(o[:], ps[:], AF.Sigmoid, [ps, cv], [o], bias=cv[:, gb + c:gb + c + 1])
                    S.dma("pool", self.GATES.t[c * 128:(c + 1) * 128, ts_], o[:], [o], [self.GATES])
            S.barrier()

    def head_norm(self, ps, ncols, gcol, out, sq, ll, ps2, post_scale=None):
        S = self.S
        S.act(sq[:, 0:ncols], ps[:, 0:ncols], AF.Square, [ps], [sq])
        S.mm(ps2[:, 0:ncols], self.ones_bf[:], sq[:, 0:ncols], [self.ones_bf, sq], [ps2])
        S.act(ll[:, 0:ncols], ps2[:, 0:ncols], AF.Ln, [ps2], [ll], bias=EPS, scale=1.0 / DH)
        S.act(ll[:, 0:ncols], ll[:, 0:ncols], AF.Exp, [ll], [ll], scale=-0.5)
        S.stt(out, ps[:, 0:ncols], self.cvt[:, gcol:gcol + 1], ll[:, 0:ncols], ALU.mult, ALU.mult,
              [ps, self.cvt, ll], [out.tile] if hasattr(out, "tile") else [])

    def pass_c(self, l, xin, xout):
        S, T, NG = self.S, self.T, self.NG
        I = self.inp
        PS = self.PS
        cv = self.cvt
        with ExitStack() as es:
            Wo = [self.sb(es, "WCo%d" % i, [128, 4, D], BF16) for i in range(3)]
            Wout = self.sb(es, "WCout", [128, 8, D], BF16)
            Wmq = self.sb(es, "WCmq", [128, 8, 512], BF16)
            Wmo = self.sb(es, "WCmo", [128, 4, D], BF16)
            Wkv = self.sb(es, "WCkv", [128, 8, D], BF16)
            stg = [self.sb(es, "stgC%d" % i, [128, 1024], F32) for i in range(2)]
            for i, n in enumerate(("w_oa", "w_ob", "w_oc")):
                self.load_w(stg, Wo[i], 0, I[n].t[l], 0, 0, D, kchunks=4)
            self.load_w(stg, Wout, 0, I["w_out"].t[l], 0, 0, D)
            self.load_w(stg, Wmq, 0, I["w_mq"].t[l], 0, 0, 512, gain_col=CV["norm_xq"])
            self.load_w(stg, Wmo, 0, I["w_mo"].t[l], 0, 0, D, kchunks=4)
            self.load_w(stg, Wkv, 0, I["w_mkv"].t[l], 0, 0, D, gain_col=CV["norm_mem"])
            xt = self.sb(es, "xtC", [128, 8, 512], F32)
            hT = self.sb(es, "hTC", [128, 8, 512], BF16)
            sq = self.sb(es, "sqC", [128, 8, 512], BF16)
            lnv = self.sb(es, "lnvC", [128, 512], F32)
            rstd = self.sb(es, "rstdC", [128, 512], F32)
            yb_ = [self.sb(es, "yC%d" % i, [128, 4, 512], BF16) for i in range(3)]
            gt = self.sb(es, "gtC", [128, 24, 512], BF16)
            tA = self.sb(es, "tAC", [128, 512], F32)
            tB = self.sb(es, "tBC", [128, 512], F32)
            mix = self.sb(es, "mixC", [128, 8, 512], BF16)
            om = self.sb(es, "omC", [128, 4, 512], BF16)
            sq2 = self.sb(es, "sq2C", [128, 512], BF16)
            ll = self.sb(es, "llC", [128, 512], F32)
            qn = self.sb(es, "qnC", [128, 512], BF16)
            pt = [self.sb(es, "ptC%d" % i, [128, 512], BF16) for i in range(2)]
            Km = self.sb(es, "KmC", [128, 4, MEMT], BF16)
            Vm = self.sb(es, "VmC", [128, 2, 512], BF16)
            mv = I["memT"].t.rearrange("(k p) t -> p k t", p=128)
            S.dma("sp", xt[:, :, 0:MEMT], mv, [I["memT"]], [xt])
            S.act(sq[:, :, 0:MEMT], xt[:, :, 0:MEMT], AF.Square, [xt], [sq])
            for k in range(8):
                S.mm(PS[0][:, 0:MEMT], self.ones_bf[:], sq[:, k, 0:MEMT], [self.ones_bf, sq], [PS[0]], start=(k == 0), stop=(k == 7))
            S.act(lnv[:, 0:MEMT], PS[0][:, 0:MEMT], AF.Ln, [PS[0]], [lnv], bias=EPS, scale=1.0 / D)
            S.act(rstd[:, 0:MEMT], lnv[:, 0:MEMT], AF.Exp, [lnv], [rstd], scale=-0.5)
            for k in range(8):
                S.tt("dve", hT[:, k, 0:MEMT], xt[:, k, 0:MEMT], rstd[:, 0:MEMT], ALU.mult, [xt, rstd], [hT])
            for h in range(4):
                ps = PS[1 + h % 2]
                for k in range(8):
                    S.mm(ps[:, 0:MEMT], Wkv[:, k, h * 128:(h + 1) * 128], hT[:, k, 0:MEMT], [Wkv, hT], [ps], start=(k == 0), stop=(k == 7))
                S.act(sq2[:, 0:MEMT], ps[:, 0:MEMT], AF.Square, [ps], [sq2])
                S.mm(PS[3][:, 0:MEMT], self.ones_bf[:], sq2[:, 0:MEMT], [self.ones_bf, sq2], [PS[3]])
                S.act(ll[:, 0:MEMT], PS[3][:, 0:MEMT], AF.Ln, [PS[3]], [ll], bias=EPS, scale=1.0 / DH)
                S.act(ll[:, 0:MEMT], ll[:, 0:MEMT], AF.Exp, [ll], [ll], scale=-0.5)
                gk = CV["mk_norm"]
                S.stt(Km[:, h, :], ps[:, 0:MEMT], cv[:, gk:gk + 1], ll[:, 0:MEMT], ALU.mult, ALU.mult, [ps, cv, ll], [Km])
            for blk in range(2):
                ps = PS[1 + blk % 2]
                for k in range(8):
                    S.mm(ps[:], hT[:, k, blk * 128:(blk + 1) * 128], Wkv[:, k, 512:1024], [Wkv, hT], [ps], start=(k == 0), stop=(k == 7))
                S.copy("act", Vm[:, blk, :], ps[:], [ps], [Vm])
            xv = xin.t.rearrange("(k p) t -> p k t", p=128)
            xo = xout.t.rearrange("(k p) t -> p k t", p=128)
            yv = [self.Y[n].t.rearrange("(h p) t -> p h t", p=128) for n in ("YA", "YB", "YC")]
            gv = self.GATES.t.rearrange("(c p) t -> p c t", p=128)
            pi = 0
            for g in range(NG):
                ts_ = slice(g * 512, (g + 1) * 512)
                S.dma("sp", xt[:], xv[:, :, ts_], [xin], [xt])
                for i, n in enumerate(("YA", "YB", "YC")):
                    S.dma("sp", yb_[i][:], yv[i][:, :, ts_], [self.Y[n]], [yb_[i]])
                S.dma("sp", gt[:], gv[:, :, ts_], [self.GATES], [gt])
                for oc in range(8):
                    ocs = slice(oc * 128, (oc + 1) * 128)
                    for br in range(3):
                        ps = PS[pi % 3]
                        pi += 1
                        for hh in range(4):
                            S.mm(ps[:], Wo[br][:, hh, ocs], yb_[br][:, hh, :], [Wo[br], yb_[br]], [ps], start=(hh == 0), stop=(hh == 3))
                        if br == 0:
                            S.tt("dve", tA[:], ps[:], gt[:, oc, :], ALU.mult, [ps, gt], [tA])
                        else:
                            S.tt("dve", tB[:], ps[:], gt[:, br * 8 + oc, :], ALU.mult, [ps, gt], [tB])
                            if br == 1:
                                S.tt("pool", tA[:], tA[:], tB[:], ALU.add, [tA, tB], [tA])
                            else:
                                S.tt("pool", mix[:, oc, :], tA[:], tB[:], ALU.add, [tA, tB], [mix])
                for oc in range(8):
                    ocs = slice(oc * 128, (oc + 1) * 128)
                    ps = PS[pi % 3]
                    pi += 1
                    for k in range(8):
                        S.mm(ps[:], Wout[:, k, ocs], mix[:, k, :], [Wout, mix], [ps], start=(k == 0), stop=(k == 7))
                    S.tt("dve", xt[:, oc, :], xt[:, oc, :], ps[:], ALU.add, [xt, ps], [xt])
                self.norm_group(xt, hT, sq, lnv, rstd, PS[3])
                for h in range(4):
                    ps = PS[pi % 3]
                    pi += 1
                    for k in range(8):
                        S.mm(ps[:], Wmq[:, k, h * 128:(h + 1) * 128], hT[:, k, :], [Wmq, hT], [ps], start=(k == 0), stop=(k == 7))
                    S.act(sq2[:], ps[:], AF.Square, [ps], [sq2])
                    S.mm(PS[3][:], self.ones_bf[:], sq2[:], [self.ones_bf, sq2], [PS[3]])
                    S.act(ll[:], PS[3][:], AF.Ln, [PS[3]], [ll], bias=EPS, scale=1.0 / DH)
                    S.act(ll[:], ll[:], AF.Exp, [ll], [ll], scale=-0.5)
                    gq = CV["mq_norm"]
                    S.stt(tA[:], ps[:], cv[:, gq:gq + 1], ll[:], ALU.mult, ALU.mult, [ps, cv, ll], [tA])
                    S.ts("pool", qn[:], tA[:], SCALE, 1.0, ALU.mult, ALU.mult, [tA], [qn])
                    O = PS[4 + h % 2]
                    DN = PS[6 + h % 2]
                    for kb in range(2):
                        ps = PS[pi % 3]
                        pi += 1
                        p_ = pt[kb]
                        S.mm(ps[:], Km[:, h, kb * 128:(kb + 1) * 128], qn[:], [Km, qn], [ps])
                        S.act(p_[:], ps[:], AF.Exp, [ps], [p_])
                        S.mm(O[:], Vm[:, kb, h * 128:(h + 1) * 128], p_[:], [Vm, p_], [O], start=(kb == 0), stop=(kb == 1))
                        S.mm(DN[:], self.ones_bf[:], p_[:], [self.ones_bf, p_], [DN], start=(kb == 0), stop=(kb == 1))
                    S.act(ll[:], DN[:], AF.Ln, [DN], [ll])
                    S.act(ll[:], ll[:], AF.Exp, [ll], [ll], scale=-1.0)
                    S.tt("dve", om[:, h, :], O[:], ll[:], ALU.mult, [O, ll], [om])
                for oc in range(8):
                    ocs = slice(oc * 128, (oc + 1) * 128)
                    ps = PS[pi % 3]
                    pi += 1
                    for hh in range(4):
                        S.mm(ps[:], Wmo[:, hh, ocs], om[:, hh, :], [Wmo, om], [ps], start=(hh == 0), stop=(hh == 3))
                    S.tt("dve", xt[:, oc, :], xt[:, oc, :], ps[:], ALU.add, [xt, ps], [xt])
                S.dma("pool", xo[:, :, ts_], xt[:], [xt], [xout])
            S.barrier()

    def pass_d(self, l, xin, xout):
        S, T, NG = self.S, self.T, self.NG
        I = self.inp
        PS = self.PS
        cv = self.cvt
        NJ = DFF // 128
        with ExitStack() as es:
            Wup = self.sb(es, "WDup", [128, 8, 2 * DFF], BF16)
            Wdn = self.sb(es, "WDdn", [128, NJ, D], BF16)
            with ExitStack() as es2:
                stg = [self.sb(es2, "stgD%d" % i, [128, 1024], F32) for i in range(2)]
                self.load_w(stg, Wup, 0, I["w_up"].t[l], 0, 0, 2 * DFF, gain_col=CV["norm_ffn"])
                self.load_w(stg, Wdn, 0, I["w_down"].t[l], 0, 0, D, kchunks=NJ)
                S.barrier()
            xt = self.sb(es, "xtD", [128, 8, 512], F32)
            hT = self.sb(es, "hTD", [128, 8, 512], BF16)
            gT = self.sb(es, "gTD", [128, NJ, 512], BF16)
            lnv = self.sb(es, "lnvD", [128, 512], F32)
            rstd = self.sb(es, "rstdD", [128, 512], F32)
            halo = self.sb(es, "haloD", [128, 2 * NJ, 2], F32)
            buf = [self.sb(es, "bufD%d" % i, [128, 516], F32) for i in range(2)]
            acc = [self.sb(es, "accD%d" % i, [128, 512], F32) for i in range(2)]
            sa = self.sb(es, "saD", [128, 512], F32)
            S.memset("pool", halo[:], 0.0, [halo])
            xv = xin.t.rearrange("(k p) t -> p k t", p=128)
            xo = xout.t.rearrange("(k p) t -> p k t", p=128)
            cw = CV["ffn_conv"]
            cb = CV["ffn_conv_b"]
            pi = 0
            for g in range(NG):
                ts_ = slice(g * 512, (g + 1) * 512)
                S.dma("sp", xt[:], xv[:, :, ts_], [xin], [xt])
                self.norm_group(xt, hT, _SqAlias(gT), lnv, rstd, PS[3])
                for j in range(NJ):
                    for ab in range(2):
                        c = ab * NJ + j
                        ps = PS[pi % 3]
                        pi += 1
                        for k in range(8):
                            S.mm(ps[:], Wup[:, k, c * 128:(c + 1) * 128], hT[:, k, :], [Wup, hT], [ps], start=(k == 0), stop=(k == 7))
                        b = buf[ab]
                        a = acc[ab]
                        S.copy("pool", b[:, 0:2], halo[:, c, 0:2], [halo], [b])
                        S.copy("act", b[:, 2:514], ps[:], [ps], [b])
                        S.copy("pool", halo[:, c, 0:2], b[:, 512:514], [b], [halo])
                        S.ts("dve", a[:], b[:, 0:512], cv[:, cw + c:cw + c + 1], cv[:, cb + c:cb + c + 1], ALU.mult, ALU.add,
                             [b, cv], [a])
                        for tap in (1, 2):
                            S.stt(a[:], b[:, tap:tap + 512], cv[:, cw + tap * 2 * NJ + c: cw + tap * 2 * NJ + c + 1], a[:],
                                  ALU.mult, ALU.add, [b, cv, a], [a])
                    S.act(sa[:], acc[0][:], AF.Silu, [acc[0]], [sa])
                    S.tt("pool", gT[:, j, :], sa[:], acc[1][:], ALU.mult, [sa, acc[1]], [gT])
                for oc in range(8):
                    ocs = slice(oc * 128, (oc + 1) * 128)
                    ps = PS[4 + oc % 2]
                    for j in range(NJ):
                        S.mm(ps[:], Wdn[:, j, ocs], gT[:, j, :], [Wdn, gT], [ps], start=(j == 0), stop=(j == NJ - 1))
                    S.tt("dve", xt[:, oc, :], xt[:, oc, :], ps[:], ALU.add, [xt, ps], [xt])
                S.dma("pool", xo[:, :, ts_], xt[:], [xt], [xout])
            S.barrier()

    def gdn_head(self, h):
        S, T = self.S, self.T
        NP = T // 128
        PS = self.PS
        cm = self.cm32
        ident = cm[:, 0, :]
        with ExitStack() as es:
            Qf = self.sb(es, "gQ", [128, T], BF16)
            Kf = self.sb(es, "gK", [128, T], BF16)
            Vf = self.sb(es, "gV", [128, T], BF16)
            Kb = self.sb(es, "gKb", [128, T], BF16)
            bc = self.sb(es, "gbc", [128, T], F32)
            G2 = self.sb(es, "gG2", [128, 128], F32)
            B2 = self.sb(es, "gB2", [128, 128], F32)
            gc2 = self.sb(es, "ggc2", [128, 128], F32)
            gl2 = self.sb(es, "ggl2", [128, 128], F32)
            rm2 = self.sb(es, "grm2", [128, 128], F32)
            gcol = self.sb(es, "ggcol", [128, NP], F32)
            bcol = self.sb(es, "gbcol", [128, NP], F32)
            eglc = self.sb(es, "geglc", [128, NP], F32)
            Sst = self.sb(es, "gS", [128, 128], F32)
            hs = slice(h * 128, (h + 1) * 128)
            S.dma("sp", Qf[:], self.QK["GQ"].t[hs, :], [self.QK["GQ"]], [Qf])
            S.dma("sp", Kf[:], self.QK["GK"].t[hs, :], [self.QK["GK"]], [Kf])
            S.dma("sp", Vf[:], self.QK["GV"].t[hs, :], [self.QK["GV"]], [Vf])
            R = self.ROW
            S.dma("sp", bc[:], R["BETA"].t[h:h + 1, :].partition_broadcast(128), [R["BETA"]], [bc])
            S.dma("sp", G2[0:NP, :], R["GLOG"].t[h:h + 1, :].rearrange("o (n p) -> (o n) p", p=128), [R["GLOG"]], [G2])
            S.dma("sp", B2[0:NP, :], R["BETA"].t[h:h + 1, :].rearrange("o (n p) -> (o n) p", p=128), [R["BETA"]], [B2])
            for c0 in range(0, T, 2048):
                c1 = min(T, c0 + 2048)
                S.tt("dve", Kb[:, c0:c1], Kf[:, c0:c1], bc[:, c0:c1], ALU.mult, [Kf, bc], [Kb])
            S.dma("sp", bc[:], R["EGC"].t[h:h + 1, :].partition_broadcast(128), [R["EGC"], Kb], [bc])
            S.memset("pool", rm2[:], 1.0, [rm2])
            S.memset("pool", rm2[:, 0:1], 0.0, [rm2])
            S.memset("pool", rm2[:, 64:65], 0.0, [rm2])
            S.op("dve", lambda e: e.tensor_tensor_scan(gc2[0:NP, :], rm2[0:NP, :], G2[0:NP, :], 0.0, ALU.mult, ALU.add),
                 [rm2, G2], [gc2])
            for half in range(2):
                cs = slice(64 * half, 64 * half + 64)
                tot = gc2[0:NP, 64 * half + 63: 64 * half + 64]
                S.ts("dve", gl2[0:NP, cs], gc2[0:NP, cs], tot, -1.0, ALU.subtract, ALU.mult, [gc2], [gl2])
            S.act(gl2[0:NP, :], gl2[0:NP, :], AF.Exp, [gl2], [gl2])
            for src, dst in ((G2, gcol), (B2, bcol), (gl2, eglc)):
                S.tr(PS[7][:, 0:NP], src[0:NP, :], cm[0:NP, 0, 0:NP], [src, cm], [PS[7]])
                S.copy("dve", dst[:], PS[7][:, 0:NP], [PS[7]], [dst])
            S.memset("pool", Sst[:], 0.0, [Sst])

            def f32t(n, k=2):
                return [self.sb(es, n + str(i), [128, 128], F32) for i in range(k)]
            lg = f32t("g_lg"); Dl = f32t("g_Dl"); DTs = f32t("g_DTs"); DTi = f32t("g_DTi")
            Lb = [f32t("g_L%d_" % j) for j in range(2)]
            Ub = [f32t("g_U%d_" % j) for j in range(2)]
            X = f32t("g_X"); TT = f32t("g_TT"); At = f32t("g_At")
            K32 = f32t("g_K32"); V32 = f32t("g_V32"); Kg = f32t("g_Kg"); Qg = f32t("g_Qg")
            Kt = f32t("g_Kt"); Vt = f32t("g_Vt")
            Y = f32t("g_Y"); vn = f32t("g_vn")
            OT = [self.sb(es, "g_OT%d" % i, [128, 512], F32) for i in range(2)]
            gz = [self.sb(es, "g_gz%d" % i, [128, 512], BF16) for i in range(2)]
            sq = self.sb(es, "g_sq", [128, 512], BF16)
            ll = self.sb(es, "g_ll", [128, 512], F32)
            t32 = self.sb(es, "g_t32", [128, 512], F32)
            ob = [self.sb(es, "g_ob%d" % i, [128, 512], BF16) for i in range(2)]

            def pre(r):
                i = r % 2
                cs = slice(r * 128, (r + 1) * 128)
                S.ts("pool", lg[i][:], cm[:, 4, :], gcol[:, r:r + 1], 1.0, ALU.mult, ALU.mult, [cm, gcol], [lg[i]])
                yield
                P0 = PS[0]
                S.mm(P0[:, 0:128], lg[i][:], cm[:, 5, :], [lg[i], cm], [P0], start=True, stop=False)
                yield
                S.mm(P0[:, 0:128], ident, cm[:, 6, :], [cm], [P0], start=False, stop=True)
                yield
                S.mm(P0[:, 128:256], cm[:, 5, :], lg[i][:], [lg[i], cm], [P0], start=True, stop=False)
                yield
                S.mm(P0[:, 128:256], ident, cm[:, 7, :], [cm], [P0], start=False, stop=True)
                yield
                S.mm(P0[:, 256:384], cm[:, 5, :], lg[i][:], [lg[i], cm], [P0], start=True, stop=False)
                yield
                S.mm(P0[:, 256:384], ident, cm[:, 8, :], [cm], [P0], start=False, stop=True)
                yield
                S.act(Dl[i][:], P0[:, 0:128], AF.Exp, [P0], [Dl[i]])
                yield
                S.act(DTs[i][:], P0[:, 128:256], AF.Exp, [P0], [DTs[i]])
                yield
                S.act(DTi[i][:], P0[:, 256:384], AF.Exp, [P0], [DTi[i]])
                yield
                P1 = PS[1]
                S.mm(P1[:, 0:128], Kb[:, cs], Kf[:, cs], [Kb, Kf], [P1])
                yield
                S.mm(P1[:, 128:256], Kf[:, cs], Kb[:, cs], [Kb, Kf], [P1])
                yield
                S.mm(P1[:, 256:384], Kf[:, cs], Qf[:, cs], [Qf, Kf], [P1])
                yield
                L = Lb[0][i]
                U = Ub[0][i]
                S.tt("dve", L[:], P1[:, 0:128], Dl[i][:], ALU.mult, [P1, Dl[i]], [L])
                yield
                S.tt("dve", U[:], P1[:, 128:256], DTs[i][:], ALU.mult, [P1, DTs[i]], [U])
                yield
                S.tt("dve", At[i][:], P1[:, 256:384], DTi[i][:], ALU.mult, [P1, DTi[i]], [At[i]])
                yield
                S.tt("pool", X[i][:], ident, U[:], ALU.subtract, [cm, U], [X[i]])
                yield
                for k in range(1, 6):
                    P2 = PS[2 + k % 2]
                    Ln_ = Lb[k % 2][i]
                    Un_ = Ub[k % 2][i]
                    S.mm(P2[:, 0:128], U[:], L[:], [U, L], [P2])
                    yield
                    S.copy("act", Ln_[:], P2[:, 0:128], [P2], [Ln_])
                    yield
                    if k < 5:
                        S.mm(P2[:, 128:256], L[:], U[:], [U, L], [P2])
                        yield
                        S.copy("dve", Un_[:], P2[:, 128:256], [P2], [Un_])
                        yield
                    S.mm(P2[:, 256:384], Ln_[:], X[i][:], [Ln_, X[i]], [P2])
                    yield
                    S.tt("dve", X[i][:], X[i][:], P2[:, 256:384], ALU.add, [X[i], P2], [X[i]])
                    yield
                    L, U = Ln_, Un_
                S.ts("dve", TT[i][:], X[i][:], bcol[:, r:r + 1], None, ALU.mult, None, [X[i], bcol], [TT[i]])
                yield
                S.copy("pool", K32[i][:], Kf[:, cs], [Kf], [K32[i]])
                yield
                S.copy("pool", V32[i][:], Vf[:, cs], [Vf], [V32[i]])
                yield
                P4 = PS[4]
                S.tr(P4[:, 0:128], K32[i][:], ident, [K32[i], cm], [P4])
                yield
                S.tr(P4[:, 128:256], V32[i][:], ident, [V32[i], cm], [P4])
                yield
                S.ts("dve", Kt[i][:], P4[:, 0:128], eglc[:, r:r + 1], None, ALU.mult, None, [P4, eglc], [Kt[i]])
                yield
                S.copy("act", Vt[i][:], P4[:, 128:256], [P4], [Vt[i]])
                yield
                S.tt("pool", Kg[i][:], K32[i][:], bc[:, cs], ALU.mult, [K32[i], bc], [Kg[i]])
                yield
                S.tt("pool", Qg[i][:], Qf[:, cs], bc[:, cs], ALU.mult, [Qf, bc], [Qg[i]])
                yield

            def scan(r):
                i = r % 2
                ot = OT[(r // 4) % 2]
                for par in range(2):
                    rows = slice(64 * par, 64 * par + 64)
                    col = r * 128 + 64 * par + 63
                    P5, P6, P7 = PS[5], PS[6], PS[7]
                    S.mm(P5[:, 0:128], Kg[i][:], Sst[:], [Kg[i], Sst], [P5])
                    yield
                    S.tt("dve", Y[par][rows, :], Vt[i][rows, :], P5[rows, 0:128], ALU.subtract, [Vt[i], P5], [Y[par]])
                    yield
                    S.mm(P6[:, 0:128], TT[i][rows, :], Y[par][rows, :], [TT[i], Y[par]], [P6])
                    yield
                    S.copy("act", vn[par][rows, :], P6[rows, 0:128], [P6], [vn[par]])
                    yield
                    S.mm(P7[:, 0:64], Sst[:], Qg[i][:, rows], [Sst, Qg[i]], [P7], start=True, stop=False)
                    yield
                    S.mm(P7[:, 0:64], vn[par][rows, :], At[i][rows, rows], [vn[par], At[i]], [P7], start=False, stop=True)
                    yield
                    oc = (r % 4) * 128 + 64 * par
                    S.copy("act", ot[:, oc:oc + 64], P7[:, 0:64], [P7], [ot])
                    yield
                    S.mm(P5[:, 128:256], Kt[i][rows, :], vn[par][rows, :], [Kt[i], vn[par]], [P5])
                    yield
                    S.stt(Sst[:], Sst[:], bc[:, col:col + 1], P5[:, 128:256], ALU.mult, ALU.add, [Sst, bc, P5], [Sst])
                    yield

            def post(blk):
                ot = OT[blk % 2]
                ts_ = slice(blk * 512, (blk + 1) * 512)
                z = gz[blk % 2]
                S.dma("sp", z[:], self.QK["GZ"].t[hs, ts_], [self.QK["GZ"]], [z])
                S.act(sq[:], ot[:], AF.Square, [ot], [sq])
                S.mm(PS[4][:], self.ones_bf[:], sq[:], [self.ones_bf, sq], [PS[4]])
                S.act(ll[:], PS[4][:], AF.Ln, [PS[4]], [ll], bias=EPS, scale=1.0 / DH)
                S.act(ll[:], ll[:], AF.Exp, [ll], [ll], scale=-0.5)
                go = CV["gdn_onorm"]
                S.stt(t32[:], ot[:], self.cvt[:, go:go + 1], ll[:], ALU.mult, ALU.mult, [ot, self.cvt, ll], [t32])
                o = ob[blk % 2]
                S.tt("pool", o[:], t32[:], z[:], ALU.mult, [t32, z], [o])
                S.dma("pool", self.Y["YB"].t[hs, ts_], o[:], [o], [self.Y["YB"]])

            for _ in pre(0):
                pass
            for r in range(NP):
                ga = pre(r + 1) if r + 1 < NP else iter(())
                gb_ = scan(r)
                da = db = False
                while not (da and db):
                    for _ in range(2):
                        if not da:
                            try:
                                next(ga)
                            except StopIteration:
                                da = True
                    if not db:
                        try:
                            next(gb_)
                        except StopIteration:
                            db = True
                if r % 4 == 3:
                    post(r // 4)
            S.barrier()

    def fox_head(self, h):
        S, T, NG = self.S, self.T, self.NG
        NB = T // 128
        PS = self.PS
        with ExitStack() as es:
            Qf = self.sb(es, "fxQ", [128, T], BF16)
            Kf = self.sb(es, "fxK", [128, T], BF16)
            Vt = self.sb(es, "fxV", [128, NB, 128], BF16)
            cbc = self.sb(es, "fxcbc", [128, T], F32)
            c2 = self.sb(es, "fxc2", [128, 128], F32)
            ncc = self.sb(es, "fxncc", [128, NB], F32)
            tmp = [self.sb(es, "fxtmp%d" % i, [128, 512], F32) for i in range(2)]
            PT = [self.sb(es, "fxPT%d" % i, [128, 512], BF16) for i in range(3)]
            lnd = self.sb(es, "fxlnd", [128, 512], F32)
            ob = [self.sb(es, "fxob%d" % i, [128, 512], BF16) for i in range(2)]
            hs = slice(h * 128, (h + 1) * 128)
            S.dma("sp", Qf[:], self.QK["FQ"].t[hs, :], [self.QK["FQ"]], [Qf])
            S.dma("sp", Kf[:], self.QK["FK"].t[hs, :], [self.QK["FK"]], [Kf])
            S.dma("sp", Vt[:], self.VT["FV"].t.rearrange("(n p) c -> p n c", p=128)[:, :, hs], [self.VT["FV"]], [Vt])
            crow = self.ROW["CROW"]
            S.dma("sp", cbc[:], crow.t[h:h + 1, :].partition_broadcast(128), [crow], [cbc])
            S.dma("sp", c2[0:NB, :], crow.t[h:h + 1, :].rearrange("o (n p) -> (o n) p", p=128), [crow], [c2])
            S.tr(PS[7][:, 0:NB], c2[0:NB, :], self.cm32[0:NB, 0, 0:NB], [c2, self.cm32], [PS[7]])
            S.ts("dve", ncc[:], PS[7][:, 0:NB], -1.0, None, ALU.mult, None, [PS[7]], [ncc])
            tiles = []
            for g in range(NG):
                tl = [(kb, 0) for kb in range(4 * g)] + [(4 * g + j, 128 * j) for j in range(4)]
                for ti, (kb, c0) in enumerate(tl):
                    tiles.append((g, kb, c0, ti == 0, ti == len(tl) - 1))
            n = len(tiles)

            def emit_S(i):
                g, kb, c0, first, last = tiles[i]
                ps = PS[i % 3]
                S.mm(ps[:, c0:512], Kf[:, kb * 128:(kb + 1) * 128], Qf[:, g * 512 + c0:(g + 1) * 512], [Kf, Qf], [ps])

            def emit_mid(i):
                g, kb, c0, first, last = tiles[i]
                ps = PS[i % 3]
                tm = tmp[i % 2]
                pt = PT[i % 3]
                S.tt("dve", tm[:, c0:512], ps[:, c0:512], cbc[:, g * 512 + c0:(g + 1) * 512], ALU.add, [ps, cbc], [tm])
                S.act(pt[:, c0:512], tm[:, c0:512], AF.Exp, [tm, ncc], [pt], bias=ncc[:, kb:kb + 1])
                if kb >= 4 * g:
                    S.tt("pool", pt[:, c0:c0 + 128], pt[:, c0:c0 + 128], self.cmbf[:, 1, :], ALU.mult,
                         [pt, self.cmbf], [pt])

            def emit_PV(i):
                g, kb, c0, first, last = tiles[i]
                pt = PT[i % 3]
                O = PS[3 + g % 2]
                DN = PS[5 + g % 2]
                S.mm(O[:, c0:512], Vt[:, kb, :], pt[:, c0:512], [Vt, pt], [O], start=first, stop=last)
                S.mm(DN[:, c0:512], self.ones_bf[:], pt[:, c0:512], [self.ones_bf, pt], [DN], start=first, stop=last)
                if last:
                    S.act(lnd[:], DN[:], AF.Ln, [DN], [lnd])
                    S.act(lnd[:], lnd[:], AF.Exp, [lnd], [lnd], scale=-1.0)
                    o = ob[g % 2]
                    S.tt("dve", o[:], O[:], lnd[:], ALU.mult, [O, lnd], [o])
                    S.dma("pool", self.Y["YA"].t[hs, g * 512:(g + 1) * 512], o[:], [o], [self.Y["YA"]])

            LA = 2
            for i in range(min(LA, n)):
                emit_S(i)
            for i in range(n):
                if i + LA < n:
                    emit_S(i + LA)
                emit_mid(i)
                emit_PV(i)
            S.barrier()

    def sb_head(self, h):
        S, T, NG = self.S, self.T, self.NG
        NB = T // 128
        PS = self.PS
        with ExitStack() as es:
            Qf = self.sb(es, "sbQ", [128, T], BF16)
            Kf = self.sb(es, "sbK", [128, T], BF16)
            Vt = self.sb(es, "sbV", [128, NB, 128], BF16)
            ntri = self.sb(es, "sbntri", [128, 128], BF16)
            nones = self.sb(es, "sbnones", [128, 128], BF16)
            zeros = self.sb(es, "sbzeros", [128, 128], BF16)
            E = [self.sb(es, "sbE%d" % i, [128, 512], F32) for i in range(2)]
            SP = [self.sb(es, "sbSP%d" % i, [128, 512], BF16) for i in range(3)]
            AT = [self.sb(es, "sbAT%d" % i, [128, 512], BF16) for i in range(3)]
            cum = [self.sb(es, "sbcum%d" % i, [128, 512], BF16) for i in range(2)]
            ob = [self.sb(es, "sbob%d" % i, [128, 512], BF16) for i in range(2)]
            hs = slice(h * 128, (h + 1) * 128)
            S.dma("sp", Qf[:], self.QK["SQ"].t[hs, :], [self.QK["SQ"]], [Qf])
            S.dma("sp", Kf[:], self.QK["SK"].t[hs, :], [self.QK["SK"]], [Kf])
            S.dma("sp", Vt[:], self.VT["SV"].t.rearrange("(n p) c -> p n c", p=128)[:, :, hs], [self.VT["SV"]], [Vt])
            S.ts("dve", ntri[:], self.cm32[:, 3, :], -1.0, None, ALU.mult, None, [self.cm32], [ntri])
            S.memset("dve", nones[:], -1.0, [nones])
            S.memset("dve", zeros[:], 0.0, [zeros])
            tiles = []
            for g in range(NG):
                tl = [(4 * g + j, 128 * j) for j in (3, 2, 1, 0)] + [(kb, 0) for kb in range(4 * g - 1, -1, -1)]
                for ti, (kb, c0) in enumerate(tl):
                    tiles.append((g, kb, c0, ti, len(tl)))
            n = len(tiles)

            def emit_A(i):
                g, kb, c0, ti, nt = tiles[i]
                A = PS[i % 3]
                S.mm(A[:, c0:512], Kf[:, kb * 128:(kb + 1) * 128], Qf[:, g * 512 + c0:(g + 1) * 512], [Kf, Qf], [A])

            def emit_sp(i):
                g, kb, c0, ti, nt = tiles[i]
                A = PS[i % 3]
                e = E[i % 2]
                sp = SP[i % 3]
                S.act(e[:, c0:512], A[:, c0:512], AF.Exp, [A], [e])
                S.act(sp[:, c0:512], e[:, c0:512], AF.Ln, [e], [sp], bias=1.0)
                if kb >= 4 * g:
                    S.tt("pool", sp[:, c0:c0 + 128], sp[:, c0:c0 + 128], self.cmbf[:, 2, :], ALU.mult,
                         [sp, self.cmbf], [sp])

            def emit_B(i):
                g, kb, c0, ti, nt = tiles[i]
                B = PS[3 + i % 2]
                sp = SP[i % 3]
                cm = cum[g % 2]
                O = PS[5 + g % 2]
                q0, q1 = g * 512 + c0, (g + 1) * 512
                kT = Kf[:, kb * 128:(kb + 1) * 128]
                if ti == 0:
                    S.memset("pool", cm[:], 0.0, [cm])
                    S.mm(O[:], zeros[:], Qf[:, g * 512:(g + 1) * 512], [zeros, Qf], [O], start=True, stop=False)
                S.mm(B[:, c0:512], kT, Qf[:, q0:q1], [Kf, Qf], [B], start=True, stop=False)
                S.mm(B[:, c0:512], ntri[:], sp[:, c0:512], [ntri, sp], [B], start=False, stop=(ti == 0))
                if ti > 0:
                    S.mm(B[:, c0:512], nones[:], cm[:, c0:512], [nones, cm], [B], start=False, stop=True)
                if ti < nt - 1:
                    S.tt("dve", cm[:, c0:512], cm[:, c0:512], sp[:, c0:512], ALU.add, [cm, sp], [cm])

            def emit_at(i):
                g, kb, c0, ti, nt = tiles[i]
                B = PS[3 + i % 2]
                at = AT[i % 3]
                S.act(at[:, c0:512], B[:, c0:512], AF.Exp, [B], [at])
                if kb >= 4 * g:
                    S.tt("pool", at[:, c0:c0 + 128], at[:, c0:c0 + 128], self.cmbf[:, 2, :], ALU.mult,
                         [at, self.cmbf], [at])

            def emit_O(i):
                g, kb, c0, ti, nt = tiles[i]
                at = AT[i % 3]
                O = PS[5 + g % 2]
                S.mm(O[:, c0:512], Vt[:, kb, :], at[:, c0:512], [Vt, at], [O], start=False, stop=(ti == nt - 1))
                if ti == nt - 1:
                    o = ob[g % 2]
                    S.copy("dve", o[:], O[:], [O], [o])
                    S.dma("pool", self.Y["YC"].t[hs, g * 512:(g + 1) * 512], o[:], [o], [self.Y["YC"]])

            emit_A(0)
            if n > 1:
                emit_A(1)
            emit_sp(0)
            for i in range(n):
                if i + 2 < n:
                    emit_A(i + 2)
                if i + 1 < n:
                    emit_sp(i + 1)
                emit_B(i)
                emit_at(i)
                if i >= 1:
                    emit_O(i - 1)
            emit_O(n - 1)
            S.barrier()


def host_consts():
    p = np.arange(128)[:, None]
    f = np.arange(128)[None, :]
    cm = np.zeros((128, 9, 128), np.float32)
    cm[:, 0, :] = (p == f)
    cm[:, 1, :] = (f >= p)
    cm[:, 2, :] = (f > p)
    cm[:, 3, :] = (p >= f)
    same = (p // 64) == (f // 64)
    cm[:, 4, :] = (p <= f) & same
    cm[:, 5, :] = (p > f) & same
    cm[:, 6, :] = np.where((p > f) & same, 0.0, NEG)
    cm[:, 7, :] = np.where((f > p) & same, 0.0, NEG)
    cm[:, 8, :] = np.where((f >= p) & same, 0.0, NEG)
    return cm


def pack_cv(inp, L):
    cv = np.zeros((L, 128, NCV), np.float32)

    def chunks(v):
        return np.ascontiguousarray(v.reshape(-1, 128).T)
    for l in range(L):
        c = cv[l]
        c[:, CV["norm_mix"]:CV["norm_mix"] + 8] = chunks(inp["norm_mix"][l])
        c[:, CV["gate_bias"]:CV["gate_bias"] + 24] = chunks(inp["gate_bias"][l])
        c[:, CV["fox_qnorm"]] = inp["fox_qnorm"][l]
        c[:, CV["fox_knorm"]] = inp["fox_knorm"][l]
        gc = inp["gdn_conv"][l]
        for tap in range(4):
            c[:, CV["gdn_conv"] + tap * 12: CV["gdn_conv"] + (tap + 1) * 12] = chunks(gc[tap])
        c[:, CV["gdn_onorm"]] = inp["gdn_onorm"][l]
        c[:, CV["norm_xq"]:CV["norm_xq"] + 8] = chunks(inp["norm_xq"][l])
        c[:, CV["norm_mem"]:CV["norm_mem"] + 8] = chunks(inp["norm_mem"][l])
        c[:, CV["mq_norm"]] = inp["mq_norm"][l]
        c[:, CV["mk_norm"]] = inp["mk_norm"][l]
        c[:, CV["norm_ffn"]:CV["norm_ffn"] + 8] = chunks(inp["norm_ffn"][l])
        fc = inp["ffn_conv"][l]
        for tap in range(3):
            c[:, CV["ffn_conv"] + tap * 44: CV["ffn_conv"] + (tap + 1) * 44] = chunks(fc[tap])
        c[:, CV["ffn_conv_b"]:CV["ffn_conv_b"] + 44] = chunks(inp["ffn_conv_b"][l])
        s = CV["small"]
        c[0:4, s] = inp["fox_fbias"][l]
        c[64:68, s + 1] = inp["gdn_dt_bias"][l]
        c[64:68, s + 2] = inp["gdn_a_log"][l]
    return cv


def prep_inputs(inp, b, T, L):
    m = {}
    m["xT"] = np.ascontiguousarray(inp["x"][b, :T].T)
    m["memT"] = np.ascontiguousarray(inp["mem"][b].T)
    for n in ("w_in", "w_oa", "w_ob", "w_oc", "w_out", "w_mq", "w_mkv", "w_mo", "w_up", "w_down"):
        m[n] = np.ascontiguousarray(inp[n][:L])
    m["cv"] = pack_cv(inp, L)
    m["cmask"] = host_consts()
    return m


_CACHE = {}


def kernel(**inputs):
    inp = {k: np.asarray(v) for k, v in inputs.items()}
    B, T, _ = inp["x"].shape
    L = inp["w_in"].shape[0]
    key = (T, L)
    nc = Builder(T, L).build()
    ncores = 8
    in_maps = [prep_inputs(inp, c % B, T, L) for c in range(B)]
    in_maps = [in_maps[c % B] for c in range(ncores)]
    res = run_bass_kernel_spmd(nc, in_maps, core_ids=list(range(ncores)))
    out = np.empty((B, T, D), np.float32)
    for b in range(B):
        out[b] = np.asarray(res.results[b]["yT"]).T
    return out
```

```python
import numpy as np
from contextlib import ExitStack
import concourse.bass as bass
import concourse.mybir as mybir
from concourse.bass_utils import run_bass_kernel_spmd

F32 = mybir.dt.float32
BF16 = mybir.dt.bfloat16
ALU = mybir.AluOpType
AF = mybir.ActivationFunctionType

D = 1024
NH = 4
DH = 128
EPS = 1e-6
MEMT = 256
DFF = 2816
OFF = dict(FQ=0, FK=512, FV=1024, FF=1536, GQ=1540, GK=2052, GV=2564, GB=3076, GA=3080,
           GZ=3084, SQ=3596, SK=4108, SV=4620, GT=5132, END=8204)
SCALE = DH ** -0.5
NEG = -30000.0

CV = {}
_o = 0
for _n, _w in [("norm_mix", 8), ("gate_bias", 24), ("fox_qnorm", 1), ("fox_knorm", 1),
               ("gdn_conv", 48), ("gdn_onorm", 1), ("norm_xq", 8), ("norm_mem", 8),
               ("mq_norm", 1), ("mk_norm", 1), ("norm_ffn", 8), ("ffn_conv", 132),
               ("ffn_conv_b", 44), ("small", 3)]:
    CV[_n] = _o
    _o += _w
NCV = _o


class Tile:
    def __init__(self, t, name, psum=False):
        self.t = t
        self.name = name
        self.psum = psum
        self.w = {}
        self.rs = {}

    def __getitem__(self, idx):
        return self.t[idx]


class Trk:
    def __init__(self):
        self.w = {}
        self.rs = {}


class Eng:
    def __init__(self, name, eng, sem):
        self.name = name
        self.eng = eng
        self.sem = sem
        self.n = 0
        self.seen = {}


class Sched:
    def __init__(self, nc, es, ndsem=12):
        self.nc = nc
        self.es = es
        self.sems = {}
        self.E = {}
        for name, eng in [("pe", nc.tensor), ("act", nc.scalar), ("dve", nc.vector),
                          ("pool", nc.gpsimd), ("sp", nc.sync)]:
            sem = es.enter_context(nc.semaphore("sem_" + name))
            self.E[name] = Eng(name, eng, sem)
            self.sems[name] = sem
        self.dpool = {}
        self.dnext = {}
        for q in ("sp", "pool", "act"):
            lst = []
            for i in range(ndsem):
                key = "d_%s_%d" % (q, i)
                sem = es.enter_context(nc.semaphore(key))
                self.sems[key] = sem
                lst.append([sem, 0, key])
            self.dpool[q] = lst
            self.dnext[q] = 0
        self.ninst = 0

    def _deps(self, reads, writes):
        toks = []
        for r in reads:
            for k, v in r.w.items():
                toks.append((k, v, True))
            if getattr(r, "psum", False):
                for k, v in r.rs.items():
                    toks.append((k, v, False))
        for w in writes:
            for k, v in w.w.items():
                toks.append((k, v, False))
            for k, v in w.rs.items():
                toks.append((k, v, False))
        return toks

    def _wait(self, E, toks):
        for key, val, raw in toks:
            if key == E.name:
                if not raw or E.name == "pe":
                    continue
            if E.seen.get(key, 0) >= val:
                continue
            E.eng.wait_ge(self.sems[key], val)
            E.seen[key] = val
            self.ninst += 1
            E.nw = getattr(E, "nw", 0) + 1

    def _mark(self, tok, reads, writes):
        k, v = tok
        for r in reads:
            if r.rs.get(k, 0) < v:
                r.rs[k] = v
        for w in writes:
            w.w[k] = v
            w.rs = {}

    def op(self, en, fn, reads, writes):
        E = self.E[en]
        self._wait(E, self._deps(reads, writes))
        ins = fn(E.eng)
        E.n += 1
        ins.then_inc(E.sem, 1)
        self.ninst += 1
        self._mark((E.name, E.n), reads, writes)

    def dma(self, q, out, in_, reads, writes, **kw):
        Q = self.E[q]
        toks = self._deps(reads, writes)
        pool = self.dpool[q]
        i = self.dnext[q]
        self.dnext[q] = (i + 1) % len(pool)
        sem, cnt, key = pool[i]
        if cnt > 0:
            toks.append((key, cnt, False))
        self._wait(Q, toks)
        Q.eng.dma_start(out=out, in_=in_, **kw).then_inc(sem, 16)
        pool[i][1] = cnt + 16
        self.ninst += 1
        self._mark((key, cnt + 16), reads, writes)

    def barrier(self):
        toks = [(n, e.n, True) for n, e in self.E.items() if e.n > 0]
        for q, pool in self.dpool.items():
            for sem, cnt, key in pool:
                if cnt > 0:
                    toks.append((key, cnt, False))
        for E in self.E.values():
            self._wait(E, toks)

    def mm(self, out, lhsT, rhs, reads, writes, start=True, stop=True):
        self.op("pe", lambda e: e.matmul(out, lhsT, rhs, start=start, stop=stop), reads, writes)

    def tr(self, out, in_, ident, reads, writes):
        self.op("pe", lambda e: e.transpose(out, in_, ident), reads, writes)

    def act(self, out, in_, func, reads, writes, bias=None, scale=None, en="act"):
        kw = {}
        if bias is not None:
            kw["bias"] = bias
        if scale is not None:
            kw["scale"] = scale
        self.op(en, lambda e: e.activation(out, in_, func, **kw), reads, writes)

    def tt(self, en, out, in0, in1, op, reads, writes):
        self.op(en, lambda e: e.tensor_tensor(out, in0, in1, op), reads, writes)

    def ts(self, en, out, in0, s1, s2, op0, op1, reads, writes):
        if op1 is None:
            self.op(en, lambda e: e.tensor_scalar(out, in0, s1, None, op0), reads, writes)
        else:
            self.op(en, lambda e: e.tensor_scalar(out, in0, s1, s2, op0, op1), reads, writes)

    def stt(self, out, in0, scalar, in1, op0, op1, reads, writes):
        self.op("dve", lambda e: e.scalar_tensor_tensor(out, in0, scalar, in1, op0, op1), reads, writes)

    def copy(self, en, out, in_, reads, writes):
        if en == "act":
            self.op(en, lambda e: e.activation(out, in_, AF.Copy), reads, writes)
        else:
            self.op(en, lambda e: e.tensor_copy(out, in_), reads, writes)

    def memset(self, en, ap, val, writes):
        self.op(en, lambda e: e.memset(ap, val), [], writes)


class _SqAlias:
    def __init__(self, base):
        self.base = base

    @property
    def w(self):
        return self.base.w

    @property
    def rs(self):
        return self.base.rs

    @rs.setter
    def rs(self, v):
        self.base.rs = v

    def __getitem__(self, idx):
        if idx == slice(None):
            return self.base.t[:, 0:8, :]
        return self.base.t[idx]


class Builder:
    def __init__(self, T, L, dbg=()):
        self.T = T
        self.L = L
        self.NG = T // 512
        self.dbg = set(dbg)
        self.nc = bass.Bass("TRN2", target_bir_lowering=False)
        self.outs = []

    def dram_in(self, name, shape, dt=F32):
        return Tile(self.nc.dram_tensor(name, list(shape), dt, kind="ExternalInput"), name)

    def dram(self, name, shape, dt):
        kind = "Internal"
        if name in self.dbg or name == "yT":
            kind = "ExternalOutput"
            self.outs.append(name)
        return Tile(self.nc.dram_tensor(name, list(shape), dt, kind=kind), name)

    def sb(self, es, name, shape, dt):
        self.uid = getattr(self, "uid", 0) + 1
        name = "%s_%d" % (name, self.uid)
        return Tile(es.enter_context(self.nc.sbuf_tensor(name, list(shape), dt)), name)

    def build(self):
        nc, T, L = self.nc, self.T, self.L
        self.inp = {}
        I = self.inp
        I["xT"] = self.dram_in("xT", [D, T])
        I["memT"] = self.dram_in("memT", [D, MEMT])
        I["w_in"] = self.dram_in("w_in", [L, D, OFF["END"]])
        I["cv"] = self.dram_in("cv", [L, 128, NCV])
        I["w_oa"] = self.dram_in("w_oa", [L, 512, D])
        I["w_ob"] = self.dram_in("w_ob", [L, 512, D])
        I["w_oc"] = self.dram_in("w_oc", [L, 512, D])
        I["w_out"] = self.dram_in("w_out", [L, D, D])
        I["w_mq"] = self.dram_in("w_mq", [L, D, 512])
        I["w_mkv"] = self.dram_in("w_mkv", [L, D, 1024])
        I["w_mo"] = self.dram_in("w_mo", [L, 512, D])
        I["w_up"] = self.dram_in("w_up", [L, D, 2 * DFF])
        I["w_down"] = self.dram_in("w_down", [L, DFF, D])
        I["cmask"] = self.dram_in("cmask", [128, 9, 128])
        self.X = [self.dram("xres0", [D, T], F32), self.dram("xres1", [D, T], F32)]
        self.yT = self.dram("yT", [D, T], F32)
        self.HT = self.dram("HT", [D, T], BF16)
        self.QK = {n: self.dram(n, [512, T], BF16) for n in
                   ("FQ", "FK", "SQ", "SK", "GQ", "GK", "GV", "GZ")}
        self.VT = {n: self.dram(n, [T, 512], BF16) for n in ("FV", "SV")}
        self.ROW = {n: self.dram(n, [4, T], F32) for n in ("CROW", "BETA", "GLOG", "EGC")}
        self.GATES = self.dram("GATES", [3072, T], BF16)
        self.Y = {n: self.dram(n, [512, T], BF16) for n in ("YA", "YB", "YC")}
        with ExitStack() as es:
            self.es = es
            self.S = Sched(nc, es)
            self.consts(es)
            for l in range(L):
                self.layer(l)
            self.finish()
        return nc

    def consts(self, es):
        S = self.S
        self.PS = [Tile(es.enter_context(self.nc.psum_tensor("ps%d" % i, [128, 512], F32)), "ps%d" % i, psum=True)
                   for i in range(8)]
        self.ones_bf = self.sb(es, "ones_bf", [128, 128], BF16)
        S.memset("dve", self.ones_bf[:], 1.0, [self.ones_bf])
        self.cm32 = self.sb(es, "cm32", [128, 9, 128], F32)
        S.dma("sp", self.cm32[:], self.inp["cmask"][:, :, :], [self.inp["cmask"]], [self.cm32])
        self.cmbf = self.sb(es, "cmbf", [128, 9, 128], BF16)
        S.copy("dve", self.cmbf[:], self.cm32[:], [self.cm32], [self.cmbf])
        self.cvt = self.sb(es, "cvt", [128, NCV], F32)
        self.dvt = self.sb(es, "dvt", [128, 4], F32)

    def finish(self):
        S = self.S
        for n in self.dbg:
            if n.startswith("nops"):
                for i in range(int(n[4:])):
                    S.dma("sp", self.cvt[:], self.inp["cv"][0, :, :], [self.inp["cv"]], [self.cvt])
        S.barrier()

    def layer(self, l):
        S = self.S
        S.barrier()
        S.dma("sp", self.cvt[:], self.inp["cv"][l, :, :], [self.inp["cv"]], [self.cvt])
        cs = CV["small"]
        S.ts("dve", self.dvt[:, 0:1], self.cvt[:, cs:cs + 1], -1.0, None, ALU.mult, None, [self.cvt], [self.dvt])
        S.act(self.dvt[:, 1:2], self.cvt[:, cs + 2:cs + 3], AF.Exp, [self.cvt], [self.dvt])
        S.ts("dve", self.dvt[:, 1:2], self.dvt[:, 1:2], -1.0, None, ALU.mult, None, [self.dvt], [self.dvt])
        xin = self.inp["xT"] if l == 0 else self.X[1]
        self.pass_a(l, xin)
        nh = 2 if "nh2" in self.dbg else (1 if "nh1" in self.dbg else NH)
        if "nofox" not in self.dbg:
            for h in range(nh):
                self.fox_head(h)
        if "nosb" not in self.dbg:
            for h in range(nh):
                self.sb_head(h)
        self.pass_b(l)
        if "nogdn" not in self.dbg:
            for h in range(nh):
                self.gdn_head(h)
        self.pass_c(l, xin, self.X[0])
        self.pass_d(l, self.X[0], self.yT if l == self.L - 1 else self.X[1])

    def load_w(self, stg, W, wcol, src, row0, c0, ncols, gain_col=None, kchunks=8, flip=[0]):
        S = self.S
        for k in range(kchunks):
            done = 0
            while done < ncols:
                n = min(1024, ncols - done)
                st = stg[flip[0] % len(stg)]
                flip[0] += 1
                S.dma("sp", st[:, 0:n], src[row0 + k * 128: row0 + (k + 1) * 128, c0 + done: c0 + done + n],
                      [], [st])
                dst = W[:, k, wcol + done: wcol + done + n]
                if gain_col is None:
                    en = "dve" if flip[0] % 2 else "pool"
                    S.copy(en, dst, st[:, 0:n], [st], [W])
                else:
                    g = self.cvt[:, gain_col + k: gain_col + k + 1]
                    if flip[0] % 2:
                        S.ts("dve", dst, st[:, 0:n], g, None, ALU.mult, None, [st, self.cvt], [W])
                    else:
                        S.act(dst, st[:, 0:n], AF.Copy, [st, self.cvt], [W], scale=g)
                done += n

    def norm_group(self, xt, hT, sq, lnv, rstd, ps):
        S = self.S
        S.act(sq[:], xt[:], AF.Square, [xt], [sq])
        for k in range(8):
            S.mm(ps[:], self.ones_bf[:], sq[:, k, :], [self.ones_bf, sq], [ps], start=(k == 0), stop=(k == 7))
        S.act(lnv[:], ps[:], AF.Ln, [ps], [lnv], bias=EPS, scale=1.0 / D)
        S.act(rstd[:], lnv[:], AF.Exp, [lnv], [rstd], scale=-0.5)
        for k in range(8):
            S.tt("dve" if k % 2 == 0 else "pool", hT[:, k, :], xt[:, k, :], rstd[:], ALU.mult, [xt, rstd], [hT])

    def pass_a(self, l, xin):
        S, T, NG = self.S, self.T, self.NG
        w_in = self.inp["w_in"]
        with ExitStack() as es:
            NWA = 2048 + 1024
            W = self.sb(es, "WA", [128, 8, NWA], BF16)
            Wsm = self.sb(es, "WAsm", [128, 8, 96], BF16)
            stg = [self.sb(es, "stgA%d" % i, [128, 1024], F32) for i in range(2)]
            S.memset("pool", Wsm[:], 0.0, [Wsm])
            gm = CV["norm_mix"]
            wl = w_in.t[l]
            for (name, wc) in (("FQ", 0), ("FK", 512), ("SQ", 1024), ("SK", 1536), ("FV", 2048), ("SV", 2560)):
                self.load_w(stg, W, wc, wl, 0, OFF[name], 512, gain_col=gm)
            for (name, wc) in (("FF", 0), ("GB", 32), ("GA", 64)):
                self.load_w(stg, Wsm, wc, wl, 0, OFF[name], 4, gain_col=gm)
            xt = self.sb(es, "xtA", [128, 8, 512], F32)
            hT = self.sb(es, "hTA", [128, 8, 512], BF16)
            sq = self.sb(es, "sqA", [128, 8, 512], BF16)
            lnv = self.sb(es, "lnvA", [128, 512], F32)
            rstd = self.sb(es, "rstdA", [128, 512], F32)
            sq2 = [self.sb(es, "sq2A%d" % i, [128, 512], BF16) for i in range(2)]
            ln2 = [self.sb(es, "ln2A%d" % i, [128, 512], F32) for i in range(2)]
            ob = [self.sb(es, "obA%d" % i, [128, 512], BF16) for i in range(4)]
            sm = {n: self.sb(es, "smA_" + n, [128, 512], F32) for n in ("e1", "l1", "c", "beta", "glog", "gc", "egc")}
            ones4 = self.sb(es, "ones4", [128, 512], F32)
            rmask = self.sb(es, "rmask", [128, 512], F32)
            S.memset("pool", ones4[:], 1.0, [ones4])
            S.memset("pool", rmask[:], 1.0, [rmask])
            for c in range(8):
                S.memset("pool", rmask[:, c * 64: c * 64 + 1], 0.0, [rmask])
            ccarry = self.sb(es, "ccarry", [128, 1], F32)
            S.memset("pool", ccarry[:], 0.0, [ccarry])
            xv = xin.t.rearrange("(k p) t -> p k t", p=128)
            hv = self.HT.t.rearrange("(k p) t -> p k t", p=128)
            cvs = CV["small"]
            nob = 0
            pi = 0
            for g in range(NG):
                ts_ = slice(g * 512, (g + 1) * 512)
                S.dma("sp", xt[:], xv[:, :, ts_], [xin], [xt])
                self.norm_group(xt, hT, sq, lnv, rstd, self.PS[0])
                S.dma("pool", hv[:, :, ts_], hT[:], [hT], [self.HT])
                for fam, wc, kind in (("FQ", 0, "nq"), ("FK", 512, "nk"), ("SQ", 1024, "s"), ("SK", 1536, "c")):
                    for h in range(4):
                        ps = self.PS[1 + pi % 3]
                        pi += 1
                        for k in range(8):
                            S.mm(ps[:], W[:, k, wc + h * 128: wc + (h + 1) * 128], hT[:, k, :], [W, hT], [ps],
                                 start=(k == 0), stop=(k == 7))
                        o = ob[nob % 4]
                        nob += 1
                        if kind in ("nq", "nk"):
                            s2 = sq2[nob % 2]
                            l2 = ln2[nob % 2]
                            ps2 = self.PS[4 + nob % 2]
                            S.act(s2[:], ps[:], AF.Square, [ps], [s2])
                            S.mm(ps2[:], self.ones_bf[:], s2[:], [self.ones_bf, s2], [ps2])
                            S.act(l2[:], ps2[:], AF.Ln, [ps2], [l2], bias=EPS, scale=1.0 / DH)
                            S.act(l2[:], l2[:], AF.Exp, [l2], [l2], scale=-0.5)
                            gcol = CV["fox_qnorm"] if kind == "nq" else CV["fox_knorm"]
                            S.stt(o[:], ps[:], self.cvt[:, gcol:gcol + 1], l2[:], ALU.mult, ALU.mult,
                                  [ps, self.cvt, l2], [o])
                            if kind == "nq":
                                S.ts("pool", o[:], o[:], SCALE, 1.0, ALU.mult, ALU.mult, [o], [o])
                        elif kind == "s":
                            S.act(o[:], ps[:], AF.Copy, [ps], [o], scale=SCALE)
                        else:
                            S.copy("dve", o[:], ps[:], [ps], [o])
                        dst = self.QK[fam]
                        S.dma("pool", dst.t[h * 128:(h + 1) * 128, ts_], o[:], [o], [dst])
                for fam, wc in (("FV", 2048), ("SV", 2560)):
                    for sub in range(4):
                        ps = self.PS[1 + pi % 3]
                        pi += 1
                        for k in range(8):
                            S.mm(ps[:], hT[:, k, sub * 128:(sub + 1) * 128], W[:, k, wc: wc + 512], [W, hT], [ps],
                                 start=(k == 0), stop=(k == 7))
                        o = ob[nob % 4]
                        nob += 1
                        S.copy("dve" if sub % 2 else "act", o[:], ps[:], [ps], [o])
                        dst = self.VT[fam]
                        r0 = g * 512 + sub * 128
                        S.dma("pool", dst.t[r0:r0 + 128, :], o[:], [o], [dst])
                ps = self.PS[6]
                for k in range(8):
                    S.mm(ps[0:96, :], Wsm[:, k, :], hT[:, k, :], [Wsm, hT], [ps], start=(k == 0), stop=(k == 7))
                cv = self.cvt
                S.act(sm["e1"][0:4, :], ps[0:4, :], AF.Exp, [ps, self.dvt], [sm["e1"]], bias=self.dvt[0:4, 0:1], scale=-1.0)
                S.act(sm["l1"][0:4, :], sm["e1"][0:4, :], AF.Ln, [sm["e1"]], [sm["l1"]], bias=1.0)
                S.op("dve", lambda e: e.tensor_tensor_scan(sm["c"][0:4, :], ones4[0:4, :], sm["l1"][0:4, :],
                                                           ccarry[0:4, 0:1], ALU.mult, ALU.subtract),
                     [ones4, sm["l1"], ccarry], [sm["c"]])
                S.copy("dve", ccarry[0:4, :], sm["c"][0:4, 511:512], [sm["c"]], [ccarry])
                S.dma("pool", self.ROW["CROW"].t[:, ts_], sm["c"][0:4, :], [sm["c"]], [self.ROW["CROW"]])
                S.act(sm["beta"][32:36, :], ps[32:36, :], AF.Sigmoid, [ps], [sm["beta"]])
                S.dma("pool", self.ROW["BETA"].t[:, ts_], sm["beta"][32:36, :], [sm["beta"]], [self.ROW["BETA"]])
                S.act(sm["e1"][64:68, :], ps[64:68, :], AF.Exp, [ps, cv], [sm["e1"]], bias=cv[64:68, cvs + 1:cvs + 2])
                S.act(sm["l1"][64:68, :], sm["e1"][64:68, :], AF.Ln, [sm["e1"]], [sm["l1"]], bias=1.0)
                S.ts("dve", sm["glog"][64:68, :], sm["l1"][64:68, :], self.dvt[64:68, 1:2], None, ALU.mult, None,
                     [sm["l1"], self.dvt], [sm["glog"]])
                S.op("dve", lambda e: e.tensor_tensor_scan(sm["gc"][64:68, :], rmask[64:68, :], sm["glog"][64:68, :],
                                                           0.0, ALU.mult, ALU.add),
                     [rmask, sm["glog"]], [sm["gc"]])
                S.act(sm["egc"][64:68, :], sm["gc"][64:68, :], AF.Exp, [sm["gc"]], [sm["egc"]])
                S.dma("pool", self.ROW["GLOG"].t[:, ts_], sm["glog"][64:68, :], [sm["glog"]], [self.ROW["GLOG"]])
                S.dma("pool", self.ROW["EGC"].t[:, ts_], sm["egc"][64:68, :], [sm["egc"]], [self.ROW["EGC"]])
            S.barrier()


    def pass_b(self, l):
        S, T, NG = self.S, self.T, self.NG
        w_in = self.inp["w_in"]
        PS = self.PS
        cv = self.cvt
        with ExitStack() as es:
            W = self.sb(es, "WB", [128, 8, 5120], BF16)
            stg = [self.sb(es, "stgB%d" % i, [128, 1024], F32) for i in range(2)]
            gm = CV["norm_mix"]
            wl = w_in.t[l]
            self.load_w(stg, W, 0, wl, 0, OFF["GQ"], 1536, gain_col=gm)
            self.load_w(stg, W, 1536, wl, 0, OFF["GZ"], 512, gain_col=gm)
            self.load_w(stg, W, 2048, wl, 0, OFF["GT"], 3072, gain_col=gm)
            hT = [self.sb(es, "hTB%d" % i, [128, 8, 512], BF16) for i in range(2)]
            halo = self.sb(es, "haloB", [128, 12, 4], F32)
            buf = [self.sb(es, "bufB%d" % i, [128, 516], F32) for i in range(2)]
            acc = [self.sb(es, "accB%d" % i, [128, 512], F32) for i in range(2)]
            sil = [self.sb(es, "silB%d" % i, [128, 512], F32) for i in range(2)]
            s2 = [self.sb(es, "sq2B%d" % i, [128, 512], BF16) for i in range(2)]
            l2 = [self.sb(es, "ln2B%d" % i, [128, 512], F32) for i in range(2)]
            ob = [self.sb(es, "obB%d" % i, [128, 512], BF16) for i in range(4)]
            S.memset("pool", halo[:], 0.0, [halo])
            hv = self.HT.t.rearrange("(k p) t -> p k t", p=128)
            pi = 0
            nob = 0
            cw = CV["gdn_conv"]
            for g in range(NG):
                ts_ = slice(g * 512, (g + 1) * 512)
                h_ = hT[g % 2]
                S.dma("sp", h_[:], hv[:, :, ts_], [self.HT], [h_])
                for c in range(12):
                    ps = PS[pi % 3]
                    pi += 1
                    for k in range(8):
                        S.mm(ps[:], W[:, k, c * 128:(c + 1) * 128], h_[:, k, :], [W, h_], [ps], start=(k == 0), stop=(k == 7))
                    b = buf[c % 2]
                    a = acc[c % 2]
                    sl = sil[c % 2]
                    S.copy("pool", b[:, 0:3], halo[:, c, 0:3], [halo], [b])
                    S.copy("act", b[:, 3:515], ps[:], [ps], [b])
                    S.copy("pool", halo[:, c, 0:3], b[:, 512:515], [b], [halo])
                    S.ts("dve", a[:], b[:, 0:512], cv[:, cw + c:cw + c + 1], None, ALU.mult, None, [b, cv], [a])
                    for tap in (1, 2, 3):
                        S.stt(a[:], b[:, tap:tap + 512], cv[:, cw + tap * 12 + c: cw + tap * 12 + c + 1], a[:],
                              ALU.mult, ALU.add, [b, cv, a], [a])
                    o = ob[nob % 4]
                    nob += 1
                    fam = ("GQ", "GK", "GV")[c // 4]
                    hh = c % 4
                    if fam == "GV":
                        S.act(o[:], a[:], AF.Silu, [a], [o])
                    else:
                        S.act(sl[:], a[:], AF.Silu, [a], [sl])
                        q2 = s2[c % 2]
                        ll = l2[c % 2]
                        ps2 = PS[3 + c % 2]
                        S.act(q2[:], sl[:], AF.Square, [sl], [q2])
                        S.mm(ps2[:], self.ones_bf[:], q2[:], [self.ones_bf, q2], [ps2])
                        S.act(ll[:], ps2[:], AF.Ln, [ps2], [ll], bias=EPS)
                        S.act(ll[:], ll[:], AF.Exp, [ll], [ll], scale=-0.5)
                        if fam == "GQ":
                            S.stt(o[:], sl[:], SCALE, ll[:], ALU.mult, ALU.mult, [sl, ll], [o])
                        else:
                            S.tt("dve", o[:], sl[:], ll[:], ALU.mult, [sl, ll], [o])
                    S.dma("pool", self.QK[fam].t[hh * 128:(hh + 1) * 128, ts_], o[:], [o], [self.QK[fam]])
                for c in range(4):
                    ps = PS[pi % 3]
                    pi += 1
                    for k in range(8):
                        S.mm(ps[:], W[:, k, 1536 + c * 128:1536 + (c + 1) * 128], h_[:, k, :], [W, h_], [ps],
                             start=(k == 0), stop=(k == 7))
                    o = ob[nob % 4]
                    nob += 1
                    S.act(o[:], ps[:], AF.Silu, [ps], [o])
                    S.dma("pool", self.QK["GZ"].t[c * 128:(c + 1) * 128, ts_], o[:], [o], [self.QK["GZ"]])
                gb = CV["gate_bias"]
                for c in range(24):
                    ps = PS[pi % 3]
                    pi += 1
                    for k in range(8):
                        S.mm(ps[:], W[:, k, 2048 + c * 128:2048 + (c + 1) * 128], h_[:, k, :], [W, h_], [ps],
                             start=(k == 0), stop=(k == 7))
                    o = ob[nob % 4]
                    nob += 1
                    S.act(o[:], ps[:], AF.Sigmoid, [ps, cv], [o], bias=cv[:, gb + c:gb + c + 1])
                    S.dma("pool", self.GATES.t[c * 128:(c + 1) * 128, ts_], o[:], [o], [self.GATES])
            S.barrier()

    def head_norm(self, ps, ncols, gcol, out, sq, ll, ps2, post_scale=None):
        S = self.S
        S.act(sq[:, 0:ncols], ps[:, 0:ncols], AF.Square, [ps], [sq])
        S.mm(ps2[:, 0:ncols], self.ones_bf[:], sq[:, 0:ncols], [self.ones_bf, sq], [ps2])
        S.act(ll[:, 0:ncols], ps2[:, 0:ncols], AF.Ln, [ps2], [ll], bias=EPS, scale=1.0 / DH)
        S.act(ll[:, 0:ncols], ll[:, 0:ncols], AF.Exp, [ll], [ll], scale=-0.5)
        S.stt(out, ps[:, 0:ncols], self.cvt[:, gcol:gcol + 1], ll[:, 0:ncols], ALU.mult, ALU.mult,
              [ps, self.cvt, ll], [out.tile] if hasattr(out, "tile") else [])

    def pass_c(self, l, xin, xout):
        S, T, NG = self.S, self.T, self.NG
        I = self.inp
        PS = self.PS
        cv = self.cvt
        with ExitStack() as es:
            Wo = [self.sb(es, "WCo%d" % i, [128, 4, D], BF16) for i in range(3)]
            Wout = self.sb(es, "WCout", [128, 8, D], BF16)
            Wmq = self.sb(es, "WCmq", [128, 8, 512], BF16)
            Wmo = self.sb(es, "WCmo", [128, 4, D], BF16)
            Wkv = self.sb(es, "WCkv", [128, 8, D], BF16)
            stg = [self.sb(es, "stgC%d" % i, [128, 1024], F32) for i in range(2)]
            for i, n in enumerate(("w_oa", "w_ob", "w_oc")):
                self.load_w(stg, Wo[i], 0, I[n].t[l], 0, 0, D, kchunks=4)
            self.load_w(stg, Wout, 0, I["w_out"].t[l], 0, 0, D)
            self.load_w(stg, Wmq, 0, I["w_mq"].t[l], 0, 0, 512, gain_col=CV["norm_xq"])
            self.load_w(stg, Wmo, 0, I["w_mo"].t[l], 0, 0, D, kchunks=4)
            self.load_w(stg, Wkv, 0, I["w_mkv"].t[l], 0, 0, D, gain_col=CV["norm_mem"])
            xt = self.sb(es, "xtC", [128, 8, 512], F32)
            hT = self.sb(es, "hTC", [128, 8, 512], BF16)
            sq = self.sb(es, "sqC", [128, 8, 512], BF16)
            lnv = self.sb(es, "lnvC", [128, 512], F32)
            rstd = self.sb(es, "rstdC", [128, 512], F32)
            yb_ = [self.sb(es, "yC%d" % i, [128, 4, 512], BF16) for i in range(3)]
            gt = self.sb(es, "gtC", [128, 24, 512], BF16)
            tA = self.sb(es, "tAC", [128, 512], F32)
            tB = self.sb(es, "tBC", [128, 512], F32)
            mix = self.sb(es, "mixC", [128, 8, 512], BF16)
            om = self.sb(es, "omC", [128, 4, 512], BF16)
            sq2 = self.sb(es, "sq2C", [128, 512], BF16)
            ll = self.sb(es, "llC", [128, 512], F32)
            qn = self.sb(es, "qnC", [128, 512], BF16)
            pt = [self.sb(es, "ptC%d" % i, [128, 512], BF16) for i in range(2)]
            Km = self.sb(es, "KmC", [128, 4, MEMT], BF16)
            Vm = self.sb(es, "VmC", [128, 2, 512], BF16)
            mv = I["memT"].t.rearrange("(k p) t -> p k t", p=128)
            S.dma("sp", xt[:, :, 0:MEMT], mv, [I["memT"]], [xt])
            S.act(sq[:, :, 0:MEMT], xt[:, :, 0:MEMT], AF.Square, [xt], [sq])
            for k in range(8):
                S.mm(PS[0][:, 0:MEMT], self.ones_bf[:], sq[:, k, 0:MEMT], [self.ones_bf, sq], [PS[0]], start=(k == 0), stop=(k == 7))
            S.act(lnv[:, 0:MEMT], PS[0][:, 0:MEMT], AF.Ln, [PS[0]], [lnv], bias=EPS, scale=1.0 / D)
            S.act(rstd[:, 0:MEMT], lnv[:, 0:MEMT], AF.Exp, [lnv], [rstd], scale=-0.5)
            for k in range(8):
                S.tt("dve", hT[:, k, 0:MEMT], xt[:, k, 0:MEMT], rstd[:, 0:MEMT], ALU.mult, [xt, rstd], [hT])
            for h in range(4):
                ps = PS[1 + h % 2]
                for k in range(8):
                    S.mm(ps[:, 0:MEMT], Wkv[:, k, h * 128:(h + 1) * 128], hT[:, k, 0:MEMT], [Wkv, hT], [ps], start=(k == 0), stop=(k == 7))
                S.act(sq2[:, 0:MEMT], ps[:, 0:MEMT], AF.Square, [ps], [sq2])
                S.mm(PS[3][:, 0:MEMT], self.ones_bf[:], sq2[:, 0:MEMT], [self.ones_bf, sq2], [PS[3]])
                S.act(ll[:, 0:MEMT], PS[3][:, 0:MEMT], AF.Ln, [PS[3]], [ll], bias=EPS, scale=1.0 / DH)
                S.act(ll[:, 0:MEMT], ll[:, 0:MEMT], AF.Exp, [ll], [ll], scale=-0.5)
                gk = CV["mk_norm"]
                S.stt(Km[:, h, :], ps[:, 0:MEMT], cv[:, gk:gk + 1], ll[:, 0:MEMT], ALU.mult, ALU.mult, [ps, cv, ll], [Km])
            for blk in range(2):
                ps = PS[1 + blk % 2]
                for k in range(8):
                    S.mm(ps[:], hT[:, k, blk * 128:(blk + 1) * 128], Wkv[:, k, 512:1024], [Wkv, hT], [ps], start=(k == 0), stop=(k == 7))
                S.copy("act", Vm[:, blk, :], ps[:], [ps], [Vm])
            xv = xin.t.rearrange("(k p) t -> p k t", p=128)
            xo = xout.t.rearrange("(k p) t -> p k t", p=128)
            yv = [self.Y[n].t.rearrange("(h p) t -> p h t", p=128) for n in ("YA", "YB", "YC")]
            gv = self.GATES.t.rearrange("(c p) t -> p c t", p=128)
            pi = 0
            for g in range(NG):
                ts_ = slice(g * 512, (g + 1) * 512)
                S.dma("sp", xt[:], xv[:, :, ts_], [xin], [xt])
                for i, n in enumerate(("YA", "YB", "YC")):
                    S.dma("sp", yb_[i][:], yv[i][:, :, ts_], [self.Y[n]], [yb_[i]])
                S.dma("sp", gt[:], gv[:, :, ts_], [self.GATES], [gt])
                for oc in range(8):
                    ocs = slice(oc * 128, (oc + 1) * 128)
                    for br in range(3):
                        ps = PS[pi % 3]
                        pi += 1
                        for hh in range(4):
                            S.mm(ps[:], Wo[br][:, hh, ocs], yb_[br][:, hh, :], [Wo[br], yb_[br]], [ps], start=(hh == 0), stop=(hh == 3))
                        if br == 0:
                            S.tt("dve", tA[:], ps[:], gt[:, oc, :], ALU.mult, [ps, gt], [tA])
                        else:
                            S.tt("dve", tB[:], ps[:], gt[:, br * 8 + oc, :], ALU.mult, [ps, gt], [tB])
                            if br == 1:
                                S.tt("pool", tA[:], tA[:], tB[:], ALU.add, [tA, tB], [tA])
                            else:
                                S.tt("pool", mix[:, oc, :], tA[:], tB[:], ALU.add, [tA, tB], [mix])
                for oc in range(8):
                    ocs = slice(oc * 128, (oc + 1) * 128)
                    ps = PS[pi % 3]
                    pi += 1
                    for k in range(8):
                        S.mm(ps[:], Wout[:, k, ocs], mix[:, k, :], [Wout, mix], [ps], start=(k == 0), stop=(k == 7))
                    S.tt("dve", xt[:, oc, :], xt[:, oc, :], ps[:], ALU.add, [xt, ps], [xt])
                self.norm_group(xt, hT, sq, lnv, rstd, PS[3])
                for h in range(4):
                    ps = PS[pi % 3]
                    pi += 1
                    for k in range(8):
                        S.mm(ps[:], Wmq[:, k, h * 128:(h + 1) * 128], hT[:, k, :], [Wmq, hT], [ps], start=(k == 0), stop=(k == 7))
                    S.act(sq2[:], ps[:], AF.Square, [ps], [sq2])
                    S.mm(PS[3][:], self.ones_bf[:], sq2[:], [self.ones_bf, sq2], [PS[3]])
                    S.act(ll[:], PS[3][:], AF.Ln, [PS[3]], [ll], bias=EPS, scale=1.0 / DH)
                    S.act(ll[:], ll[:], AF.Exp, [ll], [ll], scale=-0.5)
                    gq = CV["mq_norm"]
                    S.stt(tA[:], ps[:], cv[:, gq:gq + 1], ll[:], ALU.mult, ALU.mult, [ps, cv, ll], [tA])
                    S.ts("pool", qn[:], tA[:], SCALE, 1.0, ALU.mult, ALU.mult, [tA], [qn])
                    O = PS[4 + h % 2]
                    DN = PS[6 + h % 2]
                    for kb in range(2):
                        ps = PS[pi % 3]
                        pi += 1
                        p_ = pt[kb]
                        S.mm(ps[:], Km[:, h, kb * 128:(kb + 1) * 128], qn[:], [Km, qn], [ps])
                        S.act(p_[:], ps[:], AF.Exp, [ps], [p_])
                        S.mm(O[:], Vm[:, kb, h * 128:(h + 1) * 128], p_[:], [Vm, p_], [O], start=(kb == 0), stop=(kb == 1))
                        S.mm(DN[:], self.ones_bf[:], p_[:], [self.ones_bf, p_], [DN], start=(kb == 0), stop=(kb == 1))
                    S.act(ll[:], DN[:], AF.Ln, [DN], [ll])
                    S.act(ll[:], ll[:], AF.Exp, [ll], [ll], scale=-1.0)
                    S.tt("dve", om[:, h, :], O[:], ll[:], ALU.mult, [O, ll], [om])
                for oc in range(8):
                    ocs = slice(oc * 128, (oc + 1) * 128)
                    ps = PS[pi % 3]
                    pi += 1
                    for hh in range(4):
                        S.mm(ps[:], Wmo[:, hh, ocs], om[:, hh, :], [Wmo, om], [ps], start=(hh == 0), stop=(hh == 3))
                    S.tt("dve", xt[:, oc, :], xt[:, oc, :], ps[:], ALU.add, [xt, ps], [xt])
                S.dma("pool", xo[:, :, ts_], xt[:], [xt], [xout])
            S.barrier()

    def pass_d(self, l, xin, xout):
        S, T, NG = self.S, self.T, self.NG
        I = self.inp
        PS = self.PS
        cv = self.cvt
        NJ = DFF // 128
        with ExitStack() as es:
            Wup = self.sb(es, "WDup", [128, 8, 2 * DFF], BF16)
            Wdn = self.sb(es, "WDdn", [128, NJ, D], BF16)
            with ExitStack() as es2:
                stg = [self.sb(es2, "stgD%d" % i, [128, 1024], F32) for i in range(2)]
                self.load_w(stg, Wup, 0, I["w_up"].t[l], 0, 0, 2 * DFF, gain_col=CV["norm_ffn"])
                self.load_w(stg, Wdn, 0, I["w_down"].t[l], 0, 0, D, kchunks=NJ)
                S.barrier()
            xt = self.sb(es, "xtD", [128, 8, 512], F32)
            hT = self.sb(es, "hTD", [128, 8, 512], BF16)
            gT = self.sb(es, "gTD", [128, NJ, 512], BF16)
            lnv = self.sb(es, "lnvD", [128, 512], F32)
            rstd = self.sb(es, "rstdD", [128, 512], F32)
            halo = self.sb(es, "haloD", [128, 2 * NJ, 2], F32)
            buf = [self.sb(es, "bufD%d" % i, [128, 516], F32) for i in range(2)]
            acc = [self.sb(es, "accD%d" % i, [128, 512], F32) for i in range(2)]
            sa = self.sb(es, "saD", [128, 512], F32)
            S.memset("pool", halo[:], 0.0, [halo])
            xv = xin.t.rearrange("(k p) t -> p k t", p=128)
            xo = xout.t.rearrange("(k p) t -> p k t", p=128)
            cw = CV["ffn_conv"]
            cb = CV["ffn_conv_b"]
            pi = 0
            for g in range(NG):
                ts_ = slice(g * 512, (g + 1) * 512)
                S.dma("sp", xt[:], xv[:, :, ts_], [xin], [xt])
                self.norm_group(xt, hT, _SqAlias(gT), lnv, rstd, PS[3])
                for j in range(NJ):
                    for ab in range(2):
                        c = ab * NJ + j
                        ps = PS[pi % 3]
                        pi += 1
                        for k in range(8):
                            S.mm(ps[:], Wup[:, k, c * 128:(c + 1) * 128], hT[:, k, :], [Wup, hT], [ps], start=(k == 0), stop=(k == 7))
                        b = buf[ab]
                        a = acc[ab]
                        S.copy("pool", b[:, 0:2], halo[:, c, 0:2], [halo], [b])
                        S.copy("act", b[:, 2:514], ps[:], [ps], [b])
                        S.copy("pool", halo[:, c, 0:2], b[:, 512:514], [b], [halo])
                        S.ts("dve", a[:], b[:, 0:512], cv[:, cw + c:cw + c + 1], cv[:, cb + c:cb + c + 1], ALU.mult, ALU.add,
                             [b, cv], [a])
                        for tap in (1, 2):
                            S.stt(a[:], b[:, tap:tap + 512], cv[:, cw + tap * 2 * NJ + c: cw + tap * 2 * NJ + c + 1], a[:],
                                  ALU.mult, ALU.add, [b, cv, a], [a])
                    S.act(sa[:], acc[0][:], AF.Silu, [acc[0]], [sa])
                    S.tt("pool", gT[:, j, :], sa[:], acc[1][:], ALU.mult, [sa, acc[1]], [gT])
                for oc in range(8):
                    ocs = slice(oc * 128, (oc + 1) * 128)
                    ps = PS[4 + oc % 2]
                    for j in range(NJ):
                        S.mm(ps[:], Wdn[:, j, ocs], gT[:, j, :], [Wdn, gT], [ps], start=(j == 0), stop=(j == NJ - 1))
                    S.tt("dve", xt[:, oc, :], xt[:, oc, :], ps[:], ALU.add, [xt, ps], [xt])
                S.dma("pool", xo[:, :, ts_], xt[:], [xt], [xout])
            S.barrier()

    def gdn_head(self, h):
        S, T = self.S, self.T
        NP = T // 128
        PS = self.PS
        cm = self.cm32
        ident = cm[:, 0, :]
        with ExitStack() as es:
            Qf = self.sb(es, "gQ", [128, T], BF16)
            Kf = self.sb(es, "gK", [128, T], BF16)
            Vf = self.sb(es, "gV", [128, T], BF16)
            bc = self.sb(es, "gbc", [128, T], F32)
            G2 = self.sb(es, "gG2", [128, 128], F32)
            B2 = self.sb(es, "gB2", [128, 128], F32)
            gc2 = self.sb(es, "ggc2", [128, 128], F32)
            gl2 = self.sb(es, "ggl2", [128, 128], F32)
            rm2 = self.sb(es, "grm2", [128, 128], F32)
            gcol = self.sb(es, "ggcol", [128, NP], F32)
            bcol = self.sb(es, "gbcol", [128, NP], F32)
            eglc = self.sb(es, "geglc", [128, NP], F32)
            Sst = self.sb(es, "gS", [128, 128], F32)
            hs = slice(h * 128, (h + 1) * 128)
            S.dma("sp", Qf[:], self.QK["GQ"].t[hs, :], [self.QK["GQ"]], [Qf])
            S.dma("sp", Kf[:], self.QK["GK"].t[hs, :], [self.QK["GK"]], [Kf])
            S.dma("sp", Vf[:], self.QK["GV"].t[hs, :], [self.QK["GV"]], [Vf])
            R = self.ROW
            S.dma("sp", G2[0:NP, :], R["GLOG"].t[h:h + 1, :].rearrange("o (n p) -> (o n) p", p=128), [R["GLOG"]], [G2])
            S.dma("sp", B2[0:NP, :], R["BETA"].t[h:h + 1, :].rearrange("o (n p) -> (o n) p", p=128), [R["BETA"]], [B2])
            S.dma("sp", bc[:], R["EGC"].t[h:h + 1, :].partition_broadcast(128), [R["EGC"]], [bc])
            S.memset("pool", rm2[:], 1.0, [rm2])
            S.memset("pool", rm2[:, 0:1], 0.0, [rm2])
            S.memset("pool", rm2[:, 64:65], 0.0, [rm2])
            S.op("dve", lambda e: e.tensor_tensor_scan(gc2[0:NP, :], rm2[0:NP, :], G2[0:NP, :], 0.0, ALU.mult, ALU.add),
                 [rm2, G2], [gc2])
            for half in range(2):
                cs = slice(64 * half, 64 * half + 64)
                tot = gc2[0:NP, 64 * half + 63: 64 * half + 64]
                S.ts("dve", gl2[0:NP, cs], gc2[0:NP, cs], tot, -1.0, ALU.subtract, ALU.mult, [gc2], [gl2])
            S.act(gl2[0:NP, :], gl2[0:NP, :], AF.Exp, [gl2], [gl2])
            eg2 = self.sb(es, "geg2", [128, 128], F32)
            egcol = self.sb(es, "gegcol", [128, NP], F32)
            bgc = self.sb(es, "gbgc", [128, NP], F32)
            S.act(eg2[0:NP, :], gc2[0:NP, :], AF.Exp, [gc2], [eg2])
            for src, dst in ((G2, gcol), (B2, bcol), (gl2, eglc), (eg2, egcol)):
                S.tr(PS[7][:, 0:NP], src[0:NP, :], cm[0:NP, 0, 0:NP], [src, cm], [PS[7]])
                S.copy("dve", dst[:], PS[7][:, 0:NP], [PS[7]], [dst])
            S.tt("dve", bgc[:], bcol[:], egcol[:], ALU.mult, [bcol, egcol], [bgc])
            Sst2 = self.sb(es, "gS2", [128, 128], F32)
            SS = [Sst, Sst2]
            S.memset("pool", Sst[:], 0.0, [Sst])

            def f32t(n, k=5):
                return [self.sb(es, n + str(i), [128, 128], F32) for i in range(k)]
            lg = f32t("g_lg"); Dl = f32t("g_Dl"); DTs = f32t("g_DTs"); DTi = f32t("g_DTi")
            Lb = [f32t("g_L%d_" % j) for j in range(2)]
            Ub = [f32t("g_U%d_" % j) for j in range(2)]
            X = f32t("g_X"); At = f32t("g_At")
            K32 = f32t("g_K32"); V32 = f32t("g_V32"); Qg = f32t("g_Qg"); Qp = f32t("g_Qp")
            Kt = f32t("g_Kt")
            Rr = [self.sb(es, "g_R%d" % i_, [128, 256], F32) for i_ in range(5)]
            UW = [self.sb(es, "g_UW%d" % i_, [128, 256], F32) for i_ in range(5)]
            MT = [f32t("g_MT%d_" % j, 5) for j in range(2)]
            Nn = [f32t("g_N%d_" % j, 5) for j in range(2)]
            OT = [self.sb(es, "g_OT%d" % i, [128, 512], F32) for i in range(2)]
            gz = [self.sb(es, "g_gz%d" % i, [128, 512], BF16) for i in range(2)]
            sq = self.sb(es, "g_sq", [128, 512], BF16)
            ll = self.sb(es, "g_ll", [128, 512], F32)
            t32 = self.sb(es, "g_t32", [128, 512], F32)
            ob = [self.sb(es, "g_ob%d" % i, [128, 512], BF16) for i in range(2)]

            def pre(r):
                i = r % 5
                cs = slice(r * 128, (r + 1) * 128)
                PA = PS[r % 4]
                PB = PA
                q = [slice(0, 128), slice(128, 256), slice(256, 384), slice(384, 512)]
                S.ts("pool", lg[i][:], cm[:, 4, :], gcol[:, r:r + 1], 1.0, ALU.mult, ALU.mult, [cm, gcol], [lg[i]])
                yield
                S.mm(PA[:, q[0]], lg[i][:], cm[:, 5, :], [lg[i], cm], [PA], start=True, stop=False)
                S.mm(PA[:, q[0]], ident, cm[:, 6, :], [cm], [PA], start=False, stop=True)
                S.mm(PA[:, q[1]], cm[:, 5, :], lg[i][:], [lg[i], cm], [PA], start=True, stop=False)
                S.mm(PA[:, q[1]], ident, cm[:, 8, :], [cm], [PA], start=False, stop=True)
                yield
                S.act(Dl[i][:], PA[:, q[0]], AF.Exp, [PA], [Dl[i]])
                yield
                S.act(DTi[i][:], PA[:, q[1]], AF.Exp, [PA], [DTi[i]])
                yield
                S.mm(PA[:, q[2]], Kf[:, cs], Kf[:, cs], [Kf], [PA])
                S.mm(PA[:, q[3]], Kf[:, cs], Qf[:, cs], [Qf, Kf], [PA])
                yield
                L = Lb[0][i]
                U = Ub[0][i]
                S.stt(L[:], PA[:, q[2]], bcol[:, r:r + 1], Dl[i][:], ALU.mult, ALU.mult, [PA, bcol, Dl[i]], [L])
                yield
                S.tt("dve", At[i][:], PA[:, q[3]], DTi[i][:], ALU.mult, [PA, DTi[i]], [At[i]])
                yield
                S.tr(PA[:, q[0]], L[:], ident, [L, cm], [PA])
                yield
                S.copy("act", U[:], PA[:, q[0]], [PA], [U])
                yield
                S.tt("pool", X[i][:], ident, U[:], ALU.subtract, [cm, U], [X[i]])
                yield
                S.copy("pool", K32[i][:], Kf[:, cs], [Kf], [K32[i]])
                yield
                S.copy("pool", V32[i][:], Vf[:, cs], [Vf], [V32[i]])
                yield
                S.tr(PA[:, q[1]], K32[i][:], ident, [K32[i], cm], [PA])
                S.tr(PA[:, q[2]], V32[i][:], ident, [V32[i], cm], [PA])
                yield
                S.ts("dve", Kt[i][:], PA[:, q[1]], eglc[:, r:r + 1], None, ALU.mult, None, [PA, eglc], [Kt[i]])
                yield
                S.act(Rr[i][:, 128:256], PA[:, q[1]], AF.Copy, [PA, bgc], [Rr[i]], scale=bgc[:, r:r + 1])
                yield
                S.act(Rr[i][:, 0:128], PA[:, q[2]], AF.Copy, [PA, bcol], [Rr[i]], scale=bcol[:, r:r + 1])
                yield
                S.tt("pool", Qg[i][:], Qf[:, cs], bc[:, cs], ALU.mult, [Qf, bc], [Qg[i]])
                yield
                for k in range(1, 6):
                    Ln_ = Lb[k % 2][i]
                    Un_ = Ub[k % 2][i]
                    S.mm(PA[:, q[0]], U[:], L[:], [U, L], [PA])
                    if k < 5:
                        S.mm(PA[:, q[1]], L[:], U[:], [U, L], [PA])
                    yield
                    S.copy("act", Ln_[:], PA[:, q[0]], [PA], [Ln_])
                    yield
                    if k < 5:
                        S.copy("dve", Un_[:], PA[:, q[1]], [PA], [Un_])
                        yield
                    S.mm(PA[:, q[2]], Ln_[:], X[i][:], [Ln_, X[i]], [PA])
                    yield
                    S.tt("dve", X[i][:], X[i][:], PA[:, q[2]], ALU.add, [X[i], PA], [X[i]])
                    yield
                    L, U = Ln_, Un_
                S.mm(PA[:, 0:256], X[i][:], Rr[i][:], [X[i], Rr[i]], [PA])
                yield
                S.copy("act", UW[i][:], PA[:, 0:256], [PA], [UW[i]])
                yield
                S.mm(PA[:, q[3]], UW[i][:, 128:256], At[i][:], [UW[i], At[i]], [PA])
                yield
                S.tt("dve", Qp[i][:], Qg[i][:], PA[:, q[3]], ALU.subtract, [Qg[i], PA], [Qp[i]])
                yield
                for par in range(2):
                    rows = slice(64 * par, 64 * par + 64)
                    col = r * 128 + 64 * par + 63
                    S.mm(PA[:, q[2]], UW[i][rows, 128:256], Kt[i][rows, :], [UW[i], Kt[i]], [PA])
                    S.mm(PA[:, q[3]], Kt[i][rows, :], UW[i][rows, 0:128], [UW[i], Kt[i]], [PA])
                    yield
                    S.stt(MT[par][i][:], ident, bc[:, col:col + 1], PA[:, q[2]], ALU.mult, ALU.subtract,
                          [cm, bc, PA], [MT[par][i]])
                    yield
                    S.copy("act", Nn[par][i][:], PA[:, q[3]], [PA], [Nn[par][i]])
                    yield

            def scan(r):
                i = r % 5
                ot = OT[(r // 4) % 2]
                for par in range(2):
                    rows = slice(64 * par, 64 * par + 64)
                    c = 2 * r + par
                    Sp = SS[c % 2]
                    Sn = SS[(c + 1) % 2]
                    P5 = PS[5 + c % 2]
                    P7 = PS[7]
                    S.mm(P7[:, 0:64], Sp[:], Qp[i][:, rows], [Sp, Qp[i]], [P7], start=True, stop=False)
                    S.mm(P7[:, 0:64], UW[i][rows, 0:128], At[i][rows, rows], [UW[i], At[i]], [P7], start=False, stop=True)
                    S.mm(P5[:, 0:128], MT[par][i][:], Sp[:], [MT[par][i], Sp], [P5])
                    yield
                    S.tt("dve", Sn[:], P5[:, 0:128], Nn[par][i][:], ALU.add, [P5, Nn[par][i]], [Sn])
                    yield
                    oc = (r % 4) * 128 + 64 * par
                    S.copy("act", ot[:, oc:oc + 64], P7[:, 0:64], [P7], [ot])
                    yield

            def post(blk):
                ot = OT[blk % 2]
                ts_ = slice(blk * 512, (blk + 1) * 512)
                z = gz[blk % 2]
                S.dma("sp", z[:], self.QK["GZ"].t[hs, ts_], [self.QK["GZ"]], [z])
                S.act(sq[:], ot[:], AF.Square, [ot], [sq])
                S.mm(PS[4][:], self.ones_bf[:], sq[:], [self.ones_bf, sq], [PS[4]])
                S.act(ll[:], PS[4][:], AF.Ln, [PS[4]], [ll], bias=EPS, scale=1.0 / DH)
                S.act(ll[:], ll[:], AF.Exp, [ll], [ll], scale=-0.5)
                go = CV["gdn_onorm"]
                S.stt(t32[:], ot[:], self.cvt[:, go:go + 1], ll[:], ALU.mult, ALU.mult, [ot, self.cvt, ll], [t32])
                o = ob[blk % 2]
                S.tt("pool", o[:], t32[:], z[:], ALU.mult, [t32, z], [o])
                S.dma("pool", self.Y["YB"].t[hs, ts_], o[:], [o], [self.Y["YB"]])

            act = {}

            def adv(r, nops):
                if r >= NP:
                    return True
                if r not in act:
                    act[r] = pre(r)
                g_ = act[r]
                if g_ is None:
                    return True
                for _ in range(nops):
                    try:
                        next(g_)
                    except StopIteration:
                        act[r] = None
                        return True
                return False

            while not adv(0, 1000):
                pass
            for r in range(NP):
                gs = scan(r)
                sdone = False
                while True:
                    d1 = adv(r + 1, 2)
                    adv(r + 2, 2)
                    adv(r + 3, 1)
                    adv(r + 4, 1)
                    if not sdone:
                        try:
                            next(gs)
                        except StopIteration:
                            sdone = True
                    if sdone and d1:
                        break
                if r % 4 == 3:
                    post(r // 4)
            S.barrier()

    def fox_head(self, h):
        S, T, NG = self.S, self.T, self.NG
        NB = T // 128
        PS = self.PS
        with ExitStack() as es:
            Qf = self.sb(es, "fxQ", [128, T], BF16)
            Kf = self.sb(es, "fxK", [128, T], BF16)
            Vt = self.sb(es, "fxV", [128, NB, 128], BF16)
            cbc = self.sb(es, "fxcbc", [128, T], F32)
            c2 = self.sb(es, "fxc2", [128, 128], F32)
            ncc = self.sb(es, "fxncc", [128, NB], F32)
            tmp = [self.sb(es, "fxtmp%d" % i, [128, 512], F32) for i in range(2)]
            PT = [self.sb(es, "fxPT%d" % i, [128, 512], BF16) for i in range(3)]
            lnd = self.sb(es, "fxlnd", [128, 512], F32)
            ob = [self.sb(es, "fxob%d" % i, [128, 512], BF16) for i in range(2)]
            hs = slice(h * 128, (h + 1) * 128)
            S.dma("sp", Qf[:], self.QK["FQ"].t[hs, :], [self.QK["FQ"]], [Qf])
            S.dma("sp", Kf[:], self.QK["FK"].t[hs, :], [self.QK["FK"]], [Kf])
            S.dma("sp", Vt[:], self.VT["FV"].t.rearrange("(n p) c -> p n c", p=128)[:, :, hs], [self.VT["FV"]], [Vt])
            crow = self.ROW["CROW"]
            S.dma("sp", cbc[:], crow.t[h:h + 1, :].partition_broadcast(128), [crow], [cbc])
            S.dma("sp", c2[0:NB, :], crow.t[h:h + 1, :].rearrange("o (n p) -> (o n) p", p=128), [crow], [c2])
            S.tr(PS[7][:, 0:NB], c2[0:NB, :], self.cm32[0:NB, 0, 0:NB], [c2, self.cm32], [PS[7]])
            S.ts("dve", ncc[:], PS[7][:, 0:NB], -1.0, None, ALU.mult, None, [PS[7]], [ncc])
            tiles = []
            for g in range(NG):
                tl = [(kb, 0) for kb in range(4 * g)] + [(4 * g + j, 128 * j) for j in range(4)]
                for ti, (kb, c0) in enumerate(tl):
                    tiles.append((g, kb, c0, ti == 0, ti == len(tl) - 1))
            n = len(tiles)

            def emit_S(i):
                g, kb, c0, first, last = tiles[i]
                ps = PS[i % 3]
                S.mm(ps[:, c0:512], Kf[:, kb * 128:(kb + 1) * 128], Qf[:, g * 512 + c0:(g + 1) * 512], [Kf, Qf], [ps])

            def emit_mid(i):
                g, kb, c0, first, last = tiles[i]
                ps = PS[i % 3]
                tm = tmp[i % 2]
                pt = PT[i % 3]
                S.tt("dve", tm[:, c0:512], ps[:, c0:512], cbc[:, g * 512 + c0:(g + 1) * 512], ALU.add, [ps, cbc], [tm])
                S.act(pt[:, c0:512], tm[:, c0:512], AF.Exp, [tm, ncc], [pt], bias=ncc[:, kb:kb + 1])
                if kb >= 4 * g:
                    S.tt("pool", pt[:, c0:c0 + 128], pt[:, c0:c0 + 128], self.cmbf[:, 1, :], ALU.mult,
                         [pt, self.cmbf], [pt])

            def emit_PV(i):
                g, kb, c0, first, last = tiles[i]
                pt = PT[i % 3]
                O = PS[3 + g % 2]
                DN = PS[5 + g % 2]
                S.mm(O[:, c0:512], Vt[:, kb, :], pt[:, c0:512], [Vt, pt], [O], start=first, stop=last)
                S.mm(DN[:, c0:512], self.ones_bf[:], pt[:, c0:512], [self.ones_bf, pt], [DN], start=first, stop=last)
                if last:
                    S.act(lnd[:], DN[:], AF.Ln, [DN], [lnd])
                    S.act(lnd[:], lnd[:], AF.Exp, [lnd], [lnd], scale=-1.0)
                    o = ob[g % 2]
                    S.tt("dve", o[:], O[:], lnd[:], ALU.mult, [O, lnd], [o])
                    S.dma("pool", self.Y["YA"].t[hs, g * 512:(g + 1) * 512], o[:], [o], [self.Y["YA"]])

            LA = 2
            for i in range(min(LA, n)):
                emit_S(i)
            for i in range(n):
                if i + LA < n:
                    emit_S(i + LA)
                emit_mid(i)
                emit_PV(i)
            S.barrier()

    def sb_head(self, h):
        S, T, NG = self.S, self.T, self.NG
        NB = T // 128
        PS = self.PS
        with ExitStack() as es:
            Qf = self.sb(es, "sbQ", [128, T], BF16)
            Kf = self.sb(es, "sbK", [128, T], BF16)
            Vt = self.sb(es, "sbV", [128, NB, 128], BF16)
            ntri = self.sb(es, "sbntri", [128, 128], BF16)
            nones = self.sb(es, "sbnones", [128, 128], BF16)
            zeros = self.sb(es, "sbzeros", [128, 128], BF16)
            E = [self.sb(es, "sbE%d" % i, [128, 512], F32) for i in range(2)]
            SP = [self.sb(es, "sbSP%d" % i, [128, 512], BF16) for i in range(3)]
            AT = [self.sb(es, "sbAT%d" % i, [128, 512], BF16) for i in range(3)]
            cum = [self.sb(es, "sbcum%d" % i, [128, 512], BF16) for i in range(2)]
            ob = [self.sb(es, "sbob%d" % i, [128, 512], BF16) for i in range(2)]
            hs = slice(h * 128, (h + 1) * 128)
            S.dma("sp", Qf[:], self.QK["SQ"].t[hs, :], [self.QK["SQ"]], [Qf])
            S.dma("sp", Kf[:], self.QK["SK"].t[hs, :], [self.QK["SK"]], [Kf])
            S.dma("sp", Vt[:], self.VT["SV"].t.rearrange("(n p) c -> p n c", p=128)[:, :, hs], [self.VT["SV"]], [Vt])
            S.ts("dve", ntri[:], self.cm32[:, 3, :], -1.0, None, ALU.mult, None, [self.cm32], [ntri])
            S.memset("dve", nones[:], -1.0, [nones])
            S.memset("dve", zeros[:], 0.0, [zeros])
            tiles = []
            for g in range(NG):
                tl = [(4 * g + j, 128 * j) for j in (3, 2, 1, 0)] + [(kb, 0) for kb in range(4 * g - 1, -1, -1)]
                for ti, (kb, c0) in enumerate(tl):
                    tiles.append((g, kb, c0, ti, len(tl)))
            n = len(tiles)

            def emit_A(i):
                g, kb, c0, ti, nt = tiles[i]
                A = PS[i % 3]
                S.mm(A[:, c0:512], Kf[:, kb * 128:(kb + 1) * 128], Qf[:, g * 512 + c0:(g + 1) * 512], [Kf, Qf], [A])

            def emit_sp(i):
                g, kb, c0, ti, nt = tiles[i]
                A = PS[i % 3]
                e = E[i % 2]
                sp = SP[i % 3]
                S.act(e[:, c0:512], A[:, c0:512], AF.Exp, [A], [e])
                S.act(sp[:, c0:512], e[:, c0:512], AF.Ln, [e], [sp], bias=1.0)
                if kb >= 4 * g:
                    S.tt("pool", sp[:, c0:c0 + 128], sp[:, c0:c0 + 128], self.cmbf[:, 2, :], ALU.mult,
                         [sp, self.cmbf], [sp])

            def emit_B(i):
                g, kb, c0, ti, nt = tiles[i]
                B = PS[3 + i % 2]
                sp = SP[i % 3]
                cm = cum[g % 2]
                O = PS[5 + g % 2]
                q0, q1 = g * 512 + c0, (g + 1) * 512
                kT = Kf[:, kb * 128:(kb + 1) * 128]
                if ti == 0:
                    S.memset("pool", cm[:], 0.0, [cm])
                    S.mm(O[:], zeros[:], Qf[:, g * 512:(g + 1) * 512], [zeros, Qf], [O], start=True, stop=False)
                S.mm(B[:, c0:512], kT, Qf[:, q0:q1], [Kf, Qf], [B], start=True, stop=False)
                S.mm(B[:, c0:512], ntri[:], sp[:, c0:512], [ntri, sp], [B], start=False, stop=(ti == 0))
                if ti > 0:
                    S.mm(B[:, c0:512], nones[:], cm[:, c0:512], [nones, cm], [B], start=False, stop=True)
                if ti < nt - 1:
                    S.tt("dve", cm[:, c0:512], cm[:, c0:512], sp[:, c0:512], ALU.add, [cm, sp], [cm])

            def emit_at(i):
                g, kb, c0, ti, nt = tiles[i]
                B = PS[3 + i % 2]
                at = AT[i % 3]
                S.act(at[:, c0:512], B[:, c0:512], AF.Exp, [B], [at])
                if kb >= 4 * g:
                    S.tt("pool", at[:, c0:c0 + 128], at[:, c0:c0 + 128], self.cmbf[:, 2, :], ALU.mult,
                         [at, self.cmbf], [at])

            def emit_O(i):
                g, kb, c0, ti, nt = tiles[i]
                at = AT[i % 3]
                O = PS[5 + g % 2]
                S.mm(O[:, c0:512], Vt[:, kb, :], at[:, c0:512], [Vt, at], [O], start=False, stop=(ti == nt - 1))
                if ti == nt - 1:
                    o = ob[g % 2]
                    S.copy("dve", o[:], O[:], [O], [o])
                    S.dma("pool", self.Y["YC"].t[hs, g * 512:(g + 1) * 512], o[:], [o], [self.Y["YC"]])

            emit_A(0)
            if n > 1:
                emit_A(1)
            emit_sp(0)
            for i in range(n):
                if i + 2 < n:
                    emit_A(i + 2)
                if i + 1 < n:
                    emit_sp(i + 1)
                emit_B(i)
                emit_at(i)
                if i >= 1:
                    emit_O(i - 1)
            emit_O(n - 1)
            S.barrier()


def host_consts():
    p = np.arange(128)[:, None]
    f = np.arange(128)[None, :]
    cm = np.zeros((128, 9, 128), np.float32)
    cm[:, 0, :] = (p == f)
    cm[:, 1, :] = (f >= p)
    cm[:, 2, :] = (f > p)
    cm[:, 3, :] = (p >= f)
    same = (p // 64) == (f // 64)
    cm[:, 4, :] = (p <= f) & same
    cm[:, 5, :] = (p > f) & same
    cm[:, 6, :] = np.where((p > f) & same, 0.0, NEG)
    cm[:, 7, :] = np.where((f > p) & same, 0.0, NEG)
    cm[:, 8, :] = np.where((f >= p) & same, 0.0, NEG)
    return cm


def pack_cv(inp, L):
    cv = np.zeros((L, 128, NCV), np.float32)

    def chunks(v):
        return np.ascontiguousarray(v.reshape(-1, 128).T)
    for l in range(L):
        c = cv[l]
        c[:, CV["norm_mix"]:CV["norm_mix"] + 8] = chunks(inp["norm_mix"][l])
        c[:, CV["gate_bias"]:CV["gate_bias"] + 24] = chunks(inp["gate_bias"][l])
        c[:, CV["fox_qnorm"]] = inp["fox_qnorm"][l]
        c[:, CV["fox_knorm"]] = inp["fox_knorm"][l]
        gc = inp["gdn_conv"][l]
        for tap in range(4):
            c[:, CV["gdn_conv"] + tap * 12: CV["gdn_conv"] + (tap + 1) * 12] = chunks(gc[tap])
        c[:, CV["gdn_onorm"]] = inp["gdn_onorm"][l]
        c[:, CV["norm_xq"]:CV["norm_xq"] + 8] = chunks(inp["norm_xq"][l])
        c[:, CV["norm_mem"]:CV["norm_mem"] + 8] = chunks(inp["norm_mem"][l])
        c[:, CV["mq_norm"]] = inp["mq_norm"][l]
        c[:, CV["mk_norm"]] = inp["mk_norm"][l]
        c[:, CV["norm_ffn"]:CV["norm_ffn"] + 8] = chunks(inp["norm_ffn"][l])
        fc = inp["ffn_conv"][l]
        for tap in range(3):
            c[:, CV["ffn_conv"] + tap * 44: CV["ffn_conv"] + (tap + 1) * 44] = chunks(fc[tap])
        c[:, CV["ffn_conv_b"]:CV["ffn_conv_b"] + 44] = chunks(inp["ffn_conv_b"][l])
        s = CV["small"]
        c[0:4, s] = inp["fox_fbias"][l]
        c[64:68, s + 1] = inp["gdn_dt_bias"][l]
        c[64:68, s + 2] = inp["gdn_a_log"][l]
    return cv


def prep_inputs(inp, b, T, L):
    m = {}
    m["xT"] = np.ascontiguousarray(inp["x"][b, :T].T)
    m["memT"] = np.ascontiguousarray(inp["mem"][b].T)
    for n in ("w_in", "w_oa", "w_ob", "w_oc", "w_out", "w_mq", "w_mkv", "w_mo", "w_up", "w_down"):
        m[n] = np.ascontiguousarray(inp[n][:L])
    m["cv"] = pack_cv(inp, L)
    m["cmask"] = host_consts()
    return m


_CACHE = {}


def kernel(**inputs):
    inp = {k: np.asarray(v) for k, v in inputs.items()}
    B, T, _ = inp["x"].shape
    L = inp["w_in"].shape[0]
    key = (T, L)
    nc = Builder(T, L).build()
    ncores = 8
    in_maps = [prep_inputs(inp, c % B, T, L) for c in range(B)]
    in_maps = [in_maps[c % B] for c in range(ncores)]
    res = run_bass_kernel_spmd(nc, in_maps, core_ids=list(range(ncores)))
    out = np.empty((B, T, D), np.float32)
    for b in range(B):
        out[b] = np.asarray(res.results[b]["yT"]).T
    return out
```

```python
import numpy as np
from contextlib import ExitStack
import concourse.bass as bass
import concourse.mybir as mybir
from concourse.bass_utils import run_bass_kernel_spmd

F32 = mybir.dt.float32
BF16 = mybir.dt.bfloat16
ALU = mybir.AluOpType
AF = mybir.ActivationFunctionType

D = 1024
NH = 4
DH = 128
EPS = 1e-6
MEMT = 256
DFF = 2816
OFF = dict(FQ=0, FK=512, FV=1024, FF=1536, GQ=1540, GK=2052, GV=2564, GB=3076, GA=3080,
           GZ=3084, SQ=3596, SK=4108, SV=4620, GT=5132, END=8204)
SCALE = DH ** -0.5
NEG = -30000.0

CV = {}
_o = 0
for _n, _w in [("norm_mix", 8), ("gate_bias", 24), ("fox_qnorm", 1), ("fox_knorm", 1),
               ("gdn_conv", 48), ("gdn_onorm", 1), ("norm_xq", 8), ("norm_mem", 8),
               ("mq_norm", 1), ("mk_norm", 1), ("norm_ffn", 8), ("ffn_conv", 132),
               ("ffn_conv_b", 44), ("small", 3)]:
    CV[_n] = _o
    _o += _w
NCV = _o


class Tile:
    def __init__(self, t, name, psum=False):
        self.t = t
        self.name = name
        self.psum = psum
        self.w = {}
        self.rs = {}

    def __getitem__(self, idx):
        return self.t[idx]


class Trk:
    def __init__(self):
        self.w = {}
        self.rs = {}


class Eng:
    def __init__(self, name, eng, sem):
        self.name = name
        self.eng = eng
        self.sem = sem
        self.n = 0
        self.seen = {}


class Sched:
    def __init__(self, nc, es, ndsem=12):
        self.nc = nc
        self.es = es
        self.sems = {}
        self.E = {}
        for name, eng in [("pe", nc.tensor), ("act", nc.scalar), ("dve", nc.vector),
                          ("pool", nc.gpsimd), ("sp", nc.sync)]:
            sem = es.enter_context(nc.semaphore("sem_" + name))
            self.E[name] = Eng(name, eng, sem)
            self.sems[name] = sem
        self.dpool = {}
        self.dnext = {}
        for q in ("sp", "pool", "act"):
            lst = []
            for i in range(ndsem):
                key = "d_%s_%d" % (q, i)
                sem = es.enter_context(nc.semaphore(key))
                self.sems[key] = sem
                lst.append([sem, 0, key])
            self.dpool[q] = lst
            self.dnext[q] = 0
        self.ninst = 0

    def _deps(self, reads, writes):
        toks = []
        for r in reads:
            for k, v in r.w.items():
                toks.append((k, v, True))
            if getattr(r, "psum", False):
                for k, v in r.rs.items():
                    toks.append((k, v, False))
        for w in writes:
            for k, v in w.w.items():
                toks.append((k, v, False))
            for k, v in w.rs.items():
                toks.append((k, v, False))
        return toks

    def _wait(self, E, toks):
        for key, val, raw in toks:
            if key == E.name:
                if not raw or E.name == "pe":
                    continue
            if E.seen.get(key, 0) >= val:
                continue
            E.eng.wait_ge(self.sems[key], val)
            E.seen[key] = val
            self.ninst += 1
            E.nw = getattr(E, "nw", 0) + 1

    def _mark(self, tok, reads, writes):
        k, v = tok
        for r in reads:
            if r.rs.get(k, 0) < v:
                r.rs[k] = v
        for w in writes:
            w.w[k] = v
            w.rs = {}

    def op(self, en, fn, reads, writes):
        E = self.E[en]
        self._wait(E, self._deps(reads, writes))
        ins = fn(E.eng)
        E.n += 1
        ins.then_inc(E.sem, 1)
        self.ninst += 1
        self._mark((E.name, E.n), reads, writes)

    def dma(self, q, out, in_, reads, writes, **kw):
        Q = self.E[q]
        toks = self._deps(reads, writes)
        pool = self.dpool[q]
        i = self.dnext[q]
        self.dnext[q] = (i + 1) % len(pool)
        sem, cnt, key = pool[i]
        if cnt > 0:
            toks.append((key, cnt, False))
        self._wait(Q, toks)
        Q.eng.dma_start(out=out, in_=in_, **kw).then_inc(sem, 16)
        pool[i][1] = cnt + 16
        self.ninst += 1
        self._mark((key, cnt + 16), reads, writes)

    def barrier(self):
        toks = [(n, e.n, True) for n, e in self.E.items() if e.n > 0]
        for q, pool in self.dpool.items():
            for sem, cnt, key in pool:
                if cnt > 0:
                    toks.append((key, cnt, False))
        for E in self.E.values():
            self._wait(E, toks)

    def mm(self, out, lhsT, rhs, reads, writes, start=True, stop=True):
        self.op("pe", lambda e: e.matmul(out, lhsT, rhs, start=start, stop=stop), reads, writes)

    def tr(self, out, in_, ident, reads, writes):
        self.op("pe", lambda e: e.transpose(out, in_, ident), reads, writes)

    def act(self, out, in_, func, reads, writes, bias=None, scale=None, en="act"):
        kw = {}
        if bias is not None:
            kw["bias"] = bias
        if scale is not None:
            kw["scale"] = scale
        self.op(en, lambda e: e.activation(out, in_, func, **kw), reads, writes)

    def tt(self, en, out, in0, in1, op, reads, writes):
        self.op(en, lambda e: e.tensor_tensor(out, in0, in1, op), reads, writes)

    def ts(self, en, out, in0, s1, s2, op0, op1, reads, writes):
        if op1 is None:
            self.op(en, lambda e: e.tensor_scalar(out, in0, s1, None, op0), reads, writes)
        else:
            self.op(en, lambda e: e.tensor_scalar(out, in0, s1, s2, op0, op1), reads, writes)

    def stt(self, out, in0, scalar, in1, op0, op1, reads, writes):
        self.op("dve", lambda e: e.scalar_tensor_tensor(out, in0, scalar, in1, op0, op1), reads, writes)

    def copy(self, en, out, in_, reads, writes):
        if en == "act":
            self.op(en, lambda e: e.activation(out, in_, AF.Copy), reads, writes)
        else:
            self.op(en, lambda e: e.tensor_copy(out, in_), reads, writes)

    def memset(self, en, ap, val, writes):
        self.op(en, lambda e: e.memset(ap, val), [], writes)


class _SqAlias:
    def __init__(self, base):
        self.base = base

    @property
    def w(self):
        return self.base.w

    @property
    def rs(self):
        return self.base.rs

    @rs.setter
    def rs(self, v):
        self.base.rs = v

    def __getitem__(self, idx):
        if idx == slice(None):
            return self.base.t[:, 0:8, :]
        return self.base.t[idx]


class Builder:
    def __init__(self, T, L, dbg=()):
        self.T = T
        self.L = L
        self.NG = T // 512
        self.dbg = set(dbg)
        self.nc = bass.Bass("TRN2", target_bir_lowering=False)
        self.outs = []

    def dram_in(self, name, shape, dt=F32):
        return Tile(self.nc.dram_tensor(name, list(shape), dt, kind="ExternalInput"), name)

    def dram(self, name, shape, dt):
        kind = "Internal"
        if name in self.dbg or name == "yT":
            kind = "ExternalOutput"
            self.outs.append(name)
        return Tile(self.nc.dram_tensor(name, list(shape), dt, kind=kind), name)

    def sb(self, es, name, shape, dt):
        self.uid = getattr(self, "uid", 0) + 1
        name = "%s_%d" % (name, self.uid)
        return Tile(es.enter_context(self.nc.sbuf_tensor(name, list(shape), dt)), name)

    def build(self):
        nc, T, L = self.nc, self.T, self.L
        self.inp = {}
        I = self.inp
        I["xT"] = self.dram_in("xT", [D, T])
        I["memT"] = self.dram_in("memT", [D, MEMT])
        I["w_in"] = self.dram_in("w_in", [L, D, OFF["END"]])
        I["cv"] = self.dram_in("cv", [L, 128, NCV])
        I["w_oa"] = self.dram_in("w_oa", [L, 512, D])
        I["w_ob"] = self.dram_in("w_ob", [L, 512, D])
        I["w_oc"] = self.dram_in("w_oc", [L, 512, D])
        I["w_out"] = self.dram_in("w_out", [L, D, D])
        I["w_mq"] = self.dram_in("w_mq", [L, D, 512])
        I["w_mkv"] = self.dram_in("w_mkv", [L, D, 1024])
        I["w_mo"] = self.dram_in("w_mo", [L, 512, D])
        I["w_up"] = self.dram_in("w_up", [L, D, 2 * DFF])
        I["w_down"] = self.dram_in("w_down", [L, DFF, D])
        I["cmask"] = self.dram_in("cmask", [128, 9, 128])
        self.X = [self.dram("xres0", [D, T], F32), self.dram("xres1", [D, T], F32)]
        self.yT = self.dram("yT", [D, T], F32)
        self.HT = self.dram("HT", [D, T], BF16)
        self.QK = {n: self.dram(n, [512, T], BF16) for n in
                   ("FQ", "FK", "SQ", "SK", "GQ", "GK", "GV", "GZ")}
        self.VT = {n: self.dram(n, [T, 512], BF16) for n in ("FV", "SV")}
        self.ROW = {n: self.dram(n, [4, T], F32) for n in ("CROW", "BETA", "GLOG", "EGC")}
        self.GATES = self.dram("GATES", [3072, T], BF16)
        self.Y = {n: self.dram(n, [512, T], BF16) for n in ("YA", "YB", "YC")}
        with ExitStack() as es:
            self.es = es
            self.S = Sched(nc, es)
            self.consts(es)
            for l in range(L):
                self.layer(l)
            self.finish()
        return nc

    def consts(self, es):
        S = self.S
        self.PS = [Tile(es.enter_context(self.nc.psum_tensor("ps%d" % i, [128, 512], F32)), "ps%d" % i, psum=True)
                   for i in range(8)]
        self.ones_bf = self.sb(es, "ones_bf", [128, 128], BF16)
        S.memset("dve", self.ones_bf[:], 1.0, [self.ones_bf])
        self.cm32 = self.sb(es, "cm32", [128, 9, 128], F32)
        S.dma("sp", self.cm32[:], self.inp["cmask"][:, :, :], [self.inp["cmask"]], [self.cm32])
        self.cmbf = self.sb(es, "cmbf", [128, 9, 128], BF16)
        S.copy("dve", self.cmbf[:], self.cm32[:], [self.cm32], [self.cmbf])
        self.cvt = self.sb(es, "cvt", [128, NCV], F32)
        self.dvt = self.sb(es, "dvt", [128, 4], F32)

    def finish(self):
        S = self.S
        for n in self.dbg:
            if n.startswith("nops"):
                for i in range(int(n[4:])):
                    S.dma("sp", self.cvt[:], self.inp["cv"][0, :, :], [self.inp["cv"]], [self.cvt])
        S.barrier()

    def layer(self, l):
        S = self.S
        S.barrier()
        S.dma("sp", self.cvt[:], self.inp["cv"][l, :, :], [self.inp["cv"]], [self.cvt])
        cs = CV["small"]
        S.ts("dve", self.dvt[:, 0:1], self.cvt[:, cs:cs + 1], -1.0, None, ALU.mult, None, [self.cvt], [self.dvt])
        S.act(self.dvt[:, 1:2], self.cvt[:, cs + 2:cs + 3], AF.Exp, [self.cvt], [self.dvt])
        S.ts("dve", self.dvt[:, 1:2], self.dvt[:, 1:2], -1.0, None, ALU.mult, None, [self.dvt], [self.dvt])
        xin = self.inp["xT"] if l == 0 else self.X[1]
        self.pass_a(l, xin)
        nh = 2 if "nh2" in self.dbg else (1 if "nh1" in self.dbg else NH)
        if "nofox" not in self.dbg:
            for h in range(nh):
                self.fox_head(h)
        if "nosb" not in self.dbg:
            for h in range(nh):
                self.sb_head(h)
        self.pass_b(l)
        if "nogdn" not in self.dbg:
            for h in range(nh):
                self.gdn_head(h)
        self.pass_c(l, xin, self.X[0])
        self.pass_d(l, self.X[0], self.yT if l == self.L - 1 else self.X[1])

    def load_w(self, stg, W, wcol, src, row0, c0, ncols, gain_col=None, kchunks=8, flip=[0]):
        S = self.S
        for k in range(kchunks):
            done = 0
            while done < ncols:
                n = min(1024, ncols - done)
                st = stg[flip[0] % len(stg)]
                flip[0] += 1
                S.dma("sp", st[:, 0:n], src[row0 + k * 128: row0 + (k + 1) * 128, c0 + done: c0 + done + n],
                      [], [st])
                dst = W[:, k, wcol + done: wcol + done + n]
                if gain_col is None:
                    en = "dve" if flip[0] % 2 else "pool"
                    S.copy(en, dst, st[:, 0:n], [st], [W])
                else:
                    g = self.cvt[:, gain_col + k: gain_col + k + 1]
                    if flip[0] % 2:
                        S.ts("dve", dst, st[:, 0:n], g, None, ALU.mult, None, [st, self.cvt], [W])
                    else:
                        S.act(dst, st[:, 0:n], AF.Copy, [st, self.cvt], [W], scale=g)
                done += n

    def norm_group(self, xt, hT, sq, lnv, rstd, ps):
        S = self.S
        S.act(sq[:], xt[:], AF.Square, [xt], [sq])
        for k in range(8):
            S.mm(ps[:], self.ones_bf[:], sq[:, k, :], [self.ones_bf, sq], [ps], start=(k == 0), stop=(k == 7))
        S.act(lnv[:], ps[:], AF.Ln, [ps], [lnv], bias=EPS, scale=1.0 / D)
        S.act(rstd[:], lnv[:], AF.Exp, [lnv], [rstd], scale=-0.5)
        for k in range(8):
            S.tt("dve" if k % 2 == 0 else "pool", hT[:, k, :], xt[:, k, :], rstd[:], ALU.mult, [xt, rstd], [hT])

    def pass_a(self, l, xin):
        S, T, NG = self.S, self.T, self.NG
        w_in = self.inp["w_in"]
        with ExitStack() as es:
            NWA = 2048 + 1024
            W = self.sb(es, "WA", [128, 8, NWA], BF16)
            Wsm = self.sb(es, "WAsm", [128, 8, 96], BF16)
            stg = [self.sb(es, "stgA%d" % i, [128, 1024], F32) for i in range(4)]
            S.memset("pool", Wsm[:], 0.0, [Wsm])
            gm = CV["norm_mix"]
            wl = w_in.t[l]
            for (name, wc) in (("FQ", 0), ("FK", 512), ("SQ", 1024), ("SK", 1536), ("FV", 2048), ("SV", 2560)):
                self.load_w(stg, W, wc, wl, 0, OFF[name], 512, gain_col=gm)
            for (name, wc) in (("FF", 0), ("GB", 32), ("GA", 64)):
                self.load_w(stg, Wsm, wc, wl, 0, OFF[name], 4, gain_col=gm)
            xts = [self.sb(es, "xtA%d" % i, [128, 8, 512], F32) for i in range(2)]
            hT = self.sb(es, "hTA", [128, 8, 512], BF16)
            sq = self.sb(es, "sqA", [128, 8, 512], BF16)
            lnv = self.sb(es, "lnvA", [128, 512], F32)
            rstd = self.sb(es, "rstdA", [128, 512], F32)
            sq2 = [self.sb(es, "sq2A%d" % i, [128, 512], BF16) for i in range(2)]
            ln2 = [self.sb(es, "ln2A%d" % i, [128, 512], F32) for i in range(2)]
            ob = [self.sb(es, "obA%d" % i, [128, 512], BF16) for i in range(4)]
            sm = {n: self.sb(es, "smA_" + n, [128, 512], F32) for n in ("e1", "l1", "c", "beta", "glog", "gc", "egc")}
            ones4 = self.sb(es, "ones4", [128, 512], F32)
            rmask = self.sb(es, "rmask", [128, 512], F32)
            S.memset("pool", ones4[:], 1.0, [ones4])
            S.memset("pool", rmask[:], 1.0, [rmask])
            for c in range(8):
                S.memset("pool", rmask[:, c * 64: c * 64 + 1], 0.0, [rmask])
            ccarry = self.sb(es, "ccarry", [128, 1], F32)
            S.memset("pool", ccarry[:], 0.0, [ccarry])
            xv = xin.t.rearrange("(k p) t -> p k t", p=128)
            hv = self.HT.t.rearrange("(k p) t -> p k t", p=128)
            cvs = CV["small"]
            nob = 0
            pi = 0
            S.dma("sp", xts[0][:], xv[:, :, 0:512], [xin], [xts[0]])
            for g in range(NG):
                ts_ = slice(g * 512, (g + 1) * 512)
                xt = xts[g % 2]
                if g + 1 < NG:
                    S.dma("sp", xts[(g + 1) % 2][:], xv[:, :, (g + 1) * 512:(g + 2) * 512], [xin], [xts[(g + 1) % 2]])
                self.norm_group(xt, hT, sq, lnv, rstd, self.PS[0])
                S.dma("pool", hv[:, :, ts_], hT[:], [hT], [self.HT])
                for fam, wc, kind in (("FQ", 0, "nq"), ("FK", 512, "nk"), ("SQ", 1024, "s"), ("SK", 1536, "c")):
                    for h in range(4):
                        ps = self.PS[1 + pi % 3]
                        pi += 1
                        for k in range(8):
                            S.mm(ps[:], W[:, k, wc + h * 128: wc + (h + 1) * 128], hT[:, k, :], [W, hT], [ps],
                                 start=(k == 0), stop=(k == 7))
                        o = ob[nob % 4]
                        nob += 1
                        if kind in ("nq", "nk"):
                            s2 = sq2[nob % 2]
                            l2 = ln2[nob % 2]
                            ps2 = self.PS[4 + nob % 2]
                            S.act(s2[:], ps[:], AF.Square, [ps], [s2])
                            S.mm(ps2[:], self.ones_bf[:], s2[:], [self.ones_bf, s2], [ps2])
                            S.act(l2[:], ps2[:], AF.Ln, [ps2], [l2], bias=EPS, scale=1.0 / DH)
                            S.act(l2[:], l2[:], AF.Exp, [l2], [l2], scale=-0.5)
                            gcol = CV["fox_qnorm"] if kind == "nq" else CV["fox_knorm"]
                            S.stt(o[:], ps[:], self.cvt[:, gcol:gcol + 1], l2[:], ALU.mult, ALU.mult,
                                  [ps, self.cvt, l2], [o])
                            if kind == "nq":
                                S.ts("pool", o[:], o[:], SCALE, 1.0, ALU.mult, ALU.mult, [o], [o])
                        elif kind == "s":
                            S.act(o[:], ps[:], AF.Copy, [ps], [o], scale=SCALE)
                        else:
                            S.copy("dve", o[:], ps[:], [ps], [o])
                        dst = self.QK[fam]
                        S.dma("pool", dst.t[h * 128:(h + 1) * 128, ts_], o[:], [o], [dst])
                for fam, wc in (("FV", 2048), ("SV", 2560)):
                    for sub in range(4):
                        ps = self.PS[1 + pi % 3]
                        pi += 1
                        for k in range(8):
                            S.mm(ps[:], hT[:, k, sub * 128:(sub + 1) * 128], W[:, k, wc: wc + 512], [W, hT], [ps],
                                 start=(k == 0), stop=(k == 7))
                        o = ob[nob % 4]
                        nob += 1
                        S.copy("dve" if sub % 2 else "act", o[:], ps[:], [ps], [o])
                        dst = self.VT[fam]
                        r0 = g * 512 + sub * 128
                        S.dma("pool", dst.t[r0:r0 + 128, :], o[:], [o], [dst])
                ps = self.PS[6]
                for k in range(8):
                    S.mm(ps[0:96, :], Wsm[:, k, :], hT[:, k, :], [Wsm, hT], [ps], start=(k == 0), stop=(k == 7))
                cv = self.cvt
                S.act(sm["e1"][0:4, :], ps[0:4, :], AF.Exp, [ps, self.dvt], [sm["e1"]], bias=self.dvt[0:4, 0:1], scale=-1.0)
                S.act(sm["l1"][0:4, :], sm["e1"][0:4, :], AF.Ln, [sm["e1"]], [sm["l1"]], bias=1.0)
                S.op("dve", lambda e: e.tensor_tensor_scan(sm["c"][0:4, :], ones4[0:4, :], sm["l1"][0:4, :],
                                                           ccarry[0:4, 0:1], ALU.mult, ALU.subtract),
                     [ones4, sm["l1"], ccarry], [sm["c"]])
                S.copy("dve", ccarry[0:4, :], sm["c"][0:4, 511:512], [sm["c"]], [ccarry])
                S.dma("pool", self.ROW["CROW"].t[:, ts_], sm["c"][0:4, :], [sm["c"]], [self.ROW["CROW"]])
                S.act(sm["beta"][32:36, :], ps[32:36, :], AF.Sigmoid, [ps], [sm["beta"]])
                S.dma("pool", self.ROW["BETA"].t[:, ts_], sm["beta"][32:36, :], [sm["beta"]], [self.ROW["BETA"]])
                S.act(sm["e1"][64:68, :], ps[64:68, :], AF.Exp, [ps, cv], [sm["e1"]], bias=cv[64:68, cvs + 1:cvs + 2])
                S.act(sm["l1"][64:68, :], sm["e1"][64:68, :], AF.Ln, [sm["e1"]], [sm["l1"]], bias=1.0)
                S.ts("dve", sm["glog"][64:68, :], sm["l1"][64:68, :], self.dvt[64:68, 1:2], None, ALU.mult, None,
                     [sm["l1"], self.dvt], [sm["glog"]])
                S.op("dve", lambda e: e.tensor_tensor_scan(sm["gc"][64:68, :], rmask[64:68, :], sm["glog"][64:68, :],
                                                           0.0, ALU.mult, ALU.add),
                     [rmask, sm["glog"]], [sm["gc"]])
                S.act(sm["egc"][64:68, :], sm["gc"][64:68, :], AF.Exp, [sm["gc"]], [sm["egc"]])
                S.dma("pool", self.ROW["GLOG"].t[:, ts_], sm["glog"][64:68, :], [sm["glog"]], [self.ROW["GLOG"]])
                S.dma("pool", self.ROW["EGC"].t[:, ts_], sm["egc"][64:68, :], [sm["egc"]], [self.ROW["EGC"]])
            S.barrier()


    def pass_b(self, l):
        S, T, NG = self.S, self.T, self.NG
        w_in = self.inp["w_in"]
        PS = self.PS
        cv = self.cvt
        with ExitStack() as es:
            W = self.sb(es, "WB", [128, 8, 5120], BF16)
            stg = [self.sb(es, "stgB%d" % i, [128, 1024], F32) for i in range(4)]
            gm = CV["norm_mix"]
            wl = w_in.t[l]
            self.load_w(stg, W, 0, wl, 0, OFF["GQ"], 1536, gain_col=gm)
            self.load_w(stg, W, 1536, wl, 0, OFF["GZ"], 512, gain_col=gm)
            self.load_w(stg, W, 2048, wl, 0, OFF["GT"], 3072, gain_col=gm)
            hT = [self.sb(es, "hTB%d" % i, [128, 8, 512], BF16) for i in range(2)]
            halo = self.sb(es, "haloB", [128, 12, 4], F32)
            buf = [self.sb(es, "bufB%d" % i, [128, 516], F32) for i in range(2)]
            acc = [self.sb(es, "accB%d" % i, [128, 512], F32) for i in range(2)]
            sil = [self.sb(es, "silB%d" % i, [128, 512], F32) for i in range(2)]
            s2 = [self.sb(es, "sq2B%d" % i, [128, 512], BF16) for i in range(2)]
            l2 = [self.sb(es, "ln2B%d" % i, [128, 512], F32) for i in range(2)]
            ob = [self.sb(es, "obB%d" % i, [128, 512], BF16) for i in range(4)]
            S.memset("pool", halo[:], 0.0, [halo])
            hv = self.HT.t.rearrange("(k p) t -> p k t", p=128)
            pi = 0
            nob = 0
            cw = CV["gdn_conv"]
            for g in range(NG):
                ts_ = slice(g * 512, (g + 1) * 512)
                h_ = hT[g % 2]
                S.dma("sp", h_[:], hv[:, :, ts_], [self.HT], [h_])
                for c in range(12):
                    ps = PS[pi % 3]
                    pi += 1
                    for k in range(8):
                        S.mm(ps[:], W[:, k, c * 128:(c + 1) * 128], h_[:, k, :], [W, h_], [ps], start=(k == 0), stop=(k == 7))
                    b = buf[c % 2]
                    a = acc[c % 2]
                    sl = sil[c % 2]
                    S.copy("pool", b[:, 0:3], halo[:, c, 0:3], [halo], [b])
                    S.copy("act", b[:, 3:515], ps[:], [ps], [b])
                    S.copy("pool", halo[:, c, 0:3], b[:, 512:515], [b], [halo])
                    S.ts("dve", a[:], b[:, 0:512], cv[:, cw + c:cw + c + 1], None, ALU.mult, None, [b, cv], [a])
                    for tap in (1, 2, 3):
                        S.stt(a[:], b[:, tap:tap + 512], cv[:, cw + tap * 12 + c: cw + tap * 12 + c + 1], a[:],
                              ALU.mult, ALU.add, [b, cv, a], [a])
                    o = ob[nob % 4]
                    nob += 1
                    fam = ("GQ", "GK", "GV")[c // 4]
                    hh = c % 4
                    if fam == "GV":
                        S.act(o[:], a[:], AF.Silu, [a], [o])
                    else:
                        S.act(sl[:], a[:], AF.Silu, [a], [sl])
                        q2 = s2[c % 2]
                        ll = l2[c % 2]
                        ps2 = PS[3 + c % 2]
                        S.act(q2[:], sl[:], AF.Square, [sl], [q2])
                        S.mm(ps2[:], self.ones_bf[:], q2[:], [self.ones_bf, q2], [ps2])
                        S.act(ll[:], ps2[:], AF.Ln, [ps2], [ll], bias=EPS)
                        S.act(ll[:], ll[:], AF.Exp, [ll], [ll], scale=-0.5)
                        if fam == "GQ":
                            S.stt(o[:], sl[:], SCALE, ll[:], ALU.mult, ALU.mult, [sl, ll], [o])
                        else:
                            S.tt("dve", o[:], sl[:], ll[:], ALU.mult, [sl, ll], [o])
                    S.dma("pool", self.QK[fam].t[hh * 128:(hh + 1) * 128, ts_], o[:], [o], [self.QK[fam]])
                for c in range(4):
                    ps = PS[pi % 3]
                    pi += 1
                    for k in range(8):
                        S.mm(ps[:], W[:, k, 1536 + c * 128:1536 + (c + 1) * 128], h_[:, k, :], [W, h_], [ps],
                             start=(k == 0), stop=(k == 7))
                    o = ob[nob % 4]
                    nob += 1
                    S.act(o[:], ps[:], AF.Silu, [ps], [o])
                    S.dma("pool", self.QK["GZ"].t[c * 128:(c + 1) * 128, ts_], o[:], [o], [self.QK["GZ"]])
                gb = CV["gate_bias"]
                for c in range(24):
                    ps = PS[pi % 3]
                    pi += 1
                    for k in range(8):
                        S.mm(ps[:], W[:, k, 2048 + c * 128:2048 + (c + 1) * 128], h_[:, k, :], [W, h_], [ps],
                             start=(k == 0), stop=(k == 7))
                    o = ob[nob % 4]
                    nob += 1
                    S.act(o[:], ps[:], AF.Sigmoid, [ps, cv], [o], bias=cv[:, gb + c:gb + c + 1])
                    S.dma("pool", self.GATES.t[c * 128:(c + 1) * 128, ts_], o[:], [o], [self.GATES])
            S.barrier()

    def head_norm(self, ps, ncols, gcol, out, sq, ll, ps2, post_scale=None):
        S = self.S
        S.act(sq[:, 0:ncols], ps[:, 0:ncols], AF.Square, [ps], [sq])
        S.mm(ps2[:, 0:ncols], self.ones_bf[:], sq[:, 0:ncols], [self.ones_bf, sq], [ps2])
        S.act(ll[:, 0:ncols], ps2[:, 0:ncols], AF.Ln, [ps2], [ll], bias=EPS, scale=1.0 / DH)
        S.act(ll[:, 0:ncols], ll[:, 0:ncols], AF.Exp, [ll], [ll], scale=-0.5)
        S.stt(out, ps[:, 0:ncols], self.cvt[:, gcol:gcol + 1], ll[:, 0:ncols], ALU.mult, ALU.mult,
              [ps, self.cvt, ll], [out.tile] if hasattr(out, "tile") else [])

    def pass_c(self, l, xin, xout):
        S, T, NG = self.S, self.T, self.NG
        I = self.inp
        PS = self.PS
        cv = self.cvt
        with ExitStack() as es:
            Wo = [self.sb(es, "WCo%d" % i, [128, 4, D], BF16) for i in range(3)]
            Wout = self.sb(es, "WCout", [128, 8, D], BF16)
            Wmq = self.sb(es, "WCmq", [128, 8, 512], BF16)
            Wmo = self.sb(es, "WCmo", [128, 4, D], BF16)
            Wkv = self.sb(es, "WCkv", [128, 8, D], BF16)
            stg = [self.sb(es, "stgC%d" % i, [128, 1024], F32) for i in range(2)]
            for i, n in enumerate(("w_oa", "w_ob", "w_oc")):
                self.load_w(stg, Wo[i], 0, I[n].t[l], 0, 0, D, kchunks=4)
            self.load_w(stg, Wout, 0, I["w_out"].t[l], 0, 0, D)
            self.load_w(stg, Wmq, 0, I["w_mq"].t[l], 0, 0, 512, gain_col=CV["norm_xq"])
            self.load_w(stg, Wmo, 0, I["w_mo"].t[l], 0, 0, D, kchunks=4)
            self.load_w(stg, Wkv, 0, I["w_mkv"].t[l], 0, 0, D, gain_col=CV["norm_mem"])
            xts = [self.sb(es, "xtC%d" % i, [128, 8, 512], F32) for i in range(2)]
            xt = xts[1]
            hT = self.sb(es, "hTC", [128, 8, 512], BF16)
            sq = self.sb(es, "sqC", [128, 8, 512], BF16)
            lnv = self.sb(es, "lnvC", [128, 512], F32)
            rstd = self.sb(es, "rstdC", [128, 512], F32)
            yb_ = [self.sb(es, "yC%d" % i, [128, 4, 512], BF16) for i in range(3)]
            gt = self.sb(es, "gtC", [128, 24, 512], BF16)
            tA = self.sb(es, "tAC", [128, 512], F32)
            tB = self.sb(es, "tBC", [128, 512], F32)
            mix = self.sb(es, "mixC", [128, 8, 512], BF16)
            om = self.sb(es, "omC", [128, 4, 512], BF16)
            sq2 = self.sb(es, "sq2C", [128, 512], BF16)
            ll = self.sb(es, "llC", [128, 512], F32)
            qn = self.sb(es, "qnC", [128, 512], BF16)
            pt = [self.sb(es, "ptC%d" % i, [128, 512], BF16) for i in range(2)]
            Km = self.sb(es, "KmC", [128, 4, MEMT], BF16)
            Vm = self.sb(es, "VmC", [128, 2, 512], BF16)
            mv = I["memT"].t.rearrange("(k p) t -> p k t", p=128)
            S.dma("sp", xt[:, :, 0:MEMT], mv, [I["memT"]], [xt])
            S.act(sq[:, :, 0:MEMT], xt[:, :, 0:MEMT], AF.Square, [xt], [sq])
            for k in range(8):
                S.mm(PS[0][:, 0:MEMT], self.ones_bf[:], sq[:, k, 0:MEMT], [self.ones_bf, sq], [PS[0]], start=(k == 0), stop=(k == 7))
            S.act(lnv[:, 0:MEMT], PS[0][:, 0:MEMT], AF.Ln, [PS[0]], [lnv], bias=EPS, scale=1.0 / D)
            S.act(rstd[:, 0:MEMT], lnv[:, 0:MEMT], AF.Exp, [lnv], [rstd], scale=-0.5)
            for k in range(8):
                S.tt("dve", hT[:, k, 0:MEMT], xt[:, k, 0:MEMT], rstd[:, 0:MEMT], ALU.mult, [xt, rstd], [hT])
            for h in range(4):
                ps = PS[1 + h % 2]
                for k in range(8):
                    S.mm(ps[:, 0:MEMT], Wkv[:, k, h * 128:(h + 1) * 128], hT[:, k, 0:MEMT], [Wkv, hT], [ps], start=(k == 0), stop=(k == 7))
                S.act(sq2[:, 0:MEMT], ps[:, 0:MEMT], AF.Square, [ps], [sq2])
                S.mm(PS[3][:, 0:MEMT], self.ones_bf[:], sq2[:, 0:MEMT], [self.ones_bf, sq2], [PS[3]])
                S.act(ll[:, 0:MEMT], PS[3][:, 0:MEMT], AF.Ln, [PS[3]], [ll], bias=EPS, scale=1.0 / DH)
                S.act(ll[:, 0:MEMT], ll[:, 0:MEMT], AF.Exp, [ll], [ll], scale=-0.5)
                gk = CV["mk_norm"]
                S.stt(Km[:, h, :], ps[:, 0:MEMT], cv[:, gk:gk + 1], ll[:, 0:MEMT], ALU.mult, ALU.mult, [ps, cv, ll], [Km])
            for blk in range(2):
                ps = PS[1 + blk % 2]
                for k in range(8):
                    S.mm(ps[:], hT[:, k, blk * 128:(blk + 1) * 128], Wkv[:, k, 512:1024], [Wkv, hT], [ps], start=(k == 0), stop=(k == 7))
                S.copy("act", Vm[:, blk, :], ps[:], [ps], [Vm])
            xv = xin.t.rearrange("(k p) t -> p k t", p=128)
            xo = xout.t.rearrange("(k p) t -> p k t", p=128)
            yv = [self.Y[n].t.rearrange("(h p) t -> p h t", p=128) for n in ("YA", "YB", "YC")]
            gv = self.GATES.t.rearrange("(c p) t -> p c t", p=128)
            pi = 0
            S.dma("sp", xts[0][:], xv[:, :, 0:512], [xin], [xts[0]])
            for g in range(NG):
                ts_ = slice(g * 512, (g + 1) * 512)
                xt = xts[g % 2]
                if g + 1 < NG:
                    S.dma("sp", xts[(g + 1) % 2][:], xv[:, :, (g + 1) * 512:(g + 2) * 512], [xin], [xts[(g + 1) % 2]])
                for i, n in enumerate(("YA", "YB", "YC")):
                    S.dma("sp", yb_[i][:], yv[i][:, :, ts_], [self.Y[n]], [yb_[i]])
                S.dma("sp", gt[:], gv[:, :, ts_], [self.GATES], [gt])
                for oc in range(8):
                    ocs = slice(oc * 128, (oc + 1) * 128)
                    for br in range(3):
                        ps = PS[pi % 3]
                        pi += 1
                        for hh in range(4):
                            S.mm(ps[:], Wo[br][:, hh, ocs], yb_[br][:, hh, :], [Wo[br], yb_[br]], [ps], start=(hh == 0), stop=(hh == 3))
                        if br == 0:
                            S.tt("dve", tA[:], ps[:], gt[:, oc, :], ALU.mult, [ps, gt], [tA])
                        else:
                            S.tt("dve", tB[:], ps[:], gt[:, br * 8 + oc, :], ALU.mult, [ps, gt], [tB])
                            if br == 1:
                                S.tt("pool", tA[:], tA[:], tB[:], ALU.add, [tA, tB], [tA])
                            else:
                                S.tt("pool", mix[:, oc, :], tA[:], tB[:], ALU.add, [tA, tB], [mix])
                for oc in range(8):
                    ocs = slice(oc * 128, (oc + 1) * 128)
                    ps = PS[pi % 3]
                    pi += 1
                    for k in range(8):
                        S.mm(ps[:], Wout[:, k, ocs], mix[:, k, :], [Wout, mix], [ps], start=(k == 0), stop=(k == 7))
                    S.tt("dve", xt[:, oc, :], xt[:, oc, :], ps[:], ALU.add, [xt, ps], [xt])
                self.norm_group(xt, hT, sq, lnv, rstd, PS[3])
                for h in range(4):
                    ps = PS[pi % 3]
                    pi += 1
                    for k in range(8):
                        S.mm(ps[:], Wmq[:, k, h * 128:(h + 1) * 128], hT[:, k, :], [Wmq, hT], [ps], start=(k == 0), stop=(k == 7))
                    S.act(sq2[:], ps[:], AF.Square, [ps], [sq2])
                    S.mm(PS[3][:], self.ones_bf[:], sq2[:], [self.ones_bf, sq2], [PS[3]])
                    S.act(ll[:], PS[3][:], AF.Ln, [PS[3]], [ll], bias=EPS, scale=1.0 / DH)
                    S.act(ll[:], ll[:], AF.Exp, [ll], [ll], scale=-0.5)
                    gq = CV["mq_norm"]
                    S.stt(tA[:], ps[:], cv[:, gq:gq + 1], ll[:], ALU.mult, ALU.mult, [ps, cv, ll], [tA])
                    S.ts("pool", qn[:], tA[:], SCALE, 1.0, ALU.mult, ALU.mult, [tA], [qn])
                    O = PS[4 + h % 2]
                    DN = PS[6 + h % 2]
                    for kb in range(2):
                        ps = PS[pi % 3]
                        pi += 1
                        p_ = pt[kb]
                        S.mm(ps[:], Km[:, h, kb * 128:(kb + 1) * 128], qn[:], [Km, qn], [ps])
                        S.act(p_[:], ps[:], AF.Exp, [ps], [p_])
                        S.mm(O[:], Vm[:, kb, h * 128:(h + 1) * 128], p_[:], [Vm, p_], [O], start=(kb == 0), stop=(kb == 1))
                        S.mm(DN[:], self.ones_bf[:], p_[:], [self.ones_bf, p_], [DN], start=(kb == 0), stop=(kb == 1))
                    S.act(ll[:], DN[:], AF.Ln, [DN], [ll])
                    S.act(ll[:], ll[:], AF.Exp, [ll], [ll], scale=-1.0)
                    S.tt("dve", om[:, h, :], O[:], ll[:], ALU.mult, [O, ll], [om])
                for oc in range(8):
                    ocs = slice(oc * 128, (oc + 1) * 128)
                    ps = PS[pi % 3]
                    pi += 1
                    for hh in range(4):
                        S.mm(ps[:], Wmo[:, hh, ocs], om[:, hh, :], [Wmo, om], [ps], start=(hh == 0), stop=(hh == 3))
                    S.tt("dve", xt[:, oc, :], xt[:, oc, :], ps[:], ALU.add, [xt, ps], [xt])
                S.dma("pool", xo[:, :, ts_], xt[:], [xt], [xout])
            S.barrier()

    def pass_d(self, l, xin, xout):
        S, T, NG = self.S, self.T, self.NG
        I = self.inp
        PS = self.PS
        cv = self.cvt
        NJ = DFF // 128
        with ExitStack() as es:
            Wup = self.sb(es, "WDup", [128, 8, 2 * DFF], BF16)
            Wdn = self.sb(es, "WDdn", [128, NJ, D], BF16)
            with ExitStack() as es2:
                stg = [self.sb(es2, "stgD%d" % i, [128, 1024], F32) for i in range(2)]
                self.load_w(stg, Wup, 0, I["w_up"].t[l], 0, 0, 2 * DFF, gain_col=CV["norm_ffn"])
                self.load_w(stg, Wdn, 0, I["w_down"].t[l], 0, 0, D, kchunks=NJ)
                S.barrier()
            xt = self.sb(es, "xtD", [128, 8, 512], F32)
            hT = self.sb(es, "hTD", [128, 8, 512], BF16)
            gT = self.sb(es, "gTD", [128, NJ, 512], BF16)
            lnv = self.sb(es, "lnvD", [128, 512], F32)
            rstd = self.sb(es, "rstdD", [128, 512], F32)
            halo = self.sb(es, "haloD", [128, 2 * NJ, 2], F32)
            buf = [self.sb(es, "bufD%d" % i, [128, 516], F32) for i in range(2)]
            acc = [self.sb(es, "accD%d" % i, [128, 512], F32) for i in range(2)]
            sa = self.sb(es, "saD", [128, 512], F32)
            S.memset("pool", halo[:], 0.0, [halo])
            xv = xin.t.rearrange("(k p) t -> p k t", p=128)
            xo = xout.t.rearrange("(k p) t -> p k t", p=128)
            cw = CV["ffn_conv"]
            cb = CV["ffn_conv_b"]
            pi = 0
            for g in range(NG):
                ts_ = slice(g * 512, (g + 1) * 512)
                S.dma("sp", xt[:], xv[:, :, ts_], [xin], [xt])
                self.norm_group(xt, hT, _SqAlias(gT), lnv, rstd, PS[3])
                for j in range(NJ):
                    for ab in range(2):
                        c = ab * NJ + j
                        ps = PS[pi % 3]
                        pi += 1
                        for k in range(8):
                            S.mm(ps[:], Wup[:, k, c * 128:(c + 1) * 128], hT[:, k, :], [Wup, hT], [ps], start=(k == 0), stop=(k == 7))
                        b = buf[ab]
                        a = acc[ab]
                        S.copy("pool", b[:, 0:2], halo[:, c, 0:2], [halo], [b])
                        S.copy("act", b[:, 2:514], ps[:], [ps], [b])
                        S.copy("pool", halo[:, c, 0:2], b[:, 512:514], [b], [halo])
                        S.ts("dve", a[:], b[:, 0:512], cv[:, cw + c:cw + c + 1], cv[:, cb + c:cb + c + 1], ALU.mult, ALU.add,
                             [b, cv], [a])
                        for tap in (1, 2):
                            S.stt(a[:], b[:, tap:tap + 512], cv[:, cw + tap * 2 * NJ + c: cw + tap * 2 * NJ + c + 1], a[:],
                                  ALU.mult, ALU.add, [b, cv, a], [a])
                    S.act(sa[:], acc[0][:], AF.Silu, [acc[0]], [sa])
                    S.tt("pool", gT[:, j, :], sa[:], acc[1][:], ALU.mult, [sa, acc[1]], [gT])
                for oc in range(8):
                    ocs = slice(oc * 128, (oc + 1) * 128)
                    ps = PS[4 + oc % 2]
                    for j in range(NJ):
                        S.mm(ps[:], Wdn[:, j, ocs], gT[:, j, :], [Wdn, gT], [ps], start=(j == 0), stop=(j == NJ - 1))
                    S.tt("dve", xt[:, oc, :], xt[:, oc, :], ps[:], ALU.add, [xt, ps], [xt])
                S.dma("pool", xo[:, :, ts_], xt[:], [xt], [xout])
            S.barrier()

    def gdn_head(self, h):
        S, T = self.S, self.T
        NP = T // 128
        PS = self.PS
        cm = self.cm32
        ident = cm[:, 0, :]
        with ExitStack() as es:
            Qf = self.sb(es, "gQ", [128, T], BF16)
            Kf = self.sb(es, "gK", [128, T], BF16)
            Vf = self.sb(es, "gV", [128, T], BF16)
            bc = self.sb(es, "gbc", [128, T], F32)
            G2 = self.sb(es, "gG2", [128, 128], F32)
            B2 = self.sb(es, "gB2", [128, 128], F32)
            gc2 = self.sb(es, "ggc2", [128, 128], F32)
            gl2 = self.sb(es, "ggl2", [128, 128], F32)
            rm2 = self.sb(es, "grm2", [128, 128], F32)
            gcol = self.sb(es, "ggcol", [128, NP], F32)
            bcol = self.sb(es, "gbcol", [128, NP], F32)
            eglc = self.sb(es, "geglc", [128, NP], F32)
            Sst = self.sb(es, "gS", [128, 128], F32)
            hs = slice(h * 128, (h + 1) * 128)
            S.dma("sp", Qf[:], self.QK["GQ"].t[hs, :], [self.QK["GQ"]], [Qf])
            S.dma("sp", Kf[:], self.QK["GK"].t[hs, :], [self.QK["GK"]], [Kf])
            S.dma("sp", Vf[:], self.QK["GV"].t[hs, :], [self.QK["GV"]], [Vf])
            R = self.ROW
            S.dma("sp", G2[0:NP, :], R["GLOG"].t[h:h + 1, :].rearrange("o (n p) -> (o n) p", p=128), [R["GLOG"]], [G2])
            S.dma("sp", B2[0:NP, :], R["BETA"].t[h:h + 1, :].rearrange("o (n p) -> (o n) p", p=128), [R["BETA"]], [B2])
            S.dma("sp", bc[:], R["EGC"].t[h:h + 1, :].partition_broadcast(128), [R["EGC"]], [bc])
            S.memset("pool", rm2[:], 1.0, [rm2])
            S.memset("pool", rm2[:, 0:1], 0.0, [rm2])
            S.memset("pool", rm2[:, 64:65], 0.0, [rm2])
            S.op("dve", lambda e: e.tensor_tensor_scan(gc2[0:NP, :], rm2[0:NP, :], G2[0:NP, :], 0.0, ALU.mult, ALU.add),
                 [rm2, G2], [gc2])
            for half in range(2):
                cs = slice(64 * half, 64 * half + 64)
                tot = gc2[0:NP, 64 * half + 63: 64 * half + 64]
                S.ts("dve", gl2[0:NP, cs], gc2[0:NP, cs], tot, -1.0, ALU.subtract, ALU.mult, [gc2], [gl2])
            S.act(gl2[0:NP, :], gl2[0:NP, :], AF.Exp, [gl2], [gl2])
            eg2 = self.sb(es, "geg2", [128, 128], F32)
            egcol = self.sb(es, "gegcol", [128, NP], F32)
            bgc = self.sb(es, "gbgc", [128, NP], F32)
            S.act(eg2[0:NP, :], gc2[0:NP, :], AF.Exp, [gc2], [eg2])
            for src, dst in ((G2, gcol), (B2, bcol), (gl2, eglc), (eg2, egcol)):
                S.tr(PS[7][:, 0:NP], src[0:NP, :], cm[0:NP, 0, 0:NP], [src, cm], [PS[7]])
                S.copy("dve", dst[:], PS[7][:, 0:NP], [PS[7]], [dst])
            S.tt("dve", bgc[:], bcol[:], egcol[:], ALU.mult, [bcol, egcol], [bgc])
            Sst2 = self.sb(es, "gS2", [128, 128], F32)
            SS = [Sst, Sst2]
            S.memset("pool", Sst[:], 0.0, [Sst])

            def f32t(n, k=5):
                return [self.sb(es, n + str(i), [128, 128], F32) for i in range(k)]
            lg = f32t("g_lg"); Dl = f32t("g_Dl"); DTs = f32t("g_DTs"); DTi = f32t("g_DTi")
            Lb = [f32t("g_L%d_" % j) for j in range(2)]
            Ub = [f32t("g_U%d_" % j) for j in range(2)]
            X = f32t("g_X"); At = f32t("g_At")
            K32 = f32t("g_K32"); V32 = f32t("g_V32"); Qg = f32t("g_Qg"); Qp = f32t("g_Qp")
            Kt = f32t("g_Kt")
            Rr = [self.sb(es, "g_R%d" % i_, [128, 256], F32) for i_ in range(5)]
            UW = [self.sb(es, "g_UW%d" % i_, [128, 256], F32) for i_ in range(5)]
            MT = [f32t("g_MT%d_" % j, 5) for j in range(2)]
            Nn = [f32t("g_N%d_" % j, 5) for j in range(2)]
            OT = [self.sb(es, "g_OT%d" % i, [128, 512], F32) for i in range(2)]
            gz = [self.sb(es, "g_gz%d" % i, [128, 512], BF16) for i in range(2)]
            sq = self.sb(es, "g_sq", [128, 512], BF16)
            ll = self.sb(es, "g_ll", [128, 512], F32)
            t32 = self.sb(es, "g_t32", [128, 512], F32)
            ob = [self.sb(es, "g_ob%d" % i, [128, 512], BF16) for i in range(2)]

            def pre(r):
                i = r % 5
                cs = slice(r * 128, (r + 1) * 128)
                PA = PS[r % 4]
                PB = PA
                q = [slice(0, 128), slice(128, 256), slice(256, 384), slice(384, 512)]
                S.ts("pool", lg[i][:], cm[:, 4, :], gcol[:, r:r + 1], 1.0, ALU.mult, ALU.mult, [cm, gcol], [lg[i]])
                yield
                S.mm(PA[:, q[0]], lg[i][:], cm[:, 5, :], [lg[i], cm], [PA], start=True, stop=False)
                S.mm(PA[:, q[0]], ident, cm[:, 6, :], [cm], [PA], start=False, stop=True)
                S.mm(PA[:, q[1]], cm[:, 5, :], lg[i][:], [lg[i], cm], [PA], start=True, stop=False)
                S.mm(PA[:, q[1]], ident, cm[:, 8, :], [cm], [PA], start=False, stop=True)
                yield
                S.act(Dl[i][:], PA[:, q[0]], AF.Exp, [PA], [Dl[i]])
                yield
                S.act(DTi[i][:], PA[:, q[1]], AF.Exp, [PA], [DTi[i]])
                yield
                S.mm(PA[:, q[2]], Kf[:, cs], Kf[:, cs], [Kf], [PA])
                S.mm(PA[:, q[3]], Kf[:, cs], Qf[:, cs], [Qf, Kf], [PA])
                yield
                L = Lb[0][i]
                U = Ub[0][i]
                S.stt(L[:], PA[:, q[2]], bcol[:, r:r + 1], Dl[i][:], ALU.mult, ALU.mult, [PA, bcol, Dl[i]], [L])
                yield
                S.tt("dve", At[i][:], PA[:, q[3]], DTi[i][:], ALU.mult, [PA, DTi[i]], [At[i]])
                yield
                S.tr(PA[:, q[0]], L[:], ident, [L, cm], [PA])
                yield
                S.copy("act", U[:], PA[:, q[0]], [PA], [U])
                yield
                S.tt("pool", X[i][:], ident, U[:], ALU.subtract, [cm, U], [X[i]])
                yield
                S.copy("pool", K32[i][:], Kf[:, cs], [Kf], [K32[i]])
                yield
                S.copy("pool", V32[i][:], Vf[:, cs], [Vf], [V32[i]])
                yield
                S.tr(PA[:, q[1]], K32[i][:], ident, [K32[i], cm], [PA])
                S.tr(PA[:, q[2]], V32[i][:], ident, [V32[i], cm], [PA])
                yield
                S.ts("dve", Kt[i][:], PA[:, q[1]], eglc[:, r:r + 1], None, ALU.mult, None, [PA, eglc], [Kt[i]])
                yield
                S.act(Rr[i][:, 128:256], PA[:, q[1]], AF.Copy, [PA, bgc], [Rr[i]], scale=bgc[:, r:r + 1])
                yield
                S.act(Rr[i][:, 0:128], PA[:, q[2]], AF.Copy, [PA, bcol], [Rr[i]], scale=bcol[:, r:r + 1])
                yield
                S.tt("pool", Qg[i][:], Qf[:, cs], bc[:, cs], ALU.mult, [Qf, bc], [Qg[i]])
                yield
                for k in range(1, 6):
                    Ln_ = Lb[k % 2][i]
                    Un_ = Ub[k % 2][i]
                    S.mm(PA[:, q[0]], U[:], L[:], [U, L], [PA])
                    if k < 5:
                        S.mm(PA[:, q[1]], L[:], U[:], [U, L], [PA])
                    yield
                    S.copy("act", Ln_[:], PA[:, q[0]], [PA], [Ln_])
                    yield
                    if k < 5:
                        S.copy("dve", Un_[:], PA[:, q[1]], [PA], [Un_])
                        yield
                    S.mm(PA[:, q[2]], Ln_[:], X[i][:], [Ln_, X[i]], [PA])
                    yield
                    S.tt("dve", X[i][:], X[i][:], PA[:, q[2]], ALU.add, [X[i], PA], [X[i]])
                    yield
                    L, U = Ln_, Un_
                S.mm(PA[:, 0:256], X[i][:], Rr[i][:], [X[i], Rr[i]], [PA])
                yield
                S.copy("act", UW[i][:], PA[:, 0:256], [PA], [UW[i]])
                yield
                S.mm(PA[:, q[3]], UW[i][:, 128:256], At[i][:], [UW[i], At[i]], [PA])
                yield
                S.tt("dve", Qp[i][:], Qg[i][:], PA[:, q[3]], ALU.subtract, [Qg[i], PA], [Qp[i]])
                yield
                for par in range(2):
                    rows = slice(64 * par, 64 * par + 64)
                    col = r * 128 + 64 * par + 63
                    S.mm(PA[:, q[2]], UW[i][rows, 128:256], Kt[i][rows, :], [UW[i], Kt[i]], [PA])
                    S.mm(PA[:, q[3]], Kt[i][rows, :], UW[i][rows, 0:128], [UW[i], Kt[i]], [PA])
                    yield
                    S.stt(MT[par][i][:], ident, bc[:, col:col + 1], PA[:, q[2]], ALU.mult, ALU.subtract,
                          [cm, bc, PA], [MT[par][i]])
                    yield
                    S.copy("act", Nn[par][i][:], PA[:, q[3]], [PA], [Nn[par][i]])
                    yield

            def scan(r):
                i = r % 5
                ot = OT[(r // 4) % 2]
                for par in range(2):
                    rows = slice(64 * par, 64 * par + 64)
                    c = 2 * r + par
                    Sp = SS[c % 2]
                    Sn = SS[(c + 1) % 2]
                    P5 = PS[5 + c % 2]
                    P7 = PS[7]
                    S.mm(P7[:, 0:64], Sp[:], Qp[i][:, rows], [Sp, Qp[i]], [P7], start=True, stop=False)
                    S.mm(P7[:, 0:64], UW[i][rows, 0:128], At[i][rows, rows], [UW[i], At[i]], [P7], start=False, stop=True)
                    S.mm(P5[:, 0:128], MT[par][i][:], Sp[:], [MT[par][i], Sp], [P5])
                    yield
                    S.tt("dve", Sn[:], P5[:, 0:128], Nn[par][i][:], ALU.add, [P5, Nn[par][i]], [Sn])
                    yield
                    oc = (r % 4) * 128 + 64 * par
                    S.copy("act", ot[:, oc:oc + 64], P7[:, 0:64], [P7], [ot])
                    yield

            def post(blk):
                ot = OT[blk % 2]
                ts_ = slice(blk * 512, (blk + 1) * 512)
                z = gz[blk % 2]
                S.dma("sp", z[:], self.QK["GZ"].t[hs, ts_], [self.QK["GZ"]], [z])
                S.act(sq[:], ot[:], AF.Square, [ot], [sq])
                S.mm(PS[4][:], self.ones_bf[:], sq[:], [self.ones_bf, sq], [PS[4]])
                S.act(ll[:], PS[4][:], AF.Ln, [PS[4]], [ll], bias=EPS, scale=1.0 / DH)
                S.act(ll[:], ll[:], AF.Exp, [ll], [ll], scale=-0.5)
                go = CV["gdn_onorm"]
                S.stt(t32[:], ot[:], self.cvt[:, go:go + 1], ll[:], ALU.mult, ALU.mult, [ot, self.cvt, ll], [t32])
                o = ob[blk % 2]
                S.tt("pool", o[:], t32[:], z[:], ALU.mult, [t32, z], [o])
                S.dma("pool", self.Y["YB"].t[hs, ts_], o[:], [o], [self.Y["YB"]])

            act = {}

            def adv(r, nops):
                if r >= NP:
                    return True
                if r not in act:
                    act[r] = pre(r)
                g_ = act[r]
                if g_ is None:
                    return True
                for _ in range(nops):
                    try:
                        next(g_)
                    except StopIteration:
                        act[r] = None
                        return True
                return False

            while not adv(0, 1000):
                pass
            for r in range(NP):
                gs = scan(r)
                sdone = False
                while True:
                    d1 = adv(r + 1, 2)
                    adv(r + 2, 2)
                    adv(r + 3, 1)
                    adv(r + 4, 1)
                    if not sdone:
                        try:
                            next(gs)
                        except StopIteration:
                            sdone = True
                    if sdone and d1:
                        break
                if r % 4 == 3:
                    post(r // 4)
            S.barrier()

    def fox_head(self, h):
        S, T, NG = self.S, self.T, self.NG
        NB = T // 128
        PS = self.PS
        with ExitStack() as es:
            Qf = self.sb(es, "fxQ", [128, T], BF16)
            Kf = self.sb(es, "fxK", [128, T], BF16)
            Vt = self.sb(es, "fxV", [128, NB, 128], BF16)
            cbc = self.sb(es, "fxcbc", [128, T], F32)
            c2 = self.sb(es, "fxc2", [128, 128], F32)
            ncc = self.sb(es, "fxncc", [128, NB], F32)
            tmp = [self.sb(es, "fxtmp%d" % i, [128, 512], F32) for i in range(2)]
            PT = [self.sb(es, "fxPT%d" % i, [128, 512], BF16) for i in range(3)]
            lnd = self.sb(es, "fxlnd", [128, 512], F32)
            ob = [self.sb(es, "fxob%d" % i, [128, 512], BF16) for i in range(2)]
            hs = slice(h * 128, (h + 1) * 128)
            S.dma("sp", Qf[:], self.QK["FQ"].t[hs, :], [self.QK["FQ"]], [Qf])
            S.dma("sp", Kf[:], self.QK["FK"].t[hs, :], [self.QK["FK"]], [Kf])
            S.dma("sp", Vt[:], self.VT["FV"].t.rearrange("(n p) c -> p n c", p=128)[:, :, hs], [self.VT["FV"]], [Vt])
            crow = self.ROW["CROW"]
            S.dma("sp", cbc[:], crow.t[h:h + 1, :].partition_broadcast(128), [crow], [cbc])
            S.dma("sp", c2[0:NB, :], crow.t[h:h + 1, :].rearrange("o (n p) -> (o n) p", p=128), [crow], [c2])
            S.tr(PS[7][:, 0:NB], c2[0:NB, :], self.cm32[0:NB, 0, 0:NB], [c2, self.cm32], [PS[7]])
            S.ts("dve", ncc[:], PS[7][:, 0:NB], -1.0, None, ALU.mult, None, [PS[7]], [ncc])
            tiles = []
            for g in range(NG):
                tl = [(kb, 0) for kb in range(4 * g)] + [(4 * g + j, 128 * j) for j in range(4)]
                for ti, (kb, c0) in enumerate(tl):
                    tiles.append((g, kb, c0, ti == 0, ti == len(tl) - 1))
            n = len(tiles)

            def emit_S(i):
                g, kb, c0, first, last = tiles[i]
                ps = PS[i % 3]
                S.mm(ps[:, c0:512], Kf[:, kb * 128:(kb + 1) * 128], Qf[:, g * 512 + c0:(g + 1) * 512], [Kf, Qf], [ps])

            def emit_mid(i):
                g, kb, c0, first, last = tiles[i]
                ps = PS[i % 3]
                tm = tmp[i % 2]
                pt = PT[i % 3]
                S.tt("dve", tm[:, c0:512], ps[:, c0:512], cbc[:, g * 512 + c0:(g + 1) * 512], ALU.add, [ps, cbc], [tm])
                S.act(pt[:, c0:512], tm[:, c0:512], AF.Exp, [tm, ncc], [pt], bias=ncc[:, kb:kb + 1])
                if kb >= 4 * g:
                    S.tt("pool", pt[:, c0:c0 + 128], pt[:, c0:c0 + 128], self.cmbf[:, 1, :], ALU.mult,
                         [pt, self.cmbf], [pt])

            def emit_PV(i):
                g, kb, c0, first, last = tiles[i]
                pt = PT[i % 3]
                O = PS[3 + g % 2]
                DN = PS[5 + g % 2]
                S.mm(O[:, c0:512], Vt[:, kb, :], pt[:, c0:512], [Vt, pt], [O], start=first, stop=last)
                S.mm(DN[:, c0:512], self.ones_bf[:], pt[:, c0:512], [self.ones_bf, pt], [DN], start=first, stop=last)
                if last:
                    S.act(lnd[:], DN[:], AF.Ln, [DN], [lnd])
                    S.act(lnd[:], lnd[:], AF.Exp, [lnd], [lnd], scale=-1.0)
                    o = ob[g % 2]
                    S.tt("dve", o[:], O[:], lnd[:], ALU.mult, [O, lnd], [o])
                    S.dma("pool", self.Y["YA"].t[hs, g * 512:(g + 1) * 512], o[:], [o], [self.Y["YA"]])

            LA = 2
            for i in range(min(LA, n)):
                emit_S(i)
            for i in range(n):
                if i + LA < n:
                    emit_S(i + LA)
                emit_mid(i)
                emit_PV(i)
            S.barrier()

    def sb_head(self, h):
        S, T, NG = self.S, self.T, self.NG
        NB = T // 128
        PS = self.PS
        with ExitStack() as es:
            Qf = self.sb(es, "sbQ", [128, T], BF16)
            Kf = self.sb(es, "sbK", [128, T], BF16)
            Vt = self.sb(es, "sbV", [128, NB, 128], BF16)
            ntri = self.sb(es, "sbntri", [128, 128], BF16)
            nones = self.sb(es, "sbnones", [128, 128], BF16)
            zeros = self.sb(es, "sbzeros", [128, 128], BF16)
            E = [self.sb(es, "sbE%d" % i, [128, 512], F32) for i in range(2)]
            SP = [self.sb(es, "sbSP%d" % i, [128, 512], BF16) for i in range(3)]
            AT = [self.sb(es, "sbAT%d" % i, [128, 512], BF16) for i in range(3)]
            cum = [self.sb(es, "sbcum%d" % i, [128, 512], BF16) for i in range(2)]
            ob = [self.sb(es, "sbob%d" % i, [128, 512], BF16) for i in range(2)]
            hs = slice(h * 128, (h + 1) * 128)
            S.dma("sp", Qf[:], self.QK["SQ"].t[hs, :], [self.QK["SQ"]], [Qf])
            S.dma("sp", Kf[:], self.QK["SK"].t[hs, :], [self.QK["SK"]], [Kf])
            S.dma("sp", Vt[:], self.VT["SV"].t.rearrange("(n p) c -> p n c", p=128)[:, :, hs], [self.VT["SV"]], [Vt])
            S.ts("dve", ntri[:], self.cm32[:, 3, :], -1.0, None, ALU.mult, None, [self.cm32], [ntri])
            S.memset("dve", nones[:], -1.0, [nones])
            S.memset("dve", zeros[:], 0.0, [zeros])
            tiles = []
            for g in range(NG):
                tl = [(4 * g + j, 128 * j) for j in (3, 2, 1, 0)] + [(kb, 0) for kb in range(4 * g - 1, -1, -1)]
                for ti, (kb, c0) in enumerate(tl):
                    tiles.append((g, kb, c0, ti, len(tl)))
            n = len(tiles)

            def emit_A(i):
                g, kb, c0, ti, nt = tiles[i]
                A = PS[i % 3]
                S.mm(A[:, c0:512], Kf[:, kb * 128:(kb + 1) * 128], Qf[:, g * 512 + c0:(g + 1) * 512], [Kf, Qf], [A])

            def emit_sp(i):
                g, kb, c0, ti, nt = tiles[i]
                A = PS[i % 3]
                e = E[i % 2]
                sp = SP[i % 3]
                S.act(e[:, c0:512], A[:, c0:512], AF.Exp, [A], [e])
                S.act(sp[:, c0:512], e[:, c0:512], AF.Ln, [e], [sp], bias=1.0)
                if kb >= 4 * g:
                    S.tt("pool", sp[:, c0:c0 + 128], sp[:, c0:c0 + 128], self.cmbf[:, 2, :], ALU.mult,
                         [sp, self.cmbf], [sp])

            def emit_B(i):
                g, kb, c0, ti, nt = tiles[i]
                B = PS[3 + i % 2]
                sp = SP[i % 3]
                cm = cum[g % 2]
                O = PS[5 + g % 2]
                q0, q1 = g * 512 + c0, (g + 1) * 512
                kT = Kf[:, kb * 128:(kb + 1) * 128]
                if ti == 0:
                    S.memset("pool", cm[:], 0.0, [cm])
                    S.mm(O[:], zeros[:], Qf[:, g * 512:(g + 1) * 512], [zeros, Qf], [O], start=True, stop=False)
                S.mm(B[:, c0:512], kT, Qf[:, q0:q1], [Kf, Qf], [B], start=True, stop=False)
                S.mm(B[:, c0:512], ntri[:], sp[:, c0:512], [ntri, sp], [B], start=False, stop=(ti == 0))
                if ti > 0:
                    S.mm(B[:, c0:512], nones[:], cm[:, c0:512], [nones, cm], [B], start=False, stop=True)
                if ti < nt - 1:
                    S.tt("dve", cm[:, c0:512], cm[:, c0:512], sp[:, c0:512], ALU.add, [cm, sp], [cm])

            def emit_at(i):
                g, kb, c0, ti, nt = tiles[i]
                B = PS[3 + i % 2]
                at = AT[i % 3]
                S.act(at[:, c0:512], B[:, c0:512], AF.Exp, [B], [at])
                if kb >= 4 * g:
                    S.tt("pool", at[:, c0:c0 + 128], at[:, c0:c0 + 128], self.cmbf[:, 2, :], ALU.mult,
                         [at, self.cmbf], [at])

            def emit_O(i):
                g, kb, c0, ti, nt = tiles[i]
                at = AT[i % 3]
                O = PS[5 + g % 2]
                S.mm(O[:, c0:512], Vt[:, kb, :], at[:, c0:512], [Vt, at], [O], start=False, stop=(ti == nt - 1))
                if ti == nt - 1:
                    o = ob[g % 2]
                    S.copy("dve", o[:], O[:], [O], [o])
                    S.dma("pool", self.Y["YC"].t[hs, g * 512:(g + 1) * 512], o[:], [o], [self.Y["YC"]])

            emit_A(0)
            if n > 1:
                emit_A(1)
            emit_sp(0)
            for i in range(n):
                if i + 2 < n:
                    emit_A(i + 2)
                if i + 1 < n:
                    emit_sp(i + 1)
                emit_B(i)
                emit_at(i)
                if i >= 1:
                    emit_O(i - 1)
            emit_O(n - 1)
            S.barrier()


def host_consts():
    p = np.arange(128)[:, None]
    f = np.arange(128)[None, :]
    cm = np.zeros((128, 9, 128), np.float32)
    cm[:, 0, :] = (p == f)
    cm[:, 1, :] = (f >= p)
    cm[:, 2, :] = (f > p)
    cm[:, 3, :] = (p >= f)
    same = (p // 64) == (f // 64)
    cm[:, 4, :] = (p <= f) & same
    cm[:, 5, :] = (p > f) & same
    cm[:, 6, :] = np.where((p > f) & same, 0.0, NEG)
    cm[:, 7, :] = np.where((f > p) & same, 0.0, NEG)
    cm[:, 8, :] = np.where((f >= p) & same, 0.0, NEG)
    return cm


def pack_cv(inp, L):
    cv = np.zeros((L, 128, NCV), np.float32)

    def chunks(v):
        return np.ascontiguousarray(v.reshape(-1, 128).T)
    for l in range(L):
        c = cv[l]
        c[:, CV["norm_mix"]:CV["norm_mix"] + 8] = chunks(inp["norm_mix"][l])
        c[:, CV["gate_bias"]:CV["gate_bias"] + 24] = chunks(inp["gate_bias"][l])
        c[:, CV["fox_qnorm"]] = inp["fox_qnorm"][l]
        c[:, CV["fox_knorm"]] = inp["fox_knorm"][l]
        gc = inp["gdn_conv"][l]
        for tap in range(4):
            c[:, CV["gdn_conv"] + tap * 12: CV["gdn_conv"] + (tap + 1) * 12] = chunks(gc[tap])
        c[:, CV["gdn_onorm"]] = inp["gdn_onorm"][l]
        c[:, CV["norm_xq"]:CV["norm_xq"] + 8] = chunks(inp["norm_xq"][l])
        c[:, CV["norm_mem"]:CV["norm_mem"] + 8] = chunks(inp["norm_mem"][l])
        c[:, CV["mq_norm"]] = inp["mq_norm"][l]
        c[:, CV["mk_norm"]] = inp["mk_norm"][l]
        c[:, CV["norm_ffn"]:CV["norm_ffn"] + 8] = chunks(inp["norm_ffn"][l])
        fc = inp["ffn_conv"][l]
        for tap in range(3):
            c[:, CV["ffn_conv"] + tap * 44: CV["ffn_conv"] + (tap + 1) * 44] = chunks(fc[tap])
        c[:, CV["ffn_conv_b"]:CV["ffn_conv_b"] + 44] = chunks(inp["ffn_conv_b"][l])
        s = CV["small"]
        c[0:4, s] = inp["fox_fbias"][l]
        c[64:68, s + 1] = inp["gdn_dt_bias"][l]
        c[64:68, s + 2] = inp["gdn_a_log"][l]
    return cv


def prep_inputs(inp, b, T, L):
    m = {}
    m["xT"] = np.ascontiguousarray(inp["x"][b, :T].T)
    m["memT"] = np.ascontiguousarray(inp["mem"][b].T)
    for n in ("w_in", "w_oa", "w_ob", "w_oc", "w_out", "w_mq", "w_mkv", "w_mo", "w_up", "w_down"):
        m[n] = np.ascontiguousarray(inp[n][:L])
    m["cv"] = pack_cv(inp, L)
    m["cmask"] = host_consts()
    return m


_CACHE = {}


def kernel(**inputs):
    inp = {k: np.asarray(v) for k, v in inputs.items()}
    B, T, _ = inp["x"].shape
    L = inp["w_in"].shape[0]
    key = (T, L)
    nc = Builder(T, L).build()
    ncores = 8
    in_maps = [prep_inputs(inp, c % B, T, L) for c in range(B)]
    in_maps = [in_maps[c % B] for c in range(ncores)]
    res = run_bass_kernel_spmd(nc, in_maps, core_ids=list(range(ncores)))
    out = np.empty((B, T, D), np.float32)
    for b in range(B):
        out[b] = np.asarray(res.results[b]["yT"]).T
    return out
```

```python
import numpy as np
from contextlib import ExitStack
import concourse.bass as bass
import concourse.mybir as mybir
from concourse.bass_utils import run_bass_kernel_spmd

F32 = mybir.dt.float32
BF16 = mybir.dt.bfloat16
ALU = mybir.AluOpType
AF = mybir.ActivationFunctionType

D = 1024
NH = 4
DH = 128
EPS = 1e-6
MEMT = 256
DFF = 2816
OFF = dict(FQ=0, FK=512, FV=1024, FF=1536, GQ=1540, GK=2052, GV=2564, GB=3076, GA=3080,
           GZ=3084, SQ=3596, SK=4108, SV=4620, GT=5132, END=8204)
SCALE = DH ** -0.5
NEG = -30000.0

CV = {}
_o = 0
for _n, _w in [("norm_mix", 8), ("gate_bias", 24), ("fox_qnorm", 1), ("fox_knorm", 1),
               ("gdn_conv", 48), ("gdn_onorm", 1), ("norm_xq", 8), ("norm_mem", 8),
               ("mq_norm", 1), ("mk_norm", 1), ("norm_ffn", 8), ("ffn_conv", 132),
               ("ffn_conv_b", 44), ("small", 3)]:
    CV[_n] = _o
    _o += _w
NCV = _o


class Tile:
    def __init__(self, t, name, psum=False):
        self.t = t
        self.name = name
        self.psum = psum
        self.w = {}
        self.rs = {}

    def __getitem__(self, idx):
        return self.t[idx]


class Trk:
    def __init__(self):
        self.w = {}
        self.rs = {}


class Eng:
    def __init__(self, name, eng, sem):
        self.name = name
        self.eng = eng
        self.sem = sem
        self.n = 0
        self.seen = {}


class Sched:
    def __init__(self, nc, es, ndsem=12):
        self.nc = nc
        self.es = es
        self.sems = {}
        self.E = {}
        for name, eng in [("pe", nc.tensor), ("act", nc.scalar), ("dve", nc.vector),
                          ("pool", nc.gpsimd), ("sp", nc.sync)]:
            sem = es.enter_context(nc.semaphore("sem_" + name))
            self.E[name] = Eng(name, eng, sem)
            self.sems[name] = sem
        self.dpool = {}
        self.dnext = {}
        for q in ("sp", "pool", "act"):
            lst = []
            for i in range(ndsem):
                key = "d_%s_%d" % (q, i)
                sem = es.enter_context(nc.semaphore(key))
                self.sems[key] = sem
                lst.append([sem, 0, key])
            self.dpool[q] = lst
            self.dnext[q] = 0
        self.ninst = 0

    def _deps(self, reads, writes):
        toks = []
        for r in reads:
            for k, v in r.w.items():
                toks.append((k, v, True))
            if getattr(r, "psum", False):
                for k, v in r.rs.items():
                    toks.append((k, v, False))
        for w in writes:
            for k, v in w.w.items():
                toks.append((k, v, False))
            for k, v in w.rs.items():
                toks.append((k, v, False))
        return toks

    def _wait(self, E, toks):
        for key, val, raw in toks:
            if key == E.name:
                if not raw or E.name == "pe":
                    continue
            if E.seen.get(key, 0) >= val:
                continue
            E.eng.wait_ge(self.sems[key], val)
            E.seen[key] = val
            self.ninst += 1
            E.nw = getattr(E, "nw", 0) + 1

    def _mark(self, tok, reads, writes):
        k, v = tok
        for r in reads:
            if r.rs.get(k, 0) < v:
                r.rs[k] = v
        for w in writes:
            w.w[k] = v
            w.rs = {}

    def op(self, en, fn, reads, writes):
        E = self.E[en]
        self._wait(E, self._deps(reads, writes))
        ins = fn(E.eng)
        E.n += 1
        ins.then_inc(E.sem, 1)
        self.ninst += 1
        self._mark((E.name, E.n), reads, writes)

    def dma(self, q, out, in_, reads, writes, **kw):
        Q = self.E[q]
        toks = self._deps(reads, writes)
        pool = self.dpool[q]
        i = self.dnext[q]
        self.dnext[q] = (i + 1) % len(pool)
        sem, cnt, key = pool[i]
        if cnt > 0:
            toks.append((key, cnt, False))
        self._wait(Q, toks)
        Q.eng.dma_start(out=out, in_=in_, **kw).then_inc(sem, 16)
        pool[i][1] = cnt + 16
        self.ninst += 1
        self._mark((key, cnt + 16), reads, writes)

    def barrier(self):
        toks = [(n, e.n, True) for n, e in self.E.items() if e.n > 0]
        for q, pool in self.dpool.items():
            for sem, cnt, key in pool:
                if cnt > 0:
                    toks.append((key, cnt, False))
        for E in self.E.values():
            self._wait(E, toks)

    def mm(self, out, lhsT, rhs, reads, writes, start=True, stop=True):
        self.op("pe", lambda e: e.matmul(out, lhsT, rhs, start=start, stop=stop), reads, writes)

    def tr(self, out, in_, ident, reads, writes):
        self.op("pe", lambda e: e.transpose(out, in_, ident), reads, writes)

    def act(self, out, in_, func, reads, writes, bias=None, scale=None, en="act"):
        kw = {}
        if bias is not None:
            kw["bias"] = bias
        if scale is not None:
            kw["scale"] = scale
        self.op(en, lambda e: e.activation(out, in_, func, **kw), reads, writes)

    def tt(self, en, out, in0, in1, op, reads, writes):
        self.op(en, lambda e: e.tensor_tensor(out, in0, in1, op), reads, writes)

    def ts(self, en, out, in0, s1, s2, op0, op1, reads, writes):
        if op1 is None:
            self.op(en, lambda e: e.tensor_scalar(out, in0, s1, None, op0), reads, writes)
        else:
            self.op(en, lambda e: e.tensor_scalar(out, in0, s1, s2, op0, op1), reads, writes)

    def stt(self, out, in0, scalar, in1, op0, op1, reads, writes):
        self.op("dve", lambda e: e.scalar_tensor_tensor(out, in0, scalar, in1, op0, op1), reads, writes)

    def copy(self, en, out, in_, reads, writes):
        if en == "act":
            self.op(en, lambda e: e.activation(out, in_, AF.Copy), reads, writes)
        else:
            self.op(en, lambda e: e.tensor_copy(out, in_), reads, writes)

    def memset(self, en, ap, val, writes):
        self.op(en, lambda e: e.memset(ap, val), [], writes)


class _SqAlias:
    def __init__(self, base):
        self.base = base

    @property
    def w(self):
        return self.base.w

    @property
    def rs(self):
        return self.base.rs

    @rs.setter
    def rs(self, v):
        self.base.rs = v

    def __getitem__(self, idx):
        if idx == slice(None):
            return self.base.t[:, 0:8, :]
        return self.base.t[idx]


class Builder:
    def __init__(self, T, L, dbg=()):
        self.T = T
        self.L = L
        self.NG = T // 512
        self.dbg = set(dbg)
        self.nc = bass.Bass("TRN2", target_bir_lowering=False)
        self.outs = []

    def dram_in(self, name, shape, dt=F32):
        return Tile(self.nc.dram_tensor(name, list(shape), dt, kind="ExternalInput"), name)

    def dram(self, name, shape, dt):
        kind = "Internal"
        if name in self.dbg or name == "yT":
            kind = "ExternalOutput"
            self.outs.append(name)
        return Tile(self.nc.dram_tensor(name, list(shape), dt, kind=kind), name)

    def sb(self, es, name, shape, dt):
        self.uid = getattr(self, "uid", 0) + 1
        name = "%s_%d" % (name, self.uid)
        return Tile(es.enter_context(self.nc.sbuf_tensor(name, list(shape), dt)), name)

    def build(self):
        nc, T, L = self.nc, self.T, self.L
        self.inp = {}
        I = self.inp
        I["xT"] = self.dram_in("xT", [D, T])
        I["memT"] = self.dram_in("memT", [D, MEMT])
        I["w_in"] = self.dram_in("w_in", [L, D, OFF["END"]])
        I["cv"] = self.dram_in("cv", [L, 128, NCV])
        I["w_oa"] = self.dram_in("w_oa", [L, 512, D])
        I["w_ob"] = self.dram_in("w_ob", [L, 512, D])
        I["w_oc"] = self.dram_in("w_oc", [L, 512, D])
        I["w_out"] = self.dram_in("w_out", [L, D, D])
        I["w_mq"] = self.dram_in("w_mq", [L, D, 512])
        I["w_mkv"] = self.dram_in("w_mkv", [L, D, 1024])
        I["w_mo"] = self.dram_in("w_mo", [L, 512, D])
        I["w_up"] = self.dram_in("w_up", [L, D, 2 * DFF])
        I["w_down"] = self.dram_in("w_down", [L, DFF, D])
        I["cmask"] = self.dram_in("cmask", [128, 9, 128])
        self.X = [self.dram("xres0", [D, T], F32), self.dram("xres1", [D, T], F32)]
        self.yT = self.dram("yT", [D, T], F32)
        self.HT = self.dram("HT", [D, T], BF16)
        self.QK = {n: self.dram(n, [512, T], BF16) for n in
                   ("FQ", "FK", "SQ", "SK", "GQ", "GK", "GV", "GZ")}
        self.VT = {n: self.dram(n, [T, 512], BF16) for n in ("FV", "SV")}
        self.ROW = {n: self.dram(n, [4, T], F32) for n in ("CROW", "BETA", "GLOG", "EGC")}
        self.GATES = self.dram("GATES", [3072, T], BF16)
        self.Y = {n: self.dram(n, [512, T], BF16) for n in ("YA", "YB", "YC")}
        with ExitStack() as es:
            self.es = es
            self.S = Sched(nc, es)
            self.consts(es)
            for l in range(L):
                self.layer(l)
            self.finish()
        return nc

    def consts(self, es):
        S = self.S
        self.PS = [Tile(es.enter_context(self.nc.psum_tensor("ps%d" % i, [128, 512], F32)), "ps%d" % i, psum=True)
                   for i in range(8)]
        self.ones_bf = self.sb(es, "ones_bf", [128, 128], BF16)
        S.memset("dve", self.ones_bf[:], 1.0, [self.ones_bf])
        self.cm32 = self.sb(es, "cm32", [128, 9, 128], F32)
        S.dma("sp", self.cm32[:], self.inp["cmask"][:, :, :], [self.inp["cmask"]], [self.cm32])
        self.cmbf = self.sb(es, "cmbf", [128, 9, 128], BF16)
        S.copy("dve", self.cmbf[:], self.cm32[:], [self.cm32], [self.cmbf])
        self.cvt = self.sb(es, "cvt", [128, NCV], F32)
        self.dvt = self.sb(es, "dvt", [128, 4], F32)

    def finish(self):
        S = self.S
        for n in self.dbg:
            if n.startswith("nops"):
                for i in range(int(n[4:])):
                    S.dma("sp", self.cvt[:], self.inp["cv"][0, :, :], [self.inp["cv"]], [self.cvt])
        S.barrier()

    def layer(self, l):
        S = self.S
        S.barrier()
        S.dma("sp", self.cvt[:], self.inp["cv"][l, :, :], [self.inp["cv"]], [self.cvt])
        cs = CV["small"]
        S.ts("dve", self.dvt[:, 0:1], self.cvt[:, cs:cs + 1], -1.0, None, ALU.mult, None, [self.cvt], [self.dvt])
        S.act(self.dvt[:, 1:2], self.cvt[:, cs + 2:cs + 3], AF.Exp, [self.cvt], [self.dvt])
        S.ts("dve", self.dvt[:, 1:2], self.dvt[:, 1:2], -1.0, None, ALU.mult, None, [self.dvt], [self.dvt])
        xin = self.inp["xT"] if l == 0 else self.X[1]
        if "noA" not in self.dbg:
            self.pass_a(l, xin)
        nh = 2 if "nh2" in self.dbg else (1 if "nh1" in self.dbg else NH)
        if "nofox" not in self.dbg:
            for h in range(nh):
                self.fox_head(h)
        if "nosb" not in self.dbg:
            for h in range(nh):
                self.sb_head(h)
        if "noB" not in self.dbg:
            self.pass_b(l)
        if "nogdn" not in self.dbg:
            for h in range(nh):
                self.gdn_head(h)
        if "noC" not in self.dbg:
            self.pass_c(l, xin, self.X[0])
        if "noD" not in self.dbg:
            self.pass_d(l, self.X[0], self.yT if l == self.L - 1 else self.X[1])

    def load_w(self, stg, W, wcol, src, row0, c0, ncols, gain_col=None, kchunks=8, flip=[0]):
        S = self.S
        for k in range(kchunks):
            done = 0
            while done < ncols:
                n = min(1024, ncols - done)
                st = stg[flip[0] % len(stg)]
                flip[0] += 1
                S.dma("sp", st[:, 0:n], src[row0 + k * 128: row0 + (k + 1) * 128, c0 + done: c0 + done + n],
                      [], [st])
                dst = W[:, k, wcol + done: wcol + done + n]
                if gain_col is None:
                    en = "dve" if flip[0] % 2 else "pool"
                    S.copy(en, dst, st[:, 0:n], [st], [W])
                else:
                    g = self.cvt[:, gain_col + k: gain_col + k + 1]
                    if flip[0] % 2:
                        S.ts("dve", dst, st[:, 0:n], g, None, ALU.mult, None, [st, self.cvt], [W])
                    else:
                        S.act(dst, st[:, 0:n], AF.Copy, [st, self.cvt], [W], scale=g)
                done += n

    def norm_group(self, xt, hT, sq, lnv, rstd, ps):
        S = self.S
        S.act(sq[:], xt[:], AF.Square, [xt], [sq])
        for k in range(8):
            S.mm(ps[:], self.ones_bf[:], sq[:, k, :], [self.ones_bf, sq], [ps], start=(k == 0), stop=(k == 7))
        S.act(lnv[:], ps[:], AF.Ln, [ps], [lnv], bias=EPS, scale=1.0 / D)
        S.act(rstd[:], lnv[:], AF.Exp, [lnv], [rstd], scale=-0.5)
        for k in range(8):
            S.tt("dve" if k % 2 == 0 else "pool", hT[:, k, :], xt[:, k, :], rstd[:], ALU.mult, [xt, rstd], [hT])

    def pass_a(self, l, xin):
        S, T, NG = self.S, self.T, self.NG
        w_in = self.inp["w_in"]
        with ExitStack() as es:
            NWA = 2048 + 1024
            W = self.sb(es, "WA", [128, 8, NWA], BF16)
            Wsm = self.sb(es, "WAsm", [128, 8, 96], BF16)
            stg = [self.sb(es, "stgA%d" % i, [128, 1024], F32) for i in range(4)]
            S.memset("pool", Wsm[:], 0.0, [Wsm])
            gm = CV["norm_mix"]
            wl = w_in.t[l]
            for (name, wc) in (("FQ", 0), ("FK", 512), ("SQ", 1024), ("SK", 1536), ("FV", 2048), ("SV", 2560)):
                self.load_w(stg, W, wc, wl, 0, OFF[name], 512, gain_col=gm)
            for (name, wc) in (("FF", 0), ("GB", 32), ("GA", 64)):
                self.load_w(stg, Wsm, wc, wl, 0, OFF[name], 4, gain_col=gm)
            xts = [self.sb(es, "xtA%d" % i, [128, 8, 512], F32) for i in range(2)]
            hT = self.sb(es, "hTA", [128, 8, 512], BF16)
            sq = self.sb(es, "sqA", [128, 8, 512], BF16)
            lnv = self.sb(es, "lnvA", [128, 512], F32)
            rstd = self.sb(es, "rstdA", [128, 512], F32)
            sq2 = [self.sb(es, "sq2A%d" % i, [128, 512], BF16) for i in range(2)]
            ln2 = [self.sb(es, "ln2A%d" % i, [128, 512], F32) for i in range(2)]
            ob = [self.sb(es, "obA%d" % i, [128, 512], BF16) for i in range(4)]
            sm = {n: self.sb(es, "smA_" + n, [128, 512], F32) for n in ("e1", "l1", "c", "beta", "glog", "gc", "egc")}
            ones4 = self.sb(es, "ones4", [128, 512], F32)
            rmask = self.sb(es, "rmask", [128, 512], F32)
            S.memset("pool", ones4[:], 1.0, [ones4])
            S.memset("pool", rmask[:], 1.0, [rmask])
            for c in range(8):
                S.memset("pool", rmask[:, c * 64: c * 64 + 1], 0.0, [rmask])
            ccarry = self.sb(es, "ccarry", [128, 1], F32)
            S.memset("pool", ccarry[:], 0.0, [ccarry])
            xv = xin.t.rearrange("(k p) t -> p k t", p=128)
            hv = self.HT.t.rearrange("(k p) t -> p k t", p=128)
            cvs = CV["small"]
            nob = 0
            pi = 0
            S.dma("sp", xts[0][:], xv[:, :, 0:512], [xin], [xts[0]])
            for g in range(NG):
                ts_ = slice(g * 512, (g + 1) * 512)
                xt = xts[g % 2]
                if g + 1 < NG:
                    S.dma("sp", xts[(g + 1) % 2][:], xv[:, :, (g + 1) * 512:(g + 2) * 512], [xin], [xts[(g + 1) % 2]])
                self.norm_group(xt, hT, sq, lnv, rstd, self.PS[0])
                S.dma("sp", hv[:, :, ts_], hT[:], [hT], [self.HT])
                for fam, wc, kind in (("FQ", 0, "nq"), ("FK", 512, "nk"), ("SQ", 1024, "s"), ("SK", 1536, "c")):
                    for h in range(4):
                        ps = self.PS[1 + pi % 3]
                        pi += 1
                        for k in range(8):
                            S.mm(ps[:], W[:, k, wc + h * 128: wc + (h + 1) * 128], hT[:, k, :], [W, hT], [ps],
                                 start=(k == 0), stop=(k == 7))
                        o = ob[nob % 4]
                        nob += 1
                        if kind in ("nq", "nk"):
                            s2 = sq2[nob % 2]
                            l2 = ln2[nob % 2]
                            ps2 = self.PS[4 + nob % 2]
                            S.act(s2[:], ps[:], AF.Square, [ps], [s2])
                            S.mm(ps2[:], self.ones_bf[:], s2[:], [self.ones_bf, s2], [ps2])
                            S.act(l2[:], ps2[:], AF.Ln, [ps2], [l2], bias=EPS, scale=1.0 / DH)
                            S.act(l2[:], l2[:], AF.Exp, [l2], [l2], scale=-0.5)
                            gcol = CV["fox_qnorm"] if kind == "nq" else CV["fox_knorm"]
                            S.stt(o[:], ps[:], self.cvt[:, gcol:gcol + 1], l2[:], ALU.mult, ALU.mult,
                                  [ps, self.cvt, l2], [o])
                            if kind == "nq":
                                S.ts("pool", o[:], o[:], SCALE, 1.0, ALU.mult, ALU.mult, [o], [o])
                        elif kind == "s":
                            S.act(o[:], ps[:], AF.Copy, [ps], [o], scale=SCALE)
                        else:
                            S.copy("dve", o[:], ps[:], [ps], [o])
                        dst = self.QK[fam]
                        S.dma("sp", dst.t[h * 128:(h + 1) * 128, ts_], o[:], [o], [dst])
                for fam, wc in (("FV", 2048), ("SV", 2560)):
                    for sub in range(4):
                        ps = self.PS[1 + pi % 3]
                        pi += 1
                        for k in range(8):
                            S.mm(ps[:], hT[:, k, sub * 128:(sub + 1) * 128], W[:, k, wc: wc + 512], [W, hT], [ps],
                                 start=(k == 0), stop=(k == 7))
                        o = ob[nob % 4]
                        nob += 1
                        S.copy("dve" if sub % 2 else "act", o[:], ps[:], [ps], [o])
                        dst = self.VT[fam]
                        r0 = g * 512 + sub * 128
                        S.dma("sp", dst.t[r0:r0 + 128, :], o[:], [o], [dst])
                ps = self.PS[6]
                for k in range(8):
                    S.mm(ps[0:96, :], Wsm[:, k, :], hT[:, k, :], [Wsm, hT], [ps], start=(k == 0), stop=(k == 7))
                cv = self.cvt
                S.act(sm["e1"][0:4, :], ps[0:4, :], AF.Exp, [ps, self.dvt], [sm["e1"]], bias=self.dvt[0:4, 0:1], scale=-1.0)
                S.act(sm["l1"][0:4, :], sm["e1"][0:4, :], AF.Ln, [sm["e1"]], [sm["l1"]], bias=1.0)
                S.op("dve", lambda e: e.tensor_tensor_scan(sm["c"][0:4, :], ones4[0:4, :], sm["l1"][0:4, :],
                                                           ccarry[0:4, 0:1], ALU.mult, ALU.subtract),
                     [ones4, sm["l1"], ccarry], [sm["c"]])
                S.copy("dve", ccarry[0:4, :], sm["c"][0:4, 511:512], [sm["c"]], [ccarry])
                S.dma("sp", self.ROW["CROW"].t[:, ts_], sm["c"][0:4, :], [sm["c"]], [self.ROW["CROW"]])
                S.act(sm["beta"][32:36, :], ps[32:36, :], AF.Sigmoid, [ps], [sm["beta"]])
                S.dma("sp", self.ROW["BETA"].t[:, ts_], sm["beta"][32:36, :], [sm["beta"]], [self.ROW["BETA"]])
                S.act(sm["e1"][64:68, :], ps[64:68, :], AF.Exp, [ps, cv], [sm["e1"]], bias=cv[64:68, cvs + 1:cvs + 2])
                S.act(sm["l1"][64:68, :], sm["e1"][64:68, :], AF.Ln, [sm["e1"]], [sm["l1"]], bias=1.0)
                S.ts("dve", sm["glog"][64:68, :], sm["l1"][64:68, :], self.dvt[64:68, 1:2], None, ALU.mult, None,
                     [sm["l1"], self.dvt], [sm["glog"]])
                S.op("dve", lambda e: e.tensor_tensor_scan(sm["gc"][64:68, :], rmask[64:68, :], sm["glog"][64:68, :],
                                                           0.0, ALU.mult, ALU.add),
                     [rmask, sm["glog"]], [sm["gc"]])
                S.act(sm["egc"][64:68, :], sm["gc"][64:68, :], AF.Exp, [sm["gc"]], [sm["egc"]])
                S.dma("sp", self.ROW["GLOG"].t[:, ts_], sm["glog"][64:68, :], [sm["glog"]], [self.ROW["GLOG"]])
                S.dma("sp", self.ROW["EGC"].t[:, ts_], sm["egc"][64:68, :], [sm["egc"]], [self.ROW["EGC"]])
            S.barrier()


    def pass_b(self, l):
        S, T, NG = self.S, self.T, self.NG
        w_in = self.inp["w_in"]
        PS = self.PS
        cv = self.cvt
        with ExitStack() as es:
            W = self.sb(es, "WB", [128, 8, 5120], BF16)
            stg = [self.sb(es, "stgB%d" % i, [128, 1024], F32) for i in range(4)]
            gm = CV["norm_mix"]
            wl = w_in.t[l]
            self.load_w(stg, W, 0, wl, 0, OFF["GQ"], 1536, gain_col=gm)
            self.load_w(stg, W, 1536, wl, 0, OFF["GZ"], 512, gain_col=gm)
            self.load_w(stg, W, 2048, wl, 0, OFF["GT"], 3072, gain_col=gm)
            hT = [self.sb(es, "hTB%d" % i, [128, 8, 512], BF16) for i in range(2)]
            halo = self.sb(es, "haloB", [128, 12, 4], F32)
            buf = [self.sb(es, "bufB%d" % i, [128, 516], F32) for i in range(2)]
            acc = [self.sb(es, "accB%d" % i, [128, 512], F32) for i in range(2)]
            sil = [self.sb(es, "silB%d" % i, [128, 512], F32) for i in range(2)]
            s2 = [self.sb(es, "sq2B%d" % i, [128, 512], BF16) for i in range(2)]
            l2 = [self.sb(es, "ln2B%d" % i, [128, 512], F32) for i in range(2)]
            ob = [self.sb(es, "obB%d" % i, [128, 512], BF16) for i in range(4)]
            S.memset("pool", halo[:], 0.0, [halo])
            hv = self.HT.t.rearrange("(k p) t -> p k t", p=128)
            pi = 0
            nob = 0
            cw = CV["gdn_conv"]
            for g in range(NG):
                ts_ = slice(g * 512, (g + 1) * 512)
                h_ = hT[g % 2]
                S.dma("sp", h_[:], hv[:, :, ts_], [self.HT], [h_])
                for c in range(12):
                    ps = PS[(0, 1, 2, 5, 6)[pi % 5]]
                    pi += 1
                    for k in range(8):
                        S.mm(ps[:], W[:, k, c * 128:(c + 1) * 128], h_[:, k, :], [W, h_], [ps], start=(k == 0), stop=(k == 7))
                    b = buf[c % 2]
                    a = acc[c % 2]
                    sl = sil[c % 2]
                    S.copy("act", b[:, 0:3], halo[:, c, 0:3], [halo], [b])
                    S.copy("act", b[:, 3:515], ps[:], [ps], [b])
                    S.copy("act", halo[:, c, 0:3], b[:, 512:515], [b], [halo])
                    S.ts("dve", a[:], b[:, 0:512], cv[:, cw + c:cw + c + 1], None, ALU.mult, None, [b, cv], [a])
                    for tap in (1, 2):
                        S.stt(a[:], b[:, tap:tap + 512], cv[:, cw + tap * 12 + c: cw + tap * 12 + c + 1], a[:],
                              ALU.mult, ALU.add, [b, cv, a], [a])
                    S.stt(a[:], ps[:], cv[:, cw + 3 * 12 + c: cw + 3 * 12 + c + 1], a[:],
                          ALU.mult, ALU.add, [ps, cv, a], [a])
                    o = ob[nob % 4]
                    nob += 1
                    fam = ("GQ", "GK", "GV")[c // 4]
                    hh = c % 4
                    if fam == "GV":
                        S.act(o[:], a[:], AF.Silu, [a], [o])
                    else:
                        S.act(sl[:], a[:], AF.Silu, [a], [sl])
                        q2 = s2[c % 2]
                        ll = l2[c % 2]
                        ps2 = PS[3 + c % 2]
                        S.act(q2[:], sl[:], AF.Square, [sl], [q2])
                        S.mm(ps2[:], self.ones_bf[:], q2[:], [self.ones_bf, q2], [ps2])
                        S.act(ll[:], ps2[:], AF.Ln, [ps2], [ll], bias=EPS)
                        S.act(ll[:], ll[:], AF.Exp, [ll], [ll], scale=-0.5)
                        if fam == "GQ":
                            S.stt(o[:], sl[:], SCALE, ll[:], ALU.mult, ALU.mult, [sl, ll], [o])
                        else:
                            S.tt("dve", o[:], sl[:], ll[:], ALU.mult, [sl, ll], [o])
                    S.dma("sp", self.QK[fam].t[hh * 128:(hh + 1) * 128, ts_], o[:], [o], [self.QK[fam]])
                for c in range(4):
                    ps = PS[(0, 1, 2, 5, 6)[pi % 5]]
                    pi += 1
                    for k in range(8):
                        S.mm(ps[:], W[:, k, 1536 + c * 128:1536 + (c + 1) * 128], h_[:, k, :], [W, h_], [ps],
                             start=(k == 0), stop=(k == 7))
                    o = ob[nob % 4]
                    nob += 1
                    S.act(o[:], ps[:], AF.Silu, [ps], [o])
                    S.dma("sp", self.QK["GZ"].t[c * 128:(c + 1) * 128, ts_], o[:], [o], [self.QK["GZ"]])
                gb = CV["gate_bias"]
                for c in range(24):
                    ps = PS[(0, 1, 2, 5, 6)[pi % 5]]
                    pi += 1
                    for k in range(8):
                        S.mm(ps[:], W[:, k, 2048 + c * 128:2048 + (c + 1) * 128], h_[:, k, :], [W, h_], [ps],
                             start=(k == 0), stop=(k == 7))
                    o = ob[nob % 4]
                    nob += 1
                    S.act(o[:], ps[:], AF.Sigmoid, [ps, cv], [o], bias=cv[:, gb + c:gb + c + 1])
                    S.dma("sp", self.GATES.t[c * 128:(c + 1) * 128, ts_], o[:], [o], [self.GATES])
            S.barrier()

    def head_norm(self, ps, ncols, gcol, out, sq, ll, ps2, post_scale=None):
        S = self.S
        S.act(sq[:, 0:ncols], ps[:, 0:ncols], AF.Square, [ps], [sq])
        S.mm(ps2[:, 0:ncols], self.ones_bf[:], sq[:, 0:ncols], [self.ones_bf, sq], [ps2])
        S.act(ll[:, 0:ncols], ps2[:, 0:ncols], AF.Ln, [ps2], [ll], bias=EPS, scale=1.0 / DH)
        S.act(ll[:, 0:ncols], ll[:, 0:ncols], AF.Exp, [ll], [ll], scale=-0.5)
        S.stt(out, ps[:, 0:ncols], self.cvt[:, gcol:gcol + 1], ll[:, 0:ncols], ALU.mult, ALU.mult,
              [ps, self.cvt, ll], [out.tile] if hasattr(out, "tile") else [])

    def pass_c(self, l, xin, xout):
        S, T, NG = self.S, self.T, self.NG
        I = self.inp
        PS = self.PS
        cv = self.cvt
        with ExitStack() as es:
            Wo = [self.sb(es, "WCo%d" % i, [128, 4, D], BF16) for i in range(3)]
            Wout = self.sb(es, "WCout", [128, 8, D], BF16)
            Wmq = self.sb(es, "WCmq", [128, 8, 512], BF16)
            Wmo = self.sb(es, "WCmo", [128, 4, D], BF16)
            Wkv = self.sb(es, "WCkv", [128, 8, D], BF16)
            stg = [self.sb(es, "stgC%d" % i, [128, 1024], F32) for i in range(2)]
            for i, n in enumerate(("w_oa", "w_ob", "w_oc")):
                self.load_w(stg, Wo[i], 0, I[n].t[l], 0, 0, D, kchunks=4)
            self.load_w(stg, Wout, 0, I["w_out"].t[l], 0, 0, D)
            self.load_w(stg, Wmq, 0, I["w_mq"].t[l], 0, 0, 512, gain_col=CV["norm_xq"])
            self.load_w(stg, Wmo, 0, I["w_mo"].t[l], 0, 0, D, kchunks=4)
            self.load_w(stg, Wkv, 0, I["w_mkv"].t[l], 0, 0, D, gain_col=CV["norm_mem"])
            xts = [self.sb(es, "xtC%d" % i, [128, 8, 512], F32) for i in range(2)]
            xt = xts[1]
            hT = self.sb(es, "hTC", [128, 8, 512], BF16)
            sq = self.sb(es, "sqC", [128, 8, 512], BF16)
            lnv = self.sb(es, "lnvC", [128, 512], F32)
            rstd = self.sb(es, "rstdC", [128, 512], F32)
            yb_ = [self.sb(es, "yC%d" % i, [128, 4, 512], BF16) for i in range(3)]
            gt = self.sb(es, "gtC", [128, 24, 512], BF16)
            tA = self.sb(es, "tAC", [128, 512], F32)
            tB = self.sb(es, "tBC", [128, 512], F32)
            mix = self.sb(es, "mixC", [128, 8, 512], BF16)
            om = self.sb(es, "omC", [128, 4, 512], BF16)
            sq2 = self.sb(es, "sq2C", [128, 512], BF16)
            ll = self.sb(es, "llC", [128, 512], F32)
            qn = self.sb(es, "qnC", [128, 512], BF16)
            pt = [self.sb(es, "ptC%d" % i, [128, 512], BF16) for i in range(2)]
            Km = self.sb(es, "KmC", [128, 4, MEMT], BF16)
            Vm = self.sb(es, "VmC", [128, 2, 512], BF16)
            mv = I["memT"].t.rearrange("(k p) t -> p k t", p=128)
            S.dma("sp", xt[:, :, 0:MEMT], mv, [I["memT"]], [xt])
            S.act(sq[:, :, 0:MEMT], xt[:, :, 0:MEMT], AF.Square, [xt], [sq])
            for k in range(8):
                S.mm(PS[0][:, 0:MEMT], self.ones_bf[:], sq[:, k, 0:MEMT], [self.ones_bf, sq], [PS[0]], start=(k == 0), stop=(k == 7))
            S.act(lnv[:, 0:MEMT], PS[0][:, 0:MEMT], AF.Ln, [PS[0]], [lnv], bias=EPS, scale=1.0 / D)
            S.act(rstd[:, 0:MEMT], lnv[:, 0:MEMT], AF.Exp, [lnv], [rstd], scale=-0.5)
            for k in range(8):
                S.tt("dve", hT[:, k, 0:MEMT], xt[:, k, 0:MEMT], rstd[:, 0:MEMT], ALU.mult, [xt, rstd], [hT])
            for h in range(4):
                ps = PS[1 + h % 2]
                for k in range(8):
                    S.mm(ps[:, 0:MEMT], Wkv[:, k, h * 128:(h + 1) * 128], hT[:, k, 0:MEMT], [Wkv, hT], [ps], start=(k == 0), stop=(k == 7))
                S.act(sq2[:, 0:MEMT], ps[:, 0:MEMT], AF.Square, [ps], [sq2])
                S.mm(PS[3][:, 0:MEMT], self.ones_bf[:], sq2[:, 0:MEMT], [self.ones_bf, sq2], [PS[3]])
                S.act(ll[:, 0:MEMT], PS[3][:, 0:MEMT], AF.Ln, [PS[3]], [ll], bias=EPS, scale=1.0 / DH)
                S.act(ll[:, 0:MEMT], ll[:, 0:MEMT], AF.Exp, [ll], [ll], scale=-0.5)
                gk = CV["mk_norm"]
                S.stt(Km[:, h, :], ps[:, 0:MEMT], cv[:, gk:gk + 1], ll[:, 0:MEMT], ALU.mult, ALU.mult, [ps, cv, ll], [Km])
            for blk in range(2):
                ps = PS[1 + blk % 2]
                for k in range(8):
                    S.mm(ps[:], hT[:, k, blk * 128:(blk + 1) * 128], Wkv[:, k, 512:1024], [Wkv, hT], [ps], start=(k == 0), stop=(k == 7))
                S.copy("act", Vm[:, blk, :], ps[:], [ps], [Vm])
            xv = xin.t.rearrange("(k p) t -> p k t", p=128)
            xo = xout.t.rearrange("(k p) t -> p k t", p=128)
            yv = [self.Y[n].t.rearrange("(h p) t -> p h t", p=128) for n in ("YA", "YB", "YC")]
            gv = self.GATES.t.rearrange("(c p) t -> p c t", p=128)
            pi = 0
            S.dma("sp", xts[0][:], xv[:, :, 0:512], [xin], [xts[0]])
            for g in range(NG):
                ts_ = slice(g * 512, (g + 1) * 512)
                xt = xts[g % 2]
                if g + 1 < NG:
                    S.dma("sp", xts[(g + 1) % 2][:], xv[:, :, (g + 1) * 512:(g + 2) * 512], [xin], [xts[(g + 1) % 2]])
                for i, n in enumerate(("YA", "YB", "YC")):
                    S.dma("sp", yb_[i][:], yv[i][:, :, ts_], [self.Y[n]], [yb_[i]])
                S.dma("sp", gt[:], gv[:, :, ts_], [self.GATES], [gt])
                for oc in range(8):
                    ocs = slice(oc * 128, (oc + 1) * 128)
                    for br in range(3):
                        ps = PS[pi % 3]
                        pi += 1
                        for hh in range(4):
                            S.mm(ps[:], Wo[br][:, hh, ocs], yb_[br][:, hh, :], [Wo[br], yb_[br]], [ps], start=(hh == 0), stop=(hh == 3))
                        if br == 0:
                            S.tt("dve", tA[:], ps[:], gt[:, oc, :], ALU.mult, [ps, gt], [tA])
                        else:
                            S.tt("dve", tB[:], ps[:], gt[:, br * 8 + oc, :], ALU.mult, [ps, gt], [tB])
                            if br == 1:
                                S.tt("pool", tA[:], tA[:], tB[:], ALU.add, [tA, tB], [tA])
                            else:
                                S.tt("pool", mix[:, oc, :], tA[:], tB[:], ALU.add, [tA, tB], [mix])
                for oc in range(8):
                    ocs = slice(oc * 128, (oc + 1) * 128)
                    ps = PS[pi % 3]
                    pi += 1
                    for k in range(8):
                        S.mm(ps[:], Wout[:, k, ocs], mix[:, k, :], [Wout, mix], [ps], start=(k == 0), stop=(k == 7))
                    S.tt("dve", xt[:, oc, :], xt[:, oc, :], ps[:], ALU.add, [xt, ps], [xt])
                self.norm_group(xt, hT, sq, lnv, rstd, PS[3])
                for h in range(4):
                    ps = PS[pi % 3]
                    pi += 1
                    for k in range(8):
                        S.mm(ps[:], Wmq[:, k, h * 128:(h + 1) * 128], hT[:, k, :], [Wmq, hT], [ps], start=(k == 0), stop=(k == 7))
                    S.act(sq2[:], ps[:], AF.Square, [ps], [sq2])
                    S.mm(PS[3][:], self.ones_bf[:], sq2[:], [self.ones_bf, sq2], [PS[3]])
                    S.act(ll[:], PS[3][:], AF.Ln, [PS[3]], [ll], bias=EPS, scale=1.0 / DH)
                    S.act(ll[:], ll[:], AF.Exp, [ll], [ll], scale=-0.5)
                    gq = CV["mq_norm"]
                    S.stt(tA[:], ps[:], cv[:, gq:gq + 1], ll[:], ALU.mult, ALU.mult, [ps, cv, ll], [tA])
                    S.ts("pool", qn[:], tA[:], SCALE, 1.0, ALU.mult, ALU.mult, [tA], [qn])
                    O = PS[4 + h % 2]
                    DN = PS[6 + h % 2]
                    for kb in range(2):
                        ps = PS[pi % 3]
                        pi += 1
                        p_ = pt[kb]
                        S.mm(ps[:], Km[:, h, kb * 128:(kb + 1) * 128], qn[:], [Km, qn], [ps])
                        S.act(p_[:], ps[:], AF.Exp, [ps], [p_])
                        S.mm(O[:], Vm[:, kb, h * 128:(h + 1) * 128], p_[:], [Vm, p_], [O], start=(kb == 0), stop=(kb == 1))
                        S.mm(DN[:], self.ones_bf[:], p_[:], [self.ones_bf, p_], [DN], start=(kb == 0), stop=(kb == 1))
                    S.act(ll[:], DN[:], AF.Ln, [DN], [ll])
                    S.act(ll[:], ll[:], AF.Exp, [ll], [ll], scale=-1.0)
                    S.tt("dve", om[:, h, :], O[:], ll[:], ALU.mult, [O, ll], [om])
                for oc in range(8):
                    ocs = slice(oc * 128, (oc + 1) * 128)
                    ps = PS[pi % 3]
                    pi += 1
                    for hh in range(4):
                        S.mm(ps[:], Wmo[:, hh, ocs], om[:, hh, :], [Wmo, om], [ps], start=(hh == 0), stop=(hh == 3))
                    S.tt("dve", xt[:, oc, :], xt[:, oc, :], ps[:], ALU.add, [xt, ps], [xt])
                S.dma("sp", xo[:, :, ts_], xt[:], [xt], [xout])
            S.barrier()

    def pass_d(self, l, xin, xout):
        S, T, NG = self.S, self.T, self.NG
        I = self.inp
        PS = self.PS
        cv = self.cvt
        NJ = DFF // 128
        with ExitStack() as es:
            Wup = self.sb(es, "WDup", [128, 8, 2 * DFF], BF16)
            Wdn = self.sb(es, "WDdn", [128, NJ, D], BF16)
            with ExitStack() as es2:
                stg = [self.sb(es2, "stgD%d" % i, [128, 1024], F32) for i in range(2)]
                self.load_w(stg, Wup, 0, I["w_up"].t[l], 0, 0, 2 * DFF, gain_col=CV["norm_ffn"])
                self.load_w(stg, Wdn, 0, I["w_down"].t[l], 0, 0, D, kchunks=NJ)
                S.barrier()
            xt = self.sb(es, "xtD", [128, 8, 512], F32)
            hT = self.sb(es, "hTD", [128, 8, 512], BF16)
            gT = self.sb(es, "gTD", [128, NJ, 512], BF16)
            lnv = self.sb(es, "lnvD", [128, 512], F32)
            rstd = self.sb(es, "rstdD", [128, 512], F32)
            halo = self.sb(es, "haloD", [128, 2 * NJ, 2], F32)
            buf = [self.sb(es, "bufD%d" % i, [128, 516], F32) for i in range(2)]
            acc = [self.sb(es, "accD%d" % i, [128, 512], F32) for i in range(2)]
            sa = self.sb(es, "saD", [128, 512], F32)
            S.memset("pool", halo[:], 0.0, [halo])
            xv = xin.t.rearrange("(k p) t -> p k t", p=128)
            xo = xout.t.rearrange("(k p) t -> p k t", p=128)
            cw = CV["ffn_conv"]
            cb = CV["ffn_conv_b"]
            pi = 0
            for g in range(NG):
                ts_ = slice(g * 512, (g + 1) * 512)
                S.dma("sp", xt[:], xv[:, :, ts_], [xin], [xt])
                self.norm_group(xt, hT, _SqAlias(gT), lnv, rstd, PS[3])
                for j in range(NJ):
                    for ab in range(2):
                        c = ab * NJ + j
                        ps = PS[(0, 1, 2, 6, 7)[pi % 5]]
                        pi += 1
                        for k in range(8):
                            S.mm(ps[:], Wup[:, k, c * 128:(c + 1) * 128], hT[:, k, :], [Wup, hT], [ps], start=(k == 0), stop=(k == 7))
                        b = buf[ab]
                        a = acc[ab]
                        S.copy("act", b[:, 0:2], halo[:, c, 0:2], [halo], [b])
                        S.copy("act", b[:, 2:514], ps[:], [ps], [b])
                        S.copy("act", halo[:, c, 0:2], b[:, 512:514], [b], [halo])
                        S.ts("dve", a[:], b[:, 0:512], cv[:, cw + c:cw + c + 1], cv[:, cb + c:cb + c + 1], ALU.mult, ALU.add,
                             [b, cv], [a])
                        S.stt(a[:], b[:, 1:513], cv[:, cw + 2 * NJ + c: cw + 2 * NJ + c + 1], a[:],
                              ALU.mult, ALU.add, [b, cv, a], [a])
                        S.stt(a[:], ps[:], cv[:, cw + 2 * 2 * NJ + c: cw + 2 * 2 * NJ + c + 1], a[:],
                              ALU.mult, ALU.add, [ps, cv, a], [a])
                    S.act(sa[:], acc[0][:], AF.Silu, [acc[0]], [sa])
                    S.tt("pool", gT[:, j, :], sa[:], acc[1][:], ALU.mult, [sa, acc[1]], [gT])
                for oc in range(8):
                    ocs = slice(oc * 128, (oc + 1) * 128)
                    ps = PS[4 + oc % 2]
                    for j in range(NJ):
                        S.mm(ps[:], Wdn[:, j, ocs], gT[:, j, :], [Wdn, gT], [ps], start=(j == 0), stop=(j == NJ - 1))
                    S.tt("dve", xt[:, oc, :], xt[:, oc, :], ps[:], ALU.add, [xt, ps], [xt])
                S.dma("sp", xo[:, :, ts_], xt[:], [xt], [xout])
            S.barrier()

    def gdn_head(self, h):
        S, T = self.S, self.T
        NP = T // 128
        PS = self.PS
        cm = self.cm32
        ident = cm[:, 0, :]
        with ExitStack() as es:
            Qf = self.sb(es, "gQ", [128, T], BF16)
            Kf = self.sb(es, "gK", [128, T], BF16)
            Vf = self.sb(es, "gV", [128, T], BF16)
            bc = self.sb(es, "gbc", [128, T], F32)
            G2 = self.sb(es, "gG2", [128, 128], F32)
            B2 = self.sb(es, "gB2", [128, 128], F32)
            gc2 = self.sb(es, "ggc2", [128, 128], F32)
            gl2 = self.sb(es, "ggl2", [128, 128], F32)
            rm2 = self.sb(es, "grm2", [128, 128], F32)
            gcol = self.sb(es, "ggcol", [128, NP], F32)
            bcol = self.sb(es, "gbcol", [128, NP], F32)
            eglc = self.sb(es, "geglc", [128, NP], F32)
            Sst = self.sb(es, "gS", [128, 128], F32)
            hs = slice(h * 128, (h + 1) * 128)
            S.dma("sp", Qf[:], self.QK["GQ"].t[hs, :], [self.QK["GQ"]], [Qf])
            S.dma("sp", Kf[:], self.QK["GK"].t[hs, :], [self.QK["GK"]], [Kf])
            S.dma("sp", Vf[:], self.QK["GV"].t[hs, :], [self.QK["GV"]], [Vf])
            R = self.ROW
            S.dma("sp", G2[0:NP, :], R["GLOG"].t[h:h + 1, :].rearrange("o (n p) -> (o n) p", p=128), [R["GLOG"]], [G2])
            S.dma("sp", B2[0:NP, :], R["BETA"].t[h:h + 1, :].rearrange("o (n p) -> (o n) p", p=128), [R["BETA"]], [B2])
            S.dma("sp", bc[:], R["EGC"].t[h:h + 1, :].partition_broadcast(128), [R["EGC"]], [bc])
            S.memset("pool", rm2[:], 1.0, [rm2])
            S.memset("pool", rm2[:, 0:1], 0.0, [rm2])
            S.memset("pool", rm2[:, 64:65], 0.0, [rm2])
            S.op("dve", lambda e: e.tensor_tensor_scan(gc2[0:NP, :], rm2[0:NP, :], G2[0:NP, :], 0.0, ALU.mult, ALU.add),
                 [rm2, G2], [gc2])
            for half in range(2):
                cs = slice(64 * half, 64 * half + 64)
                tot = gc2[0:NP, 64 * half + 63: 64 * half + 64]
                S.ts("dve", gl2[0:NP, cs], gc2[0:NP, cs], tot, -1.0, ALU.subtract, ALU.mult, [gc2], [gl2])
            S.act(gl2[0:NP, :], gl2[0:NP, :], AF.Exp, [gl2], [gl2])
            eg2 = self.sb(es, "geg2", [128, 128], F32)
            egcol = self.sb(es, "gegcol", [128, NP], F32)
            bgc = self.sb(es, "gbgc", [128, NP], F32)
            S.act(eg2[0:NP, :], gc2[0:NP, :], AF.Exp, [gc2], [eg2])
            for src, dst in ((G2, gcol), (B2, bcol), (gl2, eglc), (eg2, egcol)):
                S.tr(PS[7][:, 0:NP], src[0:NP, :], cm[0:NP, 0, 0:NP], [src, cm], [PS[7]])
                S.copy("dve", dst[:], PS[7][:, 0:NP], [PS[7]], [dst])
            S.tt("dve", bgc[:], bcol[:], egcol[:], ALU.mult, [bcol, egcol], [bgc])
            Sst2 = self.sb(es, "gS2", [128, 128], F32)
            SS = [Sst, Sst2]
            S.memset("pool", Sst[:], 0.0, [Sst])

            def f32t(n, k=5):
                return [self.sb(es, n + str(i), [128, 128], F32) for i in range(k)]
            lg = f32t("g_lg"); Dl = f32t("g_Dl"); DTs = f32t("g_DTs"); DTi = f32t("g_DTi")
            Lb = [f32t("g_L%d_" % j) for j in range(2)]
            Ub = [f32t("g_U%d_" % j) for j in range(2)]
            X = f32t("g_X"); At = f32t("g_At")
            K32 = f32t("g_K32"); V32 = f32t("g_V32"); Qg = f32t("g_Qg"); Qp = f32t("g_Qp")
            Kt = f32t("g_Kt")
            Rr = [self.sb(es, "g_R%d" % i_, [128, 256], F32) for i_ in range(5)]
            UW = [self.sb(es, "g_UW%d" % i_, [128, 256], F32) for i_ in range(5)]
            MT = [f32t("g_MT%d_" % j, 5) for j in range(2)]
            Nn = [f32t("g_N%d_" % j, 5) for j in range(2)]
            OT = [self.sb(es, "g_OT%d" % i, [128, 512], F32) for i in range(2)]
            gz = [self.sb(es, "g_gz%d" % i, [128, 512], BF16) for i in range(2)]
            sq = self.sb(es, "g_sq", [128, 512], BF16)
            ll = self.sb(es, "g_ll", [128, 512], F32)
            t32 = self.sb(es, "g_t32", [128, 512], F32)
            ob = [self.sb(es, "g_ob%d" % i, [128, 512], BF16) for i in range(2)]

            def pre(r):
                i = r % 5
                cs = slice(r * 128, (r + 1) * 128)
                PA = PS[r % 4]
                PB = PA
                q = [slice(0, 128), slice(128, 256), slice(256, 384), slice(384, 512)]
                S.ts("pool", lg[i][:], cm[:, 4, :], gcol[:, r:r + 1], 1.0, ALU.mult, ALU.mult, [cm, gcol], [lg[i]])
                yield
                S.mm(PA[:, q[0]], lg[i][:], cm[:, 5, :], [lg[i], cm], [PA], start=True, stop=False)
                S.mm(PA[:, q[0]], ident, cm[:, 6, :], [cm], [PA], start=False, stop=True)
                S.mm(PA[:, q[1]], cm[:, 5, :], lg[i][:], [lg[i], cm], [PA], start=True, stop=False)
                S.mm(PA[:, q[1]], ident, cm[:, 8, :], [cm], [PA], start=False, stop=True)
                yield
                S.act(Dl[i][:], PA[:, q[0]], AF.Exp, [PA], [Dl[i]])
                yield
                S.act(DTi[i][:], PA[:, q[1]], AF.Exp, [PA], [DTi[i]])
                yield
                S.mm(PA[:, q[2]], Kf[:, cs], Kf[:, cs], [Kf], [PA])
                S.mm(PA[:, q[3]], Kf[:, cs], Qf[:, cs], [Qf, Kf], [PA])
                yield
                L = Lb[0][i]
                U = Ub[0][i]
                S.stt(L[:], PA[:, q[2]], bcol[:, r:r + 1], Dl[i][:], ALU.mult, ALU.mult, [PA, bcol, Dl[i]], [L])
                yield
                S.tt("dve", At[i][:], PA[:, q[3]], DTi[i][:], ALU.mult, [PA, DTi[i]], [At[i]])
                yield
                S.tr(PA[:, q[0]], L[:], ident, [L, cm], [PA])
                yield
                S.copy("act", U[:], PA[:, q[0]], [PA], [U])
                yield
                S.tt("pool", X[i][:], ident, U[:], ALU.subtract, [cm, U], [X[i]])
                yield
                S.copy("pool", K32[i][:], Kf[:, cs], [Kf], [K32[i]])
                yield
                S.copy("pool", V32[i][:], Vf[:, cs], [Vf], [V32[i]])
                yield
                S.tr(PA[:, q[1]], K32[i][:], ident, [K32[i], cm], [PA])
                S.tr(PA[:, q[2]], V32[i][:], ident, [V32[i], cm], [PA])
                yield
                S.ts("dve", Kt[i][:], PA[:, q[1]], eglc[:, r:r + 1], None, ALU.mult, None, [PA, eglc], [Kt[i]])
                yield
                S.act(Rr[i][:, 128:256], PA[:, q[1]], AF.Copy, [PA, bgc], [Rr[i]], scale=bgc[:, r:r + 1])
                yield
                S.act(Rr[i][:, 0:128], PA[:, q[2]], AF.Copy, [PA, bcol], [Rr[i]], scale=bcol[:, r:r + 1])
                yield
                S.tt("pool", Qg[i][:], Qf[:, cs], bc[:, cs], ALU.mult, [Qf, bc], [Qg[i]])
                yield
                for k in range(1, 6):
                    Ln_ = Lb[k % 2][i]
                    Un_ = Ub[k % 2][i]
                    S.mm(PA[:, q[0]], U[:], L[:], [U, L], [PA])
                    if k < 5:
                        S.mm(PA[:, q[1]], L[:], U[:], [U, L], [PA])
                    yield
                    S.copy("act", Ln_[:], PA[:, q[0]], [PA], [Ln_])
                    yield
                    if k < 5:
                        S.copy("dve", Un_[:], PA[:, q[1]], [PA], [Un_])
                        yield
                    S.mm(PA[:, q[2]], Ln_[:], X[i][:], [Ln_, X[i]], [PA])
                    yield
                    S.tt("dve", X[i][:], X[i][:], PA[:, q[2]], ALU.add, [X[i], PA], [X[i]])
                    yield
                    L, U = Ln_, Un_
                S.mm(PA[:, 0:256], X[i][:], Rr[i][:], [X[i], Rr[i]], [PA])
                yield
                S.copy("act", UW[i][:], PA[:, 0:256], [PA], [UW[i]])
                yield
                S.mm(PA[:, q[3]], UW[i][:, 128:256], At[i][:], [UW[i], At[i]], [PA])
                yield
                S.tt("dve", Qp[i][:], Qg[i][:], PA[:, q[3]], ALU.subtract, [Qg[i], PA], [Qp[i]])
                yield
                for par in range(2):
                    rows = slice(64 * par, 64 * par + 64)
                    col = r * 128 + 64 * par + 63
                    S.mm(PA[:, q[2]], UW[i][rows, 128:256], Kt[i][rows, :], [UW[i], Kt[i]], [PA])
                    S.mm(PA[:, q[3]], Kt[i][rows, :], UW[i][rows, 0:128], [UW[i], Kt[i]], [PA])
                    yield
                    S.stt(MT[par][i][:], ident, bc[:, col:col + 1], PA[:, q[2]], ALU.mult, ALU.subtract,
                          [cm, bc, PA], [MT[par][i]])
                    yield
                    S.copy("act", Nn[par][i][:], PA[:, q[3]], [PA], [Nn[par][i]])
                    yield

            def scan(r):
                i = r % 5
                ot = OT[(r // 4) % 2]
                for par in range(2):
                    rows = slice(64 * par, 64 * par + 64)
                    c = 2 * r + par
                    Sp = SS[c % 2]
                    Sn = SS[(c + 1) % 2]
                    P5 = PS[5 + c % 2]
                    P7 = PS[7]
                    S.mm(P7[:, 0:64], Sp[:], Qp[i][:, rows], [Sp, Qp[i]], [P7], start=True, stop=False)
                    S.mm(P7[:, 0:64], UW[i][rows, 0:128], At[i][rows, rows], [UW[i], At[i]], [P7], start=False, stop=True)
                    S.mm(P5[:, 0:128], MT[par][i][:], Sp[:], [MT[par][i], Sp], [P5])
                    yield
                    S.tt("dve", Sn[:], P5[:, 0:128], Nn[par][i][:], ALU.add, [P5, Nn[par][i]], [Sn])
                    yield
                    oc = (r % 4) * 128 + 64 * par
                    S.copy("act", ot[:, oc:oc + 64], P7[:, 0:64], [P7], [ot])
                    yield

            def post(blk):
                ot = OT[blk % 2]
                ts_ = slice(blk * 512, (blk + 1) * 512)
                z = gz[blk % 2]
                S.dma("sp", z[:], self.QK["GZ"].t[hs, ts_], [self.QK["GZ"]], [z])
                S.act(sq[:], ot[:], AF.Square, [ot], [sq])
                S.mm(PS[4][:], self.ones_bf[:], sq[:], [self.ones_bf, sq], [PS[4]])
                S.act(ll[:], PS[4][:], AF.Ln, [PS[4]], [ll], bias=EPS, scale=1.0 / DH)
                S.act(ll[:], ll[:], AF.Exp, [ll], [ll], scale=-0.5)
                go = CV["gdn_onorm"]
                S.stt(t32[:], ot[:], self.cvt[:, go:go + 1], ll[:], ALU.mult, ALU.mult, [ot, self.cvt, ll], [t32])
                o = ob[blk % 2]
                S.tt("pool", o[:], t32[:], z[:], ALU.mult, [t32, z], [o])
                S.dma("sp", self.Y["YB"].t[hs, ts_], o[:], [o], [self.Y["YB"]])

            act = {}

            def adv(r, nops):
                if r >= NP:
                    return True
                if r not in act:
                    act[r] = pre(r)
                g_ = act[r]
                if g_ is None:
                    return True
                for _ in range(nops):
                    try:
                        next(g_)
                    except StopIteration:
                        act[r] = None
                        return True
                return False

            while not adv(0, 1000):
                pass
            for r in range(NP):
                gs = scan(r)
                sdone = False
                while True:
                    d1 = adv(r + 1, 2)
                    adv(r + 2, 2)
                    adv(r + 3, 1)
                    adv(r + 4, 1)
                    if not sdone:
                        try:
                            next(gs)
                        except StopIteration:
                            sdone = True
                    if sdone and d1:
                        break
                if r % 4 == 3:
                    post(r // 4)
            S.barrier()

    def fox_head(self, h):
        S, T, NG = self.S, self.T, self.NG
        NB = T // 128
        PS = self.PS
        with ExitStack() as es:
            Qf = self.sb(es, "fxQ", [128, T], BF16)
            Kf = self.sb(es, "fxK", [128, T], BF16)
            Vt = self.sb(es, "fxV", [128, NB, 128], BF16)
            cbc = self.sb(es, "fxcbc", [128, T], F32)
            c2 = self.sb(es, "fxc2", [128, 128], F32)
            ncc = self.sb(es, "fxncc", [128, NB], F32)
            tmp = [self.sb(es, "fxtmp%d" % i, [128, 512], F32) for i in range(2)]
            PT = [self.sb(es, "fxPT%d" % i, [128, 512], BF16) for i in range(3)]
            lnd = self.sb(es, "fxlnd", [128, 512], F32)
            ob = [self.sb(es, "fxob%d" % i, [128, 512], BF16) for i in range(2)]
            hs = slice(h * 128, (h + 1) * 128)
            S.dma("sp", Qf[:], self.QK["FQ"].t[hs, :], [self.QK["FQ"]], [Qf])
            S.dma("sp", Kf[:], self.QK["FK"].t[hs, :], [self.QK["FK"]], [Kf])
            S.dma("sp", Vt[:], self.VT["FV"].t.rearrange("(n p) c -> p n c", p=128)[:, :, hs], [self.VT["FV"]], [Vt])
            crow = self.ROW["CROW"]
            S.dma("sp", cbc[:], crow.t[h:h + 1, :].partition_broadcast(128), [crow], [cbc])
            S.dma("sp", c2[0:NB, :], crow.t[h:h + 1, :].rearrange("o (n p) -> (o n) p", p=128), [crow], [c2])
            S.tr(PS[7][:, 0:NB], c2[0:NB, :], self.cm32[0:NB, 0, 0:NB], [c2, self.cm32], [PS[7]])
            S.ts("dve", ncc[:], PS[7][:, 0:NB], -1.0, None, ALU.mult, None, [PS[7]], [ncc])
            tiles = []
            for g in range(NG):
                tl = [(kb, 0) for kb in range(4 * g)] + [(4 * g + j, 128 * j) for j in range(4)]
                for ti, (kb, c0) in enumerate(tl):
                    tiles.append((g, kb, c0, ti == 0, ti == len(tl) - 1))
            n = len(tiles)

            def emit_S(i):
                g, kb, c0, first, last = tiles[i]
                ps = PS[i % 3]
                S.mm(ps[:, c0:512], Kf[:, kb * 128:(kb + 1) * 128], Qf[:, g * 512 + c0:(g + 1) * 512], [Kf, Qf], [ps])

            def emit_mid(i):
                g, kb, c0, first, last = tiles[i]
                ps = PS[i % 3]
                tm = tmp[i % 2]
                pt = PT[i % 3]
                S.tt("dve", tm[:, c0:512], ps[:, c0:512], cbc[:, g * 512 + c0:(g + 1) * 512], ALU.add, [ps, cbc], [tm])
                S.act(pt[:, c0:512], tm[:, c0:512], AF.Exp, [tm, ncc], [pt], bias=ncc[:, kb:kb + 1])
                if kb >= 4 * g:
                    S.tt("pool", pt[:, c0:c0 + 128], pt[:, c0:c0 + 128], self.cmbf[:, 1, :], ALU.mult,
                         [pt, self.cmbf], [pt])

            def emit_PV(i):
                g, kb, c0, first, last = tiles[i]
                pt = PT[i % 3]
                O = PS[3 + g % 2]
                DN = PS[5 + g % 2]
                S.mm(O[:, c0:512], Vt[:, kb, :], pt[:, c0:512], [Vt, pt], [O], start=first, stop=last)
                S.mm(DN[:, c0:512], self.ones_bf[:], pt[:, c0:512], [self.ones_bf, pt], [DN], start=first, stop=last)
                if last:
                    S.act(lnd[:], DN[:], AF.Ln, [DN], [lnd])
                    S.act(lnd[:], lnd[:], AF.Exp, [lnd], [lnd], scale=-1.0)
                    o = ob[g % 2]
                    S.tt("dve", o[:], O[:], lnd[:], ALU.mult, [O, lnd], [o])
                    S.dma("sp", self.Y["YA"].t[hs, g * 512:(g + 1) * 512], o[:], [o], [self.Y["YA"]])

            LA = 2
            for i in range(min(LA, n)):
                emit_S(i)
            for i in range(n):
                if i + LA < n:
                    emit_S(i + LA)
                emit_mid(i)
                emit_PV(i)
            S.barrier()

    def sb_head(self, h):
        S, T, NG = self.S, self.T, self.NG
        NB = T // 128
        PS = self.PS
        with ExitStack() as es:
            Qf = self.sb(es, "sbQ", [128, T], BF16)
            Kf = self.sb(es, "sbK", [128, T], BF16)
            Vt = self.sb(es, "sbV", [128, NB, 128], BF16)
            ntri = self.sb(es, "sbntri", [128, 128], BF16)
            nones = self.sb(es, "sbnones", [128, 128], BF16)
            zeros = self.sb(es, "sbzeros", [128, 128], BF16)
            E = [self.sb(es, "sbE%d" % i, [128, 512], F32) for i in range(2)]
            SP = [self.sb(es, "sbSP%d" % i, [128, 512], BF16) for i in range(3)]
            AT = [self.sb(es, "sbAT%d" % i, [128, 512], BF16) for i in range(3)]
            cum = [self.sb(es, "sbcum%d" % i, [128, 512], BF16) for i in range(2)]
            ob = [self.sb(es, "sbob%d" % i, [128, 512], BF16) for i in range(2)]
            hs = slice(h * 128, (h + 1) * 128)
            S.dma("sp", Qf[:], self.QK["SQ"].t[hs, :], [self.QK["SQ"]], [Qf])
            S.dma("sp", Kf[:], self.QK["SK"].t[hs, :], [self.QK["SK"]], [Kf])
            S.dma("sp", Vt[:], self.VT["SV"].t.rearrange("(n p) c -> p n c", p=128)[:, :, hs], [self.VT["SV"]], [Vt])
            S.ts("dve", ntri[:], self.cm32[:, 3, :], -1.0, None, ALU.mult, None, [self.cm32], [ntri])
            S.memset("dve", nones[:], -1.0, [nones])
            S.memset("dve", zeros[:], 0.0, [zeros])
            tiles = []
            for g in range(NG):
                tl = [(4 * g + j, 128 * j) for j in (3, 2, 1, 0)] + [(kb, 0) for kb in range(4 * g - 1, -1, -1)]
                for ti, (kb, c0) in enumerate(tl):
                    tiles.append((g, kb, c0, ti, len(tl)))
            n = len(tiles)

            def emit_A(i):
                g, kb, c0, ti, nt = tiles[i]
                A = PS[i % 3]
                S.mm(A[:, c0:512], Kf[:, kb * 128:(kb + 1) * 128], Qf[:, g * 512 + c0:(g + 1) * 512], [Kf, Qf], [A])

            def emit_sp(i):
                g, kb, c0, ti, nt = tiles[i]
                A = PS[i % 3]
                e = E[i % 2]
                sp = SP[i % 3]
                S.act(e[:, c0:512], A[:, c0:512], AF.Exp, [A], [e])
                S.act(sp[:, c0:512], e[:, c0:512], AF.Ln, [e], [sp], bias=1.0)
                if kb >= 4 * g:
                    S.tt("pool", sp[:, c0:c0 + 128], sp[:, c0:c0 + 128], self.cmbf[:, 2, :], ALU.mult,
                         [sp, self.cmbf], [sp])

            def emit_B(i):
                g, kb, c0, ti, nt = tiles[i]
                B = PS[3 + i % 2]
                sp = SP[i % 3]
                cm = cum[g % 2]
                O = PS[5 + g % 2]
                q0, q1 = g * 512 + c0, (g + 1) * 512
                kT = Kf[:, kb * 128:(kb + 1) * 128]
                if ti == 0:
                    S.memset("pool", cm[:], 0.0, [cm])
                    S.mm(O[:], zeros[:], Qf[:, g * 512:(g + 1) * 512], [zeros, Qf], [O], start=True, stop=False)
                S.mm(B[:, c0:512], kT, Qf[:, q0:q1], [Kf, Qf], [B], start=True, stop=False)
                S.mm(B[:, c0:512], ntri[:], sp[:, c0:512], [ntri, sp], [B], start=False, stop=(ti == 0))
                if ti > 0:
                    S.mm(B[:, c0:512], nones[:], cm[:, c0:512], [nones, cm], [B], start=False, stop=True)
                if ti < nt - 1:
                    S.tt("dve", cm[:, c0:512], cm[:, c0:512], sp[:, c0:512], ALU.add, [cm, sp], [cm])

            def emit_at(i):
                g, kb, c0, ti, nt = tiles[i]
                B = PS[3 + i % 2]
                at = AT[i % 3]
                S.act(at[:, c0:512], B[:, c0:512], AF.Exp, [B], [at])
                if kb >= 4 * g:
                    S.tt("pool", at[:, c0:c0 + 128], at[:, c0:c0 + 128], self.cmbf[:, 2, :], ALU.mult,
                         [at, self.cmbf], [at])

            def emit_O(i):
                g, kb, c0, ti, nt = tiles[i]
                at = AT[i % 3]
                O = PS[5 + g % 2]
                S.mm(O[:, c0:512], Vt[:, kb, :], at[:, c0:512], [Vt, at], [O], start=False, stop=(ti == nt - 1))
                if ti == nt - 1:
                    o = ob[g % 2]
                    S.copy("dve", o[:], O[:], [O], [o])
                    S.dma("sp", self.Y["YC"].t[hs, g * 512:(g + 1) * 512], o[:], [o], [self.Y["YC"]])

            emit_A(0)
            if n > 1:
                emit_A(1)
            emit_sp(0)
            for i in range(n):
                if i + 2 < n:
                    emit_A(i + 2)
                if i + 1 < n:
                    emit_sp(i + 1)
                emit_B(i)
                emit_at(i)
                if i >= 1:
                    emit_O(i - 1)
            emit_O(n - 1)
            S.barrier()


def host_consts():
    p = np.arange(128)[:, None]
    f = np.arange(128)[None, :]
    cm = np.zeros((128, 9, 128), np.float32)
    cm[:, 0, :] = (p == f)
    cm[:, 1, :] = (f >= p)
    cm[:, 2, :] = (f > p)
    cm[:, 3, :] = (p >= f)
    same = (p // 64) == (f // 64)
    cm[:, 4, :] = (p <= f) & same
    cm[:, 5, :] = (p > f) & same
    cm[:, 6, :] = np.where((p > f) & same, 0.0, NEG)
    cm[:, 7, :] = np.where((f > p) & same, 0.0, NEG)
    cm[:, 8, :] = np.where((f >= p) & same, 0.0, NEG)
    return cm


def pack_cv(inp, L):
    cv = np.zeros((L, 128, NCV), np.float32)

    def chunks(v):
        return np.ascontiguousarray(v.reshape(-1, 128).T)
    for l in range(L):
        c = cv[l]
        c[:, CV["norm_mix"]:CV["norm_mix"] + 8] = chunks(inp["norm_mix"][l])
        c[:, CV["gate_bias"]:CV["gate_bias"] + 24] = chunks(inp["gate_bias"][l])
        c[:, CV["fox_qnorm"]] = inp["fox_qnorm"][l]
        c[:, CV["fox_knorm"]] = inp["fox_knorm"][l]
        gc = inp["gdn_conv"][l]
        for tap in range(4):
            c[:, CV["gdn_conv"] + tap * 12: CV["gdn_conv"] + (tap + 1) * 12] = chunks(gc[tap])
        c[:, CV["gdn_onorm"]] = inp["gdn_onorm"][l]
        c[:, CV["norm_xq"]:CV["norm_xq"] + 8] = chunks(inp["norm_xq"][l])
        c[:, CV["norm_mem"]:CV["norm_mem"] + 8] = chunks(inp["norm_mem"][l])
        c[:, CV["mq_norm"]] = inp["mq_norm"][l]
        c[:, CV["mk_norm"]] = inp["mk_norm"][l]
        c[:, CV["norm_ffn"]:CV["norm_ffn"] + 8] = chunks(inp["norm_ffn"][l])
        fc = inp["ffn_conv"][l]
        for tap in range(3):
            c[:, CV["ffn_conv"] + tap * 44: CV["ffn_conv"] + (tap + 1) * 44] = chunks(fc[tap])
        c[:, CV["ffn_conv_b"]:CV["ffn_conv_b"] + 44] = chunks(inp["ffn_conv_b"][l])
        s = CV["small"]
        c[0:4, s] = inp["fox_fbias"][l]
        c[64:68, s + 1] = inp["gdn_dt_bias"][l]
        c[64:68, s + 2] = inp["gdn_a_log"][l]
    return cv


def prep_inputs(inp, b, T, L):
    m = {}
    m["xT"] = np.ascontiguousarray(inp["x"][b, :T].T)
    m["memT"] = np.ascontiguousarray(inp["mem"][b].T)
    for n in ("w_in", "w_oa", "w_ob", "w_oc", "w_out", "w_mq", "w_mkv", "w_mo", "w_up", "w_down"):
        m[n] = np.ascontiguousarray(inp[n][:L])
    m["cv"] = pack_cv(inp, L)
    m["cmask"] = host_consts()
    return m


_CACHE = {}


def kernel(**inputs):
    inp = {k: np.asarray(v) for k, v in inputs.items()}
    B, T, _ = inp["x"].shape
    L = inp["w_in"].shape[0]
    key = (T, L)
    nc = Builder(T, L).build()
    ncores = 8
    in_maps = [prep_inputs(inp, c % B, T, L) for c in range(B)]
    in_maps = [in_maps[c % B] for c in range(ncores)]
    res = run_bass_kernel_spmd(nc, in_maps, core_ids=list(range(ncores)))
    out = np.empty((B, T, D), np.float32)
    for b in range(B):
        out[b] = np.asarray(res.results[b]["yT"]).T
    return out
```

```python
import numpy as np
from contextlib import ExitStack
import concourse.bass as bass
import concourse.mybir as mybir
from concourse.bass_utils import run_bass_kernel_spmd

F32 = mybir.dt.float32
BF16 = mybir.dt.bfloat16
ALU = mybir.AluOpType
AF = mybir.ActivationFunctionType

D = 1024
NH = 4
DH = 128
EPS = 1e-6
MEMT = 256
DFF = 2816
OFF = dict(FQ=0, FK=512, FV=1024, FF=1536, GQ=1540, GK=2052, GV=2564, GB=3076, GA=3080,
           GZ=3084, SQ=3596, SK=4108, SV=4620, GT=5132, END=8204)
SCALE = DH ** -0.5
NEG = -30000.0

CV = {}
_o = 0
for _n, _w in [("norm_mix", 8), ("gate_bias", 24), ("fox_qnorm", 1), ("fox_knorm", 1),
               ("gdn_conv", 48), ("gdn_onorm", 1), ("norm_xq", 8), ("norm_mem", 8),
               ("mq_norm", 1), ("mk_norm", 1), ("norm_ffn", 8), ("ffn_conv", 132),
               ("ffn_conv_b", 44), ("small", 3)]:
    CV[_n] = _o
    _o += _w
NCV = _o


class Tile:
    def __init__(self, t, name, psum=False):
        self.t = t
        self.name = name
        self.psum = psum
        self.w = {}
        self.rs = {}

    def __getitem__(self, idx):
        return self.t[idx]


class Trk:
    def __init__(self):
        self.w = {}
        self.rs = {}


class Eng:
    def __init__(self, name, eng, sem):
        self.name = name
        self.eng = eng
        self.sem = sem
        self.n = 0
        self.seen = {}


class Sched:
    def __init__(self, nc, es, ndsem=12):
        self.nc = nc
        self.es = es
        self.sems = {}
        self.E = {}
        for name, eng in [("pe", nc.tensor), ("act", nc.scalar), ("dve", nc.vector),
                          ("pool", nc.gpsimd), ("sp", nc.sync)]:
            sem = es.enter_context(nc.semaphore("sem_" + name))
            self.E[name] = Eng(name, eng, sem)
            self.sems[name] = sem
        self.dpool = {}
        self.dnext = {}
        for q in ("sp", "pool", "act"):
            lst = []
            for i in range(ndsem):
                key = "d_%s_%d" % (q, i)
                sem = es.enter_context(nc.semaphore(key))
                self.sems[key] = sem
                lst.append([sem, 0, key])
            self.dpool[q] = lst
            self.dnext[q] = 0
        self.ninst = 0

    def _deps(self, reads, writes):
        toks = []
        for r in reads:
            for k, v in r.w.items():
                toks.append((k, v, True))
            if getattr(r, "psum", False):
                for k, v in r.rs.items():
                    toks.append((k, v, False))
        for w in writes:
            for k, v in w.w.items():
                toks.append((k, v, False))
            for k, v in w.rs.items():
                toks.append((k, v, False))
        return toks

    def _wait(self, E, toks):
        for key, val, raw in toks:
            if key == E.name:
                if not raw or E.name == "pe":
                    continue
            if E.seen.get(key, 0) >= val:
                continue
            E.eng.wait_ge(self.sems[key], val)
            E.seen[key] = val
            self.ninst += 1
            E.nw = getattr(E, "nw", 0) + 1

    def _mark(self, tok, reads, writes):
        k, v = tok
        for r in reads:
            if r.rs.get(k, 0) < v:
                r.rs[k] = v
        for w in writes:
            w.w[k] = v
            w.rs = {}

    def op(self, en, fn, reads, writes):
        E = self.E[en]
        self._wait(E, self._deps(reads, writes))
        ins = fn(E.eng)
        E.n += 1
        ins.then_inc(E.sem, 1)
        self.ninst += 1
        self._mark((E.name, E.n), reads, writes)

    def dma(self, q, out, in_, reads, writes, **kw):
        Q = self.E[q]
        toks = self._deps(reads, writes)
        pool = self.dpool[q]
        i = self.dnext[q]
        self.dnext[q] = (i + 1) % len(pool)
        sem, cnt, key = pool[i]
        if cnt > 0:
            toks.append((key, cnt, False))
        self._wait(Q, toks)
        Q.eng.dma_start(out=out, in_=in_, **kw).then_inc(sem, 16)
        pool[i][1] = cnt + 16
        self.ninst += 1
        self._mark((key, cnt + 16), reads, writes)

    def barrier(self):
        toks = [(n, e.n, True) for n, e in self.E.items() if e.n > 0]
        for q, pool in self.dpool.items():
            for sem, cnt, key in pool:
                if cnt > 0:
                    toks.append((key, cnt, False))
        for E in self.E.values():
            self._wait(E, toks)

    def mm(self, out, lhsT, rhs, reads, writes, start=True, stop=True):
        self.op("pe", lambda e: e.matmul(out, lhsT, rhs, start=start, stop=stop), reads, writes)

    def tr(self, out, in_, ident, reads, writes):
        self.op("pe", lambda e: e.transpose(out, in_, ident), reads, writes)

    def act(self, out, in_, func, reads, writes, bias=None, scale=None, en="act"):
        kw = {}
        if bias is not None:
            kw["bias"] = bias
        if scale is not None:
            kw["scale"] = scale
        self.op(en, lambda e: e.activation(out, in_, func, **kw), reads, writes)

    def tt(self, en, out, in0, in1, op, reads, writes):
        self.op(en, lambda e: e.tensor_tensor(out, in0, in1, op), reads, writes)

    def ts(self, en, out, in0, s1, s2, op0, op1, reads, writes):
        if op1 is None:
            self.op(en, lambda e: e.tensor_scalar(out, in0, s1, None, op0), reads, writes)
        else:
            self.op(en, lambda e: e.tensor_scalar(out, in0, s1, s2, op0, op1), reads, writes)

    def stt(self, out, in0, scalar, in1, op0, op1, reads, writes):
        self.op("dve", lambda e: e.scalar_tensor_tensor(out, in0, scalar, in1, op0, op1), reads, writes)

    def copy(self, en, out, in_, reads, writes):
        if en == "act":
            self.op(en, lambda e: e.activation(out, in_, AF.Copy), reads, writes)
        else:
            self.op(en, lambda e: e.tensor_copy(out, in_), reads, writes)

    def memset(self, en, ap, val, writes):
        self.op(en, lambda e: e.memset(ap, val), [], writes)


class _SqAlias:
    def __init__(self, base):
        self.base = base

    @property
    def w(self):
        return self.base.w

    @property
    def rs(self):
        return self.base.rs

    @rs.setter
    def rs(self, v):
        self.base.rs = v

    def __getitem__(self, idx):
        if idx == slice(None):
            return self.base.t[:, 0:8, :]
        return self.base.t[idx]


class Builder:
    def __init__(self, T, L, dbg=()):
        self.T = T
        self.L = L
        self.NG = T // 512
        self.dbg = set(dbg)
        self.nc = bass.Bass("TRN2", target_bir_lowering=False)
        self.outs = []

    def dram_in(self, name, shape, dt=F32):
        return Tile(self.nc.dram_tensor(name, list(shape), dt, kind="ExternalInput"), name)

    def dram(self, name, shape, dt):
        kind = "Internal"
        if name in self.dbg or name == "yT":
            kind = "ExternalOutput"
            self.outs.append(name)
        return Tile(self.nc.dram_tensor(name, list(shape), dt, kind=kind), name)

    def sb(self, es, name, shape, dt):
        self.uid = getattr(self, "uid", 0) + 1
        name = "%s_%d" % (name, self.uid)
        return Tile(es.enter_context(self.nc.sbuf_tensor(name, list(shape), dt)), name)

    def build(self):
        nc, T, L = self.nc, self.T, self.L
        self.inp = {}
        I = self.inp
        I["xT"] = self.dram_in("xT", [D, T])
        I["memT"] = self.dram_in("memT", [D, MEMT])
        I["w_in"] = self.dram_in("w_in", [L, D, OFF["END"]])
        I["cv"] = self.dram_in("cv", [L, 128, NCV])
        I["w_oa"] = self.dram_in("w_oa", [L, 512, D])
        I["w_ob"] = self.dram_in("w_ob", [L, 512, D])
        I["w_oc"] = self.dram_in("w_oc", [L, 512, D])
        I["w_out"] = self.dram_in("w_out", [L, D, D])
        I["w_mq"] = self.dram_in("w_mq", [L, D, 512])
        I["w_mkv"] = self.dram_in("w_mkv", [L, D, 1024])
        I["w_mo"] = self.dram_in("w_mo", [L, 512, D])
        I["w_up"] = self.dram_in("w_up", [L, D, 2 * DFF])
        I["w_down"] = self.dram_in("w_down", [L, DFF, D])
        I["cmask"] = self.dram_in("cmask", [128, 9, 128])
        self.X = [self.dram("xres0", [D, T], F32), self.dram("xres1", [D, T], F32)]
        self.yT = self.dram("yT", [D, T], F32)
        self.HT = self.dram("HT", [D, T], BF16)
        self.QK = {n: self.dram(n, [512, T], BF16) for n in
                   ("FQ", "FK", "SQ", "SK", "GQ", "GK", "GV", "GZ")}
        self.VT = {n: self.dram(n, [T, 512], BF16) for n in ("FV", "SV")}
        self.ROW = {n: self.dram(n, [4, T], F32) for n in ("CROW", "BETA", "GLOG", "EGC")}
        self.GATES = self.dram("GATES", [3072, T], BF16)
        self.Y = {n: self.dram(n, [512, T], BF16) for n in ("YA", "YB", "YC")}
        with ExitStack() as es:
            self.es = es
            self.S = Sched(nc, es)
            self.consts(es)
            for l in range(L):
                self.layer(l)
            self.finish()
        return nc

    def consts(self, es):
        S = self.S
        self.PS = [Tile(es.enter_context(self.nc.psum_tensor("ps%d" % i, [128, 512], F32)), "ps%d" % i, psum=True)
                   for i in range(8)]
        self.ones_bf = self.sb(es, "ones_bf", [128, 128], BF16)
        S.memset("dve", self.ones_bf[:], 1.0, [self.ones_bf])
        self.cm32 = self.sb(es, "cm32", [128, 9, 128], F32)
        S.dma("sp", self.cm32[:], self.inp["cmask"][:, :, :], [self.inp["cmask"]], [self.cm32])
        self.cmbf = self.sb(es, "cmbf", [128, 9, 128], BF16)
        S.copy("dve", self.cmbf[:], self.cm32[:], [self.cm32], [self.cmbf])
        self.cvt = self.sb(es, "cvt", [128, NCV], F32)
        self.dvt = self.sb(es, "dvt", [128, 4], F32)

    def finish(self):
        S = self.S
        for n in self.dbg:
            if n.startswith("nops"):
                for i in range(int(n[4:])):
                    S.dma("sp", self.cvt[:], self.inp["cv"][0, :, :], [self.inp["cv"]], [self.cvt])
        S.barrier()

    def layer(self, l):
        S = self.S
        S.barrier()
        S.dma("sp", self.cvt[:], self.inp["cv"][l, :, :], [self.inp["cv"]], [self.cvt])
        cs = CV["small"]
        S.ts("dve", self.dvt[:, 0:1], self.cvt[:, cs:cs + 1], -1.0, None, ALU.mult, None, [self.cvt], [self.dvt])
        S.act(self.dvt[:, 1:2], self.cvt[:, cs + 2:cs + 3], AF.Exp, [self.cvt], [self.dvt])
        S.ts("dve", self.dvt[:, 1:2], self.dvt[:, 1:2], -1.0, None, ALU.mult, None, [self.dvt], [self.dvt])
        xin = self.inp["xT"] if l == 0 else self.X[1]
        if "noA" not in self.dbg:
            self.pass_a(l, xin)
        nh = 2 if "nh2" in self.dbg else (1 if "nh1" in self.dbg else NH)
        if "nofox" not in self.dbg:
            for h in range(nh):
                self.fox_head(h)
        if "nosb" not in self.dbg:
            for h in range(nh):
                self.sb_head(h)
        if "noB" not in self.dbg:
            self.pass_b(l)
        if "nogdn" not in self.dbg:
            for h in range(nh):
                self.gdn_head(h)
        if "noC" not in self.dbg:
            self.pass_c(l, xin, self.X[0])
        if "noD" not in self.dbg:
            self.pass_d(l, self.X[0], self.yT if l == self.L - 1 else self.X[1])

    def load_w(self, stg, W, wcol, src, row0, c0, ncols, gain_col=None, kchunks=8, flip=[0]):
        S = self.S
        for k in range(kchunks):
            done = 0
            while done < ncols:
                n = min(1024, ncols - done)
                st = stg[flip[0] % len(stg)]
                flip[0] += 1
                S.dma("sp", st[:, 0:n], src[row0 + k * 128: row0 + (k + 1) * 128, c0 + done: c0 + done + n],
                      [], [st])
                dst = W[:, k, wcol + done: wcol + done + n]
                if gain_col is None:
                    en = "dve" if flip[0] % 2 else "pool"
                    S.copy(en, dst, st[:, 0:n], [st], [W])
                else:
                    g = self.cvt[:, gain_col + k: gain_col + k + 1]
                    if flip[0] % 2:
                        S.ts("dve", dst, st[:, 0:n], g, None, ALU.mult, None, [st, self.cvt], [W])
                    else:
                        S.act(dst, st[:, 0:n], AF.Copy, [st, self.cvt], [W], scale=g)
                done += n

    def norm_group(self, xt, hT, sq, lnv, rstd, ps):
        S = self.S
        S.act(sq[:], xt[:], AF.Square, [xt], [sq])
        for k in range(8):
            S.mm(ps[:], self.ones_bf[:], sq[:, k, :], [self.ones_bf, sq], [ps], start=(k == 0), stop=(k == 7))
        S.act(lnv[:], ps[:], AF.Ln, [ps], [lnv], bias=EPS, scale=1.0 / D)
        S.act(rstd[:], lnv[:], AF.Exp, [lnv], [rstd], scale=-0.5)
        for k in range(8):
            S.tt("dve" if k % 2 == 0 else "pool", hT[:, k, :], xt[:, k, :], rstd[:], ALU.mult, [xt, rstd], [hT])

    def pass_a(self, l, xin):
        S, T, NG = self.S, self.T, self.NG
        w_in = self.inp["w_in"]
        with ExitStack() as es:
            NWA = 2048 + 1024
            W = self.sb(es, "WA", [128, 8, NWA], BF16)
            Wsm = self.sb(es, "WAsm", [128, 8, 96], BF16)
            stg = [self.sb(es, "stgA%d" % i, [128, 1024], F32) for i in range(4)]
            S.memset("pool", Wsm[:], 0.0, [Wsm])
            gm = CV["norm_mix"]
            wl = w_in.t[l]
            for (name, wc) in (("FQ", 0), ("FK", 512), ("SQ", 1024), ("SK", 1536), ("FV", 2048), ("SV", 2560)):
                self.load_w(stg, W, wc, wl, 0, OFF[name], 512, gain_col=gm)
            for (name, wc) in (("FF", 0), ("GB", 32), ("GA", 64)):
                self.load_w(stg, Wsm, wc, wl, 0, OFF[name], 4, gain_col=gm)
            xts = [self.sb(es, "xtA%d" % i, [128, 8, 512], F32) for i in range(2)]
            hT = self.sb(es, "hTA", [128, 8, 512], BF16)
            sq = self.sb(es, "sqA", [128, 8, 512], BF16)
            lnv = self.sb(es, "lnvA", [128, 512], F32)
            rstd = self.sb(es, "rstdA", [128, 512], F32)
            sq2 = [self.sb(es, "sq2A%d" % i, [128, 512], BF16) for i in range(2)]
            ln2 = [self.sb(es, "ln2A%d" % i, [128, 512], F32) for i in range(2)]
            ob = [self.sb(es, "obA%d" % i, [128, 512], BF16) for i in range(4)]
            sm = {n: self.sb(es, "smA_" + n, [128, 512], F32) for n in ("e1", "l1", "c", "beta", "glog", "gc", "egc")}
            ones4 = self.sb(es, "ones4", [128, 512], F32)
            rmask = self.sb(es, "rmask", [128, 512], F32)
            S.memset("pool", ones4[:], 1.0, [ones4])
            S.memset("pool", rmask[:], 1.0, [rmask])
            for c in range(8):
                S.memset("pool", rmask[:, c * 64: c * 64 + 1], 0.0, [rmask])
            ccarry = self.sb(es, "ccarry", [128, 1], F32)
            S.memset("pool", ccarry[:], 0.0, [ccarry])
            xv = xin.t.rearrange("(k p) t -> p k t", p=128)
            hv = self.HT.t.rearrange("(k p) t -> p k t", p=128)
            cvs = CV["small"]
            nob = 0
            pi = 0
            S.dma("sp", xts[0][:], xv[:, :, 0:512], [xin], [xts[0]])
            for g in range(NG):
                ts_ = slice(g * 512, (g + 1) * 512)
                xt = xts[g % 2]
                if g + 1 < NG:
                    S.dma("sp", xts[(g + 1) % 2][:], xv[:, :, (g + 1) * 512:(g + 2) * 512], [xin], [xts[(g + 1) % 2]])
                self.norm_group(xt, hT, sq, lnv, rstd, self.PS[0])
                S.dma("sp", hv[:, :, ts_], hT[:], [hT], [self.HT])
                for fam, wc, kind in (("FQ", 0, "nq"), ("FK", 512, "nk"), ("SQ", 1024, "s"), ("SK", 1536, "c")):
                    for h in range(4):
                        ps = self.PS[1 + pi % 3]
                        pi += 1
                        for k in range(8):
                            S.mm(ps[:], W[:, k, wc + h * 128: wc + (h + 1) * 128], hT[:, k, :], [W, hT], [ps],
                                 start=(k == 0), stop=(k == 7))
                        o = ob[nob % 4]
                        nob += 1
                        if kind in ("nq", "nk"):
                            s2 = sq2[nob % 2]
                            l2 = ln2[nob % 2]
                            ps2 = self.PS[4 + nob % 2]
                            S.act(s2[:], ps[:], AF.Square, [ps], [s2])
                            S.mm(ps2[:], self.ones_bf[:], s2[:], [self.ones_bf, s2], [ps2])
                            S.act(l2[:], ps2[:], AF.Ln, [ps2], [l2], bias=EPS, scale=1.0 / DH)
                            S.act(l2[:], l2[:], AF.Exp, [l2], [l2], scale=-0.5)
                            gcol = CV["fox_qnorm"] if kind == "nq" else CV["fox_knorm"]
                            S.stt(o[:], ps[:], self.cvt[:, gcol:gcol + 1], l2[:], ALU.mult, ALU.mult,
                                  [ps, self.cvt, l2], [o])
                            if kind == "nq":
                                S.ts("pool", o[:], o[:], SCALE, 1.0, ALU.mult, ALU.mult, [o], [o])
                        elif kind == "s":
                            S.act(o[:], ps[:], AF.Copy, [ps], [o], scale=SCALE)
                        else:
                            S.copy("dve", o[:], ps[:], [ps], [o])
                        dst = self.QK[fam]
                        S.dma("sp", dst.t[h * 128:(h + 1) * 128, ts_], o[:], [o], [dst])
                for fam, wc in (("FV", 2048), ("SV", 2560)):
                    for sub in range(4):
                        ps = self.PS[1 + pi % 3]
                        pi += 1
                        for k in range(8):
                            S.mm(ps[:], hT[:, k, sub * 128:(sub + 1) * 128], W[:, k, wc: wc + 512], [W, hT], [ps],
                                 start=(k == 0), stop=(k == 7))
                        o = ob[nob % 4]
                        nob += 1
                        S.copy("dve" if sub % 2 else "act", o[:], ps[:], [ps], [o])
                        dst = self.VT[fam]
                        r0 = g * 512 + sub * 128
                        S.dma("sp", dst.t[r0:r0 + 128, :], o[:], [o], [dst])
                ps = self.PS[6]
                for k in range(8):
                    S.mm(ps[0:96, :], Wsm[:, k, :], hT[:, k, :], [Wsm, hT], [ps], start=(k == 0), stop=(k == 7))
                cv = self.cvt
                S.act(sm["e1"][0:4, :], ps[0:4, :], AF.Exp, [ps, self.dvt], [sm["e1"]], bias=self.dvt[0:4, 0:1], scale=-1.0)
                S.act(sm["l1"][0:4, :], sm["e1"][0:4, :], AF.Ln, [sm["e1"]], [sm["l1"]], bias=1.0)
                S.op("dve", lambda e: e.tensor_tensor_scan(sm["c"][0:4, :], ones4[0:4, :], sm["l1"][0:4, :],
                                                           ccarry[0:4, 0:1], ALU.mult, ALU.subtract),
                     [ones4, sm["l1"], ccarry], [sm["c"]])
                S.copy("dve", ccarry[0:4, :], sm["c"][0:4, 511:512], [sm["c"]], [ccarry])
                S.dma("sp", self.ROW["CROW"].t[:, ts_], sm["c"][0:4, :], [sm["c"]], [self.ROW["CROW"]])
                S.act(sm["beta"][32:36, :], ps[32:36, :], AF.Sigmoid, [ps], [sm["beta"]])
                S.dma("sp", self.ROW["BETA"].t[:, ts_], sm["beta"][32:36, :], [sm["beta"]], [self.ROW["BETA"]])
                S.act(sm["e1"][64:68, :], ps[64:68, :], AF.Exp, [ps, cv], [sm["e1"]], bias=cv[64:68, cvs + 1:cvs + 2])
                S.act(sm["l1"][64:68, :], sm["e1"][64:68, :], AF.Ln, [sm["e1"]], [sm["l1"]], bias=1.0)
                S.ts("dve", sm["glog"][64:68, :], sm["l1"][64:68, :], self.dvt[64:68, 1:2], None, ALU.mult, None,
                     [sm["l1"], self.dvt], [sm["glog"]])
                S.op("dve", lambda e: e.tensor_tensor_scan(sm["gc"][64:68, :], rmask[64:68, :], sm["glog"][64:68, :],
                                                           0.0, ALU.mult, ALU.add),
                     [rmask, sm["glog"]], [sm["gc"]])
                S.act(sm["egc"][64:68, :], sm["gc"][64:68, :], AF.Exp, [sm["gc"]], [sm["egc"]])
                S.dma("sp", self.ROW["GLOG"].t[:, ts_], sm["glog"][64:68, :], [sm["glog"]], [self.ROW["GLOG"]])
                S.dma("sp", self.ROW["EGC"].t[:, ts_], sm["egc"][64:68, :], [sm["egc"]], [self.ROW["EGC"]])
            S.barrier()


    def pass_b(self, l):
        S, T, NG = self.S, self.T, self.NG
        w_in = self.inp["w_in"]
        PS = self.PS
        cv = self.cvt
        with ExitStack() as es:
            W = self.sb(es, "WB", [128, 8, 5120], BF16)
            stg = [self.sb(es, "stgB%d" % i, [128, 1024], F32) for i in range(4)]
            gm = CV["norm_mix"]
            wl = w_in.t[l]
            self.load_w(stg, W, 0, wl, 0, OFF["GQ"], 1536, gain_col=gm)
            self.load_w(stg, W, 1536, wl, 0, OFF["GZ"], 512, gain_col=gm)
            self.load_w(stg, W, 2048, wl, 0, OFF["GT"], 3072, gain_col=gm)
            hT = [self.sb(es, "hTB%d" % i, [128, 8, 512], BF16) for i in range(2)]
            halo = self.sb(es, "haloB", [128, 12, 4], F32)
            buf = [self.sb(es, "bufB%d" % i, [128, 516], F32) for i in range(4)]
            acc = [self.sb(es, "accB%d" % i, [128, 512], F32) for i in range(4)]
            sil = [self.sb(es, "silB%d" % i, [128, 512], F32) for i in range(4)]
            s2 = [self.sb(es, "sq2B%d" % i, [128, 512], BF16) for i in range(4)]
            l2 = [self.sb(es, "ln2B%d" % i, [128, 512], F32) for i in range(4)]
            ob = [self.sb(es, "obB%d" % i, [128, 512], BF16) for i in range(6)]
            S.memset("pool", halo[:], 0.0, [halo])
            hv = self.HT.t.rearrange("(k p) t -> p k t", p=128)
            pi = 0
            nob = 0
            cw = CV["gdn_conv"]
            for g in range(NG):
                ts_ = slice(g * 512, (g + 1) * 512)
                h_ = hT[g % 2]
                S.dma("sp", h_[:], hv[:, :, ts_], [self.HT], [h_])
                pend = []
                for c in range(12):
                    ps = PS[(0, 1, 2, 5, 6)[pi % 5]]
                    pi += 1
                    for k in range(8):
                        S.mm(ps[:], W[:, k, c * 128:(c + 1) * 128], h_[:, k, :], [W, h_], [ps], start=(k == 0), stop=(k == 7))
                    b = buf[c % 4]
                    a = acc[c % 4]
                    sl = sil[c % 4]
                    S.copy("act", b[:, 0:3], halo[:, c, 0:3], [halo], [b])
                    S.copy("act", b[:, 3:515], ps[:], [ps], [b])
                    S.copy("act", halo[:, c, 0:3], b[:, 512:515], [b], [halo])
                    S.ts("dve", a[:], b[:, 0:512], cv[:, cw + c:cw + c + 1], None, ALU.mult, None, [b, cv], [a])
                    for tap in (1, 2):
                        S.stt(a[:], b[:, tap:tap + 512], cv[:, cw + tap * 12 + c: cw + tap * 12 + c + 1], a[:],
                              ALU.mult, ALU.add, [b, cv, a], [a])
                    S.stt(a[:], ps[:], cv[:, cw + 3 * 12 + c: cw + 3 * 12 + c + 1], a[:],
                          ALU.mult, ALU.add, [ps, cv, a], [a])
                    o = ob[nob % 6]
                    nob += 1
                    fam = ("GQ", "GK", "GV")[c // 4]
                    hh = c % 4
                    if fam == "GV":
                        S.act(o[:], a[:], AF.Silu, [a], [o])
                    else:
                        S.act(sl[:], a[:], AF.Silu, [a], [sl])
                        q2 = s2[c % 4]
                        S.act(q2[:], sl[:], AF.Square, [sl], [q2])

                        def tail(c=c, sl=sl, q2=q2, o=o, fam=fam, hh=hh):
                            ll = l2[c % 4]
                            ps2 = PS[(3, 4, 7)[c % 3]]
                            S.mm(ps2[:], self.ones_bf[:], q2[:], [self.ones_bf, q2], [ps2])
                            S.act(ll[:], ps2[:], AF.Ln, [ps2], [ll], bias=EPS)
                            S.act(ll[:], ll[:], AF.Exp, [ll], [ll], scale=-0.5)
                            if fam == "GQ":
                                S.stt(o[:], sl[:], SCALE, ll[:], ALU.mult, ALU.mult, [sl, ll], [o])
                            else:
                                S.tt("dve", o[:], sl[:], ll[:], ALU.mult, [sl, ll], [o])
                            S.dma("sp", self.QK[fam].t[hh * 128:(hh + 1) * 128, ts_], o[:], [o], [self.QK[fam]])
                        pend.append(tail)
                        if len(pend) > 2:
                            pend.pop(0)()
                        continue
                    S.dma("sp", self.QK[fam].t[hh * 128:(hh + 1) * 128, ts_], o[:], [o], [self.QK[fam]])
                while pend:
                    pend.pop(0)()
                for c in range(4):
                    ps = PS[(0, 1, 2, 5, 6)[pi % 5]]
                    pi += 1
                    for k in range(8):
                        S.mm(ps[:], W[:, k, 1536 + c * 128:1536 + (c + 1) * 128], h_[:, k, :], [W, h_], [ps],
                             start=(k == 0), stop=(k == 7))
                    o = ob[nob % 6]
                    nob += 1
                    S.act(o[:], ps[:], AF.Silu, [ps], [o])
                    S.dma("sp", self.QK["GZ"].t[c * 128:(c + 1) * 128, ts_], o[:], [o], [self.QK["GZ"]])
                gb = CV["gate_bias"]
                for c in range(24):
                    ps = PS[(0, 1, 2, 5, 6)[pi % 5]]
                    pi += 1
                    for k in range(8):
                        S.mm(ps[:], W[:, k, 2048 + c * 128:2048 + (c + 1) * 128], h_[:, k, :], [W, h_], [ps],
                             start=(k == 0), stop=(k == 7))
                    o = ob[nob % 6]
                    nob += 1
                    S.act(o[:], ps[:], AF.Sigmoid, [ps, cv], [o], bias=cv[:, gb + c:gb + c + 1])
                    S.dma("sp", self.GATES.t[c * 128:(c + 1) * 128, ts_], o[:], [o], [self.GATES])
            S.barrier()

    def head_norm(self, ps, ncols, gcol, out, sq, ll, ps2, post_scale=None):
        S = self.S
        S.act(sq[:, 0:ncols], ps[:, 0:ncols], AF.Square, [ps], [sq])
        S.mm(ps2[:, 0:ncols], self.ones_bf[:], sq[:, 0:ncols], [self.ones_bf, sq], [ps2])
        S.act(ll[:, 0:ncols], ps2[:, 0:ncols], AF.Ln, [ps2], [ll], bias=EPS, scale=1.0 / DH)
        S.act(ll[:, 0:ncols], ll[:, 0:ncols], AF.Exp, [ll], [ll], scale=-0.5)
        S.stt(out, ps[:, 0:ncols], self.cvt[:, gcol:gcol + 1], ll[:, 0:ncols], ALU.mult, ALU.mult,
              [ps, self.cvt, ll], [out.tile] if hasattr(out, "tile") else [])

    def pass_c(self, l, xin, xout):
        S, T, NG = self.S, self.T, self.NG
        I = self.inp
        PS = self.PS
        cv = self.cvt
        with ExitStack() as es:
            Wo = [self.sb(es, "WCo%d" % i, [128, 4, D], BF16) for i in range(3)]
            Wout = self.sb(es, "WCout", [128, 8, D], BF16)
            Wmq = self.sb(es, "WCmq", [128, 8, 512], BF16)
            Wmo = self.sb(es, "WCmo", [128, 4, D], BF16)
            Wkv = self.sb(es, "WCkv", [128, 8, D], BF16)
            stg = [self.sb(es, "stgC%d" % i, [128, 1024], F32) for i in range(2)]
            for i, n in enumerate(("w_oa", "w_ob", "w_oc")):
                self.load_w(stg, Wo[i], 0, I[n].t[l], 0, 0, D, kchunks=4)
            self.load_w(stg, Wout, 0, I["w_out"].t[l], 0, 0, D)
            self.load_w(stg, Wmq, 0, I["w_mq"].t[l], 0, 0, 512, gain_col=CV["norm_xq"])
            self.load_w(stg, Wmo, 0, I["w_mo"].t[l], 0, 0, D, kchunks=4)
            self.load_w(stg, Wkv, 0, I["w_mkv"].t[l], 0, 0, D, gain_col=CV["norm_mem"])
            xts = [self.sb(es, "xtC%d" % i, [128, 8, 512], F32) for i in range(2)]
            xt = xts[1]
            hT = self.sb(es, "hTC", [128, 8, 512], BF16)
            sq = self.sb(es, "sqC", [128, 8, 512], BF16)
            lnv = self.sb(es, "lnvC", [128, 512], F32)
            rstd = self.sb(es, "rstdC", [128, 512], F32)
            yb_ = [self.sb(es, "yC%d" % i, [128, 4, 512], BF16) for i in range(3)]
            gt = self.sb(es, "gtC", [128, 24, 512], BF16)
            tA = self.sb(es, "tAC", [128, 512], F32)
            tB = self.sb(es, "tBC", [128, 512], F32)
            mix = self.sb(es, "mixC", [128, 8, 512], BF16)
            om = self.sb(es, "omC", [128, 4, 512], BF16)
            sq2 = self.sb(es, "sq2C", [128, 512], BF16)
            ll = self.sb(es, "llC", [128, 512], F32)
            qn = self.sb(es, "qnC", [128, 512], BF16)
            pt = [self.sb(es, "ptC%d" % i, [128, 512], BF16) for i in range(2)]
            Km = self.sb(es, "KmC", [128, 4, MEMT], BF16)
            Vm = self.sb(es, "VmC", [128, 2, 512], BF16)
            mv = I["memT"].t.rearrange("(k p) t -> p k t", p=128)
            S.dma("sp", xt[:, :, 0:MEMT], mv, [I["memT"]], [xt])
            S.act(sq[:, :, 0:MEMT], xt[:, :, 0:MEMT], AF.Square, [xt], [sq])
            for k in range(8):
                S.mm(PS[0][:, 0:MEMT], self.ones_bf[:], sq[:, k, 0:MEMT], [self.ones_bf, sq], [PS[0]], start=(k == 0), stop=(k == 7))
            S.act(lnv[:, 0:MEMT], PS[0][:, 0:MEMT], AF.Ln, [PS[0]], [lnv], bias=EPS, scale=1.0 / D)
            S.act(rstd[:, 0:MEMT], lnv[:, 0:MEMT], AF.Exp, [lnv], [rstd], scale=-0.5)
            for k in range(8):
                S.tt("dve", hT[:, k, 0:MEMT], xt[:, k, 0:MEMT], rstd[:, 0:MEMT], ALU.mult, [xt, rstd], [hT])
            for h in range(4):
                ps = PS[1 + h % 2]
                for k in range(8):
                    S.mm(ps[:, 0:MEMT], Wkv[:, k, h * 128:(h + 1) * 128], hT[:, k, 0:MEMT], [Wkv, hT], [ps], start=(k == 0), stop=(k == 7))
                S.act(sq2[:, 0:MEMT], ps[:, 0:MEMT], AF.Square, [ps], [sq2])
                S.mm(PS[3][:, 0:MEMT], self.ones_bf[:], sq2[:, 0:MEMT], [self.ones_bf, sq2], [PS[3]])
                S.act(ll[:, 0:MEMT], PS[3][:, 0:MEMT], AF.Ln, [PS[3]], [ll], bias=EPS, scale=1.0 / DH)
                S.act(ll[:, 0:MEMT], ll[:, 0:MEMT], AF.Exp, [ll], [ll], scale=-0.5)
                gk = CV["mk_norm"]
                S.stt(Km[:, h, :], ps[:, 0:MEMT], cv[:, gk:gk + 1], ll[:, 0:MEMT], ALU.mult, ALU.mult, [ps, cv, ll], [Km])
            for blk in range(2):
                ps = PS[1 + blk % 2]
                for k in range(8):
                    S.mm(ps[:], hT[:, k, blk * 128:(blk + 1) * 128], Wkv[:, k, 512:1024], [Wkv, hT], [ps], start=(k == 0), stop=(k == 7))
                S.copy("act", Vm[:, blk, :], ps[:], [ps], [Vm])
            xv = xin.t.rearrange("(k p) t -> p k t", p=128)
            xo = xout.t.rearrange("(k p) t -> p k t", p=128)
            yv = [self.Y[n].t.rearrange("(h p) t -> p h t", p=128) for n in ("YA", "YB", "YC")]
            gv = self.GATES.t.rearrange("(c p) t -> p c t", p=128)
            pi = 0
            S.dma("sp", xts[0][:], xv[:, :, 0:512], [xin], [xts[0]])
            for g in range(NG):
                ts_ = slice(g * 512, (g + 1) * 512)
                xt = xts[g % 2]
                if g + 1 < NG:
                    S.dma("sp", xts[(g + 1) % 2][:], xv[:, :, (g + 1) * 512:(g + 2) * 512], [xin], [xts[(g + 1) % 2]])
                for i, n in enumerate(("YA", "YB", "YC")):
                    S.dma("sp", yb_[i][:], yv[i][:, :, ts_], [self.Y[n]], [yb_[i]])
                S.dma("sp", gt[:], gv[:, :, ts_], [self.GATES], [gt])
                for oc in range(8):
                    ocs = slice(oc * 128, (oc + 1) * 128)
                    for br in range(3):
                        ps = PS[pi % 3]
                        pi += 1
                        for hh in range(4):
                            S.mm(ps[:], Wo[br][:, hh, ocs], yb_[br][:, hh, :], [Wo[br], yb_[br]], [ps], start=(hh == 0), stop=(hh == 3))
                        if br == 0:
                            S.tt("dve", tA[:], ps[:], gt[:, oc, :], ALU.mult, [ps, gt], [tA])
                        else:
                            S.tt("dve", tB[:], ps[:], gt[:, br * 8 + oc, :], ALU.mult, [ps, gt], [tB])
                            if br == 1:
                                S.tt("pool", tA[:], tA[:], tB[:], ALU.add, [tA, tB], [tA])
                            else:
                                S.tt("pool", mix[:, oc, :], tA[:], tB[:], ALU.add, [tA, tB], [mix])
                for oc in range(8):
                    ocs = slice(oc * 128, (oc + 1) * 128)
                    ps = PS[pi % 3]
                    pi += 1
                    for k in range(8):
                        S.mm(ps[:], Wout[:, k, ocs], mix[:, k, :], [Wout, mix], [ps], start=(k == 0), stop=(k == 7))
                    S.tt("dve", xt[:, oc, :], xt[:, oc, :], ps[:], ALU.add, [xt, ps], [xt])
                self.norm_group(xt, hT, sq, lnv, rstd, PS[3])
                for h in range(4):
                    ps = PS[pi % 3]
                    pi += 1
                    for k in range(8):
                        S.mm(ps[:], Wmq[:, k, h * 128:(h + 1) * 128], hT[:, k, :], [Wmq, hT], [ps], start=(k == 0), stop=(k == 7))
                    S.act(sq2[:], ps[:], AF.Square, [ps], [sq2])
                    S.mm(PS[3][:], self.ones_bf[:], sq2[:], [self.ones_bf, sq2], [PS[3]])
                    S.act(ll[:], PS[3][:], AF.Ln, [PS[3]], [ll], bias=EPS, scale=1.0 / DH)
                    S.act(ll[:], ll[:], AF.Exp, [ll], [ll], scale=-0.5)
                    gq = CV["mq_norm"]
                    S.stt(tA[:], ps[:], cv[:, gq:gq + 1], ll[:], ALU.mult, ALU.mult, [ps, cv, ll], [tA])
                    S.ts("pool", qn[:], tA[:], SCALE, 1.0, ALU.mult, ALU.mult, [tA], [qn])
                    O = PS[4 + h % 2]
                    DN = PS[6 + h % 2]
                    for kb in range(2):
                        ps = PS[pi % 3]
                        pi += 1
                        p_ = pt[kb]
                        S.mm(ps[:], Km[:, h, kb * 128:(kb + 1) * 128], qn[:], [Km, qn], [ps])
                        S.act(p_[:], ps[:], AF.Exp, [ps], [p_])
                        S.mm(O[:], Vm[:, kb, h * 128:(h + 1) * 128], p_[:], [Vm, p_], [O], start=(kb == 0), stop=(kb == 1))
                        S.mm(DN[:], self.ones_bf[:], p_[:], [self.ones_bf, p_], [DN], start=(kb == 0), stop=(kb == 1))
                    S.act(ll[:], DN[:], AF.Ln, [DN], [ll])
                    S.act(ll[:], ll[:], AF.Exp, [ll], [ll], scale=-1.0)
                    S.tt("dve", om[:, h, :], O[:], ll[:], ALU.mult, [O, ll], [om])
                for oc in range(8):
                    ocs = slice(oc * 128, (oc + 1) * 128)
                    ps = PS[pi % 3]
                    pi += 1
                    for hh in range(4):
                        S.mm(ps[:], Wmo[:, hh, ocs], om[:, hh, :], [Wmo, om], [ps], start=(hh == 0), stop=(hh == 3))
                    S.tt("dve", xt[:, oc, :], xt[:, oc, :], ps[:], ALU.add, [xt, ps], [xt])
                S.dma("sp", xo[:, :, ts_], xt[:], [xt], [xout])
            S.barrier()

    def pass_d(self, l, xin, xout):
        S, T, NG = self.S, self.T, self.NG
        I = self.inp
        PS = self.PS
        cv = self.cvt
        NJ = DFF // 128
        with ExitStack() as es:
            Wup = self.sb(es, "WDup", [128, 8, 2 * DFF], BF16)
            Wdn = self.sb(es, "WDdn", [128, NJ, D], BF16)
            with ExitStack() as es2:
                stg = [self.sb(es2, "stgD%d" % i, [128, 1024], F32) for i in range(2)]
                self.load_w(stg, Wup, 0, I["w_up"].t[l], 0, 0, 2 * DFF, gain_col=CV["norm_ffn"])
                self.load_w(stg, Wdn, 0, I["w_down"].t[l], 0, 0, D, kchunks=NJ)
                S.barrier()
            xt = self.sb(es, "xtD", [128, 8, 512], F32)
            hT = self.sb(es, "hTD", [128, 8, 512], BF16)
            gT = self.sb(es, "gTD", [128, NJ, 512], BF16)
            lnv = self.sb(es, "lnvD", [128, 512], F32)
            rstd = self.sb(es, "rstdD", [128, 512], F32)
            halo = self.sb(es, "haloD", [128, 2 * NJ, 2], F32)
            buf = [self.sb(es, "bufD%d" % i, [128, 516], F32) for i in range(2)]
            acc = [self.sb(es, "accD%d" % i, [128, 512], F32) for i in range(2)]
            sa = self.sb(es, "saD", [128, 512], F32)
            S.memset("pool", halo[:], 0.0, [halo])
            xv = xin.t.rearrange("(k p) t -> p k t", p=128)
            xo = xout.t.rearrange("(k p) t -> p k t", p=128)
            cw = CV["ffn_conv"]
            cb = CV["ffn_conv_b"]
            pi = 0
            for g in range(NG):
                ts_ = slice(g * 512, (g + 1) * 512)
                S.dma("sp", xt[:], xv[:, :, ts_], [xin], [xt])
                self.norm_group(xt, hT, _SqAlias(gT), lnv, rstd, PS[3])
                for j in range(NJ):
                    for ab in range(2):
                        c = ab * NJ + j
                        ps = PS[(0, 1, 2, 6, 7)[pi % 5]]
                        pi += 1
                        for k in range(8):
                            S.mm(ps[:], Wup[:, k, c * 128:(c + 1) * 128], hT[:, k, :], [Wup, hT], [ps], start=(k == 0), stop=(k == 7))
                        b = buf[ab]
                        a = acc[ab]
                        S.copy("act", b[:, 0:2], halo[:, c, 0:2], [halo], [b])
                        S.copy("act", b[:, 2:514], ps[:], [ps], [b])
                        S.copy("act", halo[:, c, 0:2], b[:, 512:514], [b], [halo])
                        S.ts("dve", a[:], b[:, 0:512], cv[:, cw + c:cw + c + 1], cv[:, cb + c:cb + c + 1], ALU.mult, ALU.add,
                             [b, cv], [a])
                        S.stt(a[:], b[:, 1:513], cv[:, cw + 2 * NJ + c: cw + 2 * NJ + c + 1], a[:],
                              ALU.mult, ALU.add, [b, cv, a], [a])
                        S.stt(a[:], ps[:], cv[:, cw + 2 * 2 * NJ + c: cw + 2 * 2 * NJ + c + 1], a[:],
                              ALU.mult, ALU.add, [ps, cv, a], [a])
                    S.act(sa[:], acc[0][:], AF.Silu, [acc[0]], [sa])
                    S.tt("pool", gT[:, j, :], sa[:], acc[1][:], ALU.mult, [sa, acc[1]], [gT])
                for oc in range(8):
                    ocs = slice(oc * 128, (oc + 1) * 128)
                    ps = PS[4 + oc % 2]
                    for j in range(NJ):
                        S.mm(ps[:], Wdn[:, j, ocs], gT[:, j, :], [Wdn, gT], [ps], start=(j == 0), stop=(j == NJ - 1))
                    S.tt("dve", xt[:, oc, :], xt[:, oc, :], ps[:], ALU.add, [xt, ps], [xt])
                S.dma("sp", xo[:, :, ts_], xt[:], [xt], [xout])
            S.barrier()

    def gdn_head(self, h):
        S, T = self.S, self.T
        NP = T // 128
        PS = self.PS
        cm = self.cm32
        ident = cm[:, 0, :]
        with ExitStack() as es:
            Qf = self.sb(es, "gQ", [128, T], BF16)
            Kf = self.sb(es, "gK", [128, T], BF16)
            Vf = self.sb(es, "gV", [128, T], BF16)
            bc = self.sb(es, "gbc", [128, T], F32)
            G2 = self.sb(es, "gG2", [128, 128], F32)
            B2 = self.sb(es, "gB2", [128, 128], F32)
            gc2 = self.sb(es, "ggc2", [128, 128], F32)
            gl2 = self.sb(es, "ggl2", [128, 128], F32)
            rm2 = self.sb(es, "grm2", [128, 128], F32)
            gcol = self.sb(es, "ggcol", [128, NP], F32)
            bcol = self.sb(es, "gbcol", [128, NP], F32)
            eglc = self.sb(es, "geglc", [128, NP], F32)
            Sst = self.sb(es, "gS", [128, 128], F32)
            hs = slice(h * 128, (h + 1) * 128)
            S.dma("sp", Qf[:], self.QK["GQ"].t[hs, :], [self.QK["GQ"]], [Qf])
            S.dma("sp", Kf[:], self.QK["GK"].t[hs, :], [self.QK["GK"]], [Kf])
            S.dma("sp", Vf[:], self.QK["GV"].t[hs, :], [self.QK["GV"]], [Vf])
            R = self.ROW
            S.dma("sp", G2[0:NP, :], R["GLOG"].t[h:h + 1, :].rearrange("o (n p) -> (o n) p", p=128), [R["GLOG"]], [G2])
            S.dma("sp", B2[0:NP, :], R["BETA"].t[h:h + 1, :].rearrange("o (n p) -> (o n) p", p=128), [R["BETA"]], [B2])
            S.dma("sp", bc[:], R["EGC"].t[h:h + 1, :].partition_broadcast(128), [R["EGC"]], [bc])
            S.memset("pool", rm2[:], 1.0, [rm2])
            S.memset("pool", rm2[:, 0:1], 0.0, [rm2])
            S.memset("pool", rm2[:, 64:65], 0.0, [rm2])
            S.op("dve", lambda e: e.tensor_tensor_scan(gc2[0:NP, :], rm2[0:NP, :], G2[0:NP, :], 0.0, ALU.mult, ALU.add),
                 [rm2, G2], [gc2])
            for half in range(2):
                cs = slice(64 * half, 64 * half + 64)
                tot = gc2[0:NP, 64 * half + 63: 64 * half + 64]
                S.ts("dve", gl2[0:NP, cs], gc2[0:NP, cs], tot, -1.0, ALU.subtract, ALU.mult, [gc2], [gl2])
            S.act(gl2[0:NP, :], gl2[0:NP, :], AF.Exp, [gl2], [gl2])
            eg2 = self.sb(es, "geg2", [128, 128], F32)
            egcol = self.sb(es, "gegcol", [128, NP], F32)
            bgc = self.sb(es, "gbgc", [128, NP], F32)
            S.act(eg2[0:NP, :], gc2[0:NP, :], AF.Exp, [gc2], [eg2])
            for src, dst in ((G2, gcol), (B2, bcol), (gl2, eglc), (eg2, egcol)):
                S.tr(PS[7][:, 0:NP], src[0:NP, :], cm[0:NP, 0, 0:NP], [src, cm], [PS[7]])
                S.copy("dve", dst[:], PS[7][:, 0:NP], [PS[7]], [dst])
            S.tt("dve", bgc[:], bcol[:], egcol[:], ALU.mult, [bcol, egcol], [bgc])
            Sst2 = self.sb(es, "gS2", [128, 128], F32)
            SS = [Sst, Sst2]
            S.memset("pool", Sst[:], 0.0, [Sst])

            def f32t(n, k=5):
                return [self.sb(es, n + str(i), [128, 128], F32) for i in range(k)]
            lg = f32t("g_lg"); Dl = f32t("g_Dl"); DTs = f32t("g_DTs"); DTi = f32t("g_DTi")
            Lb = [f32t("g_L%d_" % j) for j in range(2)]
            Ub = [f32t("g_U%d_" % j) for j in range(2)]
            X = f32t("g_X"); At = f32t("g_At")
            K32 = f32t("g_K32"); V32 = f32t("g_V32"); Qg = f32t("g_Qg"); Qp = f32t("g_Qp")
            Kt = f32t("g_Kt")
            Rr = [self.sb(es, "g_R%d" % i_, [128, 256], F32) for i_ in range(5)]
            UW = [self.sb(es, "g_UW%d" % i_, [128, 256], F32) for i_ in range(5)]
            MT = [f32t("g_MT%d_" % j, 5) for j in range(2)]
            Nn = [f32t("g_N%d_" % j, 5) for j in range(2)]
            OT = [self.sb(es, "g_OT%d" % i, [128, 512], F32) for i in range(2)]
            gz = [self.sb(es, "g_gz%d" % i, [128, 512], BF16) for i in range(2)]
            sq = self.sb(es, "g_sq", [128, 512], BF16)
            ll = self.sb(es, "g_ll", [128, 512], F32)
            t32 = self.sb(es, "g_t32", [128, 512], F32)
            ob = [self.sb(es, "g_ob%d" % i, [128, 512], BF16) for i in range(2)]

            def pre(r):
                i = r % 5
                cs = slice(r * 128, (r + 1) * 128)
                PA = PS[r % 4]
                PB = PA
                q = [slice(0, 128), slice(128, 256), slice(256, 384), slice(384, 512)]
                S.ts("pool", lg[i][:], cm[:, 4, :], gcol[:, r:r + 1], 1.0, ALU.mult, ALU.mult, [cm, gcol], [lg[i]])
                yield
                S.mm(PA[:, q[0]], lg[i][:], cm[:, 5, :], [lg[i], cm], [PA], start=True, stop=False)
                S.mm(PA[:, q[0]], ident, cm[:, 6, :], [cm], [PA], start=False, stop=True)
                S.mm(PA[:, q[1]], cm[:, 5, :], lg[i][:], [lg[i], cm], [PA], start=True, stop=False)
                S.mm(PA[:, q[1]], ident, cm[:, 8, :], [cm], [PA], start=False, stop=True)
                yield
                S.act(Dl[i][:], PA[:, q[0]], AF.Exp, [PA], [Dl[i]])
                yield
                S.act(DTi[i][:], PA[:, q[1]], AF.Exp, [PA], [DTi[i]])
                yield
                S.mm(PA[:, q[2]], Kf[:, cs], Kf[:, cs], [Kf], [PA])
                S.mm(PA[:, q[3]], Kf[:, cs], Qf[:, cs], [Qf, Kf], [PA])
                yield
                L = Lb[0][i]
                U = Ub[0][i]
                S.stt(L[:], PA[:, q[2]], bcol[:, r:r + 1], Dl[i][:], ALU.mult, ALU.mult, [PA, bcol, Dl[i]], [L])
                yield
                S.tt("dve", At[i][:], PA[:, q[3]], DTi[i][:], ALU.mult, [PA, DTi[i]], [At[i]])
                yield
                S.tr(PA[:, q[0]], L[:], ident, [L, cm], [PA])
                yield
                S.copy("act", U[:], PA[:, q[0]], [PA], [U])
                yield
                S.tt("pool", X[i][:], ident, U[:], ALU.subtract, [cm, U], [X[i]])
                yield
                S.copy("pool", K32[i][:], Kf[:, cs], [Kf], [K32[i]])
                yield
                S.copy("pool", V32[i][:], Vf[:, cs], [Vf], [V32[i]])
                yield
                S.tr(PA[:, q[1]], K32[i][:], ident, [K32[i], cm], [PA])
                S.tr(PA[:, q[2]], V32[i][:], ident, [V32[i], cm], [PA])
                yield
                S.ts("dve", Kt[i][:], PA[:, q[1]], eglc[:, r:r + 1], None, ALU.mult, None, [PA, eglc], [Kt[i]])
                yield
                S.act(Rr[i][:, 128:256], PA[:, q[1]], AF.Copy, [PA, bgc], [Rr[i]], scale=bgc[:, r:r + 1])
                yield
                S.act(Rr[i][:, 0:128], PA[:, q[2]], AF.Copy, [PA, bcol], [Rr[i]], scale=bcol[:, r:r + 1])
                yield
                S.tt("pool", Qg[i][:], Qf[:, cs], bc[:, cs], ALU.mult, [Qf, bc], [Qg[i]])
                yield
                for k in range(1, 6):
                    Ln_ = Lb[k % 2][i]
                    Un_ = Ub[k % 2][i]
                    S.mm(PA[:, q[0]], U[:], L[:], [U, L], [PA])
                    if k < 5:
                        S.mm(PA[:, q[1]], L[:], U[:], [U, L], [PA])
                    yield
                    S.copy("act", Ln_[:], PA[:, q[0]], [PA], [Ln_])
                    yield
                    if k < 5:
                        S.copy("dve", Un_[:], PA[:, q[1]], [PA], [Un_])
                        yield
                    S.mm(PA[:, q[2]], Ln_[:], X[i][:], [Ln_, X[i]], [PA])
                    yield
                    S.tt("dve", X[i][:], X[i][:], PA[:, q[2]], ALU.add, [X[i], PA], [X[i]])
                    yield
                    L, U = Ln_, Un_
                S.mm(PA[:, 0:256], X[i][:], Rr[i][:], [X[i], Rr[i]], [PA])
                yield
                S.copy("act", UW[i][:], PA[:, 0:256], [PA], [UW[i]])
                yield
                S.mm(PA[:, q[3]], UW[i][:, 128:256], At[i][:], [UW[i], At[i]], [PA])
                yield
                S.tt("dve", Qp[i][:], Qg[i][:], PA[:, q[3]], ALU.subtract, [Qg[i], PA], [Qp[i]])
                yield
                for par in range(2):
                    rows = slice(64 * par, 64 * par + 64)
                    col = r * 128 + 64 * par + 63
                    S.mm(PA[:, q[2]], UW[i][rows, 128:256], Kt[i][rows, :], [UW[i], Kt[i]], [PA])
                    S.mm(PA[:, q[3]], Kt[i][rows, :], UW[i][rows, 0:128], [UW[i], Kt[i]], [PA])
                    yield
                    S.stt(MT[par][i][:], ident, bc[:, col:col + 1], PA[:, q[2]], ALU.mult, ALU.subtract,
                          [cm, bc, PA], [MT[par][i]])
                    yield
                    S.copy("act", Nn[par][i][:], PA[:, q[3]], [PA], [Nn[par][i]])
                    yield

            def scan(r):
                i = r % 5
                ot = OT[(r // 4) % 2]
                for par in range(2):
                    rows = slice(64 * par, 64 * par + 64)
                    c = 2 * r + par
                    Sp = SS[c % 2]
                    Sn = SS[(c + 1) % 2]
                    P5 = PS[5 + c % 2]
                    P7 = PS[7]
                    S.mm(P7[:, 0:64], Sp[:], Qp[i][:, rows], [Sp, Qp[i]], [P7], start=True, stop=False)
                    S.mm(P7[:, 0:64], UW[i][rows, 0:128], At[i][rows, rows], [UW[i], At[i]], [P7], start=False, stop=True)
                    S.mm(P5[:, 0:128], MT[par][i][:], Sp[:], [MT[par][i], Sp], [P5])
                    yield
                    S.tt("dve", Sn[:], P5[:, 0:128], Nn[par][i][:], ALU.add, [P5, Nn[par][i]], [Sn])
                    yield
                    oc = (r % 4) * 128 + 64 * par
                    S.copy("act", ot[:, oc:oc + 64], P7[:, 0:64], [P7], [ot])
                    yield

            def post(blk):
                ot = OT[blk % 2]
                ts_ = slice(blk * 512, (blk + 1) * 512)
                z = gz[blk % 2]
                S.dma("sp", z[:], self.QK["GZ"].t[hs, ts_], [self.QK["GZ"]], [z])
                S.act(sq[:], ot[:], AF.Square, [ot], [sq])
                S.mm(PS[4][:], self.ones_bf[:], sq[:], [self.ones_bf, sq], [PS[4]])
                S.act(ll[:], PS[4][:], AF.Ln, [PS[4]], [ll], bias=EPS, scale=1.0 / DH)
                S.act(ll[:], ll[:], AF.Exp, [ll], [ll], scale=-0.5)
                go = CV["gdn_onorm"]
                S.stt(t32[:], ot[:], self.cvt[:, go:go + 1], ll[:], ALU.mult, ALU.mult, [ot, self.cvt, ll], [t32])
                o = ob[blk % 2]
                S.tt("pool", o[:], t32[:], z[:], ALU.mult, [t32, z], [o])
                S.dma("sp", self.Y["YB"].t[hs, ts_], o[:], [o], [self.Y["YB"]])

            act = {}

            def adv(r, nops):
                if r >= NP:
                    return True
                if r not in act:
                    act[r] = pre(r)
                g_ = act[r]
                if g_ is None:
                    return True
                for _ in range(nops):
                    try:
                        next(g_)
                    except StopIteration:
                        act[r] = None
                        return True
                return False

            while not adv(0, 1000):
                pass
            for r in range(NP):
                gs = scan(r)
                sdone = False
                while True:
                    d1 = adv(r + 1, 2)
                    adv(r + 2, 2)
                    adv(r + 3, 1)
                    adv(r + 4, 1)
                    if not sdone:
                        try:
                            next(gs)
                        except StopIteration:
                            sdone = True
                    if sdone and d1:
                        break
                if r % 4 == 3:
                    post(r // 4)
            S.barrier()

    def fox_head(self, h):
        S, T, NG = self.S, self.T, self.NG
        NB = T // 128
        PS = self.PS
        with ExitStack() as es:
            Qf = self.sb(es, "fxQ", [128, T], BF16)
            Kf = self.sb(es, "fxK", [128, T], BF16)
            Vt = self.sb(es, "fxV", [128, NB, 128], BF16)
            cbc = self.sb(es, "fxcbc", [128, T], F32)
            c2 = self.sb(es, "fxc2", [128, 128], F32)
            ncc = self.sb(es, "fxncc", [128, NB], F32)
            tmp = [self.sb(es, "fxtmp%d" % i, [128, 512], F32) for i in range(2)]
            PT = [self.sb(es, "fxPT%d" % i, [128, 512], BF16) for i in range(3)]
            lnd = self.sb(es, "fxlnd", [128, 512], F32)
            ob = [self.sb(es, "fxob%d" % i, [128, 512], BF16) for i in range(2)]
            hs = slice(h * 128, (h + 1) * 128)
            S.dma("sp", Qf[:], self.QK["FQ"].t[hs, :], [self.QK["FQ"]], [Qf])
            S.dma("sp", Kf[:], self.QK["FK"].t[hs, :], [self.QK["FK"]], [Kf])
            S.dma("sp", Vt[:], self.VT["FV"].t.rearrange("(n p) c -> p n c", p=128)[:, :, hs], [self.VT["FV"]], [Vt])
            crow = self.ROW["CROW"]
            S.dma("sp", cbc[:], crow.t[h:h + 1, :].partition_broadcast(128), [crow], [cbc])
            S.dma("sp", c2[0:NB, :], crow.t[h:h + 1, :].rearrange("o (n p) -> (o n) p", p=128), [crow], [c2])
            S.tr(PS[7][:, 0:NB], c2[0:NB, :], self.cm32[0:NB, 0, 0:NB], [c2, self.cm32], [PS[7]])
            S.ts("dve", ncc[:], PS[7][:, 0:NB], -1.0, None, ALU.mult, None, [PS[7]], [ncc])
            tiles = []
            for g in range(NG):
                tl = [(kb, 0) for kb in range(4 * g)] + [(4 * g + j, 128 * j) for j in range(4)]
                for ti, (kb, c0) in enumerate(tl):
                    tiles.append((g, kb, c0, ti == 0, ti == len(tl) - 1))
            n = len(tiles)

            def emit_S(i):
                g, kb, c0, first, last = tiles[i]
                ps = PS[i % 3]
                S.mm(ps[:, c0:512], Kf[:, kb * 128:(kb + 1) * 128], Qf[:, g * 512 + c0:(g + 1) * 512], [Kf, Qf], [ps])

            def emit_mid(i):
                g, kb, c0, first, last = tiles[i]
                ps = PS[i % 3]
                tm = tmp[i % 2]
                pt = PT[i % 3]
                S.tt("dve", tm[:, c0:512], ps[:, c0:512], cbc[:, g * 512 + c0:(g + 1) * 512], ALU.add, [ps, cbc], [tm])
                S.act(pt[:, c0:512], tm[:, c0:512], AF.Exp, [tm, ncc], [pt], bias=ncc[:, kb:kb + 1])
                if kb >= 4 * g:
                    S.tt("pool", pt[:, c0:c0 + 128], pt[:, c0:c0 + 128], self.cmbf[:, 1, :], ALU.mult,
                         [pt, self.cmbf], [pt])

            def emit_PV(i):
                g, kb, c0, first, last = tiles[i]
                pt = PT[i % 3]
                O = PS[3 + g % 2]
                DN = PS[5 + g % 2]
                S.mm(O[:, c0:512], Vt[:, kb, :], pt[:, c0:512], [Vt, pt], [O], start=first, stop=last)
                S.mm(DN[:, c0:512], self.ones_bf[:], pt[:, c0:512], [self.ones_bf, pt], [DN], start=first, stop=last)
                if last:
                    S.act(lnd[:], DN[:], AF.Ln, [DN], [lnd])
                    S.act(lnd[:], lnd[:], AF.Exp, [lnd], [lnd], scale=-1.0)
                    o = ob[g % 2]
                    S.tt("dve", o[:], O[:], lnd[:], ALU.mult, [O, lnd], [o])
                    S.dma("sp", self.Y["YA"].t[hs, g * 512:(g + 1) * 512], o[:], [o], [self.Y["YA"]])

            LA = 2
            for i in range(min(LA, n)):
                emit_S(i)
            for i in range(n):
                if i + LA < n:
                    emit_S(i + LA)
                emit_mid(i)
                emit_PV(i)
            S.barrier()

    def sb_head(self, h):
        S, T, NG = self.S, self.T, self.NG
        NB = T // 128
        PS = self.PS
        with ExitStack() as es:
            Qf = self.sb(es, "sbQ", [128, T], BF16)
            Kf = self.sb(es, "sbK", [128, T], BF16)
            Vt = self.sb(es, "sbV", [128, NB, 128], BF16)
            ntri = self.sb(es, "sbntri", [128, 128], BF16)
            nones = self.sb(es, "sbnones", [128, 128], BF16)
            zeros = self.sb(es, "sbzeros", [128, 128], BF16)
            E = [self.sb(es, "sbE%d" % i, [128, 512], F32) for i in range(2)]
            SP = [self.sb(es, "sbSP%d" % i, [128, 512], BF16) for i in range(3)]
            AT = [self.sb(es, "sbAT%d" % i, [128, 512], BF16) for i in range(3)]
            cum = [self.sb(es, "sbcum%d" % i, [128, 512], BF16) for i in range(2)]
            ob = [self.sb(es, "sbob%d" % i, [128, 512], BF16) for i in range(2)]
            hs = slice(h * 128, (h + 1) * 128)
            S.dma("sp", Qf[:], self.QK["SQ"].t[hs, :], [self.QK["SQ"]], [Qf])
            S.dma("sp", Kf[:], self.QK["SK"].t[hs, :], [self.QK["SK"]], [Kf])
            S.dma("sp", Vt[:], self.VT["SV"].t.rearrange("(n p) c -> p n c", p=128)[:, :, hs], [self.VT["SV"]], [Vt])
            S.ts("dve", ntri[:], self.cm32[:, 3, :], -1.0, None, ALU.mult, None, [self.cm32], [ntri])
            S.memset("dve", nones[:], -1.0, [nones])
            S.memset("dve", zeros[:], 0.0, [zeros])
            tiles = []
            for g in range(NG):
                tl = [(4 * g + j, 128 * j) for j in (3, 2, 1, 0)] + [(kb, 0) for kb in range(4 * g - 1, -1, -1)]
                for ti, (kb, c0) in enumerate(tl):
                    tiles.append((g, kb, c0, ti, len(tl)))
            n = len(tiles)

            def emit_A(i):
                g, kb, c0, ti, nt = tiles[i]
                A = PS[i % 3]
                S.mm(A[:, c0:512], Kf[:, kb * 128:(kb + 1) * 128], Qf[:, g * 512 + c0:(g + 1) * 512], [Kf, Qf], [A])

            def emit_sp(i):
                g, kb, c0, ti, nt = tiles[i]
                A = PS[i % 3]
                e = E[i % 2]
                sp = SP[i % 3]
                S.act(e[:, c0:512], A[:, c0:512], AF.Exp, [A], [e])
                S.act(sp[:, c0:512], e[:, c0:512], AF.Ln, [e], [sp], bias=1.0)
                if kb >= 4 * g:
                    S.tt("pool", sp[:, c0:c0 + 128], sp[:, c0:c0 + 128], self.cmbf[:, 2, :], ALU.mult,
                         [sp, self.cmbf], [sp])

            def emit_B(i):
                g, kb, c0, ti, nt = tiles[i]
                B = PS[3 + i % 2]
                sp = SP[i % 3]
                cm = cum[g % 2]
                O = PS[5 + g % 2]
                q0, q1 = g * 512 + c0, (g + 1) * 512
                kT = Kf[:, kb * 128:(kb + 1) * 128]
                if ti == 0:
                    S.memset("pool", cm[:], 0.0, [cm])
                    S.mm(O[:], zeros[:], Qf[:, g * 512:(g + 1) * 512], [zeros, Qf], [O], start=True, stop=False)
                S.mm(B[:, c0:512], kT, Qf[:, q0:q1], [Kf, Qf], [B], start=True, stop=False)
                S.mm(B[:, c0:512], ntri[:], sp[:, c0:512], [ntri, sp], [B], start=False, stop=(ti == 0))
                if ti > 0:
                    S.mm(B[:, c0:512], nones[:], cm[:, c0:512], [nones, cm], [B], start=False, stop=True)
                if ti < nt - 1:
                    S.tt("dve", cm[:, c0:512], cm[:, c0:512], sp[:, c0:512], ALU.add, [cm, sp], [cm])

            def emit_at(i):
                g, kb, c0, ti, nt = tiles[i]
                B = PS[3 + i % 2]
                at = AT[i % 3]
                S.act(at[:, c0:512], B[:, c0:512], AF.Exp, [B], [at])
                if kb >= 4 * g:
                    S.tt("pool", at[:, c0:c0 + 128], at[:, c0:c0 + 128], self.cmbf[:, 2, :], ALU.mult,
                         [at, self.cmbf], [at])

            def emit_O(i):
                g, kb, c0, ti, nt = tiles[i]
                at = AT[i % 3]
                O = PS[5 + g % 2]
                S.mm(O[:, c0:512], Vt[:, kb, :], at[:, c0:512], [Vt, at], [O], start=False, stop=(ti == nt - 1))
                if ti == nt - 1:
                    o = ob[g % 2]
                    S.copy("dve", o[:], O[:], [O], [o])
                    S.dma("sp", self.Y["YC"].t[hs, g * 512:(g + 1) * 512], o[:], [o], [self.Y["YC"]])

            emit_A(0)
            if n > 1:
                emit_A(1)
            emit_sp(0)
            for i in range(n):
                if i + 2 < n:
                    emit_A(i + 2)
                if i + 1 < n:
                    emit_sp(i + 1)
                emit_B(i)
                emit_at(i)
                if i >= 1:
                    emit_O(i - 1)
            emit_O(n - 1)
            S.barrier()


def host_consts():
    p = np.arange(128)[:, None]
    f = np.arange(128)[None, :]
    cm = np.zeros((128, 9, 128), np.float32)
    cm[:, 0, :] = (p == f)
    cm[:, 1, :] = (f >= p)
    cm[:, 2, :] = (f > p)
    cm[:, 3, :] = (p >= f)
    same = (p // 64) == (f // 64)
    cm[:, 4, :] = (p <= f) & same
    cm[:, 5, :] = (p > f) & same
    cm[:, 6, :] = np.where((p > f) & same, 0.0, NEG)
    cm[:, 7, :] = np.where((f > p) & same, 0.0, NEG)
    cm[:, 8, :] = np.where((f >= p) & same, 0.0, NEG)
    return cm


def pack_cv(inp, L):
    cv = np.zeros((L, 128, NCV), np.float32)

    def chunks(v):
        return np.ascontiguousarray(v.reshape(-1, 128).T)
    for l in range(L):
        c = cv[l]
        c[:, CV["norm_mix"]:CV["norm_mix"] + 8] = chunks(inp["norm_mix"][l])
        c[:, CV["gate_bias"]:CV["gate_bias"] + 24] = chunks(inp["gate_bias"][l])
        c[:, CV["fox_qnorm"]] = inp["fox_qnorm"][l]
        c[:, CV["fox_knorm"]] = inp["fox_knorm"][l]
        gc = inp["gdn_conv"][l]
        for tap in range(4):
            c[:, CV["gdn_conv"] + tap * 12: CV["gdn_conv"] + (tap + 1) * 12] = chunks(gc[tap])
        c[:, CV["gdn_onorm"]] = inp["gdn_onorm"][l]
        c[:, CV["norm_xq"]:CV["norm_xq"] + 8] = chunks(inp["norm_xq"][l])
        c[:, CV["norm_mem"]:CV["norm_mem"] + 8] = chunks(inp["norm_mem"][l])
        c[:, CV["mq_norm"]] = inp["mq_norm"][l]
        c[:, CV["mk_norm"]] = inp["mk_norm"][l]
        c[:, CV["norm_ffn"]:CV["norm_ffn"] + 8] = chunks(inp["norm_ffn"][l])
        fc = inp["ffn_conv"][l]
        for tap in range(3):
            c[:, CV["ffn_conv"] + tap * 44: CV["ffn_conv"] + (tap + 1) * 44] = chunks(fc[tap])
        c[:, CV["ffn_conv_b"]:CV["ffn_conv_b"] + 44] = chunks(inp["ffn_conv_b"][l])
        s = CV["small"]
        c[0:4, s] = inp["fox_fbias"][l]
        c[64:68, s + 1] = inp["gdn_dt_bias"][l]
        c[64:68, s + 2] = inp["gdn_a_log"][l]
    return cv


def prep_inputs(inp, b, T, L):
    m = {}
    m["xT"] = np.ascontiguousarray(inp["x"][b, :T].T)
    m["memT"] = np.ascontiguousarray(inp["mem"][b].T)
    for n in ("w_in", "w_oa", "w_ob", "w_oc", "w_out", "w_mq", "w_mkv", "w_mo", "w_up", "w_down"):
        m[n] = np.ascontiguousarray(inp[n][:L])
    m["cv"] = pack_cv(inp, L)
    m["cmask"] = host_consts()
    return m


_CACHE = {}


def kernel(**inputs):
    inp = {k: np.asarray(v) for k, v in inputs.items()}
    B, T, _ = inp["x"].shape
    L = inp["w_in"].shape[0]
    key = (T, L)
    nc = Builder(T, L).build()
    ncores = 8
    in_maps = [prep_inputs(inp, c % B, T, L) for c in range(B)]
    in_maps = [in_maps[c % B] for c in range(ncores)]
    res = run_bass_kernel_spmd(nc, in_maps, core_ids=list(range(ncores)))
    out = np.empty((B, T, D), np.float32)
    for b in range(B):
        out[b] = np.asarray(res.results[b]["yT"]).T
    return out
```

```python
import numpy as np
from contextlib import ExitStack
import concourse.bass as bass
import concourse.mybir as mybir
from concourse.bass_utils import run_bass_kernel_spmd

F32 = mybir.dt.float32
BF16 = mybir.dt.bfloat16
ALU = mybir.AluOpType
AF = mybir.ActivationFunctionType

D = 1024
NH = 4
DH = 128
EPS = 1e-6
MEMT = 256
DFF = 2816
OFF = dict(FQ=0, FK=512, FV=1024, FF=1536, GQ=1540, GK=2052, GV=2564, GB=3076, GA=3080,
           GZ=3084, SQ=3596, SK=4108, SV=4620, GT=5132, END=8204)
SCALE = DH ** -0.5
NEG = -30000.0

CV = {}
_o = 0
for _n, _w in [("norm_mix", 8), ("gate_bias", 24), ("fox_qnorm", 1), ("fox_knorm", 1),
               ("gdn_conv", 48), ("gdn_onorm", 1), ("norm_xq", 8), ("norm_mem", 8),
               ("mq_norm", 1), ("mk_norm", 1), ("norm_ffn", 8), ("ffn_conv", 132),
               ("ffn_conv_b", 44), ("small", 3)]:
    CV[_n] = _o
    _o += _w
NCV = _o


class Tile:
    def __init__(self, t, name, psum=False):
        self.t = t
        self.name = name
        self.psum = psum
        self.w = {}
        self.rs = {}

    def __getitem__(self, idx):
        return self.t[idx]


class Trk:
    def __init__(self):
        self.w = {}
        self.rs = {}


class Eng:
    def __init__(self, name, eng, sem):
        self.name = name
        self.eng = eng
        self.sem = sem
        self.n = 0
        self.seen = {}


class Sched:
    def __init__(self, nc, es, ndsem=12):
        self.nc = nc
        self.es = es
        self.sems = {}
        self.E = {}
        for name, eng in [("pe", nc.tensor), ("act", nc.scalar), ("dve", nc.vector),
                          ("pool", nc.gpsimd), ("sp", nc.sync)]:
            sem = es.enter_context(nc.semaphore("sem_" + name))
            self.E[name] = Eng(name, eng, sem)
            self.sems[name] = sem
        self.dpool = {}
        self.dnext = {}
        for q in ("sp", "pool", "act"):
            lst = []
            for i in range(ndsem):
                key = "d_%s_%d" % (q, i)
                sem = es.enter_context(nc.semaphore(key))
                self.sems[key] = sem
                lst.append([sem, 0, key])
            self.dpool[q] = lst
            self.dnext[q] = 0
        self.ninst = 0

    def _deps(self, reads, writes):
        toks = []
        for r in reads:
            for k, v in r.w.items():
                toks.append((k, v, True))
            if getattr(r, "psum", False):
                for k, v in r.rs.items():
                    toks.append((k, v, False))
        for w in writes:
            for k, v in w.w.items():
                toks.append((k, v, False))
            for k, v in w.rs.items():
                toks.append((k, v, False))
        return toks

    def _wait(self, E, toks):
        for key, val, raw in toks:
            if key == E.name:
                if not raw or E.name == "pe":
                    continue
            if E.seen.get(key, 0) >= val:
                continue
            E.eng.wait_ge(self.sems[key], val)
            E.seen[key] = val
            self.ninst += 1
            E.nw = getattr(E, "nw", 0) + 1

    def _mark(self, tok, reads, writes):
        k, v = tok
        for r in reads:
            if r.rs.get(k, 0) < v:
                r.rs[k] = v
        for w in writes:
            w.w[k] = v
            w.rs = {}

    def op(self, en, fn, reads, writes):
        E = self.E[en]
        self._wait(E, self._deps(reads, writes))
        ins = fn(E.eng)
        E.n += 1
        ins.then_inc(E.sem, 1)
        self.ninst += 1
        self._mark((E.name, E.n), reads, writes)

    def dma(self, q, out, in_, reads, writes, **kw):
        Q = self.E[q]
        toks = self._deps(reads, writes)
        pool = self.dpool[q]
        i = self.dnext[q]
        self.dnext[q] = (i + 1) % len(pool)
        sem, cnt, key = pool[i]
        if cnt > 0:
            toks.append((key, cnt, False))
        self._wait(Q, toks)
        Q.eng.dma_start(out=out, in_=in_, **kw).then_inc(sem, 16)
        pool[i][1] = cnt + 16
        self.ninst += 1
        self._mark((key, cnt + 16), reads, writes)

    def barrier(self):
        toks = [(n, e.n, True) for n, e in self.E.items() if e.n > 0]
        for q, pool in self.dpool.items():
            for sem, cnt, key in pool:
                if cnt > 0:
                    toks.append((key, cnt, False))
        for E in self.E.values():
            self._wait(E, toks)

    def mm(self, out, lhsT, rhs, reads, writes, start=True, stop=True):
        self.op("pe", lambda e: e.matmul(out, lhsT, rhs, start=start, stop=stop), reads, writes)

    def tr(self, out, in_, ident, reads, writes):
        self.op("pe", lambda e: e.transpose(out, in_, ident), reads, writes)

    def act(self, out, in_, func, reads, writes, bias=None, scale=None, en="act"):
        kw = {}
        if bias is not None:
            kw["bias"] = bias
        if scale is not None:
            kw["scale"] = scale
        self.op(en, lambda e: e.activation(out, in_, func, **kw), reads, writes)

    def tt(self, en, out, in0, in1, op, reads, writes):
        self.op(en, lambda e: e.tensor_tensor(out, in0, in1, op), reads, writes)

    def ts(self, en, out, in0, s1, s2, op0, op1, reads, writes):
        if op1 is None:
            self.op(en, lambda e: e.tensor_scalar(out, in0, s1, None, op0), reads, writes)
        else:
            self.op(en, lambda e: e.tensor_scalar(out, in0, s1, s2, op0, op1), reads, writes)

    def stt(self, out, in0, scalar, in1, op0, op1, reads, writes):
        self.op("dve", lambda e: e.scalar_tensor_tensor(out, in0, scalar, in1, op0, op1), reads, writes)

    def copy(self, en, out, in_, reads, writes):
        if en == "act":
            self.op(en, lambda e: e.activation(out, in_, AF.Copy), reads, writes)
        else:
            self.op(en, lambda e: e.tensor_copy(out, in_), reads, writes)

    def memset(self, en, ap, val, writes):
        self.op(en, lambda e: e.memset(ap, val), [], writes)


class _SqAlias:
    def __init__(self, base):
        self.base = base

    @property
    def w(self):
        return self.base.w

    @property
    def rs(self):
        return self.base.rs

    @rs.setter
    def rs(self, v):
        self.base.rs = v

    def __getitem__(self, idx):
        if idx == slice(None):
            return self.base.t[:, 0:8, :]
        return self.base.t[idx]


class Builder:
    def __init__(self, T, L, dbg=()):
        self.T = T
        self.L = L
        self.NG = T // 512
        self.dbg = set(dbg)
        self.nc = bass.Bass("TRN2", target_bir_lowering=False)
        self.outs = []

    def dram_in(self, name, shape, dt=F32):
        return Tile(self.nc.dram_tensor(name, list(shape), dt, kind="ExternalInput"), name)

    def dram(self, name, shape, dt):
        kind = "Internal"
        if name in self.dbg or name == "yT":
            kind = "ExternalOutput"
            self.outs.append(name)
        return Tile(self.nc.dram_tensor(name, list(shape), dt, kind=kind), name)

    def sb(self, es, name, shape, dt):
        self.uid = getattr(self, "uid", 0) + 1
        name = "%s_%d" % (name, self.uid)
        return Tile(es.enter_context(self.nc.sbuf_tensor(name, list(shape), dt)), name)

    def build(self):
        nc, T, L = self.nc, self.T, self.L
        self.inp = {}
        I = self.inp
        I["xT"] = self.dram_in("xT", [D, T])
        I["memT"] = self.dram_in("memT", [D, MEMT])
        I["w_in"] = self.dram_in("w_in", [L, D, OFF["END"]])
        I["cv"] = self.dram_in("cv", [L, 128, NCV])
        I["w_oa"] = self.dram_in("w_oa", [L, 512, D])
        I["w_ob"] = self.dram_in("w_ob", [L, 512, D])
        I["w_oc"] = self.dram_in("w_oc", [L, 512, D])
        I["w_out"] = self.dram_in("w_out", [L, D, D])
        I["w_mq"] = self.dram_in("w_mq", [L, D, 512])
        I["w_mkv"] = self.dram_in("w_mkv", [L, D, 1024])
        I["w_mo"] = self.dram_in("w_mo", [L, 512, D])
        I["w_up"] = self.dram_in("w_up", [L, D, 2 * DFF])
        I["w_down"] = self.dram_in("w_down", [L, DFF, D])
        I["cmask"] = self.dram_in("cmask", [128, 9, 128])
        self.X = [self.dram("xres0", [D, T], F32), self.dram("xres1", [D, T], F32)]
        self.yT = self.dram("yT", [D, T], F32)
        self.HT = self.dram("HT", [D, T], BF16)
        self.QK = {n: self.dram(n, [512, T], BF16) for n in
                   ("FQ", "FK", "SQ", "SK", "GQ", "GK", "GV", "GZ")}
        self.VT = {n: self.dram(n, [T, 512], BF16) for n in ("FV", "SV")}
        self.ROW = {n: self.dram(n, [4, T], F32) for n in ("CROW", "BETA", "GLOG", "EGC")}
        self.GATES = self.dram("GATES", [3072, T], BF16)
        self.Y = {n: self.dram(n, [512, T], BF16) for n in ("YA", "YB", "YC")}
        with ExitStack() as es:
            self.es = es
            self.S = Sched(nc, es)
            self.consts(es)
            for l in range(L):
                self.layer(l)
            self.finish()
        return nc

    def consts(self, es):
        S = self.S
        self.PS = [Tile(es.enter_context(self.nc.psum_tensor("ps%d" % i, [128, 512], F32)), "ps%d" % i, psum=True)
                   for i in range(8)]
        self.ones_bf = self.sb(es, "ones_bf", [128, 128], BF16)
        S.memset("dve", self.ones_bf[:], 1.0, [self.ones_bf])
        self.cm32 = self.sb(es, "cm32", [128, 9, 128], F32)
        S.dma("sp", self.cm32[:], self.inp["cmask"][:, :, :], [self.inp["cmask"]], [self.cm32])
        self.cmbf = self.sb(es, "cmbf", [128, 9, 128], BF16)
        S.copy("dve", self.cmbf[:], self.cm32[:], [self.cm32], [self.cmbf])
        self.cvt = self.sb(es, "cvt", [128, NCV], F32)
        self.dvt = self.sb(es, "dvt", [128, 4], F32)

    def finish(self):
        S = self.S
        for n in self.dbg:
            if n.startswith("nops"):
                for i in range(int(n[4:])):
                    S.dma("sp", self.cvt[:], self.inp["cv"][0, :, :], [self.inp["cv"]], [self.cvt])
        S.barrier()

    def layer(self, l):
        S = self.S
        S.barrier()
        S.dma("sp", self.cvt[:], self.inp["cv"][l, :, :], [self.inp["cv"]], [self.cvt])
        cs = CV["small"]
        S.ts("dve", self.dvt[:, 0:1], self.cvt[:, cs:cs + 1], -1.0, None, ALU.mult, None, [self.cvt], [self.dvt])
        S.act(self.dvt[:, 1:2], self.cvt[:, cs + 2:cs + 3], AF.Exp, [self.cvt], [self.dvt])
        S.ts("dve", self.dvt[:, 1:2], self.dvt[:, 1:2], -1.0, None, ALU.mult, None, [self.dvt], [self.dvt])
        xin = self.inp["xT"] if l == 0 else self.X[1]
        if "noA" not in self.dbg:
            self.pass_a(l, xin)
        nh = 2 if "nh2" in self.dbg else (1 if "nh1" in self.dbg else NH)
        if "nofox" not in self.dbg:
            self.fox_all(nh)
        if "nosb" not in self.dbg:
            self.sb_all(nh)
        if "noB" not in self.dbg:
            self.pass_b(l)
        if "nogdn" not in self.dbg:
            for h in range(nh):
                self.gdn_head(h)
        if "noC" not in self.dbg:
            self.pass_c(l, xin, self.X[0])
        if "noD" not in self.dbg:
            self.pass_d(l, self.X[0], self.yT if l == self.L - 1 else self.X[1])

    def load_w(self, stg, W, wcol, src, row0, c0, ncols, gain_col=None, kchunks=8, flip=[0]):
        S = self.S
        for k in range(kchunks):
            done = 0
            while done < ncols:
                n = min(1024, ncols - done)
                st = stg[flip[0] % len(stg)]
                flip[0] += 1
                S.dma("sp", st[:, 0:n], src[row0 + k * 128: row0 + (k + 1) * 128, c0 + done: c0 + done + n],
                      [], [st])
                dst = W[:, k, wcol + done: wcol + done + n]
                if gain_col is None:
                    en = "dve" if flip[0] % 2 else "pool"
                    S.copy(en, dst, st[:, 0:n], [st], [W])
                else:
                    g = self.cvt[:, gain_col + k: gain_col + k + 1]
                    if flip[0] % 2:
                        S.ts("dve", dst, st[:, 0:n], g, None, ALU.mult, None, [st, self.cvt], [W])
                    else:
                        S.act(dst, st[:, 0:n], AF.Copy, [st, self.cvt], [W], scale=g)
                done += n

    def norm_group(self, xt, hT, sq, lnv, rstd, ps):
        S = self.S
        S.act(sq[:], xt[:], AF.Square, [xt], [sq])
        for k in range(8):
            S.mm(ps[:], self.ones_bf[:], sq[:, k, :], [self.ones_bf, sq], [ps], start=(k == 0), stop=(k == 7))
        S.act(lnv[:], ps[:], AF.Ln, [ps], [lnv], bias=EPS, scale=1.0 / D)
        S.act(rstd[:], lnv[:], AF.Exp, [lnv], [rstd], scale=-0.5)
        for k in range(8):
            S.tt("dve" if k % 2 == 0 else "pool", hT[:, k, :], xt[:, k, :], rstd[:], ALU.mult, [xt, rstd], [hT])

    def pass_a(self, l, xin):
        S, T, NG = self.S, self.T, self.NG
        w_in = self.inp["w_in"]
        with ExitStack() as es:
            NWA = 2048 + 1024
            W = self.sb(es, "WA", [128, 8, NWA], BF16)
            Wsm = self.sb(es, "WAsm", [128, 8, 96], BF16)
            stg = [self.sb(es, "stgA%d" % i, [128, 1024], F32) for i in range(4)]
            S.memset("pool", Wsm[:], 0.0, [Wsm])
            gm = CV["norm_mix"]
            wl = w_in.t[l]
            for (name, wc) in (("FQ", 0), ("FK", 512), ("SQ", 1024), ("SK", 1536), ("FV", 2048), ("SV", 2560)):
                self.load_w(stg, W, wc, wl, 0, OFF[name], 512, gain_col=gm)
            for (name, wc) in (("FF", 0), ("GB", 32), ("GA", 64)):
                self.load_w(stg, Wsm, wc, wl, 0, OFF[name], 4, gain_col=gm)
            xts = [self.sb(es, "xtA%d" % i, [128, 8, 512], F32) for i in range(2)]
            hT = self.sb(es, "hTA", [128, 8, 512], BF16)
            sq = self.sb(es, "sqA", [128, 8, 512], BF16)
            lnv = self.sb(es, "lnvA", [128, 512], F32)
            rstd = self.sb(es, "rstdA", [128, 512], F32)
            sq2 = [self.sb(es, "sq2A%d" % i, [128, 512], BF16) for i in range(2)]
            ln2 = [self.sb(es, "ln2A%d" % i, [128, 512], F32) for i in range(2)]
            ob = [self.sb(es, "obA%d" % i, [128, 512], BF16) for i in range(4)]
            sm = {n: self.sb(es, "smA_" + n, [128, 512], F32) for n in ("e1", "l1", "c", "beta", "glog", "gc", "egc")}
            ones4 = self.sb(es, "ones4", [128, 512], F32)
            rmask = self.sb(es, "rmask", [128, 512], F32)
            S.memset("pool", ones4[:], 1.0, [ones4])
            S.memset("pool", rmask[:], 1.0, [rmask])
            for c in range(8):
                S.memset("pool", rmask[:, c * 64: c * 64 + 1], 0.0, [rmask])
            ccarry = self.sb(es, "ccarry", [128, 1], F32)
            S.memset("pool", ccarry[:], 0.0, [ccarry])
            xv = xin.t.rearrange("(k p) t -> p k t", p=128)
            hv = self.HT.t.rearrange("(k p) t -> p k t", p=128)
            cvs = CV["small"]
            nob = 0
            pi = 0
            S.dma("sp", xts[0][:], xv[:, :, 0:512], [xin], [xts[0]])
            for g in range(NG):
                ts_ = slice(g * 512, (g + 1) * 512)
                xt = xts[g % 2]
                if g + 1 < NG:
                    S.dma("sp", xts[(g + 1) % 2][:], xv[:, :, (g + 1) * 512:(g + 2) * 512], [xin], [xts[(g + 1) % 2]])
                self.norm_group(xt, hT, sq, lnv, rstd, self.PS[0])
                S.dma("sp", hv[:, :, ts_], hT[:], [hT], [self.HT])
                for fam, wc, kind in (("FQ", 0, "nq"), ("FK", 512, "nk"), ("SQ", 1024, "s"), ("SK", 1536, "c")):
                    for h in range(4):
                        ps = self.PS[1 + pi % 3]
                        pi += 1
                        for k in range(8):
                            S.mm(ps[:], W[:, k, wc + h * 128: wc + (h + 1) * 128], hT[:, k, :], [W, hT], [ps],
                                 start=(k == 0), stop=(k == 7))
                        o = ob[nob % 4]
                        nob += 1
                        if kind in ("nq", "nk"):
                            s2 = sq2[nob % 2]
                            l2 = ln2[nob % 2]
                            ps2 = self.PS[4 + nob % 2]
                            S.act(s2[:], ps[:], AF.Square, [ps], [s2])
                            S.mm(ps2[:], self.ones_bf[:], s2[:], [self.ones_bf, s2], [ps2])
                            S.act(l2[:], ps2[:], AF.Ln, [ps2], [l2], bias=EPS, scale=1.0 / DH)
                            S.act(l2[:], l2[:], AF.Exp, [l2], [l2], scale=-0.5)
                            gcol = CV["fox_qnorm"] if kind == "nq" else CV["fox_knorm"]
                            S.stt(o[:], ps[:], self.cvt[:, gcol:gcol + 1], l2[:], ALU.mult, ALU.mult,
                                  [ps, self.cvt, l2], [o])
                            if kind == "nq":
                                S.ts("pool", o[:], o[:], SCALE, 1.0, ALU.mult, ALU.mult, [o], [o])
                        elif kind == "s":
                            S.act(o[:], ps[:], AF.Copy, [ps], [o], scale=SCALE)
                        else:
                            S.copy("dve", o[:], ps[:], [ps], [o])
                        dst = self.QK[fam]
                        S.dma("sp", dst.t[h * 128:(h + 1) * 128, ts_], o[:], [o], [dst])
                for fam, wc in (("FV", 2048), ("SV", 2560)):
                    for sub in range(4):
                        ps = self.PS[1 + pi % 3]
                        pi += 1
                        for k in range(8):
                            S.mm(ps[:], hT[:, k, sub * 128:(sub + 1) * 128], W[:, k, wc: wc + 512], [W, hT], [ps],
                                 start=(k == 0), stop=(k == 7))
                        o = ob[nob % 4]
                        nob += 1
                        S.copy("dve" if sub % 2 else "act", o[:], ps[:], [ps], [o])
                        dst = self.VT[fam]
                        r0 = g * 512 + sub * 128
                        S.dma("sp", dst.t[r0:r0 + 128, :], o[:], [o], [dst])
                ps = self.PS[6]
                for k in range(8):
                    S.mm(ps[0:96, :], Wsm[:, k, :], hT[:, k, :], [Wsm, hT], [ps], start=(k == 0), stop=(k == 7))
                cv = self.cvt
                S.act(sm["e1"][0:4, :], ps[0:4, :], AF.Exp, [ps, self.dvt], [sm["e1"]], bias=self.dvt[0:4, 0:1], scale=-1.0)
                S.act(sm["l1"][0:4, :], sm["e1"][0:4, :], AF.Ln, [sm["e1"]], [sm["l1"]], bias=1.0)
                S.op("dve", lambda e: e.tensor_tensor_scan(sm["c"][0:4, :], ones4[0:4, :], sm["l1"][0:4, :],
                                                           ccarry[0:4, 0:1], ALU.mult, ALU.subtract),
                     [ones4, sm["l1"], ccarry], [sm["c"]])
                S.copy("dve", ccarry[0:4, :], sm["c"][0:4, 511:512], [sm["c"]], [ccarry])
                S.dma("sp", self.ROW["CROW"].t[:, ts_], sm["c"][0:4, :], [sm["c"]], [self.ROW["CROW"]])
                S.act(sm["beta"][32:36, :], ps[32:36, :], AF.Sigmoid, [ps], [sm["beta"]])
                S.dma("sp", self.ROW["BETA"].t[:, ts_], sm["beta"][32:36, :], [sm["beta"]], [self.ROW["BETA"]])
                S.act(sm["e1"][64:68, :], ps[64:68, :], AF.Exp, [ps, cv], [sm["e1"]], bias=cv[64:68, cvs + 1:cvs + 2])
                S.act(sm["l1"][64:68, :], sm["e1"][64:68, :], AF.Ln, [sm["e1"]], [sm["l1"]], bias=1.0)
                S.ts("dve", sm["glog"][64:68, :], sm["l1"][64:68, :], self.dvt[64:68, 1:2], None, ALU.mult, None,
                     [sm["l1"], self.dvt], [sm["glog"]])
                S.op("dve", lambda e: e.tensor_tensor_scan(sm["gc"][64:68, :], rmask[64:68, :], sm["glog"][64:68, :],
                                                           0.0, ALU.mult, ALU.add),
                     [rmask, sm["glog"]], [sm["gc"]])
                S.act(sm["egc"][64:68, :], sm["gc"][64:68, :], AF.Exp, [sm["gc"]], [sm["egc"]])
                S.dma("sp", self.ROW["GLOG"].t[:, ts_], sm["glog"][64:68, :], [sm["glog"]], [self.ROW["GLOG"]])
                S.dma("sp", self.ROW["EGC"].t[:, ts_], sm["egc"][64:68, :], [sm["egc"]], [self.ROW["EGC"]])
            S.barrier()


    def pass_b(self, l):
        S, T, NG = self.S, self.T, self.NG
        w_in = self.inp["w_in"]
        PS = self.PS
        cv = self.cvt
        with ExitStack() as es:
            W = self.sb(es, "WB", [128, 8, 5120], BF16)
            stg = [self.sb(es, "stgB%d" % i, [128, 1024], F32) for i in range(4)]
            gm = CV["norm_mix"]
            wl = w_in.t[l]
            self.load_w(stg, W, 0, wl, 0, OFF["GQ"], 1536, gain_col=gm)
            self.load_w(stg, W, 1536, wl, 0, OFF["GZ"], 512, gain_col=gm)
            self.load_w(stg, W, 2048, wl, 0, OFF["GT"], 3072, gain_col=gm)
            hT = [self.sb(es, "hTB%d" % i, [128, 8, 512], BF16) for i in range(2)]
            halo = self.sb(es, "haloB", [128, 12, 4], F32)
            buf = [self.sb(es, "bufB%d" % i, [128, 516], F32) for i in range(4)]
            acc = [self.sb(es, "accB%d" % i, [128, 512], F32) for i in range(4)]
            sil = [self.sb(es, "silB%d" % i, [128, 512], F32) for i in range(4)]
            s2 = [self.sb(es, "sq2B%d" % i, [128, 512], BF16) for i in range(4)]
            l2 = [self.sb(es, "ln2B%d" % i, [128, 512], F32) for i in range(4)]
            ob = [self.sb(es, "obB%d" % i, [128, 512], BF16) for i in range(6)]
            S.memset("pool", halo[:], 0.0, [halo])
            hv = self.HT.t.rearrange("(k p) t -> p k t", p=128)
            pi = 0
            nob = 0
            cw = CV["gdn_conv"]
            for g in range(NG):
                ts_ = slice(g * 512, (g + 1) * 512)
                h_ = hT[g % 2]
                S.dma("sp", h_[:], hv[:, :, ts_], [self.HT], [h_])
                pend = []
                for c in range(12):
                    ps = PS[(0, 1, 2, 5, 6)[pi % 5]]
                    pi += 1
                    for k in range(8):
                        S.mm(ps[:], W[:, k, c * 128:(c + 1) * 128], h_[:, k, :], [W, h_], [ps], start=(k == 0), stop=(k == 7))
                    b = buf[c % 4]
                    a = acc[c % 4]
                    sl = sil[c % 4]
                    S.copy("act", b[:, 0:3], halo[:, c, 0:3], [halo], [b])
                    S.copy("act", b[:, 3:515], ps[:], [ps], [b])
                    S.copy("act", halo[:, c, 0:3], b[:, 512:515], [b], [halo])
                    S.ts("dve", a[:], b[:, 0:512], cv[:, cw + c:cw + c + 1], None, ALU.mult, None, [b, cv], [a])
                    for tap in (1, 2):
                        S.stt(a[:], b[:, tap:tap + 512], cv[:, cw + tap * 12 + c: cw + tap * 12 + c + 1], a[:],
                              ALU.mult, ALU.add, [b, cv, a], [a])
                    S.stt(a[:], ps[:], cv[:, cw + 3 * 12 + c: cw + 3 * 12 + c + 1], a[:],
                          ALU.mult, ALU.add, [ps, cv, a], [a])
                    o = ob[nob % 6]
                    nob += 1
                    fam = ("GQ", "GK", "GV")[c // 4]
                    hh = c % 4
                    if fam == "GV":
                        S.act(o[:], a[:], AF.Silu, [a], [o])
                    else:
                        S.act(sl[:], a[:], AF.Silu, [a], [sl])
                        q2 = s2[c % 4]
                        S.act(q2[:], sl[:], AF.Square, [sl], [q2])

                        def tail(c=c, sl=sl, q2=q2, o=o, fam=fam, hh=hh):
                            ll = l2[c % 4]
                            ps2 = PS[(3, 4, 7)[c % 3]]
                            S.mm(ps2[:], self.ones_bf[:], q2[:], [self.ones_bf, q2], [ps2])
                            S.act(ll[:], ps2[:], AF.Ln, [ps2], [ll], bias=EPS)
                            S.act(ll[:], ll[:], AF.Exp, [ll], [ll], scale=-0.5)
                            if fam == "GQ":
                                S.stt(o[:], sl[:], SCALE, ll[:], ALU.mult, ALU.mult, [sl, ll], [o])
                            else:
                                S.tt("dve", o[:], sl[:], ll[:], ALU.mult, [sl, ll], [o])
                            S.dma("sp", self.QK[fam].t[hh * 128:(hh + 1) * 128, ts_], o[:], [o], [self.QK[fam]])
                        pend.append(tail)
                        if len(pend) > 2:
                            pend.pop(0)()
                        continue
                    S.dma("sp", self.QK[fam].t[hh * 128:(hh + 1) * 128, ts_], o[:], [o], [self.QK[fam]])
                while pend:
                    pend.pop(0)()
                for c in range(4):
                    ps = PS[(0, 1, 2, 5, 6)[pi % 5]]
                    pi += 1
                    for k in range(8):
                        S.mm(ps[:], W[:, k, 1536 + c * 128:1536 + (c + 1) * 128], h_[:, k, :], [W, h_], [ps],
                             start=(k == 0), stop=(k == 7))
                    o = ob[nob % 6]
                    nob += 1
                    S.act(o[:], ps[:], AF.Silu, [ps], [o])
                    S.dma("sp", self.QK["GZ"].t[c * 128:(c + 1) * 128, ts_], o[:], [o], [self.QK["GZ"]])
                gb = CV["gate_bias"]
                for c in range(24):
                    ps = PS[(0, 1, 2, 5, 6)[pi % 5]]
                    pi += 1
                    for k in range(8):
                        S.mm(ps[:], W[:, k, 2048 + c * 128:2048 + (c + 1) * 128], h_[:, k, :], [W, h_], [ps],
                             start=(k == 0), stop=(k == 7))
                    o = ob[nob % 6]
                    nob += 1
                    S.act(o[:], ps[:], AF.Sigmoid, [ps, cv], [o], bias=cv[:, gb + c:gb + c + 1])
                    S.dma("sp", self.GATES.t[c * 128:(c + 1) * 128, ts_], o[:], [o], [self.GATES])
            S.barrier()

    def head_norm(self, ps, ncols, gcol, out, sq, ll, ps2, post_scale=None):
        S = self.S
        S.act(sq[:, 0:ncols], ps[:, 0:ncols], AF.Square, [ps], [sq])
        S.mm(ps2[:, 0:ncols], self.ones_bf[:], sq[:, 0:ncols], [self.ones_bf, sq], [ps2])
        S.act(ll[:, 0:ncols], ps2[:, 0:ncols], AF.Ln, [ps2], [ll], bias=EPS, scale=1.0 / DH)
        S.act(ll[:, 0:ncols], ll[:, 0:ncols], AF.Exp, [ll], [ll], scale=-0.5)
        S.stt(out, ps[:, 0:ncols], self.cvt[:, gcol:gcol + 1], ll[:, 0:ncols], ALU.mult, ALU.mult,
              [ps, self.cvt, ll], [out.tile] if hasattr(out, "tile") else [])

    def pass_c(self, l, xin, xout):
        S, T, NG = self.S, self.T, self.NG
        I = self.inp
        PS = self.PS
        cv = self.cvt
        with ExitStack() as es:
            Wo = [self.sb(es, "WCo%d" % i, [128, 4, D], BF16) for i in range(3)]
            Wout = self.sb(es, "WCout", [128, 8, D], BF16)
            Wmq = self.sb(es, "WCmq", [128, 8, 512], BF16)
            Wmo = self.sb(es, "WCmo", [128, 4, D], BF16)
            Wkv = self.sb(es, "WCkv", [128, 8, D], BF16)
            stg = [self.sb(es, "stgC%d" % i, [128, 1024], F32) for i in range(2)]
            for i, n in enumerate(("w_oa", "w_ob", "w_oc")):
                self.load_w(stg, Wo[i], 0, I[n].t[l], 0, 0, D, kchunks=4)
            self.load_w(stg, Wout, 0, I["w_out"].t[l], 0, 0, D)
            self.load_w(stg, Wmq, 0, I["w_mq"].t[l], 0, 0, 512, gain_col=CV["norm_xq"])
            self.load_w(stg, Wmo, 0, I["w_mo"].t[l], 0, 0, D, kchunks=4)
            self.load_w(stg, Wkv, 0, I["w_mkv"].t[l], 0, 0, D, gain_col=CV["norm_mem"])
            xts = [self.sb(es, "xtC%d" % i, [128, 8, 512], F32) for i in range(2)]
            xt = xts[1]
            hT = self.sb(es, "hTC", [128, 8, 512], BF16)
            sq = self.sb(es, "sqC", [128, 8, 512], BF16)
            lnv = self.sb(es, "lnvC", [128, 512], F32)
            rstd = self.sb(es, "rstdC", [128, 512], F32)
            yb_ = [self.sb(es, "yC%d" % i, [128, 4, 512], BF16) for i in range(3)]
            gt = self.sb(es, "gtC", [128, 24, 512], BF16)
            tA = self.sb(es, "tAC", [128, 512], F32)
            tB = self.sb(es, "tBC", [128, 512], F32)
            mix = self.sb(es, "mixC", [128, 8, 512], BF16)
            om = self.sb(es, "omC", [128, 4, 512], BF16)
            sq2 = self.sb(es, "sq2C", [128, 512], BF16)
            ll = self.sb(es, "llC", [128, 512], F32)
            qn = self.sb(es, "qnC", [128, 512], BF16)
            pt = [self.sb(es, "ptC%d" % i, [128, 512], BF16) for i in range(2)]
            Km = self.sb(es, "KmC", [128, 4, MEMT], BF16)
            Vm = self.sb(es, "VmC", [128, 2, 512], BF16)
            mv = I["memT"].t.rearrange("(k p) t -> p k t", p=128)
            S.dma("sp", xt[:, :, 0:MEMT], mv, [I["memT"]], [xt])
            S.act(sq[:, :, 0:MEMT], xt[:, :, 0:MEMT], AF.Square, [xt], [sq])
            for k in range(8):
                S.mm(PS[0][:, 0:MEMT], self.ones_bf[:], sq[:, k, 0:MEMT], [self.ones_bf, sq], [PS[0]], start=(k == 0), stop=(k == 7))
            S.act(lnv[:, 0:MEMT], PS[0][:, 0:MEMT], AF.Ln, [PS[0]], [lnv], bias=EPS, scale=1.0 / D)
            S.act(rstd[:, 0:MEMT], lnv[:, 0:MEMT], AF.Exp, [lnv], [rstd], scale=-0.5)
            for k in range(8):
                S.tt("dve", hT[:, k, 0:MEMT], xt[:, k, 0:MEMT], rstd[:, 0:MEMT], ALU.mult, [xt, rstd], [hT])
            for h in range(4):
                ps = PS[1 + h % 2]
                for k in range(8):
                    S.mm(ps[:, 0:MEMT], Wkv[:, k, h * 128:(h + 1) * 128], hT[:, k, 0:MEMT], [Wkv, hT], [ps], start=(k == 0), stop=(k == 7))
                S.act(sq2[:, 0:MEMT], ps[:, 0:MEMT], AF.Square, [ps], [sq2])
                S.mm(PS[3][:, 0:MEMT], self.ones_bf[:], sq2[:, 0:MEMT], [self.ones_bf, sq2], [PS[3]])
                S.act(ll[:, 0:MEMT], PS[3][:, 0:MEMT], AF.Ln, [PS[3]], [ll], bias=EPS, scale=1.0 / DH)
                S.act(ll[:, 0:MEMT], ll[:, 0:MEMT], AF.Exp, [ll], [ll], scale=-0.5)
                gk = CV["mk_norm"]
                S.stt(Km[:, h, :], ps[:, 0:MEMT], cv[:, gk:gk + 1], ll[:, 0:MEMT], ALU.mult, ALU.mult, [ps, cv, ll], [Km])
            for blk in range(2):
                ps = PS[1 + blk % 2]
                for k in range(8):
                    S.mm(ps[:], hT[:, k, blk * 128:(blk + 1) * 128], Wkv[:, k, 512:1024], [Wkv, hT], [ps], start=(k == 0), stop=(k == 7))
                S.copy("act", Vm[:, blk, :], ps[:], [ps], [Vm])
            xv = xin.t.rearrange("(k p) t -> p k t", p=128)
            xo = xout.t.rearrange("(k p) t -> p k t", p=128)
            yv = [self.Y[n].t.rearrange("(h p) t -> p h t", p=128) for n in ("YA", "YB", "YC")]
            gv = self.GATES.t.rearrange("(c p) t -> p c t", p=128)
            pi = 0
            S.dma("sp", xts[0][:], xv[:, :, 0:512], [xin], [xts[0]])
            for g in range(NG):
                ts_ = slice(g * 512, (g + 1) * 512)
                xt = xts[g % 2]
                if g + 1 < NG:
                    S.dma("sp", xts[(g + 1) % 2][:], xv[:, :, (g + 1) * 512:(g + 2) * 512], [xin], [xts[(g + 1) % 2]])
                for i, n in enumerate(("YA", "YB", "YC")):
                    S.dma("sp", yb_[i][:], yv[i][:, :, ts_], [self.Y[n]], [yb_[i]])
                S.dma("sp", gt[:], gv[:, :, ts_], [self.GATES], [gt])
                for oc in range(8):
                    ocs = slice(oc * 128, (oc + 1) * 128)
                    for br in range(3):
                        ps = PS[pi % 3]
                        pi += 1
                        for hh in range(4):
                            S.mm(ps[:], Wo[br][:, hh, ocs], yb_[br][:, hh, :], [Wo[br], yb_[br]], [ps], start=(hh == 0), stop=(hh == 3))
                        if br == 0:
                            S.tt("dve", tA[:], ps[:], gt[:, oc, :], ALU.mult, [ps, gt], [tA])
                        else:
                            S.tt("dve", tB[:], ps[:], gt[:, br * 8 + oc, :], ALU.mult, [ps, gt], [tB])
                            if br == 1:
                                S.tt("pool", tA[:], tA[:], tB[:], ALU.add, [tA, tB], [tA])
                            else:
                                S.tt("pool", mix[:, oc, :], tA[:], tB[:], ALU.add, [tA, tB], [mix])
                for oc in range(8):
                    ocs = slice(oc * 128, (oc + 1) * 128)
                    ps = PS[pi % 3]
                    pi += 1
                    for k in range(8):
                        S.mm(ps[:], Wout[:, k, ocs], mix[:, k, :], [Wout, mix], [ps], start=(k == 0), stop=(k == 7))
                    S.tt("dve", xt[:, oc, :], xt[:, oc, :], ps[:], ALU.add, [xt, ps], [xt])
                self.norm_group(xt, hT, sq, lnv, rstd, PS[3])
                for h in range(4):
                    ps = PS[pi % 3]
                    pi += 1
                    for k in range(8):
                        S.mm(ps[:], Wmq[:, k, h * 128:(h + 1) * 128], hT[:, k, :], [Wmq, hT], [ps], start=(k == 0), stop=(k == 7))
                    S.act(sq2[:], ps[:], AF.Square, [ps], [sq2])
                    S.mm(PS[3][:], self.ones_bf[:], sq2[:], [self.ones_bf, sq2], [PS[3]])
                    S.act(ll[:], PS[3][:], AF.Ln, [PS[3]], [ll], bias=EPS, scale=1.0 / DH)
                    S.act(ll[:], ll[:], AF.Exp, [ll], [ll], scale=-0.5)
                    gq = CV["mq_norm"]
                    S.stt(tA[:], ps[:], cv[:, gq:gq + 1], ll[:], ALU.mult, ALU.mult, [ps, cv, ll], [tA])
                    S.ts("pool", qn[:], tA[:], SCALE, 1.0, ALU.mult, ALU.mult, [tA], [qn])
                    O = PS[4 + h % 2]
                    DN = PS[6 + h % 2]
                    for kb in range(2):
                        ps = PS[pi % 3]
                        pi += 1
                        p_ = pt[kb]
                        S.mm(ps[:], Km[:, h, kb * 128:(kb + 1) * 128], qn[:], [Km, qn], [ps])
                        S.act(p_[:], ps[:], AF.Exp, [ps], [p_])
                        S.mm(O[:], Vm[:, kb, h * 128:(h + 1) * 128], p_[:], [Vm, p_], [O], start=(kb == 0), stop=(kb == 1))
                        S.mm(DN[:], self.ones_bf[:], p_[:], [self.ones_bf, p_], [DN], start=(kb == 0), stop=(kb == 1))
                    S.act(ll[:], DN[:], AF.Ln, [DN], [ll])
                    S.act(ll[:], ll[:], AF.Exp, [ll], [ll], scale=-1.0)
                    S.tt("dve", om[:, h, :], O[:], ll[:], ALU.mult, [O, ll], [om])
                for oc in range(8):
                    ocs = slice(oc * 128, (oc + 1) * 128)
                    ps = PS[pi % 3]
                    pi += 1
                    for hh in range(4):
                        S.mm(ps[:], Wmo[:, hh, ocs], om[:, hh, :], [Wmo, om], [ps], start=(hh == 0), stop=(hh == 3))
                    S.tt("dve", xt[:, oc, :], xt[:, oc, :], ps[:], ALU.add, [xt, ps], [xt])
                S.dma("sp", xo[:, :, ts_], xt[:], [xt], [xout])
            S.barrier()

    def pass_d(self, l, xin, xout):
        S, T, NG = self.S, self.T, self.NG
        I = self.inp
        PS = self.PS
        cv = self.cvt
        NJ = DFF // 128
        with ExitStack() as es:
            Wup = self.sb(es, "WDup", [128, 8, 2 * DFF], BF16)
            Wdn = self.sb(es, "WDdn", [128, NJ, D], BF16)
            with ExitStack() as es2:
                stg = [self.sb(es2, "stgD%d" % i, [128, 1024], F32) for i in range(2)]
                self.load_w(stg, Wup, 0, I["w_up"].t[l], 0, 0, 2 * DFF, gain_col=CV["norm_ffn"])
                self.load_w(stg, Wdn, 0, I["w_down"].t[l], 0, 0, D, kchunks=NJ)
                S.barrier()
            xt = self.sb(es, "xtD", [128, 8, 512], F32)
            hT = self.sb(es, "hTD", [128, 8, 512], BF16)
            gT = self.sb(es, "gTD", [128, NJ, 512], BF16)
            lnv = self.sb(es, "lnvD", [128, 512], F32)
            rstd = self.sb(es, "rstdD", [128, 512], F32)
            halo = self.sb(es, "haloD", [128, 2 * NJ, 2], F32)
            buf = [self.sb(es, "bufD%d" % i, [128, 516], F32) for i in range(2)]
            acc = [self.sb(es, "accD%d" % i, [128, 512], F32) for i in range(2)]
            sa = self.sb(es, "saD", [128, 512], F32)
            S.memset("pool", halo[:], 0.0, [halo])
            xv = xin.t.rearrange("(k p) t -> p k t", p=128)
            xo = xout.t.rearrange("(k p) t -> p k t", p=128)
            cw = CV["ffn_conv"]
            cb = CV["ffn_conv_b"]
            pi = 0
            for g in range(NG):
                ts_ = slice(g * 512, (g + 1) * 512)
                S.dma("sp", xt[:], xv[:, :, ts_], [xin], [xt])
                self.norm_group(xt, hT, _SqAlias(gT), lnv, rstd, PS[3])
                for j in range(NJ):
                    for ab in range(2):
                        c = ab * NJ + j
                        ps = PS[(0, 1, 2, 6, 7)[pi % 5]]
                        pi += 1
                        for k in range(8):
                            S.mm(ps[:], Wup[:, k, c * 128:(c + 1) * 128], hT[:, k, :], [Wup, hT], [ps], start=(k == 0), stop=(k == 7))
                        b = buf[ab]
                        a = acc[ab]
                        S.copy("act", b[:, 0:2], halo[:, c, 0:2], [halo], [b])
                        S.copy("act", b[:, 2:514], ps[:], [ps], [b])
                        S.copy("act", halo[:, c, 0:2], b[:, 512:514], [b], [halo])
                        S.ts("dve", a[:], b[:, 0:512], cv[:, cw + c:cw + c + 1], cv[:, cb + c:cb + c + 1], ALU.mult, ALU.add,
                             [b, cv], [a])
                        S.stt(a[:], b[:, 1:513], cv[:, cw + 2 * NJ + c: cw + 2 * NJ + c + 1], a[:],
                              ALU.mult, ALU.add, [b, cv, a], [a])
                        S.stt(a[:], ps[:], cv[:, cw + 2 * 2 * NJ + c: cw + 2 * 2 * NJ + c + 1], a[:],
                              ALU.mult, ALU.add, [ps, cv, a], [a])
                    S.act(sa[:], acc[0][:], AF.Silu, [acc[0]], [sa])
                    S.tt("pool", gT[:, j, :], sa[:], acc[1][:], ALU.mult, [sa, acc[1]], [gT])
                for oc in range(8):
                    ocs = slice(oc * 128, (oc + 1) * 128)
                    ps = PS[4 + oc % 2]
                    for j in range(NJ):
                        S.mm(ps[:], Wdn[:, j, ocs], gT[:, j, :], [Wdn, gT], [ps], start=(j == 0), stop=(j == NJ - 1))
                    S.tt("dve", xt[:, oc, :], xt[:, oc, :], ps[:], ALU.add, [xt, ps], [xt])
                S.dma("sp", xo[:, :, ts_], xt[:], [xt], [xout])
            S.barrier()

    def gdn_head(self, h):
        S, T = self.S, self.T
        NP = T // 128
        PS = self.PS
        cm = self.cm32
        ident = cm[:, 0, :]
        with ExitStack() as es:
            Qf = self.sb(es, "gQ", [128, T], BF16)
            Kf = self.sb(es, "gK", [128, T], BF16)
            Vf = self.sb(es, "gV", [128, T], BF16)
            bc = self.sb(es, "gbc", [128, T], F32)
            G2 = self.sb(es, "gG2", [128, 128], F32)
            B2 = self.sb(es, "gB2", [128, 128], F32)
            gc2 = self.sb(es, "ggc2", [128, 128], F32)
            gl2 = self.sb(es, "ggl2", [128, 128], F32)
            rm2 = self.sb(es, "grm2", [128, 128], F32)
            gcol = self.sb(es, "ggcol", [128, NP], F32)
            bcol = self.sb(es, "gbcol", [128, NP], F32)
            eglc = self.sb(es, "geglc", [128, NP], F32)
            Sst = self.sb(es, "gS", [128, 128], F32)
            hs = slice(h * 128, (h + 1) * 128)
            S.dma("sp", Qf[:], self.QK["GQ"].t[hs, :], [self.QK["GQ"]], [Qf])
            S.dma("sp", Kf[:], self.QK["GK"].t[hs, :], [self.QK["GK"]], [Kf])
            S.dma("sp", Vf[:], self.QK["GV"].t[hs, :], [self.QK["GV"]], [Vf])
            R = self.ROW
            S.dma("sp", G2[0:NP, :], R["GLOG"].t[h:h + 1, :].rearrange("o (n p) -> (o n) p", p=128), [R["GLOG"]], [G2])
            S.dma("sp", B2[0:NP, :], R["BETA"].t[h:h + 1, :].rearrange("o (n p) -> (o n) p", p=128), [R["BETA"]], [B2])
            S.dma("sp", bc[:], R["EGC"].t[h:h + 1, :].partition_broadcast(128), [R["EGC"]], [bc])
            S.memset("pool", rm2[:], 1.0, [rm2])
            S.memset("pool", rm2[:, 0:1], 0.0, [rm2])
            S.memset("pool", rm2[:, 64:65], 0.0, [rm2])
            S.op("dve", lambda e: e.tensor_tensor_scan(gc2[0:NP, :], rm2[0:NP, :], G2[0:NP, :], 0.0, ALU.mult, ALU.add),
                 [rm2, G2], [gc2])
            for half in range(2):
                cs = slice(64 * half, 64 * half + 64)
                tot = gc2[0:NP, 64 * half + 63: 64 * half + 64]
                S.ts("dve", gl2[0:NP, cs], gc2[0:NP, cs], tot, -1.0, ALU.subtract, ALU.mult, [gc2], [gl2])
            S.act(gl2[0:NP, :], gl2[0:NP, :], AF.Exp, [gl2], [gl2])
            eg2 = self.sb(es, "geg2", [128, 128], F32)
            egcol = self.sb(es, "gegcol", [128, NP], F32)
            bgc = self.sb(es, "gbgc", [128, NP], F32)
            S.act(eg2[0:NP, :], gc2[0:NP, :], AF.Exp, [gc2], [eg2])
            for src, dst in ((G2, gcol), (B2, bcol), (gl2, eglc), (eg2, egcol)):
                S.tr(PS[7][:, 0:NP], src[0:NP, :], cm[0:NP, 0, 0:NP], [src, cm], [PS[7]])
                S.copy("dve", dst[:], PS[7][:, 0:NP], [PS[7]], [dst])
            S.tt("dve", bgc[:], bcol[:], egcol[:], ALU.mult, [bcol, egcol], [bgc])
            Sst2 = self.sb(es, "gS2", [128, 128], F32)
            SS = [Sst, Sst2]
            S.memset("pool", Sst[:], 0.0, [Sst])

            def f32t(n, k=5):
                return [self.sb(es, n + str(i), [128, 128], F32) for i in range(k)]
            lg = f32t("g_lg"); Dl = f32t("g_Dl"); DTs = f32t("g_DTs"); DTi = f32t("g_DTi")
            Lb = [f32t("g_L%d_" % j) for j in range(2)]
            Ub = [f32t("g_U%d_" % j) for j in range(2)]
            X = f32t("g_X"); At = f32t("g_At")
            K32 = f32t("g_K32"); V32 = f32t("g_V32"); Qg = f32t("g_Qg"); Qp = f32t("g_Qp")
            Kt = f32t("g_Kt")
            Rr = [self.sb(es, "g_R%d" % i_, [128, 256], F32) for i_ in range(5)]
            UW = [self.sb(es, "g_UW%d" % i_, [128, 256], F32) for i_ in range(5)]
            MT = [f32t("g_MT%d_" % j, 5) for j in range(2)]
            Nn = [f32t("g_N%d_" % j, 5) for j in range(2)]
            OT = [self.sb(es, "g_OT%d" % i, [128, 512], F32) for i in range(2)]
            gz = [self.sb(es, "g_gz%d" % i, [128, 512], BF16) for i in range(2)]
            sq = self.sb(es, "g_sq", [128, 512], BF16)
            ll = self.sb(es, "g_ll", [128, 512], F32)
            t32 = self.sb(es, "g_t32", [128, 512], F32)
            ob = [self.sb(es, "g_ob%d" % i, [128, 512], BF16) for i in range(2)]

            def pre(r):
                i = r % 5
                cs = slice(r * 128, (r + 1) * 128)
                PA = PS[r % 4]
                PB = PA
                q = [slice(0, 128), slice(128, 256), slice(256, 384), slice(384, 512)]
                S.ts("pool", lg[i][:], cm[:, 4, :], gcol[:, r:r + 1], 1.0, ALU.mult, ALU.mult, [cm, gcol], [lg[i]])
                yield
                S.mm(PA[:, q[0]], lg[i][:], cm[:, 5, :], [lg[i], cm], [PA], start=True, stop=False)
                S.mm(PA[:, q[0]], ident, cm[:, 6, :], [cm], [PA], start=False, stop=True)
                S.mm(PA[:, q[1]], cm[:, 5, :], lg[i][:], [lg[i], cm], [PA], start=True, stop=False)
                S.mm(PA[:, q[1]], ident, cm[:, 8, :], [cm], [PA], start=False, stop=True)
                yield
                S.act(Dl[i][:], PA[:, q[0]], AF.Exp, [PA], [Dl[i]])
                yield
                S.act(DTi[i][:], PA[:, q[1]], AF.Exp, [PA], [DTi[i]])
                yield
                S.mm(PA[:, q[2]], Kf[:, cs], Kf[:, cs], [Kf], [PA])
                S.mm(PA[:, q[3]], Kf[:, cs], Qf[:, cs], [Qf, Kf], [PA])
                yield
                L = Lb[0][i]
                U = Ub[0][i]
                S.stt(L[:], PA[:, q[2]], bcol[:, r:r + 1], Dl[i][:], ALU.mult, ALU.mult, [PA, bcol, Dl[i]], [L])
                yield
                S.tt("dve", At[i][:], PA[:, q[3]], DTi[i][:], ALU.mult, [PA, DTi[i]], [At[i]])
                yield
                S.tr(PA[:, q[0]], L[:], ident, [L, cm], [PA])
                yield
                S.copy("act", U[:], PA[:, q[0]], [PA], [U])
                yield
                S.tt("pool", X[i][:], ident, U[:], ALU.subtract, [cm, U], [X[i]])
                yield
                S.copy("pool", K32[i][:], Kf[:, cs], [Kf], [K32[i]])
                yield
                S.copy("pool", V32[i][:], Vf[:, cs], [Vf], [V32[i]])
                yield
                S.tr(PA[:, q[1]], K32[i][:], ident, [K32[i], cm], [PA])
                S.tr(PA[:, q[2]], V32[i][:], ident, [V32[i], cm], [PA])
                yield
                S.ts("dve", Kt[i][:], PA[:, q[1]], eglc[:, r:r + 1], None, ALU.mult, None, [PA, eglc], [Kt[i]])
                yield
                S.act(Rr[i][:, 128:256], PA[:, q[1]], AF.Copy, [PA, bgc], [Rr[i]], scale=bgc[:, r:r + 1])
                yield
                S.act(Rr[i][:, 0:128], PA[:, q[2]], AF.Copy, [PA, bcol], [Rr[i]], scale=bcol[:, r:r + 1])
                yield
                S.tt("pool", Qg[i][:], Qf[:, cs], bc[:, cs], ALU.mult, [Qf, bc], [Qg[i]])
                yield
                for k in range(1, 6):
                    Ln_ = Lb[k % 2][i]
                    Un_ = Ub[k % 2][i]
                    S.mm(PA[:, q[0]], U[:], L[:], [U, L], [PA])
                    if k < 5:
                        S.mm(PA[:, q[1]], L[:], U[:], [U, L], [PA])
                    yield
                    S.copy("act", Ln_[:], PA[:, q[0]], [PA], [Ln_])
                    yield
                    if k < 5:
                        S.copy("dve", Un_[:], PA[:, q[1]], [PA], [Un_])
                        yield
                    S.mm(PA[:, q[2]], Ln_[:], X[i][:], [Ln_, X[i]], [PA])
                    yield
                    S.tt("dve", X[i][:], X[i][:], PA[:, q[2]], ALU.add, [X[i], PA], [X[i]])
                    yield
                    L, U = Ln_, Un_
                S.mm(PA[:, 0:256], X[i][:], Rr[i][:], [X[i], Rr[i]], [PA])
                yield
                S.copy("act", UW[i][:], PA[:, 0:256], [PA], [UW[i]])
                yield
                S.mm(PA[:, q[3]], UW[i][:, 128:256], At[i][:], [UW[i], At[i]], [PA])
                yield
                S.tt("dve", Qp[i][:], Qg[i][:], PA[:, q[3]], ALU.subtract, [Qg[i], PA], [Qp[i]])
                yield
                for par in range(2):
                    rows = slice(64 * par, 64 * par + 64)
                    col = r * 128 + 64 * par + 63
                    S.mm(PA[:, q[2]], UW[i][rows, 128:256], Kt[i][rows, :], [UW[i], Kt[i]], [PA])
                    S.mm(PA[:, q[3]], Kt[i][rows, :], UW[i][rows, 0:128], [UW[i], Kt[i]], [PA])
                    yield
                    S.stt(MT[par][i][:], ident, bc[:, col:col + 1], PA[:, q[2]], ALU.mult, ALU.subtract,
                          [cm, bc, PA], [MT[par][i]])
                    yield
                    S.copy("act", Nn[par][i][:], PA[:, q[3]], [PA], [Nn[par][i]])
                    yield

            def scan(r):
                i = r % 5
                ot = OT[(r // 4) % 2]
                for par in range(2):
                    rows = slice(64 * par, 64 * par + 64)
                    c = 2 * r + par
                    Sp = SS[c % 2]
                    Sn = SS[(c + 1) % 2]
                    P5 = PS[5 + c % 2]
                    P7 = PS[7]
                    S.mm(P7[:, 0:64], Sp[:], Qp[i][:, rows], [Sp, Qp[i]], [P7], start=True, stop=False)
                    S.mm(P7[:, 0:64], UW[i][rows, 0:128], At[i][rows, rows], [UW[i], At[i]], [P7], start=False, stop=True)
                    S.mm(P5[:, 0:128], MT[par][i][:], Sp[:], [MT[par][i], Sp], [P5])
                    yield
                    S.tt("dve", Sn[:], P5[:, 0:128], Nn[par][i][:], ALU.add, [P5, Nn[par][i]], [Sn])
                    yield
                    oc = (r % 4) * 128 + 64 * par
                    S.copy("act", ot[:, oc:oc + 64], P7[:, 0:64], [P7], [ot])
                    yield

            def post(blk):
                ot = OT[blk % 2]
                ts_ = slice(blk * 512, (blk + 1) * 512)
                z = gz[blk % 2]
                S.dma("sp", z[:], self.QK["GZ"].t[hs, ts_], [self.QK["GZ"]], [z])
                S.act(sq[:], ot[:], AF.Square, [ot], [sq])
                S.mm(PS[4][:], self.ones_bf[:], sq[:], [self.ones_bf, sq], [PS[4]])
                S.act(ll[:], PS[4][:], AF.Ln, [PS[4]], [ll], bias=EPS, scale=1.0 / DH)
                S.act(ll[:], ll[:], AF.Exp, [ll], [ll], scale=-0.5)
                go = CV["gdn_onorm"]
                S.stt(t32[:], ot[:], self.cvt[:, go:go + 1], ll[:], ALU.mult, ALU.mult, [ot, self.cvt, ll], [t32])
                o = ob[blk % 2]
                S.tt("pool", o[:], t32[:], z[:], ALU.mult, [t32, z], [o])
                S.dma("sp", self.Y["YB"].t[hs, ts_], o[:], [o], [self.Y["YB"]])

            act = {}

            def adv(r, nops):
                if r >= NP:
                    return True
                if r not in act:
                    act[r] = pre(r)
                g_ = act[r]
                if g_ is None:
                    return True
                for _ in range(nops):
                    try:
                        next(g_)
                    except StopIteration:
                        act[r] = None
                        return True
                return False

            while not adv(0, 1000):
                pass
            for r in range(NP):
                gs = scan(r)
                sdone = False
                while True:
                    d1 = adv(r + 1, 2)
                    adv(r + 2, 2)
                    adv(r + 3, 1)
                    adv(r + 4, 1)
                    if not sdone:
                        try:
                            next(gs)
                        except StopIteration:
                            sdone = True
                    if sdone and d1:
                        break
                if r % 4 == 3:
                    post(r // 4)
            S.barrier()

    def fox_all(self, nh):
        S, T, NG = self.S, self.T, self.NG
        NB = T // 128
        PS = self.PS
        with ExitStack() as es:
            sets = []
            for i in range(2):
                sets.append(dict(
                    Qf=self.sb(es, "fxQ", [128, T], BF16), Kf=self.sb(es, "fxK", [128, T], BF16),
                    Vt=self.sb(es, "fxV", [128, NB, 128], BF16), cbc=self.sb(es, "fxcbc", [128, T], F32),
                    c2=self.sb(es, "fxc2", [128, 128], F32), ncc=self.sb(es, "fxncc", [128, NB], F32)))
            tmp = [self.sb(es, "fxtmp%d" % i, [128, 512], F32) for i in range(2)]
            PT = [self.sb(es, "fxPT%d" % i, [128, 512], BF16) for i in range(3)]
            lnd = self.sb(es, "fxlnd", [128, 512], F32)
            ob = [self.sb(es, "fxob%d" % i, [128, 512], BF16) for i in range(2)]
            crow = self.ROW["CROW"]

            def load_dma(h):
                d = sets[h % 2]
                hs = slice(h * 128, (h + 1) * 128)
                S.dma("sp", d["c2"][0:NB, :], crow.t[h:h + 1, :].rearrange("o (n p) -> (o n) p", p=128), [crow], [d["c2"]])
                S.dma("sp", d["Qf"][:], self.QK["FQ"].t[hs, :], [self.QK["FQ"]], [d["Qf"]])
                S.dma("sp", d["Kf"][:], self.QK["FK"].t[hs, :], [self.QK["FK"]], [d["Kf"]])
                S.dma("sp", d["cbc"][:], crow.t[h:h + 1, :].partition_broadcast(128), [crow], [d["cbc"]])
                S.dma("sp", d["Vt"][:], self.VT["FV"].t.rearrange("(n p) c -> p n c", p=128)[:, :, hs], [self.VT["FV"]], [d["Vt"]])

            def load_fin(h):
                d = sets[h % 2]
                S.tr(PS[7][:, 0:NB], d["c2"][0:NB, :], self.cm32[0:NB, 0, 0:NB], [d["c2"], self.cm32], [PS[7]])
                S.ts("dve", d["ncc"][:], PS[7][:, 0:NB], -1.0, None, ALU.mult, None, [PS[7]], [d["ncc"]])

            def compute(h):
                d = sets[h % 2]
                Qf, Kf, Vt, cbc, ncc = d["Qf"], d["Kf"], d["Vt"], d["cbc"], d["ncc"]
                hs = slice(h * 128, (h + 1) * 128)
                tiles = []
                for g in range(NG):
                    tl = [(kb, 0) for kb in range(4 * g)] + [(4 * g + j, 128 * j) for j in range(4)]
                    for ti, (kb, c0) in enumerate(tl):
                        tiles.append((g, kb, c0, ti == 0, ti == len(tl) - 1))
                n = len(tiles)

                def emit_S(i):
                    g, kb, c0, first, last = tiles[i]
                    ps = PS[i % 3]
                    S.mm(ps[:, c0:512], Kf[:, kb * 128:(kb + 1) * 128], Qf[:, g * 512 + c0:(g + 1) * 512], [Kf, Qf], [ps])

                def emit_mid(i):
                    g, kb, c0, first, last = tiles[i]
                    ps = PS[i % 3]
                    tm = tmp[i % 2]
                    pt = PT[i % 3]
                    S.tt("dve", tm[:, c0:512], ps[:, c0:512], cbc[:, g * 512 + c0:(g + 1) * 512], ALU.add, [ps, cbc], [tm])
                    S.act(pt[:, c0:512], tm[:, c0:512], AF.Exp, [tm, ncc], [pt], bias=ncc[:, kb:kb + 1])
                    if kb >= 4 * g:
                        S.tt("pool", pt[:, c0:c0 + 128], pt[:, c0:c0 + 128], self.cmbf[:, 1, :], ALU.mult,
                             [pt, self.cmbf], [pt])

                def emit_PV(i):
                    g, kb, c0, first, last = tiles[i]
                    pt = PT[i % 3]
                    O = PS[3 + g % 2]
                    DN = PS[5 + g % 2]
                    S.mm(O[:, c0:512], Vt[:, kb, :], pt[:, c0:512], [Vt, pt], [O], start=first, stop=last)
                    S.mm(DN[:, c0:512], self.ones_bf[:], pt[:, c0:512], [self.ones_bf, pt], [DN], start=first, stop=last)
                    if last:
                        S.act(lnd[:], DN[:], AF.Ln, [DN], [lnd])
                        S.act(lnd[:], lnd[:], AF.Exp, [lnd], [lnd], scale=-1.0)
                        o = ob[g % 2]
                        S.tt("dve", o[:], O[:], lnd[:], ALU.mult, [O, lnd], [o])
                        S.dma("sp", self.Y["YA"].t[hs, g * 512:(g + 1) * 512], o[:], [o], [self.Y["YA"]])

                LA = 2
                for i in range(min(LA, n)):
                    emit_S(i)
                for i in range(n):
                    if i + LA < n:
                        emit_S(i + LA)
                    emit_mid(i)
                    emit_PV(i)
            load_dma(0)
            load_fin(0)
            for h in range(nh):
                if h + 1 < nh:
                    load_dma(h + 1)
                compute(h)
                if h + 1 < nh:
                    load_fin(h + 1)
            S.barrier()

    def sb_all(self, nh):
        S, T, NG = self.S, self.T, self.NG
        NB = T // 128
        PS = self.PS
        with ExitStack() as es:
            sets = []
            for i in range(2):
                sets.append(dict(
                    Qf=self.sb(es, "sbQ", [128, T], BF16), Kf=self.sb(es, "sbK", [128, T], BF16),
                    Vt=self.sb(es, "sbV", [128, NB, 128], BF16)))
            ntri = self.sb(es, "sbntri", [128, 128], BF16)
            nones = self.sb(es, "sbnones", [128, 128], BF16)
            zeros = self.sb(es, "sbzeros", [128, 128], BF16)
            E = [self.sb(es, "sbE%d" % i, [128, 512], F32) for i in range(2)]
            SP = [self.sb(es, "sbSP%d" % i, [128, 512], BF16) for i in range(3)]
            AT = [self.sb(es, "sbAT%d" % i, [128, 512], BF16) for i in range(3)]
            cum = [self.sb(es, "sbcum%d" % i, [128, 512], BF16) for i in range(2)]
            ob = [self.sb(es, "sbob%d" % i, [128, 512], BF16) for i in range(2)]
            S.ts("dve", ntri[:], self.cm32[:, 3, :], -1.0, None, ALU.mult, None, [self.cm32], [ntri])
            S.memset("dve", nones[:], -1.0, [nones])
            S.memset("dve", zeros[:], 0.0, [zeros])

            def load_dma(h):
                d = sets[h % 2]
                hs = slice(h * 128, (h + 1) * 128)
                S.dma("sp", d["Qf"][:], self.QK["SQ"].t[hs, :], [self.QK["SQ"]], [d["Qf"]])
                S.dma("sp", d["Kf"][:], self.QK["SK"].t[hs, :], [self.QK["SK"]], [d["Kf"]])
                S.dma("sp", d["Vt"][:], self.VT["SV"].t.rearrange("(n p) c -> p n c", p=128)[:, :, hs], [self.VT["SV"]], [d["Vt"]])

            def compute(h):
                d = sets[h % 2]
                Qf, Kf, Vt = d["Qf"], d["Kf"], d["Vt"]
                hs = slice(h * 128, (h + 1) * 128)
                tiles = []
                for g in range(NG):
                    tl = [(4 * g + j, 128 * j) for j in (3, 2, 1, 0)] + [(kb, 0) for kb in range(4 * g - 1, -1, -1)]
                    for ti, (kb, c0) in enumerate(tl):
                        tiles.append((g, kb, c0, ti, len(tl)))
                n = len(tiles)

                def emit_A(i):
                    g, kb, c0, ti, nt = tiles[i]
                    A = PS[i % 3]
                    S.mm(A[:, c0:512], Kf[:, kb * 128:(kb + 1) * 128], Qf[:, g * 512 + c0:(g + 1) * 512], [Kf, Qf], [A])

                def emit_sp(i):
                    g, kb, c0, ti, nt = tiles[i]
                    A = PS[i % 3]
                    e = E[i % 2]
                    sp = SP[i % 3]
                    S.act(e[:, c0:512], A[:, c0:512], AF.Exp, [A], [e])
                    S.act(sp[:, c0:512], e[:, c0:512], AF.Ln, [e], [sp], bias=1.0)
                    if kb >= 4 * g:
                        S.tt("pool", sp[:, c0:c0 + 128], sp[:, c0:c0 + 128], self.cmbf[:, 2, :], ALU.mult,
                             [sp, self.cmbf], [sp])

                def emit_B(i):
                    g, kb, c0, ti, nt = tiles[i]
                    B = PS[3 + i % 2]
                    sp = SP[i % 3]
                    cm = cum[g % 2]
                    O = PS[5 + g % 2]
                    q0, q1 = g * 512 + c0, (g + 1) * 512
                    kT = Kf[:, kb * 128:(kb + 1) * 128]
                    if ti == 0:
                        S.memset("pool", cm[:], 0.0, [cm])
                        S.mm(O[:], zeros[:], Qf[:, g * 512:(g + 1) * 512], [zeros, Qf], [O], start=True, stop=False)
                    S.mm(B[:, c0:512], kT, Qf[:, q0:q1], [Kf, Qf], [B], start=True, stop=False)
                    S.mm(B[:, c0:512], ntri[:], sp[:, c0:512], [ntri, sp], [B], start=False, stop=(ti == 0))
                    if ti > 0:
                        S.mm(B[:, c0:512], nones[:], cm[:, c0:512], [nones, cm], [B], start=False, stop=True)
                    if ti < nt - 1:
                        S.tt("dve", cm[:, c0:512], cm[:, c0:512], sp[:, c0:512], ALU.add, [cm, sp], [cm])

                def emit_at(i):
                    g, kb, c0, ti, nt = tiles[i]
                    B = PS[3 + i % 2]
                    at = AT[i % 3]
                    S.act(at[:, c0:512], B[:, c0:512], AF.Exp, [B], [at])
                    if kb >= 4 * g:
                        S.tt("pool", at[:, c0:c0 + 128], at[:, c0:c0 + 128], self.cmbf[:, 2, :], ALU.mult,
                             [at, self.cmbf], [at])

                def emit_O(i):
                    g, kb, c0, ti, nt = tiles[i]
                    at = AT[i % 3]
                    O = PS[5 + g % 2]
                    S.mm(O[:, c0:512], Vt[:, kb, :], at[:, c0:512], [Vt, at], [O], start=False, stop=(ti == nt - 1))
                    if ti == nt - 1:
                        o = ob[g % 2]
                        S.copy("dve", o[:], O[:], [O], [o])
                        S.dma("sp", self.Y["YC"].t[hs, g * 512:(g + 1) * 512], o[:], [o], [self.Y["YC"]])

                emit_A(0)
                if n > 1:
                    emit_A(1)
                emit_sp(0)
                for i in range(n):
                    if i + 2 < n:
                        emit_A(i + 2)
                    if i + 1 < n:
                        emit_sp(i + 1)
                    emit_B(i)
                    emit_at(i)
                    if i >= 1:
                        emit_O(i - 1)
                emit_O(n - 1)
            load_dma(0)
            for h in range(nh):
                if h + 1 < nh:
                    load_dma(h + 1)
                compute(h)
            S.barrier()


def host_consts():
    p = np.arange(128)[:, None]
    f = np.arange(128)[None, :]
    cm = np.zeros((128, 9, 128), np.float32)
    cm[:, 0, :] = (p == f)
    cm[:, 1, :] = (f >= p)
    cm[:, 2, :] = (f > p)
    cm[:, 3, :] = (p >= f)
    same = (p // 64) == (f // 64)
    cm[:, 4, :] = (p <= f) & same
    cm[:, 5, :] = (p > f) & same
    cm[:, 6, :] = np.where((p > f) & same, 0.0, NEG)
    cm[:, 7, :] = np.where((f > p) & same, 0.0, NEG)
    cm[:, 8, :] = np.where((f >= p) & same, 0.0, NEG)
    return cm


def pack_cv(inp, L):
    cv = np.zeros((L, 128, NCV), np.float32)

    def chunks(v):
        return np.ascontiguousarray(v.reshape(-1, 128).T)
    for l in range(L):
        c = cv[l]
        c[:, CV["norm_mix"]:CV["norm_mix"] + 8] = chunks(inp["norm_mix"][l])
        c[:, CV["gate_bias"]:CV["gate_bias"] + 24] = chunks(inp["gate_bias"][l])
        c[:, CV["fox_qnorm"]] = inp["fox_qnorm"][l]
        c[:, CV["fox_knorm"]] = inp["fox_knorm"][l]
        gc = inp["gdn_conv"][l]
        for tap in range(4):
            c[:, CV["gdn_conv"] + tap * 12: CV["gdn_conv"] + (tap + 1) * 12] = chunks(gc[tap])
        c[:, CV["gdn_onorm"]] = inp["gdn_onorm"][l]
        c[:, CV["norm_xq"]:CV["norm_xq"] + 8] = chunks(inp["norm_xq"][l])
        c[:, CV["norm_mem"]:CV["norm_mem"] + 8] = chunks(inp["norm_mem"][l])
        c[:, CV["mq_norm"]] = inp["mq_norm"][l]
        c[:, CV["mk_norm"]] = inp["mk_norm"][l]
        c[:, CV["norm_ffn"]:CV["norm_ffn"] + 8] = chunks(inp["norm_ffn"][l])
        fc = inp["ffn_conv"][l]
        for tap in range(3):
            c[:, CV["ffn_conv"] + tap * 44: CV["ffn_conv"] + (tap + 1) * 44] = chunks(fc[tap])
        c[:, CV["ffn_conv_b"]:CV["ffn_conv_b"] + 44] = chunks(inp["ffn_conv_b"][l])
        s = CV["small"]
        c[0:4, s] = inp["fox_fbias"][l]
        c[64:68, s + 1] = inp["gdn_dt_bias"][l]
        c[64:68, s + 2] = inp["gdn_a_log"][l]
    return cv


def prep_inputs(inp, b, T, L):
    m = {}
    m["xT"] = np.ascontiguousarray(inp["x"][b, :T].T)
    m["memT"] = np.ascontiguousarray(inp["mem"][b].T)
    for n in ("w_in", "w_oa", "w_ob", "w_oc", "w_out", "w_mq", "w_mkv", "w_mo", "w_up", "w_down"):
        m[n] = np.ascontiguousarray(inp[n][:L])
    m["cv"] = pack_cv(inp, L)
    m["cmask"] = host_consts()
    return m


_CACHE = {}


def kernel(**inputs):
    inp = {k: np.asarray(v) for k, v in inputs.items()}
    B, T, _ = inp["x"].shape
    L = inp["w_in"].shape[0]
    key = (T, L)
    nc = Builder(T, L).build()
    ncores = 8
    in_maps = [prep_inputs(inp, c % B, T, L) for c in range(B)]
    in_maps = [in_maps[c % B] for c in range(ncores)]
    res = run_bass_kernel_spmd(nc, in_maps, core_ids=list(range(ncores)))
    out = np.empty((B, T, D), np.float32)
    for b in range(B):
        out[b] = np.asarray(res.results[b]["yT"]).T
    return out
```

```python
import numpy as np
from contextlib import ExitStack
import concourse.bass as bass
import concourse.mybir as mybir
from concourse.bass_utils import run_bass_kernel_spmd

F32 = mybir.dt.float32
BF16 = mybir.dt.bfloat16
ALU = mybir.AluOpType
AF = mybir.ActivationFunctionType

D = 1024
NH = 4
DH = 128
EPS = 1e-6
MEMT = 256
DFF = 2816
OFF = dict(FQ=0, FK=512, FV=1024, FF=1536, GQ=1540, GK=2052, GV=2564, GB=3076, GA=3080,
           GZ=3084, SQ=3596, SK=4108, SV=4620, GT=5132, END=8204)
SCALE = DH ** -0.5
NEG = -30000.0

CV = {}
_o = 0
for _n, _w in [("norm_mix", 8), ("gate_bias", 24), ("fox_qnorm", 1), ("fox_knorm", 1),
               ("gdn_conv", 48), ("gdn_onorm", 1), ("norm_xq", 8), ("norm_mem", 8),
               ("mq_norm", 1), ("mk_norm", 1), ("norm_ffn", 8), ("ffn_conv", 132),
               ("ffn_conv_b", 44), ("small", 3)]:
    CV[_n] = _o
    _o += _w
NCV = _o


class Tile:
    def __init__(self, t, name, psum=False):
        self.t = t
        self.name = name
        self.psum = psum
        self.w = {}
        self.rs = {}

    def __getitem__(self, idx):
        return self.t[idx]


class Trk:
    def __init__(self):
        self.w = {}
        self.rs = {}


class Eng:
    def __init__(self, name, eng, sem):
        self.name = name
        self.eng = eng
        self.sem = sem
        self.n = 0
        self.seen = {}


class Sched:
    def __init__(self, nc, es, ndsem=12):
        self.nc = nc
        self.es = es
        self.sems = {}
        self.E = {}
        for name, eng in [("pe", nc.tensor), ("act", nc.scalar), ("dve", nc.vector),
                          ("pool", nc.gpsimd), ("sp", nc.sync)]:
            sem = es.enter_context(nc.semaphore("sem_" + name))
            self.E[name] = Eng(name, eng, sem)
            self.sems[name] = sem
        self.dpool = {}
        self.dnext = {}
        for q in ("sp", "pool", "act"):
            lst = []
            for i in range(ndsem):
                key = "d_%s_%d" % (q, i)
                sem = es.enter_context(nc.semaphore(key))
                self.sems[key] = sem
                lst.append([sem, 0, key])
            self.dpool[q] = lst
            self.dnext[q] = 0
        self.ninst = 0

    def _deps(self, reads, writes):
        toks = []
        for r in reads:
            for k, v in r.w.items():
                toks.append((k, v, True))
            if getattr(r, "psum", False):
                for k, v in r.rs.items():
                    toks.append((k, v, False))
        for w in writes:
            for k, v in w.w.items():
                toks.append((k, v, False))
            for k, v in w.rs.items():
                toks.append((k, v, False))
        return toks

    def _wait(self, E, toks):
        for key, val, raw in toks:
            if key == E.name:
                if not raw or E.name == "pe":
                    continue
            if E.seen.get(key, 0) >= val:
                continue
            E.eng.wait_ge(self.sems[key], val)
            E.seen[key] = val
            self.ninst += 1
            E.nw = getattr(E, "nw", 0) + 1

    def _mark(self, tok, reads, writes):
        k, v = tok
        for r in reads:
            if r.rs.get(k, 0) < v:
                r.rs[k] = v
        for w in writes:
            w.w[k] = v
            w.rs = {}

    def op(self, en, fn, reads, writes):
        E = self.E[en]
        self._wait(E, self._deps(reads, writes))
        ins = fn(E.eng)
        E.n += 1
        ins.then_inc(E.sem, 1)
        self.ninst += 1
        self._mark((E.name, E.n), reads, writes)

    def dma(self, q, out, in_, reads, writes, **kw):
        Q = self.E[q]
        toks = self._deps(reads, writes)
        pool = self.dpool[q]
        i = self.dnext[q]
        self.dnext[q] = (i + 1) % len(pool)
        sem, cnt, key = pool[i]
        if cnt > 0:
            toks.append((key, cnt, False))
        self._wait(Q, toks)
        Q.eng.dma_start(out=out, in_=in_, **kw).then_inc(sem, 16)
        pool[i][1] = cnt + 16
        self.ninst += 1
        self._mark((key, cnt + 16), reads, writes)

    def barrier(self):
        toks = [(n, e.n, True) for n, e in self.E.items() if e.n > 0]
        for q, pool in self.dpool.items():
            for sem, cnt, key in pool:
                if cnt > 0:
                    toks.append((key, cnt, False))
        for E in self.E.values():
            self._wait(E, toks)

    def mm(self, out, lhsT, rhs, reads, writes, start=True, stop=True):
        self.op("pe", lambda e: e.matmul(out, lhsT, rhs, start=start, stop=stop), reads, writes)

    def tr(self, out, in_, ident, reads, writes):
        self.op("pe", lambda e: e.transpose(out, in_, ident), reads, writes)

    def act(self, out, in_, func, reads, writes, bias=None, scale=None, en="act"):
        kw = {}
        if bias is not None:
            kw["bias"] = bias
        if scale is not None:
            kw["scale"] = scale
        self.op(en, lambda e: e.activation(out, in_, func, **kw), reads, writes)

    def tt(self, en, out, in0, in1, op, reads, writes):
        self.op(en, lambda e: e.tensor_tensor(out, in0, in1, op), reads, writes)

    def ts(self, en, out, in0, s1, s2, op0, op1, reads, writes):
        if op1 is None:
            self.op(en, lambda e: e.tensor_scalar(out, in0, s1, None, op0), reads, writes)
        else:
            self.op(en, lambda e: e.tensor_scalar(out, in0, s1, s2, op0, op1), reads, writes)

    def stt(self, out, in0, scalar, in1, op0, op1, reads, writes):
        self.op("dve", lambda e: e.scalar_tensor_tensor(out, in0, scalar, in1, op0, op1), reads, writes)

    def copy(self, en, out, in_, reads, writes):
        if en == "act":
            self.op(en, lambda e: e.activation(out, in_, AF.Copy), reads, writes)
        else:
            self.op(en, lambda e: e.tensor_copy(out, in_), reads, writes)

    def memset(self, en, ap, val, writes):
        self.op(en, lambda e: e.memset(ap, val), [], writes)


class _SqAlias:
    def __init__(self, base):
        self.base = base

    @property
    def w(self):
        return self.base.w

    @property
    def rs(self):
        return self.base.rs

    @rs.setter
    def rs(self, v):
        self.base.rs = v

    def __getitem__(self, idx):
        if idx == slice(None):
            return self.base.t[:, 0:8, :]
        return self.base.t[idx]


class Builder:
    def __init__(self, T, L, dbg=()):
        self.T = T
        self.L = L
        self.NG = T // 512
        self.dbg = set(dbg)
        self.nc = bass.Bass("TRN2", target_bir_lowering=False)
        self.outs = []

    def dram_in(self, name, shape, dt=F32):
        return Tile(self.nc.dram_tensor(name, list(shape), dt, kind="ExternalInput"), name)

    def dram(self, name, shape, dt):
        kind = "Internal"
        if name in self.dbg or name == "yT":
            kind = "ExternalOutput"
            self.outs.append(name)
        return Tile(self.nc.dram_tensor(name, list(shape), dt, kind=kind), name)

    def sb(self, es, name, shape, dt):
        self.uid = getattr(self, "uid", 0) + 1
        name = "%s_%d" % (name, self.uid)
        return Tile(es.enter_context(self.nc.sbuf_tensor(name, list(shape), dt)), name)

    def build(self):
        nc, T, L = self.nc, self.T, self.L
        self.inp = {}
        I = self.inp
        I["xT"] = self.dram_in("xT", [D, T])
        I["memT"] = self.dram_in("memT", [D, MEMT])
        I["w_in"] = self.dram_in("w_in", [L, D, OFF["END"]])
        I["cv"] = self.dram_in("cv", [L, 128, NCV])
        I["w_oa"] = self.dram_in("w_oa", [L, 512, D])
        I["w_ob"] = self.dram_in("w_ob", [L, 512, D])
        I["w_oc"] = self.dram_in("w_oc", [L, 512, D])
        I["w_out"] = self.dram_in("w_out", [L, D, D])
        I["w_mq"] = self.dram_in("w_mq", [L, D, 512])
        I["w_mkv"] = self.dram_in("w_mkv", [L, D, 1024])
        I["w_mo"] = self.dram_in("w_mo", [L, 512, D])
        I["w_up"] = self.dram_in("w_up", [L, D, 2 * DFF])
        I["w_down"] = self.dram_in("w_down", [L, DFF, D])
        I["cmask"] = self.dram_in("cmask", [128, 9, 128])
        self.X = [self.dram("xres0", [D, T], F32), self.dram("xres1", [D, T], F32)]
        self.yT = self.dram("yT", [D, T], F32)
        self.HT = self.dram("HT", [D, T], BF16)
        self.QK = {n: self.dram(n, [512, T], BF16) for n in
                   ("FQ", "FK", "SQ", "SK", "GQ", "GK", "GV", "GZ")}
        self.VT = {n: self.dram(n, [T, 512], BF16) for n in ("FV", "SV")}
        self.ROW = {n: self.dram(n, [4, T], F32) for n in ("CROW", "BETA", "GLOG", "EGC")}
        self.GATES = self.dram("GATES", [3072, T], BF16)
        self.Y = {n: self.dram(n, [512, T], BF16) for n in ("YA", "YB", "YC")}
        with ExitStack() as es:
            self.es = es
            self.S = Sched(nc, es)
            self.consts(es)
            for l in range(L):
                self.layer(l)
            self.finish()
        return nc

    def consts(self, es):
        S = self.S
        self.PS = [Tile(es.enter_context(self.nc.psum_tensor("ps%d" % i, [128, 512], F32)), "ps%d" % i, psum=True)
                   for i in range(8)]
        self.ones_bf = self.sb(es, "ones_bf", [128, 128], BF16)
        S.memset("dve", self.ones_bf[:], 1.0, [self.ones_bf])
        self.cm32 = self.sb(es, "cm32", [128, 9, 128], F32)
        S.dma("sp", self.cm32[:], self.inp["cmask"][:, :, :], [self.inp["cmask"]], [self.cm32])
        self.cmbf = self.sb(es, "cmbf", [128, 9, 128], BF16)
        S.copy("dve", self.cmbf[:], self.cm32[:], [self.cm32], [self.cmbf])
        self.cvt = self.sb(es, "cvt", [128, NCV], F32)
        self.dvt = self.sb(es, "dvt", [128, 4], F32)

    def finish(self):
        S = self.S
        for n in self.dbg:
            if n.startswith("nops"):
                for i in range(int(n[4:])):
                    S.dma("sp", self.cvt[:], self.inp["cv"][0, :, :], [self.inp["cv"]], [self.cvt])
        S.barrier()

    def layer(self, l):
        S = self.S
        S.barrier()
        S.dma("sp", self.cvt[:], self.inp["cv"][l, :, :], [self.inp["cv"]], [self.cvt])
        cs = CV["small"]
        S.ts("dve", self.dvt[:, 0:1], self.cvt[:, cs:cs + 1], -1.0, None, ALU.mult, None, [self.cvt], [self.dvt])
        S.act(self.dvt[:, 1:2], self.cvt[:, cs + 2:cs + 3], AF.Exp, [self.cvt], [self.dvt])
        S.ts("dve", self.dvt[:, 1:2], self.dvt[:, 1:2], -1.0, None, ALU.mult, None, [self.dvt], [self.dvt])
        xin = self.inp["xT"] if l == 0 else self.X[1]
        if "noA" not in self.dbg:
            self.pass_a(l, xin)
        nh = 2 if "nh2" in self.dbg else (1 if "nh1" in self.dbg else NH)
        if "nofox" not in self.dbg:
            self.fox_all(nh)
        if "nosb" not in self.dbg:
            self.sb_all(nh)
        if "noB" not in self.dbg:
            self.pass_b(l)
        if "nogdn" not in self.dbg:
            for h in range(nh):
                self.gdn_head(h)
        if "noC" not in self.dbg:
            self.pass_c(l, xin, self.X[0])
        if "noD" not in self.dbg:
            self.pass_d(l, self.X[0], self.yT if l == self.L - 1 else self.X[1])

    def load_w(self, stg, W, wcol, src, row0, c0, ncols, gain_col=None, kchunks=8, flip=[0]):
        S = self.S
        for k in range(kchunks):
            done = 0
            while done < ncols:
                n = min(1024, ncols - done)
                st = stg[flip[0] % len(stg)]
                flip[0] += 1
                S.dma("sp", st[:, 0:n], src[row0 + k * 128: row0 + (k + 1) * 128, c0 + done: c0 + done + n],
                      [], [st])
                dst = W[:, k, wcol + done: wcol + done + n]
                if gain_col is None:
                    en = "dve" if flip[0] % 2 else "pool"
                    S.copy(en, dst, st[:, 0:n], [st], [W])
                else:
                    g = self.cvt[:, gain_col + k: gain_col + k + 1]
                    if flip[0] % 2:
                        S.ts("dve", dst, st[:, 0:n], g, None, ALU.mult, None, [st, self.cvt], [W])
                    else:
                        S.act(dst, st[:, 0:n], AF.Copy, [st, self.cvt], [W], scale=g)
                done += n

    def norm_group(self, xt, hT, sq, lnv, rstd, ps):
        S = self.S
        S.act(sq[:], xt[:], AF.Square, [xt], [sq])
        for k in range(8):
            S.mm(ps[:], self.ones_bf[:], sq[:, k, :], [self.ones_bf, sq], [ps], start=(k == 0), stop=(k == 7))
        S.act(lnv[:], ps[:], AF.Ln, [ps], [lnv], bias=EPS, scale=1.0 / D)
        S.act(rstd[:], lnv[:], AF.Exp, [lnv], [rstd], scale=-0.5)
        for k in range(8):
            S.tt("pool" if k in (3, 7) else "dve", hT[:, k, :], xt[:, k, :], rstd[:], ALU.mult, [xt, rstd], [hT])

    def pass_a(self, l, xin):
        S, T, NG = self.S, self.T, self.NG
        w_in = self.inp["w_in"]
        with ExitStack() as es:
            NWA = 2048 + 1024
            W = self.sb(es, "WA", [128, 8, NWA], BF16)
            Wsm = self.sb(es, "WAsm", [128, 8, 96], BF16)
            stg = [self.sb(es, "stgA%d" % i, [128, 1024], F32) for i in range(4)]
            S.memset("pool", Wsm[:], 0.0, [Wsm])
            gm = CV["norm_mix"]
            wl = w_in.t[l]
            for (name, wc) in (("FQ", 0), ("FK", 512), ("SQ", 1024), ("SK", 1536), ("FV", 2048), ("SV", 2560)):
                self.load_w(stg, W, wc, wl, 0, OFF[name], 512, gain_col=gm)
            for (name, wc) in (("FF", 0), ("GB", 32), ("GA", 64)):
                self.load_w(stg, Wsm, wc, wl, 0, OFF[name], 4, gain_col=gm)
            xts = [self.sb(es, "xtA%d" % i, [128, 8, 512], F32) for i in range(2)]
            hT = self.sb(es, "hTA", [128, 8, 512], BF16)
            sq = self.sb(es, "sqA", [128, 8, 512], BF16)
            lnv = self.sb(es, "lnvA", [128, 512], F32)
            rstd = self.sb(es, "rstdA", [128, 512], F32)
            sq2 = [self.sb(es, "sq2A%d" % i, [128, 512], BF16) for i in range(2)]
            ln2 = [self.sb(es, "ln2A%d" % i, [128, 512], F32) for i in range(2)]
            ob = [self.sb(es, "obA%d" % i, [128, 512], BF16) for i in range(4)]
            sm = {n: self.sb(es, "smA_" + n, [128, 512], F32) for n in ("e1", "l1", "c", "beta", "glog", "gc", "egc")}
            ones4 = self.sb(es, "ones4", [128, 512], F32)
            rmask = self.sb(es, "rmask", [128, 512], F32)
            S.memset("pool", ones4[:], 1.0, [ones4])
            S.memset("pool", rmask[:], 1.0, [rmask])
            for c in range(8):
                S.memset("pool", rmask[:, c * 64: c * 64 + 1], 0.0, [rmask])
            ccarry = self.sb(es, "ccarry", [128, 1], F32)
            S.memset("pool", ccarry[:], 0.0, [ccarry])
            xv = xin.t.rearrange("(k p) t -> p k t", p=128)
            hv = self.HT.t.rearrange("(k p) t -> p k t", p=128)
            cvs = CV["small"]
            nob = 0
            pi = 0
            S.dma("sp", xts[0][:], xv[:, :, 0:512], [xin], [xts[0]])
            for g in range(NG):
                ts_ = slice(g * 512, (g + 1) * 512)
                xt = xts[g % 2]
                if g + 1 < NG:
                    S.dma("sp", xts[(g + 1) % 2][:], xv[:, :, (g + 1) * 512:(g + 2) * 512], [xin], [xts[(g + 1) % 2]])
                self.norm_group(xt, hT, sq, lnv, rstd, self.PS[0])
                S.dma("sp", hv[:, :, ts_], hT[:], [hT], [self.HT])
                pendA = []
                for fam, wc, kind in (("FQ", 0, "nq"), ("FK", 512, "nk"), ("SQ", 1024, "s"), ("SK", 1536, "c")):
                    if kind == "s":
                        while pendA:
                            pendA.pop(0)()
                    for h in range(4):
                        ps = self.PS[(1, 2, 3, 7)[pi % 4]]
                        pi += 1
                        for k in range(8):
                            S.mm(ps[:], W[:, k, wc + h * 128: wc + (h + 1) * 128], hT[:, k, :], [W, hT], [ps],
                                 start=(k == 0), stop=(k == 7))
                        o = ob[nob % 4]
                        nob += 1
                        if kind in ("nq", "nk"):
                            s2 = sq2[nob % 2]
                            S.act(s2[:], ps[:], AF.Square, [ps], [s2])

                            def tail(ps=ps, s2=s2, o=o, kind=kind, fam=fam, h=h, nob=nob):
                                l2 = ln2[nob % 2]
                                ps2 = self.PS[4 + nob % 2]
                                S.mm(ps2[:], self.ones_bf[:], s2[:], [self.ones_bf, s2], [ps2])
                                S.act(l2[:], ps2[:], AF.Ln, [ps2], [l2], bias=EPS, scale=1.0 / DH)
                                S.act(l2[:], l2[:], AF.Exp, [l2], [l2], scale=-0.5)
                                gcol = CV["fox_qnorm"] if kind == "nq" else CV["fox_knorm"]
                                S.stt(o[:], ps[:], self.cvt[:, gcol:gcol + 1], l2[:], ALU.mult, ALU.mult,
                                      [ps, self.cvt, l2], [o])
                                if kind == "nq":
                                    S.ts("pool", o[:], o[:], SCALE, 1.0, ALU.mult, ALU.mult, [o], [o])
                                S.dma("sp", self.QK[fam].t[h * 128:(h + 1) * 128, ts_], o[:], [o], [self.QK[fam]])
                            pendA.append(tail)
                            if len(pendA) > 1:
                                pendA.pop(0)()
                            continue
                        elif kind == "s":
                            S.act(o[:], ps[:], AF.Copy, [ps], [o], scale=SCALE)
                        else:
                            S.copy("dve", o[:], ps[:], [ps], [o])
                        dst = self.QK[fam]
                        S.dma("sp", dst.t[h * 128:(h + 1) * 128, ts_], o[:], [o], [dst])
                for fam, wc in (("FV", 2048), ("SV", 2560)):
                    for sub in range(4):
                        ps = self.PS[1 + pi % 3]
                        pi += 1
                        for k in range(8):
                            S.mm(ps[:], hT[:, k, sub * 128:(sub + 1) * 128], W[:, k, wc: wc + 512], [W, hT], [ps],
                                 start=(k == 0), stop=(k == 7))
                        o = ob[nob % 4]
                        nob += 1
                        S.copy("dve" if sub % 2 else "act", o[:], ps[:], [ps], [o])
                        dst = self.VT[fam]
                        r0 = g * 512 + sub * 128
                        S.dma("sp", dst.t[r0:r0 + 128, :], o[:], [o], [dst])
                ps = self.PS[6]
                for k in range(8):
                    S.mm(ps[0:96, :], Wsm[:, k, :], hT[:, k, :], [Wsm, hT], [ps], start=(k == 0), stop=(k == 7))
                cv = self.cvt
                S.act(sm["e1"][0:4, :], ps[0:4, :], AF.Exp, [ps, self.dvt], [sm["e1"]], bias=self.dvt[0:4, 0:1], scale=-1.0)
                S.act(sm["l1"][0:4, :], sm["e1"][0:4, :], AF.Ln, [sm["e1"]], [sm["l1"]], bias=1.0)
                S.op("dve", lambda e: e.tensor_tensor_scan(sm["c"][0:4, :], ones4[0:4, :], sm["l1"][0:4, :],
                                                           ccarry[0:4, 0:1], ALU.mult, ALU.subtract),
                     [ones4, sm["l1"], ccarry], [sm["c"]])
                S.copy("dve", ccarry[0:4, :], sm["c"][0:4, 511:512], [sm["c"]], [ccarry])
                S.dma("sp", self.ROW["CROW"].t[:, ts_], sm["c"][0:4, :], [sm["c"]], [self.ROW["CROW"]])
                S.act(sm["beta"][32:36, :], ps[32:36, :], AF.Sigmoid, [ps], [sm["beta"]])
                S.dma("sp", self.ROW["BETA"].t[:, ts_], sm["beta"][32:36, :], [sm["beta"]], [self.ROW["BETA"]])
                S.act(sm["e1"][64:68, :], ps[64:68, :], AF.Exp, [ps, cv], [sm["e1"]], bias=cv[64:68, cvs + 1:cvs + 2])
                S.act(sm["l1"][64:68, :], sm["e1"][64:68, :], AF.Ln, [sm["e1"]], [sm["l1"]], bias=1.0)
                S.ts("dve", sm["glog"][64:68, :], sm["l1"][64:68, :], self.dvt[64:68, 1:2], None, ALU.mult, None,
                     [sm["l1"], self.dvt], [sm["glog"]])
                S.op("dve", lambda e: e.tensor_tensor_scan(sm["gc"][64:68, :], rmask[64:68, :], sm["glog"][64:68, :],
                                                           0.0, ALU.mult, ALU.add),
                     [rmask, sm["glog"]], [sm["gc"]])
                S.act(sm["egc"][64:68, :], sm["gc"][64:68, :], AF.Exp, [sm["gc"]], [sm["egc"]])
                S.dma("sp", self.ROW["GLOG"].t[:, ts_], sm["glog"][64:68, :], [sm["glog"]], [self.ROW["GLOG"]])
                S.dma("sp", self.ROW["EGC"].t[:, ts_], sm["egc"][64:68, :], [sm["egc"]], [self.ROW["EGC"]])
            S.barrier()


    def pass_b(self, l):
        S, T, NG = self.S, self.T, self.NG
        w_in = self.inp["w_in"]
        PS = self.PS
        cv = self.cvt
        with ExitStack() as es:
            W = self.sb(es, "WB", [128, 8, 5120], BF16)
            stg = [self.sb(es, "stgB%d" % i, [128, 1024], F32) for i in range(4)]
            gm = CV["norm_mix"]
            wl = w_in.t[l]
            self.load_w(stg, W, 0, wl, 0, OFF["GQ"], 1536, gain_col=gm)
            self.load_w(stg, W, 1536, wl, 0, OFF["GZ"], 512, gain_col=gm)
            self.load_w(stg, W, 2048, wl, 0, OFF["GT"], 3072, gain_col=gm)
            hT = [self.sb(es, "hTB%d" % i, [128, 8, 512], BF16) for i in range(2)]
            halo = self.sb(es, "haloB", [128, 12, 4], F32)
            buf = [self.sb(es, "bufB%d" % i, [128, 516], F32) for i in range(4)]
            acc = [self.sb(es, "accB%d" % i, [128, 512], F32) for i in range(4)]
            sil = [self.sb(es, "silB%d" % i, [128, 512], F32) for i in range(4)]
            s2 = [self.sb(es, "sq2B%d" % i, [128, 512], BF16) for i in range(4)]
            l2 = [self.sb(es, "ln2B%d" % i, [128, 512], F32) for i in range(4)]
            ob = [self.sb(es, "obB%d" % i, [128, 512], BF16) for i in range(6)]
            S.memset("pool", halo[:], 0.0, [halo])
            hv = self.HT.t.rearrange("(k p) t -> p k t", p=128)
            pi = 0
            nob = 0
            cw = CV["gdn_conv"]
            for g in range(NG):
                ts_ = slice(g * 512, (g + 1) * 512)
                h_ = hT[g % 2]
                S.dma("sp", h_[:], hv[:, :, ts_], [self.HT], [h_])
                pend = []
                for c in range(12):
                    ps = PS[(0, 1, 2, 5, 6)[pi % 5]]
                    pi += 1
                    for k in range(8):
                        S.mm(ps[:], W[:, k, c * 128:(c + 1) * 128], h_[:, k, :], [W, h_], [ps], start=(k == 0), stop=(k == 7))
                    b = buf[c % 4]
                    a = acc[c % 4]
                    sl = sil[c % 4]
                    S.copy("act", b[:, 0:3], halo[:, c, 0:3], [halo], [b])
                    S.copy("act", b[:, 3:515], ps[:], [ps], [b])
                    S.copy("act", halo[:, c, 0:3], b[:, 512:515], [b], [halo])
                    S.ts("dve", a[:], b[:, 0:512], cv[:, cw + c:cw + c + 1], None, ALU.mult, None, [b, cv], [a])
                    for tap in (1, 2):
                        S.stt(a[:], b[:, tap:tap + 512], cv[:, cw + tap * 12 + c: cw + tap * 12 + c + 1], a[:],
                              ALU.mult, ALU.add, [b, cv, a], [a])
                    S.stt(a[:], ps[:], cv[:, cw + 3 * 12 + c: cw + 3 * 12 + c + 1], a[:],
                          ALU.mult, ALU.add, [ps, cv, a], [a])
                    o = ob[nob % 6]
                    nob += 1
                    fam = ("GQ", "GK", "GV")[c // 4]
                    hh = c % 4
                    if fam == "GV":
                        S.act(o[:], a[:], AF.Silu, [a], [o])
                    else:
                        S.act(sl[:], a[:], AF.Silu, [a], [sl])
                        q2 = s2[c % 4]
                        S.act(q2[:], sl[:], AF.Square, [sl], [q2])

                        def tail(c=c, sl=sl, q2=q2, o=o, fam=fam, hh=hh):
                            ll = l2[c % 4]
                            ps2 = PS[(3, 4, 7)[c % 3]]
                            S.mm(ps2[:], self.ones_bf[:], q2[:], [self.ones_bf, q2], [ps2])
                            S.act(ll[:], ps2[:], AF.Ln, [ps2], [ll], bias=EPS)
                            S.act(ll[:], ll[:], AF.Exp, [ll], [ll], scale=-0.5)
                            if fam == "GQ":
                                S.stt(o[:], sl[:], SCALE, ll[:], ALU.mult, ALU.mult, [sl, ll], [o])
                            else:
                                S.tt("dve", o[:], sl[:], ll[:], ALU.mult, [sl, ll], [o])
                            S.dma("sp", self.QK[fam].t[hh * 128:(hh + 1) * 128, ts_], o[:], [o], [self.QK[fam]])
                        pend.append(tail)
                        if len(pend) > 2:
                            pend.pop(0)()
                        continue
                    S.dma("sp", self.QK[fam].t[hh * 128:(hh + 1) * 128, ts_], o[:], [o], [self.QK[fam]])
                while pend:
                    pend.pop(0)()
                for c in range(4):
                    ps = PS[(0, 1, 2, 5, 6)[pi % 5]]
                    pi += 1
                    for k in range(8):
                        S.mm(ps[:], W[:, k, 1536 + c * 128:1536 + (c + 1) * 128], h_[:, k, :], [W, h_], [ps],
                             start=(k == 0), stop=(k == 7))
                    o = ob[nob % 6]
                    nob += 1
                    S.act(o[:], ps[:], AF.Silu, [ps], [o])
                    S.dma("sp", self.QK["GZ"].t[c * 128:(c + 1) * 128, ts_], o[:], [o], [self.QK["GZ"]])
                gb = CV["gate_bias"]
                for c in range(24):
                    ps = PS[(0, 1, 2, 5, 6)[pi % 5]]
                    pi += 1
                    for k in range(8):
                        S.mm(ps[:], W[:, k, 2048 + c * 128:2048 + (c + 1) * 128], h_[:, k, :], [W, h_], [ps],
                             start=(k == 0), stop=(k == 7))
                    o = ob[nob % 6]
                    nob += 1
                    S.act(o[:], ps[:], AF.Sigmoid, [ps, cv], [o], bias=cv[:, gb + c:gb + c + 1])
                    S.dma("sp", self.GATES.t[c * 128:(c + 1) * 128, ts_], o[:], [o], [self.GATES])
            S.barrier()

    def head_norm(self, ps, ncols, gcol, out, sq, ll, ps2, post_scale=None):
        S = self.S
        S.act(sq[:, 0:ncols], ps[:, 0:ncols], AF.Square, [ps], [sq])
        S.mm(ps2[:, 0:ncols], self.ones_bf[:], sq[:, 0:ncols], [self.ones_bf, sq], [ps2])
        S.act(ll[:, 0:ncols], ps2[:, 0:ncols], AF.Ln, [ps2], [ll], bias=EPS, scale=1.0 / DH)
        S.act(ll[:, 0:ncols], ll[:, 0:ncols], AF.Exp, [ll], [ll], scale=-0.5)
        S.stt(out, ps[:, 0:ncols], self.cvt[:, gcol:gcol + 1], ll[:, 0:ncols], ALU.mult, ALU.mult,
              [ps, self.cvt, ll], [out.tile] if hasattr(out, "tile") else [])

    def pass_c(self, l, xin, xout):
        S, T, NG = self.S, self.T, self.NG
        I = self.inp
        PS = self.PS
        cv = self.cvt
        with ExitStack() as es:
            Wo = [self.sb(es, "WCo%d" % i, [128, 4, D], BF16) for i in range(3)]
            Wout = self.sb(es, "WCout", [128, 8, D], BF16)
            Wmq = self.sb(es, "WCmq", [128, 8, 512], BF16)
            Wmo = self.sb(es, "WCmo", [128, 4, D], BF16)
            Wkv = self.sb(es, "WCkv", [128, 8, D], BF16)
            stg = [self.sb(es, "stgC%d" % i, [128, 1024], F32) for i in range(2)]
            for i, n in enumerate(("w_oa", "w_ob", "w_oc")):
                self.load_w(stg, Wo[i], 0, I[n].t[l], 0, 0, D, kchunks=4)
            self.load_w(stg, Wout, 0, I["w_out"].t[l], 0, 0, D)
            self.load_w(stg, Wmq, 0, I["w_mq"].t[l], 0, 0, 512, gain_col=CV["norm_xq"])
            self.load_w(stg, Wmo, 0, I["w_mo"].t[l], 0, 0, D, kchunks=4)
            self.load_w(stg, Wkv, 0, I["w_mkv"].t[l], 0, 0, D, gain_col=CV["norm_mem"])
            xts = [self.sb(es, "xtC%d" % i, [128, 8, 512], F32) for i in range(2)]
            xt = xts[1]
            hT = self.sb(es, "hTC", [128, 8, 512], BF16)
            sq = self.sb(es, "sqC", [128, 8, 512], BF16)
            lnv = self.sb(es, "lnvC", [128, 512], F32)
            rstd = self.sb(es, "rstdC", [128, 512], F32)
            yb_ = [self.sb(es, "yC%d" % i, [128, 4, 512], BF16) for i in range(3)]
            gt = self.sb(es, "gtC", [128, 24, 512], BF16)
            tA = self.sb(es, "tAC", [128, 512], F32)
            tB = self.sb(es, "tBC", [128, 512], F32)
            mix = self.sb(es, "mixC", [128, 8, 512], BF16)
            om = self.sb(es, "omC", [128, 4, 512], BF16)
            sq2 = self.sb(es, "sq2C", [128, 512], BF16)
            ll = self.sb(es, "llC", [128, 512], F32)
            qn = self.sb(es, "qnC", [128, 512], BF16)
            pt = [self.sb(es, "ptC%d" % i, [128, 512], BF16) for i in range(2)]
            Km = self.sb(es, "KmC", [128, 4, MEMT], BF16)
            Vm = self.sb(es, "VmC", [128, 2, 512], BF16)
            mv = I["memT"].t.rearrange("(k p) t -> p k t", p=128)
            S.dma("sp", xt[:, :, 0:MEMT], mv, [I["memT"]], [xt])
            S.act(sq[:, :, 0:MEMT], xt[:, :, 0:MEMT], AF.Square, [xt], [sq])
            for k in range(8):
                S.mm(PS[0][:, 0:MEMT], self.ones_bf[:], sq[:, k, 0:MEMT], [self.ones_bf, sq], [PS[0]], start=(k == 0), stop=(k == 7))
            S.act(lnv[:, 0:MEMT], PS[0][:, 0:MEMT], AF.Ln, [PS[0]], [lnv], bias=EPS, scale=1.0 / D)
            S.act(rstd[:, 0:MEMT], lnv[:, 0:MEMT], AF.Exp, [lnv], [rstd], scale=-0.5)
            for k in range(8):
                S.tt("dve", hT[:, k, 0:MEMT], xt[:, k, 0:MEMT], rstd[:, 0:MEMT], ALU.mult, [xt, rstd], [hT])
            for h in range(4):
                ps = PS[1 + h % 2]
                for k in range(8):
                    S.mm(ps[:, 0:MEMT], Wkv[:, k, h * 128:(h + 1) * 128], hT[:, k, 0:MEMT], [Wkv, hT], [ps], start=(k == 0), stop=(k == 7))
                S.act(sq2[:, 0:MEMT], ps[:, 0:MEMT], AF.Square, [ps], [sq2])
                S.mm(PS[3][:, 0:MEMT], self.ones_bf[:], sq2[:, 0:MEMT], [self.ones_bf, sq2], [PS[3]])
                S.act(ll[:, 0:MEMT], PS[3][:, 0:MEMT], AF.Ln, [PS[3]], [ll], bias=EPS, scale=1.0 / DH)
                S.act(ll[:, 0:MEMT], ll[:, 0:MEMT], AF.Exp, [ll], [ll], scale=-0.5)
                gk = CV["mk_norm"]
                S.stt(Km[:, h, :], ps[:, 0:MEMT], cv[:, gk:gk + 1], ll[:, 0:MEMT], ALU.mult, ALU.mult, [ps, cv, ll], [Km])
            for blk in range(2):
                ps = PS[1 + blk % 2]
                for k in range(8):
                    S.mm(ps[:], hT[:, k, blk * 128:(blk + 1) * 128], Wkv[:, k, 512:1024], [Wkv, hT], [ps], start=(k == 0), stop=(k == 7))
                S.copy("act", Vm[:, blk, :], ps[:], [ps], [Vm])
            xv = xin.t.rearrange("(k p) t -> p k t", p=128)
            xo = xout.t.rearrange("(k p) t -> p k t", p=128)
            yv = [self.Y[n].t.rearrange("(h p) t -> p h t", p=128) for n in ("YA", "YB", "YC")]
            gv = self.GATES.t.rearrange("(c p) t -> p c t", p=128)
            pi = 0
            S.dma("sp", xts[0][:], xv[:, :, 0:512], [xin], [xts[0]])
            for g in range(NG):
                ts_ = slice(g * 512, (g + 1) * 512)
                xt = xts[g % 2]
                if g + 1 < NG:
                    S.dma("sp", xts[(g + 1) % 2][:], xv[:, :, (g + 1) * 512:(g + 2) * 512], [xin], [xts[(g + 1) % 2]])
                for i, n in enumerate(("YA", "YB", "YC")):
                    S.dma("sp", yb_[i][:], yv[i][:, :, ts_], [self.Y[n]], [yb_[i]])
                S.dma("sp", gt[:], gv[:, :, ts_], [self.GATES], [gt])
                for oc in range(8):
                    ocs = slice(oc * 128, (oc + 1) * 128)
                    for br in range(3):
                        ps = PS[pi % 3]
                        pi += 1
                        for hh in range(4):
                            S.mm(ps[:], Wo[br][:, hh, ocs], yb_[br][:, hh, :], [Wo[br], yb_[br]], [ps], start=(hh == 0), stop=(hh == 3))
                        if br == 0:
                            S.tt("dve", tA[:], ps[:], gt[:, oc, :], ALU.mult, [ps, gt], [tA])
                        else:
                            S.tt("dve", tB[:], ps[:], gt[:, br * 8 + oc, :], ALU.mult, [ps, gt], [tB])
                            if br == 1:
                                S.tt("pool", tA[:], tA[:], tB[:], ALU.add, [tA, tB], [tA])
                            else:
                                S.tt("pool", mix[:, oc, :], tA[:], tB[:], ALU.add, [tA, tB], [mix])
                for oc in range(8):
                    ocs = slice(oc * 128, (oc + 1) * 128)
                    ps = PS[pi % 3]
                    pi += 1
                    for k in range(8):
                        S.mm(ps[:], Wout[:, k, ocs], mix[:, k, :], [Wout, mix], [ps], start=(k == 0), stop=(k == 7))
                    S.tt("dve", xt[:, oc, :], xt[:, oc, :], ps[:], ALU.add, [xt, ps], [xt])
                self.norm_group(xt, hT, sq, lnv, rstd, PS[3])
                for h in range(4):
                    ps = PS[pi % 3]
                    pi += 1
                    for k in range(8):
                        S.mm(ps[:], Wmq[:, k, h * 128:(h + 1) * 128], hT[:, k, :], [Wmq, hT], [ps], start=(k == 0), stop=(k == 7))
                    S.act(sq2[:], ps[:], AF.Square, [ps], [sq2])
                    S.mm(PS[3][:], self.ones_bf[:], sq2[:], [self.ones_bf, sq2], [PS[3]])
                    S.act(ll[:], PS[3][:], AF.Ln, [PS[3]], [ll], bias=EPS, scale=1.0 / DH)
                    S.act(ll[:], ll[:], AF.Exp, [ll], [ll], scale=-0.5)
                    gq = CV["mq_norm"]
                    S.stt(tA[:], ps[:], cv[:, gq:gq + 1], ll[:], ALU.mult, ALU.mult, [ps, cv, ll], [tA])
                    S.ts("pool", qn[:], tA[:], SCALE, 1.0, ALU.mult, ALU.mult, [tA], [qn])
                    O = PS[4 + h % 2]
                    DN = PS[6 + h % 2]
                    for kb in range(2):
                        ps = PS[pi % 3]
                        pi += 1
                        p_ = pt[kb]
                        S.mm(ps[:], Km[:, h, kb * 128:(kb + 1) * 128], qn[:], [Km, qn], [ps])
                        S.act(p_[:], ps[:], AF.Exp, [ps], [p_])
                        S.mm(O[:], Vm[:, kb, h * 128:(h + 1) * 128], p_[:], [Vm, p_], [O], start=(kb == 0), stop=(kb == 1))
                        S.mm(DN[:], self.ones_bf[:], p_[:], [self.ones_bf, p_], [DN], start=(kb == 0), stop=(kb == 1))
                    S.act(ll[:], DN[:], AF.Ln, [DN], [ll])
                    S.act(ll[:], ll[:], AF.Exp, [ll], [ll], scale=-1.0)
                    S.tt("dve", om[:, h, :], O[:], ll[:], ALU.mult, [O, ll], [om])
                for oc in range(8):
                    ocs = slice(oc * 128, (oc + 1) * 128)
                    ps = PS[pi % 3]
                    pi += 1
                    for hh in range(4):
                        S.mm(ps[:], Wmo[:, hh, ocs], om[:, hh, :], [Wmo, om], [ps], start=(hh == 0), stop=(hh == 3))
                    S.tt("dve", xt[:, oc, :], xt[:, oc, :], ps[:], ALU.add, [xt, ps], [xt])
                S.dma("sp", xo[:, :, ts_], xt[:], [xt], [xout])
            S.barrier()

    def pass_d(self, l, xin, xout):
        S, T, NG = self.S, self.T, self.NG
        I = self.inp
        PS = self.PS
        cv = self.cvt
        NJ = DFF // 128
        with ExitStack() as es:
            Wup = self.sb(es, "WDup", [128, 8, 2 * DFF], BF16)
            Wdn = self.sb(es, "WDdn", [128, NJ, D], BF16)
            with ExitStack() as es2:
                stg = [self.sb(es2, "stgD%d" % i, [128, 1024], F32) for i in range(2)]
                self.load_w(stg, Wup, 0, I["w_up"].t[l], 0, 0, 2 * DFF, gain_col=CV["norm_ffn"])
                self.load_w(stg, Wdn, 0, I["w_down"].t[l], 0, 0, D, kchunks=NJ)
                S.barrier()
            xt = self.sb(es, "xtD", [128, 8, 512], F32)
            hT = self.sb(es, "hTD", [128, 8, 512], BF16)
            gT = self.sb(es, "gTD", [128, NJ, 512], BF16)
            lnv = self.sb(es, "lnvD", [128, 512], F32)
            rstd = self.sb(es, "rstdD", [128, 512], F32)
            halo = self.sb(es, "haloD", [128, 2 * NJ, 2], F32)
            buf = [self.sb(es, "bufD%d" % i, [128, 516], F32) for i in range(2)]
            acc = [self.sb(es, "accD%d" % i, [128, 512], F32) for i in range(2)]
            sa = self.sb(es, "saD", [128, 512], F32)
            S.memset("pool", halo[:], 0.0, [halo])
            xv = xin.t.rearrange("(k p) t -> p k t", p=128)
            xo = xout.t.rearrange("(k p) t -> p k t", p=128)
            cw = CV["ffn_conv"]
            cb = CV["ffn_conv_b"]
            pi = 0
            for g in range(NG):
                ts_ = slice(g * 512, (g + 1) * 512)
                S.dma("sp", xt[:], xv[:, :, ts_], [xin], [xt])
                self.norm_group(xt, hT, _SqAlias(gT), lnv, rstd, PS[3])
                for j in range(NJ):
                    for ab in range(2):
                        c = ab * NJ + j
                        ps = PS[(0, 1, 2, 6, 7)[pi % 5]]
                        pi += 1
                        for k in range(8):
                            S.mm(ps[:], Wup[:, k, c * 128:(c + 1) * 128], hT[:, k, :], [Wup, hT], [ps], start=(k == 0), stop=(k == 7))
                        b = buf[ab]
                        a = acc[ab]
                        S.copy("act", b[:, 0:2], halo[:, c, 0:2], [halo], [b])
                        S.copy("act", b[:, 2:514], ps[:], [ps], [b])
                        S.copy("act", halo[:, c, 0:2], b[:, 512:514], [b], [halo])
                        S.ts("dve", a[:], b[:, 0:512], cv[:, cw + c:cw + c + 1], cv[:, cb + c:cb + c + 1], ALU.mult, ALU.add,
                             [b, cv], [a])
                        S.stt(a[:], b[:, 1:513], cv[:, cw + 2 * NJ + c: cw + 2 * NJ + c + 1], a[:],
                              ALU.mult, ALU.add, [b, cv, a], [a])
                        S.stt(a[:], ps[:], cv[:, cw + 2 * 2 * NJ + c: cw + 2 * 2 * NJ + c + 1], a[:],
                              ALU.mult, ALU.add, [ps, cv, a], [a])
                    S.act(sa[:], acc[0][:], AF.Silu, [acc[0]], [sa])
                    S.tt("pool", gT[:, j, :], sa[:], acc[1][:], ALU.mult, [sa, acc[1]], [gT])
                for oc in range(8):
                    ocs = slice(oc * 128, (oc + 1) * 128)
                    ps = PS[4 + oc % 2]
                    for j in range(NJ):
                        S.mm(ps[:], Wdn[:, j, ocs], gT[:, j, :], [Wdn, gT], [ps], start=(j == 0), stop=(j == NJ - 1))
                    S.tt("dve", xt[:, oc, :], xt[:, oc, :], ps[:], ALU.add, [xt, ps], [xt])
                S.dma("sp", xo[:, :, ts_], xt[:], [xt], [xout])
            S.barrier()

    def gdn_head(self, h):
        S, T = self.S, self.T
        NP = T // 128
        PS = self.PS
        cm = self.cm32
        ident = cm[:, 0, :]
        with ExitStack() as es:
            Qf = self.sb(es, "gQ", [128, T], BF16)
            Kf = self.sb(es, "gK", [128, T], BF16)
            Vf = self.sb(es, "gV", [128, T], BF16)
            bc = self.sb(es, "gbc", [128, T], F32)
            G2 = self.sb(es, "gG2", [128, 128], F32)
            B2 = self.sb(es, "gB2", [128, 128], F32)
            gc2 = self.sb(es, "ggc2", [128, 128], F32)
            gl2 = self.sb(es, "ggl2", [128, 128], F32)
            rm2 = self.sb(es, "grm2", [128, 128], F32)
            gcol = self.sb(es, "ggcol", [128, NP], F32)
            bcol = self.sb(es, "gbcol", [128, NP], F32)
            eglc = self.sb(es, "geglc", [128, NP], F32)
            Sst = self.sb(es, "gS", [128, 128], F32)
            hs = slice(h * 128, (h + 1) * 128)
            S.dma("sp", Qf[:], self.QK["GQ"].t[hs, :], [self.QK["GQ"]], [Qf])
            S.dma("sp", Kf[:], self.QK["GK"].t[hs, :], [self.QK["GK"]], [Kf])
            S.dma("sp", Vf[:], self.QK["GV"].t[hs, :], [self.QK["GV"]], [Vf])
            R = self.ROW
            S.dma("sp", G2[0:NP, :], R["GLOG"].t[h:h + 1, :].rearrange("o (n p) -> (o n) p", p=128), [R["GLOG"]], [G2])
            S.dma("sp", B2[0:NP, :], R["BETA"].t[h:h + 1, :].rearrange("o (n p) -> (o n) p", p=128), [R["BETA"]], [B2])
            S.dma("sp", bc[:], R["EGC"].t[h:h + 1, :].partition_broadcast(128), [R["EGC"]], [bc])
            S.memset("pool", rm2[:], 1.0, [rm2])
            S.memset("pool", rm2[:, 0:1], 0.0, [rm2])
            S.memset("pool", rm2[:, 64:65], 0.0, [rm2])
            S.op("dve", lambda e: e.tensor_tensor_scan(gc2[0:NP, :], rm2[0:NP, :], G2[0:NP, :], 0.0, ALU.mult, ALU.add),
                 [rm2, G2], [gc2])
            for half in range(2):
                cs = slice(64 * half, 64 * half + 64)
                tot = gc2[0:NP, 64 * half + 63: 64 * half + 64]
                S.ts("dve", gl2[0:NP, cs], gc2[0:NP, cs], tot, -1.0, ALU.subtract, ALU.mult, [gc2], [gl2])
            S.act(gl2[0:NP, :], gl2[0:NP, :], AF.Exp, [gl2], [gl2])
            eg2 = self.sb(es, "geg2", [128, 128], F32)
            egcol = self.sb(es, "gegcol", [128, NP], F32)
            bgc = self.sb(es, "gbgc", [128, NP], F32)
            S.act(eg2[0:NP, :], gc2[0:NP, :], AF.Exp, [gc2], [eg2])
            for src, dst in ((G2, gcol), (B2, bcol), (gl2, eglc), (eg2, egcol)):
                S.tr(PS[7][:, 0:NP], src[0:NP, :], cm[0:NP, 0, 0:NP], [src, cm], [PS[7]])
                S.copy("dve", dst[:], PS[7][:, 0:NP], [PS[7]], [dst])
            S.tt("dve", bgc[:], bcol[:], egcol[:], ALU.mult, [bcol, egcol], [bgc])
            Sst2 = self.sb(es, "gS2", [128, 128], F32)
            SS = [Sst, Sst2]
            S.memset("pool", Sst[:], 0.0, [Sst])

            def f32t(n, k=5):
                return [self.sb(es, n + str(i), [128, 128], F32) for i in range(k)]
            lg = f32t("g_lg"); Dl = f32t("g_Dl"); DTs = f32t("g_DTs"); DTi = f32t("g_DTi")
            Lb = [f32t("g_L%d_" % j) for j in range(2)]
            Ub = [f32t("g_U%d_" % j) for j in range(2)]
            X = f32t("g_X"); At = f32t("g_At")
            K32 = f32t("g_K32"); V32 = f32t("g_V32"); Qg = f32t("g_Qg"); Qp = f32t("g_Qp")
            Kt = f32t("g_Kt")
            Rr = [self.sb(es, "g_R%d" % i_, [128, 256], F32) for i_ in range(5)]
            UW = [self.sb(es, "g_UW%d" % i_, [128, 256], F32) for i_ in range(5)]
            MT = [f32t("g_MT%d_" % j, 5) for j in range(2)]
            Nn = [f32t("g_N%d_" % j, 5) for j in range(2)]
            OT = [self.sb(es, "g_OT%d" % i, [128, 512], F32) for i in range(2)]
            gz = [self.sb(es, "g_gz%d" % i, [128, 512], BF16) for i in range(2)]
            sq = self.sb(es, "g_sq", [128, 512], BF16)
            ll = self.sb(es, "g_ll", [128, 512], F32)
            t32 = self.sb(es, "g_t32", [128, 512], F32)
            ob = [self.sb(es, "g_ob%d" % i, [128, 512], BF16) for i in range(2)]

            def pre(r):
                i = r % 5
                cs = slice(r * 128, (r + 1) * 128)
                PA = PS[r % 4]
                PB = PA
                q = [slice(0, 128), slice(128, 256), slice(256, 384), slice(384, 512)]
                S.ts("pool", lg[i][:], cm[:, 4, :], gcol[:, r:r + 1], 1.0, ALU.mult, ALU.mult, [cm, gcol], [lg[i]])
                yield
                S.mm(PA[:, q[0]], lg[i][:], cm[:, 5, :], [lg[i], cm], [PA], start=True, stop=False)
                S.mm(PA[:, q[0]], ident, cm[:, 6, :], [cm], [PA], start=False, stop=True)
                S.mm(PA[:, q[1]], cm[:, 5, :], lg[i][:], [lg[i], cm], [PA], start=True, stop=False)
                S.mm(PA[:, q[1]], ident, cm[:, 8, :], [cm], [PA], start=False, stop=True)
                yield
                S.act(Dl[i][:], PA[:, q[0]], AF.Exp, [PA], [Dl[i]])
                yield
                S.act(DTi[i][:], PA[:, q[1]], AF.Exp, [PA], [DTi[i]])
                yield
                S.mm(PA[:, q[2]], Kf[:, cs], Kf[:, cs], [Kf], [PA])
                S.mm(PA[:, q[3]], Kf[:, cs], Qf[:, cs], [Qf, Kf], [PA])
                yield
                L = Lb[0][i]
                U = Ub[0][i]
                S.stt(L[:], PA[:, q[2]], bcol[:, r:r + 1], Dl[i][:], ALU.mult, ALU.mult, [PA, bcol, Dl[i]], [L])
                yield
                S.tt("dve", At[i][:], PA[:, q[3]], DTi[i][:], ALU.mult, [PA, DTi[i]], [At[i]])
                yield
                S.tr(PA[:, q[0]], L[:], ident, [L, cm], [PA])
                yield
                S.copy("act", U[:], PA[:, q[0]], [PA], [U])
                yield
                S.tt("pool", X[i][:], ident, U[:], ALU.subtract, [cm, U], [X[i]])
                yield
                S.copy("pool", K32[i][:], Kf[:, cs], [Kf], [K32[i]])
                yield
                S.copy("pool", V32[i][:], Vf[:, cs], [Vf], [V32[i]])
                yield
                S.tr(PA[:, q[1]], K32[i][:], ident, [K32[i], cm], [PA])
                S.tr(PA[:, q[2]], V32[i][:], ident, [V32[i], cm], [PA])
                yield
                S.ts("dve", Kt[i][:], PA[:, q[1]], eglc[:, r:r + 1], None, ALU.mult, None, [PA, eglc], [Kt[i]])
                yield
                S.act(Rr[i][:, 128:256], PA[:, q[1]], AF.Copy, [PA, bgc], [Rr[i]], scale=bgc[:, r:r + 1])
                yield
                S.act(Rr[i][:, 0:128], PA[:, q[2]], AF.Copy, [PA, bcol], [Rr[i]], scale=bcol[:, r:r + 1])
                yield
                S.tt("pool", Qg[i][:], Qf[:, cs], bc[:, cs], ALU.mult, [Qf, bc], [Qg[i]])
                yield
                for k in range(1, 6):
                    Ln_ = Lb[k % 2][i]
                    Un_ = Ub[k % 2][i]
                    S.mm(PA[:, q[0]], U[:], L[:], [U, L], [PA])
                    if k < 5:
                        S.mm(PA[:, q[1]], L[:], U[:], [U, L], [PA])
                    yield
                    S.copy("act", Ln_[:], PA[:, q[0]], [PA], [Ln_])
                    yield
                    if k < 5:
                        S.copy("dve", Un_[:], PA[:, q[1]], [PA], [Un_])
                        yield
                    S.mm(PA[:, q[2]], Ln_[:], X[i][:], [Ln_, X[i]], [PA])
                    yield
                    S.tt("dve", X[i][:], X[i][:], PA[:, q[2]], ALU.add, [X[i], PA], [X[i]])
                    yield
                    L, U = Ln_, Un_
                S.mm(PA[:, 0:256], X[i][:], Rr[i][:], [X[i], Rr[i]], [PA])
                yield
                S.copy("act", UW[i][:], PA[:, 0:256], [PA], [UW[i]])
                yield
                S.mm(PA[:, q[3]], UW[i][:, 128:256], At[i][:], [UW[i], At[i]], [PA])
                yield
                S.tt("dve", Qp[i][:], Qg[i][:], PA[:, q[3]], ALU.subtract, [Qg[i], PA], [Qp[i]])
                yield
                for par in range(2):
                    rows = slice(64 * par, 64 * par + 64)
                    col = r * 128 + 64 * par + 63
                    S.mm(PA[:, q[2]], UW[i][rows, 128:256], Kt[i][rows, :], [UW[i], Kt[i]], [PA])
                    S.mm(PA[:, q[3]], Kt[i][rows, :], UW[i][rows, 0:128], [UW[i], Kt[i]], [PA])
                    yield
                    S.stt(MT[par][i][:], ident, bc[:, col:col + 1], PA[:, q[2]], ALU.mult, ALU.subtract,
                          [cm, bc, PA], [MT[par][i]])
                    yield
                    S.copy("act", Nn[par][i][:], PA[:, q[3]], [PA], [Nn[par][i]])
                    yield

            def scan(r):
                i = r % 5
                ot = OT[(r // 4) % 2]
                for par in range(2):
                    rows = slice(64 * par, 64 * par + 64)
                    c = 2 * r + par
                    Sp = SS[c % 2]
                    Sn = SS[(c + 1) % 2]
                    P5 = PS[5 + c % 2]
                    P7 = PS[7]
                    S.mm(P7[:, 0:64], Sp[:], Qp[i][:, rows], [Sp, Qp[i]], [P7], start=True, stop=False)
                    S.mm(P7[:, 0:64], UW[i][rows, 0:128], At[i][rows, rows], [UW[i], At[i]], [P7], start=False, stop=True)
                    S.mm(P5[:, 0:128], MT[par][i][:], Sp[:], [MT[par][i], Sp], [P5])
                    yield
                    S.tt("dve", Sn[:], P5[:, 0:128], Nn[par][i][:], ALU.add, [P5, Nn[par][i]], [Sn])
                    yield
                    oc = (r % 4) * 128 + 64 * par
                    S.copy("act", ot[:, oc:oc + 64], P7[:, 0:64], [P7], [ot])
                    yield

            def post(blk):
                ot = OT[blk % 2]
                ts_ = slice(blk * 512, (blk + 1) * 512)
                z = gz[blk % 2]
                S.dma("sp", z[:], self.QK["GZ"].t[hs, ts_], [self.QK["GZ"]], [z])
                S.act(sq[:], ot[:], AF.Square, [ot], [sq])
                S.mm(PS[4][:], self.ones_bf[:], sq[:], [self.ones_bf, sq], [PS[4]])
                S.act(ll[:], PS[4][:], AF.Ln, [PS[4]], [ll], bias=EPS, scale=1.0 / DH)
                S.act(ll[:], ll[:], AF.Exp, [ll], [ll], scale=-0.5)
                go = CV["gdn_onorm"]
                S.stt(t32[:], ot[:], self.cvt[:, go:go + 1], ll[:], ALU.mult, ALU.mult, [ot, self.cvt, ll], [t32])
                o = ob[blk % 2]
                S.tt("pool", o[:], t32[:], z[:], ALU.mult, [t32, z], [o])
                S.dma("sp", self.Y["YB"].t[hs, ts_], o[:], [o], [self.Y["YB"]])

            act = {}

            def adv(r, nops):
                if r >= NP:
                    return True
                if r not in act:
                    act[r] = pre(r)
                g_ = act[r]
                if g_ is None:
                    return True
                for _ in range(nops):
                    try:
                        next(g_)
                    except StopIteration:
                        act[r] = None
                        return True
                return False

            while not adv(0, 1000):
                pass
            for r in range(NP):
                gs = scan(r)
                sdone = False
                while True:
                    d1 = adv(r + 1, 2)
                    adv(r + 2, 2)
                    adv(r + 3, 1)
                    adv(r + 4, 1)
                    if not sdone:
                        try:
                            next(gs)
                        except StopIteration:
                            sdone = True
                    if sdone and d1:
                        break
                if r % 4 == 3:
                    post(r // 4)
            S.barrier()

    def fox_all(self, nh):
        S, T, NG = self.S, self.T, self.NG
        NB = T // 128
        PS = self.PS
        with ExitStack() as es:
            sets = []
            for i in range(2):
                sets.append(dict(
                    Qf=self.sb(es, "fxQ", [128, T], BF16), Kf=self.sb(es, "fxK", [128, T], BF16),
                    Vt=self.sb(es, "fxV", [128, NB, 128], BF16), cbc=self.sb(es, "fxcbc", [128, T], F32),
                    c2=self.sb(es, "fxc2", [128, 128], F32), ncc=self.sb(es, "fxncc", [128, NB], F32)))
            tmp = [self.sb(es, "fxtmp%d" % i, [128, 512], F32) for i in range(2)]
            PT = [self.sb(es, "fxPT%d" % i, [128, 512], BF16) for i in range(3)]
            lnd = self.sb(es, "fxlnd", [128, 512], F32)
            ob = [self.sb(es, "fxob%d" % i, [128, 512], BF16) for i in range(2)]
            crow = self.ROW["CROW"]

            def load_dma(h):
                d = sets[h % 2]
                hs = slice(h * 128, (h + 1) * 128)
                S.dma("sp", d["c2"][0:NB, :], crow.t[h:h + 1, :].rearrange("o (n p) -> (o n) p", p=128), [crow], [d["c2"]])
                S.dma("sp", d["Qf"][:], self.QK["FQ"].t[hs, :], [self.QK["FQ"]], [d["Qf"]])
                S.dma("sp", d["Kf"][:], self.QK["FK"].t[hs, :], [self.QK["FK"]], [d["Kf"]])
                S.dma("sp", d["cbc"][:], crow.t[h:h + 1, :].partition_broadcast(128), [crow], [d["cbc"]])
                S.dma("sp", d["Vt"][:], self.VT["FV"].t.rearrange("(n p) c -> p n c", p=128)[:, :, hs], [self.VT["FV"]], [d["Vt"]])

            def load_fin(h):
                d = sets[h % 2]
                S.tr(PS[7][:, 0:NB], d["c2"][0:NB, :], self.cm32[0:NB, 0, 0:NB], [d["c2"], self.cm32], [PS[7]])
                S.ts("dve", d["ncc"][:], PS[7][:, 0:NB], -1.0, None, ALU.mult, None, [PS[7]], [d["ncc"]])

            def compute(h):
                d = sets[h % 2]
                Qf, Kf, Vt, cbc, ncc = d["Qf"], d["Kf"], d["Vt"], d["cbc"], d["ncc"]
                hs = slice(h * 128, (h + 1) * 128)
                tiles = []
                for g in range(NG):
                    tl = [(kb, 0) for kb in range(4 * g)] + [(4 * g + j, 128 * j) for j in range(4)]
                    for ti, (kb, c0) in enumerate(tl):
                        tiles.append((g, kb, c0, ti == 0, ti == len(tl) - 1))
                n = len(tiles)

                def emit_S(i):
                    g, kb, c0, first, last = tiles[i]
                    ps = PS[i % 3]
                    S.mm(ps[:, c0:512], Kf[:, kb * 128:(kb + 1) * 128], Qf[:, g * 512 + c0:(g + 1) * 512], [Kf, Qf], [ps])

                def emit_mid(i):
                    g, kb, c0, first, last = tiles[i]
                    ps = PS[i % 3]
                    tm = tmp[i % 2]
                    pt = PT[i % 3]
                    S.tt("dve", tm[:, c0:512], ps[:, c0:512], cbc[:, g * 512 + c0:(g + 1) * 512], ALU.add, [ps, cbc], [tm])
                    S.act(pt[:, c0:512], tm[:, c0:512], AF.Exp, [tm, ncc], [pt], bias=ncc[:, kb:kb + 1])
                    if kb >= 4 * g:
                        S.tt("pool", pt[:, c0:c0 + 128], pt[:, c0:c0 + 128], self.cmbf[:, 1, :], ALU.mult,
                             [pt, self.cmbf], [pt])

                def emit_PV(i):
                    g, kb, c0, first, last = tiles[i]
                    pt = PT[i % 3]
                    O = PS[3 + g % 2]
                    DN = PS[5 + g % 2]
                    S.mm(O[:, c0:512], Vt[:, kb, :], pt[:, c0:512], [Vt, pt], [O], start=first, stop=last)
                    S.mm(DN[:, c0:512], self.ones_bf[:], pt[:, c0:512], [self.ones_bf, pt], [DN], start=first, stop=last)
                    if last:
                        S.act(lnd[:], DN[:], AF.Ln, [DN], [lnd])
                        S.act(lnd[:], lnd[:], AF.Exp, [lnd], [lnd], scale=-1.0)
                        o = ob[g % 2]
                        S.tt("dve", o[:], O[:], lnd[:], ALU.mult, [O, lnd], [o])
                        S.dma("sp", self.Y["YA"].t[hs, g * 512:(g + 1) * 512], o[:], [o], [self.Y["YA"]])

                LA = 2
                for i in range(min(LA, n)):
                    emit_S(i)
                for i in range(n):
                    if i + LA < n:
                        emit_S(i + LA)
                    emit_mid(i)
                    emit_PV(i)
            load_dma(0)
            load_fin(0)
            for h in range(nh):
                if h + 1 < nh:
                    load_dma(h + 1)
                compute(h)
                if h + 1 < nh:
                    load_fin(h + 1)
            S.barrier()

    def sb_all(self, nh):
        S, T, NG = self.S, self.T, self.NG
        NB = T // 128
        PS = self.PS
        with ExitStack() as es:
            sets = []
            for i in range(2):
                sets.append(dict(
                    Qf=self.sb(es, "sbQ", [128, T], BF16), Kf=self.sb(es, "sbK", [128, T], BF16),
                    Vt=self.sb(es, "sbV", [128, NB, 128], BF16)))
            ntri = self.sb(es, "sbntri", [128, 128], BF16)
            nones = self.sb(es, "sbnones", [128, 128], BF16)
            zeros = self.sb(es, "sbzeros", [128, 128], BF16)
            E = [self.sb(es, "sbE%d" % i, [128, 512], F32) for i in range(2)]
            SP = [self.sb(es, "sbSP%d" % i, [128, 512], BF16) for i in range(3)]
            AT = [self.sb(es, "sbAT%d" % i, [128, 512], BF16) for i in range(3)]
            cum = [self.sb(es, "sbcum%d" % i, [128, 512], BF16) for i in range(2)]
            ob = [self.sb(es, "sbob%d" % i, [128, 512], BF16) for i in range(2)]
            S.ts("dve", ntri[:], self.cm32[:, 3, :], -1.0, None, ALU.mult, None, [self.cm32], [ntri])
            S.memset("dve", nones[:], -1.0, [nones])
            S.memset("dve", zeros[:], 0.0, [zeros])

            def load_dma(h):
                d = sets[h % 2]
                hs = slice(h * 128, (h + 1) * 128)
                S.dma("sp", d["Qf"][:], self.QK["SQ"].t[hs, :], [self.QK["SQ"]], [d["Qf"]])
                S.dma("sp", d["Kf"][:], self.QK["SK"].t[hs, :], [self.QK["SK"]], [d["Kf"]])
                S.dma("sp", d["Vt"][:], self.VT["SV"].t.rearrange("(n p) c -> p n c", p=128)[:, :, hs], [self.VT["SV"]], [d["Vt"]])

            def compute(h):
                d = sets[h % 2]
                Qf, Kf, Vt = d["Qf"], d["Kf"], d["Vt"]
                hs = slice(h * 128, (h + 1) * 128)
                tiles = []
                for g in range(NG):
                    tl = [(4 * g + j, 128 * j) for j in (3, 2, 1, 0)] + [(kb, 0) for kb in range(4 * g - 1, -1, -1)]
                    for ti, (kb, c0) in enumerate(tl):
                        tiles.append((g, kb, c0, ti, len(tl)))
                n = len(tiles)

                def emit_A(i):
                    g, kb, c0, ti, nt = tiles[i]
                    A = PS[i % 3]
                    S.mm(A[:, c0:512], Kf[:, kb * 128:(kb + 1) * 128], Qf[:, g * 512 + c0:(g + 1) * 512], [Kf, Qf], [A])

                def emit_sp(i):
                    g, kb, c0, ti, nt = tiles[i]
                    A = PS[i % 3]
                    e = E[i % 2]
                    sp = SP[i % 3]
                    S.act(e[:, c0:512], A[:, c0:512], AF.Exp, [A], [e])
                    S.act(sp[:, c0:512], e[:, c0:512], AF.Ln, [e], [sp], bias=1.0)
                    if kb >= 4 * g:
                        S.tt("pool", sp[:, c0:c0 + 128], sp[:, c0:c0 + 128], self.cmbf[:, 2, :], ALU.mult,
                             [sp, self.cmbf], [sp])

                def emit_B(i):
                    g, kb, c0, ti, nt = tiles[i]
                    B = PS[3 + i % 2]
                    sp = SP[i % 3]
                    cm = cum[g % 2]
                    O = PS[5 + g % 2]
                    q0, q1 = g * 512 + c0, (g + 1) * 512
                    kT = Kf[:, kb * 128:(kb + 1) * 128]
                    if ti == 0:
                        S.memset("pool", cm[:], 0.0, [cm])
                        S.mm(O[:], zeros[:], Qf[:, g * 512:(g + 1) * 512], [zeros, Qf], [O], start=True, stop=False)
                    S.mm(B[:, c0:512], kT, Qf[:, q0:q1], [Kf, Qf], [B], start=True, stop=False)
                    S.mm(B[:, c0:512], ntri[:], sp[:, c0:512], [ntri, sp], [B], start=False, stop=(ti == 0))
                    if ti > 0:
                        S.mm(B[:, c0:512], nones[:], cm[:, c0:512], [nones, cm], [B], start=False, stop=True)
                    if ti < nt - 1:
                        S.tt("dve", cm[:, c0:512], cm[:, c0:512], sp[:, c0:512], ALU.add, [cm, sp], [cm])

                def emit_at(i):
                    g, kb, c0, ti, nt = tiles[i]
                    B = PS[3 + i % 2]
                    at = AT[i % 3]
                    S.act(at[:, c0:512], B[:, c0:512], AF.Exp, [B], [at])
                    if kb >= 4 * g:
                        S.tt("pool", at[:, c0:c0 + 128], at[:, c0:c0 + 128], self.cmbf[:, 2, :], ALU.mult,
                             [at, self.cmbf], [at])

                def emit_O(i):
                    g, kb, c0, ti, nt = tiles[i]
                    at = AT[i % 3]
                    O = PS[5 + g % 2]
                    S.mm(O[:, c0:512], Vt[:, kb, :], at[:, c0:512], [Vt, at], [O], start=False, stop=(ti == nt - 1))
                    if ti == nt - 1:
                        o = ob[g % 2]
                        S.copy("dve", o[:], O[:], [O], [o])
                        S.dma("sp", self.Y["YC"].t[hs, g * 512:(g + 1) * 512], o[:], [o], [self.Y["YC"]])

                emit_A(0)
                if n > 1:
                    emit_A(1)
                emit_sp(0)
                for i in range(n):
                    if i + 2 < n:
                        emit_A(i + 2)
                    if i + 1 < n:
                        emit_sp(i + 1)
                    emit_B(i)
                    emit_at(i)
                    if i >= 1:
                        emit_O(i - 1)
                emit_O(n - 1)
            load_dma(0)
            for h in range(nh):
                if h + 1 < nh:
                    load_dma(h + 1)
                compute(h)
            S.barrier()


def host_consts():
    p = np.arange(128)[:, None]
    f = np.arange(128)[None, :]
    cm = np.zeros((128, 9, 128), np.float32)
    cm[:, 0, :] = (p == f)
    cm[:, 1, :] = (f >= p)
    cm[:, 2, :] = (f > p)
    cm[:, 3, :] = (p >= f)
    same = (p // 64) == (f // 64)
    cm[:, 4, :] = (p <= f) & same
    cm[:, 5, :] = (p > f) & same
    cm[:, 6, :] = np.where((p > f) & same, 0.0, NEG)
    cm[:, 7, :] = np.where((f > p) & same, 0.0, NEG)
    cm[:, 8, :] = np.where((f >= p) & same, 0.0, NEG)
    return cm


def pack_cv(inp, L):
    cv = np.zeros((L, 128, NCV), np.float32)

    def chunks(v):
        return np.ascontiguousarray(v.reshape(-1, 128).T)
    for l in range(L):
        c = cv[l]
        c[:, CV["norm_mix"]:CV["norm_mix"] + 8] = chunks(inp["norm_mix"][l])
        c[:, CV["gate_bias"]:CV["gate_bias"] + 24] = chunks(inp["gate_bias"][l])
        c[:, CV["fox_qnorm"]] = inp["fox_qnorm"][l]
        c[:, CV["fox_knorm"]] = inp["fox_knorm"][l]
        gc = inp["gdn_conv"][l]
        for tap in range(4):
            c[:, CV["gdn_conv"] + tap * 12: CV["gdn_conv"] + (tap + 1) * 12] = chunks(gc[tap])
        c[:, CV["gdn_onorm"]] = inp["gdn_onorm"][l]
        c[:, CV["norm_xq"]:CV["norm_xq"] + 8] = chunks(inp["norm_xq"][l])
        c[:, CV["norm_mem"]:CV["norm_mem"] + 8] = chunks(inp["norm_mem"][l])
        c[:, CV["mq_norm"]] = inp["mq_norm"][l]
        c[:, CV["mk_norm"]] = inp["mk_norm"][l]
        c[:, CV["norm_ffn"]:CV["norm_ffn"] + 8] = chunks(inp["norm_ffn"][l])
        fc = inp["ffn_conv"][l]
        for tap in range(3):
            c[:, CV["ffn_conv"] + tap * 44: CV["ffn_conv"] + (tap + 1) * 44] = chunks(fc[tap])
        c[:, CV["ffn_conv_b"]:CV["ffn_conv_b"] + 44] = chunks(inp["ffn_conv_b"][l])
        s = CV["small"]
        c[0:4, s] = inp["fox_fbias"][l]
        c[64:68, s + 1] = inp["gdn_dt_bias"][l]
        c[64:68, s + 2] = inp["gdn_a_log"][l]
    return cv


def prep_inputs(inp, b, T, L):
    m = {}
    m["xT"] = np.ascontiguousarray(inp["x"][b, :T].T)
    m["memT"] = np.ascontiguousarray(inp["mem"][b].T)
    for n in ("w_in", "w_oa", "w_ob", "w_oc", "w_out", "w_mq", "w_mkv", "w_mo", "w_up", "w_down"):
        m[n] = np.ascontiguousarray(inp[n][:L])
    m["cv"] = pack_cv(inp, L)
    m["cmask"] = host_consts()
    return m


_CACHE = {}


def kernel(**inputs):
    inp = {k: np.asarray(v) for k, v in inputs.items()}
    B, T, _ = inp["x"].shape
    L = inp["w_in"].shape[0]
    key = (T, L)
    nc = Builder(T, L).build()
    ncores = 8
    in_maps = [prep_inputs(inp, c % B, T, L) for c in range(B)]
    in_maps = [in_maps[c % B] for c in range(ncores)]
    res = run_bass_kernel_spmd(nc, in_maps, core_ids=list(range(ncores)))
    out = np.empty((B, T, D), np.float32)
    for b in range(B):
        out[b] = np.asarray(res.results[b]["yT"]).T
    return out
```
